# Optimizing a Trainium2 kernel written in Bass

```python
import math
import jax, jax.numpy as jnp
from jax import lax
import numpy as np


D_MODEL = 1024
BATCH = 8
SEQ = 4096
DEPTH = 2

N_HEADS = 8
HEAD_DIM = 128
N_KV_HEADS = 2
Q_PER_KV = N_HEADS // N_KV_HEADS
ATTN_WIDTH = N_HEADS * HEAD_DIM
KV_WIDTH = N_KV_HEADS * HEAD_DIM
IDX_HEADS = 8
IDX_DIM = 64
TOPK_MAX = 256
Q_BLOCK = 128
ROPE_THETA = 500000.0
ROPE_FRACTION = 4

SSD_EXPAND = 2
SSD_INNER = SSD_EXPAND * D_MODEL
SSD_HEAD_DIM = 64
SSD_HEADS = SSD_INNER // SSD_HEAD_DIM
SSD_GROUPS = 4
SSD_HEADS_PER_GROUP = SSD_HEADS // SSD_GROUPS
SSD_STATE = 128
SSD_CONV = 4
SSD_CHUNK = 128
SSD_CONV_DIM = SSD_INNER + 2 * SSD_GROUPS * SSD_STATE

FFN_DIM = 2816
FFN_CONV = 3
NORM_EPS = 1e-6

IN_SPLITS = (ATTN_WIDTH, KV_WIDTH, KV_WIDTH, IDX_HEADS * IDX_DIM, IDX_DIM, IDX_HEADS,
             SSD_INNER, SSD_CONV_DIM, SSD_HEADS, D_MODEL, D_MODEL)
IN_COLS = (ATTN_WIDTH + 2 * KV_WIDTH + IDX_HEADS * IDX_DIM + IDX_DIM + IDX_HEADS
           + SSD_INNER + SSD_CONV_DIM + SSD_HEADS + 2 * D_MODEL)

kernel_name = 'hybrid_dsa_ssd_convffn'


def rms_norm(x, w):
    xf = x.astype(jnp.float32)
    y = xf * lax.rsqrt(jnp.mean(xf * xf, axis=-1, keepdims=True) + NORM_EPS)
    return (y * w.astype(jnp.float32)).astype(x.dtype)


def rope_tables(seq, rot_dim):
    inv = ROPE_THETA ** (-jnp.arange(0, rot_dim, 2, dtype=jnp.float32) / rot_dim)
    ang = jnp.arange(seq, dtype=jnp.float32)[:, None] * inv[None, :]
    return jnp.cos(ang), jnp.sin(ang)


def apply_partial_rope(t, cos, sin):
    half = cos.shape[-1]
    rot = 2 * half
    c = cos[None, :, None, :]
    s = sin[None, :, None, :]
    tf = t[..., :rot].astype(jnp.float32)
    x1, x2 = tf[..., :half], tf[..., half:]
    r = jnp.concatenate([x1 * c - x2 * s, x2 * c + x1 * s], axis=-1).astype(t.dtype)
    return jnp.concatenate([r, t[..., rot:]], axis=-1)


def causal_dwconv(u, w, b):
    width = w.shape[0]
    seq = u.shape[1]
    up = jnp.pad(u, ((0, 0), (width - 1, 0), (0, 0)))
    out = b
    for j in range(width):
        out = out + up[:, j:j + seq] * w[j]
    return out


def dsa_attention(q, k, v, q_idx, k_idx, w_idx):
    bsz, seq = q.shape[0], q.shape[1]
    topk = min(TOPK_MAX, seq // 4)
    n_blk = seq // Q_BLOCK
    kv = jnp.concatenate([k, v], axis=-1)
    k_idx_f = k_idx.astype(jnp.float32)
    kpos = jnp.arange(seq)
    scale = HEAD_DIM ** -0.5

    def block(i):
        t0 = i * Q_BLOCK
        sl = lambda a: lax.dynamic_slice_in_dim(a, t0, Q_BLOCK, axis=1)
        qpos = t0 + jnp.arange(Q_BLOCK)
        qi = sl(q_idx).astype(jnp.float32)
        wi = sl(w_idx).astype(jnp.float32)
        rel = jax.nn.relu(jnp.einsum('bqhd,bsd->bqhs', qi, k_idx_f))
        score = jnp.einsum('bqhs,bqh->bqs', rel, wi)
        causal = kpos[None, :] <= qpos[:, None]
        score = jnp.where(causal[None], score, -jnp.inf)
        _, sel = lax.top_k(score, topk)
        valid = sel <= qpos[None, :, None]
        kv_sel = jax.vmap(lambda a, ix: a[ix])(kv, sel)
        k_sel, v_sel = kv_sel[..., :HEAD_DIM], kv_sel[..., HEAD_DIM:]
        qb = sl(q).reshape(bsz, Q_BLOCK, N_KV_HEADS, Q_PER_KV, HEAD_DIM)
        logits = jnp.einsum('bqhgd,bqkhd->bqhgk', qb, k_sel).astype(jnp.float32) * scale
        logits = jnp.where(valid[:, :, None, None, :], logits, -jnp.inf)
        p = jax.nn.softmax(logits, axis=-1).astype(v.dtype)
        o = jnp.einsum('bqhgk,bqkhd->bqhgd', p, v_sel)
        return o.reshape(bsz, Q_BLOCK, ATTN_WIDTH)

    out = lax.map(block, jnp.arange(n_blk))
    return out.transpose(1, 0, 2, 3).reshape(bsz, seq, ATTN_WIDTH)


def ssd_scan(xdt, adt, bm, cm):
    bsz, seq = xdt.shape[0], xdt.shape[1]
    nc = seq // SSD_CHUNK
    X = xdt.reshape(bsz, nc, SSD_CHUNK, SSD_GROUPS, SSD_HEADS_PER_GROUP, SSD_HEAD_DIM)
    Bc = bm.reshape(bsz, nc, SSD_CHUNK, SSD_GROUPS, SSD_STATE)
    Cc = cm.reshape(bsz, nc, SSD_CHUNK, SSD_GROUPS, SSD_STATE)
    A = adt.reshape(bsz, nc, SSD_CHUNK, SSD_GROUPS, SSD_HEADS_PER_GROUP).transpose(0, 3, 4, 1, 2)
    A_cs = jnp.cumsum(A, axis=-1)
    tril = jnp.tril(jnp.ones((SSD_CHUNK, SSD_CHUNK), dtype=bool))
    seg = A_cs[..., :, None] - A_cs[..., None, :]
    Lmat = jnp.exp(jnp.where(tril, seg, -jnp.inf))
    CB = jnp.einsum('bclgn,bcsgn->bcgls', Cc, Bc)
    y_diag = jnp.einsum('bcgls,bgecls,bcsgep->bclgep', CB, Lmat, X)
    decay_states = jnp.exp(A_cs[..., -1:] - A_cs)
    states = jnp.einsum('bclgn,bgecl,bclgep->bcgepn', Bc, decay_states, X)
    chunk_decay = jnp.exp(A_cs[..., -1])

    def step(h, inp):
        s_c, d_c = inp
        return h * d_c[..., None, None] + s_c, h

    h0 = jnp.zeros(states.shape[:1] + states.shape[2:], states.dtype)
    _, states_in = lax.scan(step, h0, (states.transpose(1, 0, 2, 3, 4, 5), chunk_decay.transpose(3, 0, 1, 2)))
    states_in = states_in.transpose(1, 0, 2, 3, 4, 5)
    y_off = jnp.einsum('bclgn,bcgepn,bgecl->bclgep', Cc, states_in, jnp.exp(A_cs))
    return (y_diag + y_off).reshape(bsz, seq, SSD_HEADS, SSD_HEAD_DIM)


def mamba2_branch(z, xbc, dt_raw, conv_w, conv_b, dt_bias, a_log, d_skip, norm_w):
    bsz, seq = z.shape[0], z.shape[1]
    xbc = jax.nn.silu(causal_dwconv(xbc, conv_w, conv_b))
    gn = SSD_GROUPS * SSD_STATE
    xs = xbc[..., :SSD_INNER].reshape(bsz, seq, SSD_HEADS, SSD_HEAD_DIM)
    bm = xbc[..., SSD_INNER:SSD_INNER + gn].reshape(bsz, seq, SSD_GROUPS, SSD_STATE)
    cm = xbc[..., SSD_INNER + gn:].reshape(bsz, seq, SSD_GROUPS, SSD_STATE)
    dt = jax.nn.softplus(dt_raw.astype(jnp.float32) + dt_bias.astype(jnp.float32))
    A = -jnp.exp(a_log.astype(jnp.float32))
    y = ssd_scan(xs * dt[..., None], dt * A, bm, cm) + xs * d_skip[:, None]
    y = y.reshape(bsz, seq, SSD_INNER) * jax.nn.silu(z.astype(jnp.float32))
    yg = y.astype(jnp.float32).reshape(bsz, seq, SSD_GROUPS, SSD_INNER // SSD_GROUPS)
    yg = yg * lax.rsqrt(jnp.mean(yg * yg, axis=-1, keepdims=True) + NORM_EPS)
    y = yg.reshape(bsz, seq, SSD_INNER) * norm_w.astype(jnp.float32)
    return y.astype(z.dtype)


def hybrid_mixer(h, w_in, ssd_conv_w, ssd_conv_b, ssd_dt_bias, ssd_a_log, ssd_d, ssd_norm_w,
                 w_proj_attn, w_proj_ssd, w_out):
    bsz, seq = h.shape[0], h.shape[1]
    proj = h @ w_in
    offsets = np.cumsum(np.array(IN_SPLITS))[:-1].tolist()
    (q, k, v, qi, ki, wi, z, xbc, dt_raw, g_attn, g_ssd) = jnp.split(proj, offsets, axis=-1)
    cos_a, sin_a = rope_tables(seq, HEAD_DIM // ROPE_FRACTION)
    cos_i, sin_i = rope_tables(seq, IDX_DIM // ROPE_FRACTION)
    q = apply_partial_rope(q.reshape(bsz, seq, N_HEADS, HEAD_DIM), cos_a, sin_a)
    k = apply_partial_rope(k.reshape(bsz, seq, N_KV_HEADS, HEAD_DIM), cos_a, sin_a)
    v = v.reshape(bsz, seq, N_KV_HEADS, HEAD_DIM)
    qi = apply_partial_rope(qi.reshape(bsz, seq, IDX_HEADS, IDX_DIM), cos_i, sin_i)
    ki = apply_partial_rope(ki.reshape(bsz, seq, 1, IDX_DIM), cos_i, sin_i)[:, :, 0]
    a = dsa_attention(q, k, v, qi, ki, wi)
    b = mamba2_branch(z, xbc, dt_raw, ssd_conv_w, ssd_conv_b, ssd_dt_bias, ssd_a_log, ssd_d, ssd_norm_w)
    merged = jax.nn.sigmoid(g_attn) * (a @ w_proj_attn) + jax.nn.sigmoid(g_ssd) * (b @ w_proj_ssd)
    return merged @ w_out


def conv_glu_ffn(h, w_up, conv_w, conv_b, w_down):
    u = causal_dwconv(h @ w_up, conv_w, conv_b)
    gate, val = u[..., :FFN_DIM], u[..., FFN_DIM:]
    return (jax.nn.silu(gate) * val) @ w_down


def setup_inputs(seed: int = 0) -> dict:
    key = jax.random.key(seed)
    ks = jax.random.split(key, 20)
    nrm = lambda k, shape, s: jax.random.normal(k, shape, jnp.float32) * s
    x = nrm(ks[0], (BATCH, SEQ, D_MODEL), 1.0)
    norm_mix_w = 1.0 + nrm(ks[1], (DEPTH, D_MODEL), 0.02)
    w_in = nrm(ks[2], (DEPTH, D_MODEL, IN_COLS), D_MODEL ** -0.5)
    ssd_conv_w = nrm(ks[3], (DEPTH, SSD_CONV, SSD_CONV_DIM), SSD_CONV ** -0.5)
    ssd_conv_b = nrm(ks[4], (DEPTH, SSD_CONV_DIM), 0.01)
    u = jax.random.uniform(ks[5], (DEPTH, SSD_HEADS), jnp.float32)
    dt0 = jnp.exp(u * (math.log(0.1) - math.log(0.001)) + math.log(0.001))
    ssd_dt_bias = dt0 + jnp.log(-jnp.expm1(-dt0))
    ssd_a_log = jnp.log(jax.random.uniform(ks[6], (DEPTH, SSD_HEADS), jnp.float32, minval=1.0, maxval=16.0))
    ssd_d = 1.0 + nrm(ks[7], (DEPTH, SSD_HEADS), 0.1)
    ssd_norm_w = 1.0 + nrm(ks[8], (DEPTH, SSD_INNER), 0.02)
    w_proj_attn = nrm(ks[9], (DEPTH, ATTN_WIDTH, D_MODEL), ATTN_WIDTH ** -0.5)
    w_proj_ssd = nrm(ks[10], (DEPTH, SSD_INNER, D_MODEL), SSD_INNER ** -0.5)
    w_out = nrm(ks[11], (DEPTH, D_MODEL, D_MODEL), D_MODEL ** -0.5)
    norm_ffn_w = 1.0 + nrm(ks[12], (DEPTH, D_MODEL), 0.02)
    ffn_w_up = nrm(ks[13], (DEPTH, D_MODEL, 2 * FFN_DIM), D_MODEL ** -0.5)
    ffn_conv_w = nrm(ks[14], (DEPTH, FFN_CONV, 2 * FFN_DIM), FFN_CONV ** -0.5)
    ffn_conv_b = nrm(ks[15], (DEPTH, 2 * FFN_DIM), 0.01)
    ffn_w_down = nrm(ks[16], (DEPTH, FFN_DIM, D_MODEL), FFN_DIM ** -0.5)
    norm_final_w = 1.0 + nrm(ks[17], (D_MODEL,), 0.02)
    return {'x': x, 'norm_mix_w': norm_mix_w, 'w_in': w_in, 'ssd_conv_w': ssd_conv_w,
            'ssd_conv_b': ssd_conv_b, 'ssd_dt_bias': ssd_dt_bias, 'ssd_a_log': ssd_a_log,
            'ssd_d': ssd_d, 'ssd_norm_w': ssd_norm_w, 'w_proj_attn': w_proj_attn,
            'w_proj_ssd': w_proj_ssd, 'w_out': w_out, 'norm_ffn_w': norm_ffn_w,
            'ffn_w_up': ffn_w_up, 'ffn_conv_w': ffn_conv_w, 'ffn_conv_b': ffn_conv_b,
            'ffn_w_down': ffn_w_down, 'norm_final_w': norm_final_w}


def reference(x, norm_mix_w, w_in, ssd_conv_w, ssd_conv_b, ssd_dt_bias, ssd_a_log, ssd_d, ssd_norm_w,
              w_proj_attn, w_proj_ssd, w_out, norm_ffn_w, ffn_w_up, ffn_conv_w, ffn_conv_b,
              ffn_w_down, norm_final_w):
    for l in range(DEPTH):
        h = rms_norm(x, norm_mix_w[l])
        x = x + hybrid_mixer(h, w_in[l], ssd_conv_w[l], ssd_conv_b[l], ssd_dt_bias[l], ssd_a_log[l],
                             ssd_d[l], ssd_norm_w[l], w_proj_attn[l], w_proj_ssd[l], w_out[l])
        h = rms_norm(x, norm_ffn_w[l])
        x = x + conv_glu_ffn(h, ffn_w_up[l], ffn_conv_w[l], ffn_conv_b[l], ffn_w_down[l])
    return rms_norm(x, norm_final_w)
```

```python
import contextlib
import numpy as np
import concourse.bass as bass
import concourse.mybir as mybir
from concourse.bass_utils import run_bass_kernel_spmd

F32 = mybir.dt.float32
BF16 = mybir.dt.bfloat16
AF = mybir.ActivationFunctionType
OP = mybir.AluOpType
AX = mybir.AxisListType

D = 1024
NH, HD, NKV = 8, 128, 2
IH, IDM = 8, 64
SSD_INNER, SSD_HD, SSD_H, SSD_G, SSD_N = 2048, 64, 32, 4, 128
CONV_DIM = 3072
FFN = 2816
EPS = 1e-6
IN_COLS = 9320
O_Q, O_K, O_V, O_QI, O_KI, O_WI, O_Z, O_XBC, O_DT, O_GA, O_GB = (
    0, 1024, 1280, 1536, 2048, 2112, 2120, 4168, 7240, 7272, 8296)
T = 512
NIT = 22

ENGS = ["pe", "act", "dve", "pool", "sp"]
NDMASEM = 24


class Prog:
    def __init__(self, nc):
        self.nc = nc
        self.ops = {e: [] for e in ENGS}
        self.last_w = {}
        self.readers = {}
        self.ndma = 0
        self.last_real = {}
        self.ps_last = {}

    @staticmethod
    def _isps(k):
        return isinstance(k, str) and k.startswith("ps") and k[2:].isdigit()

    def _deps(self, eng, r, w):
        deps = []
        for k in list(r) + list(w):
            if self._isps(k):
                is_w = k in w
                last = self.ps_last.get(k)
                if last is not None:
                    ref, lw = last
                    same = (ref[0] == "eng" and ref[1] == eng)
                    if (not same) or is_w or lw:
                        deps.append(ref)
        r = [k for k in r if not self._isps(k)]
        w = [k for k in w if not self._isps(k)]
        for k in r:
            lw = self.last_w.get(k)
            if lw is not None:
                deps.append(lw)
        for k in w:
            lw = self.last_w.get(k)
            if lw is not None:
                deps.append(lw)
            deps.extend(self.readers.get(k, ()))
        out = []
        for d in deps:
            if d[0] == "eng" and d[1] == "pe" and eng == "pe":
                continue
            if d not in out:
                out.append(d)
        return out

    def _commit(self, ref, r, w):
        for k in list(r) + list(w):
            if self._isps(k):
                self.ps_last[k] = (ref, k in w)
        r = [k for k in r if not self._isps(k)]
        w = [k for k in w if not self._isps(k)]
        for k in r:
            self.readers.setdefault(k, []).append(ref)
        for k in w:
            self.last_w[k] = ref
            self.readers[k] = []

    def _mark(self, deps):
        for d in deps:
            if d[0] == "eng":
                self.ops[d[1]][d[2]]["sig"] = True

    def op(self, eng, fn, r=(), w=()):
        deps = self._deps(eng, r, w)
        self._mark(deps)
        idx = len(self.ops[eng])
        self.ops[eng].append(dict(fn=fn, deps=deps, sig=False, dma=None))
        self.last_real[eng] = idx
        self._commit(("eng", eng, idx), r, w)

    def dma(self, out, in_, r=(), w=(), q="sp"):
        deps = self._deps(q, r, w)
        self._mark(deps)
        i = self.ndma
        self.ndma += 1
        if i >= NDMASEM:
            deps.append(("dma", i - NDMASEM))
        self.ops[q].append(dict(fn=lambda e: e.dma_start(out=out, in_=in_), deps=deps, sig=False, dma=i))
        self._commit(("dma", i), r, w)

    def barrier(self):
        deps = [("eng", e, i) for e, i in self.last_real.items()]
        deps += [("dma", i) for i in range(max(0, self.ndma - NDMASEM), self.ndma)]
        self._mark(deps)
        for e in ENGS:
            self.ops[e].append(dict(fn=None, deps=[d for d in deps if not (d[0] == "eng" and d[1] == e)],
                                    sig=False, dma=None))
        self.last_w.clear()
        self.readers.clear()
        self.ps_last.clear()

    def simulate(self):
        sigcnt = {}
        for e in ENGS:
            c = 0
            arr = []
            for o in self.ops[e]:
                if o["sig"]:
                    c += 1
                arr.append(c)
            sigcnt[e] = arr
        sem = {e: 0 for e in ENGS}
        dsem = [0] * NDMASEM
        ptr = {e: 0 for e in ENGS}
        progress = True
        while progress:
            progress = False
            for e in ENGS:
                while ptr[e] < len(self.ops[e]):
                    o = self.ops[e][ptr[e]]
                    ok = True
                    for d in o["deps"]:
                        if d[0] == "eng":
                            if sem[d[1]] < sigcnt[d[1]][d[2]]:
                                ok = False
                        else:
                            if dsem[d[1] % NDMASEM] < 16 * (d[1] // NDMASEM + 1):
                                ok = False
                    if not ok:
                        break
                    if o["dma"] is not None:
                        dsem[o["dma"] % NDMASEM] += 16
                    elif o["sig"] and o["fn"] is not None:
                        sem[e] += 1
                    ptr[e] += 1
                    progress = True
        stuck = {e: (ptr[e], len(self.ops[e])) for e in ENGS if ptr[e] < len(self.ops[e])}
        if stuck:
            for e in stuck:
                o = self.ops[e][ptr[e]]
                print("STUCK", e, ptr[e], o["deps"], "sig", o["sig"], "fn", o["fn"] is not None)
            raise RuntimeError("deadlock in semaphore protocol: %r" % stuck)
        print("simulate ok:", {e: len(self.ops[e]) for e in ENGS}, "dmas", self.ndma)

    def emit(self):
        nc = self.nc
        self.simulate()
        with contextlib.ExitStack() as st:
            esem = {e: st.enter_context(nc.semaphore("s_" + e)) for e in ENGS}
            dsem = [st.enter_context(nc.semaphore("d_%d" % i)) for i in range(NDMASEM)]
            block = st.enter_context(nc.Block())
            sigcnt = {}
            for e in ENGS:
                c = 0
                arr = []
                for o in self.ops[e]:
                    if o["sig"]:
                        c += 1
                    arr.append(c)
                sigcnt[e] = arr
            ndma = self.ndma

            def run(e, engobj):
                waited = {}
                for o in self.ops[e]:
                    for d in o["deps"]:
                        if d[0] == "eng":
                            sem = esem[d[1]]
                            val = sigcnt[d[1]][d[2]]
                            key = "e" + d[1]
                        else:
                            sem = dsem[d[1] % NDMASEM]
                            val = 16 * (d[1] // NDMASEM + 1)
                            key = "d%d" % (d[1] % NDMASEM)
                        if waited.get(key, 0) >= val:
                            continue
                        engobj.wait_ge(sem, val)
                        waited[key] = val
                    if o["fn"] is None:
                        continue
                    ins = o["fn"](engobj)
                    if o["dma"] is not None:
                        ins.then_inc(dsem[o["dma"] % NDMASEM], 16)
                    elif o["sig"]:
                        ins.then_inc(esem[e], 1)
                if e == "sp":
                    for s in range(min(NDMASEM, ndma)):
                        n = (ndma - 1 - s) // NDMASEM + 1
                        engobj.wait_ge(dsem[s], 16 * n)

            @block.sync
            def _(eng):
                run("sp", eng)

            @block.scalar
            def _(eng):
                run("act", eng)

            @block.vector
            def _(eng):
                run("dve", eng)

            @block.gpsimd
            def _(eng):
                run("pool", eng)

            @block.tensor
            def _(eng):
                run("pe", eng)


CST_COLS = {}


def _cst_layout():
    off = 0
    lay = {}
    for name, n in [("ident", 128), ("tri_le", 128), ("caus", 128), ("sgt", 128), ("ones", 128),
                    ("pow2", NIT + 2)]:
        lay[name] = (off, n)
        off += n
    return lay, off


def _lc_layout():
    off = 0
    lay = {}
    for name, n in [("nmix", 8), ("nffn", 8), ("snw", 16), ("xcw", 96), ("xcb", 24), ("fcw", 132),
                    ("fcb", 44), ("dtb", 32), ("alog", 32), ("dsk", 32)]:
        lay[name] = (off, n)
        off += n
    return lay, off


def host_consts(S):
    lay, n = _cst_layout()
    c = np.zeros((128, n), np.float32)
    i = np.arange(128)
    c[:, lay["ident"][0]:lay["ident"][0] + 128] = np.eye(128, dtype=np.float32)
    c[:, lay["tri_le"][0]:lay["tri_le"][0] + 128] = (i[:, None] <= i[None, :]).astype(np.float32)
    c[:, lay["caus"][0]:lay["caus"][0] + 128] = np.where(i[None, :] <= i[:, None], 0.0, -1e30).astype(np.float32)
    c[:, lay["sgt"][0]:lay["sgt"][0] + 128] = (i[:, None] > i[None, :]).astype(np.float32)
    c[:, lay["ones"][0]:lay["ones"][0] + 128] = 1.0
    c[:, lay["pow2"][0]:lay["pow2"][0] + NIT + 2] = (0.5 ** np.arange(1, NIT + 3))[None, :]
    nch = S // 128

    def tab(rot):
        inv = (500000.0 ** (-np.arange(0, rot, 2, dtype=np.float32) / rot)).astype(np.float32)
        ang = np.arange(S, dtype=np.float32)[:, None] * inv[None, :]
        return np.cos(ang).astype(np.float32), np.sin(ang).astype(np.float32)

    ca, sa = tab(32)
    ci, si = tab(16)
    tb = np.concatenate([ca, sa, ci, si], axis=1).reshape(nch, 128, 48).transpose(1, 0, 2)
    return c, np.ascontiguousarray(tb.reshape(128, nch * 48))


def host_layer_consts(inp, l):
    lay, n = _lc_layout()
    c = np.zeros((128, n), np.float32)

    def put(name, arr):
        o, m = lay[name]
        c[:, o:o + m] = arr.reshape(128, m)

    put("nmix", np.asarray(inp["norm_mix_w"][l]).reshape(8, 128).T)
    put("nffn", np.asarray(inp["norm_ffn_w"][l]).reshape(8, 128).T)
    put("snw", np.asarray(inp["ssd_norm_w"][l]).reshape(16, 128).T)
    put("xcw", np.asarray(inp["ssd_conv_w"][l]).reshape(4, 24, 128).transpose(2, 1, 0))
    put("xcb", np.asarray(inp["ssd_conv_b"][l]).reshape(24, 128).T)
    put("fcw", np.asarray(inp["ffn_conv_w"][l]).reshape(3, 44, 128).transpose(2, 1, 0))
    put("fcb", np.asarray(inp["ffn_conv_b"][l]).reshape(44, 128).T)
    put("dtb", np.broadcast_to(np.asarray(inp["ssd_dt_bias"][l])[None, :], (128, 32)))
    put("alog", np.broadcast_to(np.asarray(inp["ssd_a_log"][l])[None, :], (128, 32)))
    put("dsk", np.broadcast_to(np.asarray(inp["ssd_d"][l])[None, :], (128, 32)))
    return c


def build(S, layers=(0, 1), final=True, dbg=(), stop=99):
    nc = bass.Bass("TRN2", target_bir_lowering=False)
    NCH = S // 128
    NST = S // T
    TOPK = min(256, S // 4)
    L = 2
    dt_in = lambda name, shape: nc.dram_tensor(name, list(shape), F32, kind="ExternalInput").ap()
    x_d = dt_in("x", (S, D))
    w_in_d = dt_in("w_in", (L, D, IN_COLS))
    w_pa_d = dt_in("w_proj_attn", (L, D, D))
    w_pb_d = dt_in("w_proj_ssd", (L, SSD_INNER, D))
    w_out_d = dt_in("w_out", (L, D, D))
    w_up_d = dt_in("ffn_w_up", (L, D, 2 * FFN))
    w_dn_d = dt_in("ffn_w_down", (L, FFN, D))
    clay, ncst = _cst_layout()
    llay, nlc = _lc_layout()
    cst_d = dt_in("cst", (128, ncst))
    tab_d = dt_in("tab", (128, NCH * 48))
    lc_d = dt_in("lc", (L, 128, nlc))
    nfin_d = dt_in("nfin", (128, D))
    out_d = nc.dram_tensor("out", [S, D], F32, kind="ExternalOutput").ap()
    xmid_d = nc.dram_tensor("xmid", [S, D], F32, kind="Internal").ap()
    xl1_d = nc.dram_tensor("xl1", [S, D], F32, kind="Internal").ap()
    dbg_d = {}

    P = Prog(nc)
    st = contextlib.ExitStack()
    sb = lambda name, shape, dt: st.enter_context(nc.sbuf_tensor("sb_" + name, list(shape), dt))

    kT = sb("kT", (128, NKV, S), BF16)
    Vc = sb("Vc", (128, NCH, NKV, 129), BF16)
    kiT2 = sb("kiT2", (128, S), BF16)
    hst = sb("hst", (128, SSD_INNER), F32)
    wst = [sb("wst%d" % i, (128, 4, 512), F32) for i in range(2)]
    wbf = [sb("wbf%d" % i, (128, 8, 512), BF16) for i in range(2)]
    hT = sb("hT", (128, 8, T), BF16)
    aT = sb("aT", (128, 8, T), BF16)
    bT = sb("bT", (128, 16, T), BF16)
    cst = sb("cst", (128, ncst), F32)
    lcs = sb("lcs", (128, nlc), F32)
    tabs = sb("tabs", (128, 4, 48), F32)
    identb = sb("identb", (128, 128), BF16)
    trib = sb("trib", (128, 128), F32)
    xtail = sb("xtail", (128, 24, 3), F32)
    ftail = sb("ftail", (128, 44, 2), F32)
    negA = sb("negA", (128, 32), F32)
    epsT = sb("epsT", (128, 1), F32)
    negb = sb("negb", (128, 1), F32)
    AR_BYTES = 66 * 1024
    arena = sb("arena", (128, AR_BYTES // 4), F32)
    psum = st.enter_context(nc.psum_tensor("psum", [128, 8, 512], F32))

    def cv(name, j0=0, j1=None):
        o, n = clay[name]
        j1 = n if j1 is None else j1
        return cst[:, o + j0:o + j1]

    def lv(name, j0=0, j1=None):
        o, n = llay[name]
        j1 = n if j1 is None else j1
        return lcs[:, o + j0:o + j1]

    class View:
        pass

    def aview(off, shape, dt):
        n = int(np.prod(shape[1:]))
        esz = 4 if dt == F32 else 2
        assert off % 4 == 0 and off + n * esz <= AR_BYTES, (off, shape)
        nf = (n * esz + 3) // 4
        ap = arena[:, off // 4: off // 4 + nf]
        if dt != F32:
            ap = ap.bitcast(dt)
        if len(shape) == 3:
            ap = ap.rearrange("p (a b) -> p a b", a=shape[1])
        elif len(shape) == 4:
            ap = ap.rearrange("p (a b c) -> p a b c", a=shape[1], b=shape[2])
        return ap

    def psb(b):
        return psum[:, b, :].bitcast(BF16)

    def bc(ap, shape):
        return ap.to_broadcast(list(shape))

    wcnt = [0]
    hcnt = [0]

    def load_slab(wd, r0, nk, c0, ncols):
        slot = wcnt[0] % 2
        wcnt[0] += 1
        for h0 in range(0, nk, 4):
            hn_ = min(4, nk - h0)
            hs = hcnt[0] % 2
            hcnt[0] += 1
            src = wd[r0 + h0 * 128: r0 + (h0 + hn_) * 128, c0:c0 + ncols].rearrange("(k p) n -> p k n", p=128)
            P.dma(wst[hs][:, 0:hn_, 0:ncols], src, w=["wst%d" % hs])
            if hcnt[0] % 3 == 0:
                P.op("pool", lambda e, hs=hs, h0=h0, hn_=hn_, slot=slot: e.tensor_copy(
                    out=wbf[slot][:, h0:h0 + hn_, 0:ncols], in_=wst[hs][:, 0:hn_, 0:ncols]),
                    r=["wst%d" % hs], w=["wbf%d" % slot])
            else:
                P.op("act", lambda e, hs=hs, h0=h0, hn_=hn_, slot=slot: e.activation(
                    out=wbf[slot][:, h0:h0 + hn_, 0:ncols], in_=wst[hs][:, 0:hn_, 0:ncols], func=AF.Copy),
                    r=["wst%d" % hs], w=["wbf%d" % slot])
        return slot

    pending = []

    def sched(loads, fn, extra=0):
        pending.append((loads, fn, extra))

    def flush():
        tasks = pending[:]
        del pending[:]
        loaded = {}

        def do_load(i):
            if i not in loaded:
                loaded[i] = [load_slab(*a) for a in tasks[i][0]]

        for i, (loads, fn, extra) in enumerate(tasks):
            do_load(i)
            if i + 1 < len(tasks) and len(loads) + extra + len(tasks[i + 1][0]) <= 2:
                do_load(i + 1)
            fn(loaded[i])

    bankctr = [0]

    def mm(out, lhsT, rhs, start, stop, r, w):
        P.op("pe", lambda e: e.matmul(out, lhsT, rhs, start=start, stop=stop), r=r, w=w)

    def transpose(out, in_, r, w):
        P.op("pe", lambda e: e.transpose(out, in_, identb[:, :]), r=list(r) + ["identb"], w=w)

    def dbg_dump(name, ap, shape, keys):
        if name not in dbg:
            return
        if name not in dbg_d:
            dbg_d[name] = nc.dram_tensor("dbg_" + name, list(shape), ap.dtype, kind="ExternalOutput").ap()
        return dbg_d[name]

    P.dma(cst[:, :], cst_d[:, :], w=["cst"])
    P.op("dve", lambda e: e.tensor_copy(out=identb[:, :], in_=cv("ident")), r=["cst"], w=["identb"])
    P.op("dve", lambda e: e.memset(epsT[:, :], EPS), w=["epsT"])
    P.op("dve", lambda e: e.memset(negb[:, :], -30000.0), w=["negb"])
    P.op("pool", lambda e: e.memset(Vc[:, :, :, 128:129], 1.0), w=["Vc"])

    mix_scale = float(HD) ** -0.5

    def rmsnorm_to_hT(src_d, st_i, nw_name, xin_views, hn_view, small, key_prefix):
        for c in range(4):
            gc = st_i * 4 + c
            xin = xin_views[c % 2]
            xk = "xin%d" % (c % 2)
            P.dma(xin, src_d[gc * 128:(gc + 1) * 128, :], r=[(key_prefix, gc)], w=[xk])
            ss = small[:, c:c + 1]
            rt = small[:, 4 + c:5 + c]
            rstd = small[:, 8 + c:9 + c]
            P.op("act", lambda e, xin=xin, ss=ss: e.activation(out=hn_view, in_=xin, func=AF.Square, accum_out=ss),
                 r=[xk], w=["hn", "nsm%d" % c])
            P.op("act", lambda e, ss=ss, rt=rt: e.activation(out=rt, in_=ss, func=AF.Sqrt, bias=epsT[:, :], scale=1.0 / D),
                 r=["nsm%d" % c, "epsT"], w=["nsm%d" % c])
            P.op("dve", lambda e, rt=rt, rstd=rstd: e.reciprocal(out=rstd, in_=rt), r=["nsm%d" % c], w=["nsm%d" % c])
            P.op("dve", lambda e, xin=xin, rstd=rstd: e.tensor_scalar(out=hn_view, in0=xin, scalar1=rstd, scalar2=None,
                                                                       op0=OP.mult), r=[xk, "nsm%d" % c], w=["hn"])
            b = 6 + (c % 2)
            for k in range(8):
                transpose(psb(b)[:, k * 128:(k + 1) * 128], hn_view[:, k * 128:(k + 1) * 128], r=["hn"], w=["ps%d" % b])
            P.op("dve", lambda e, b=b, c=c: e.tensor_tensor(
                out=hT[:, :, c * 128:(c + 1) * 128],
                in0=psb(b).rearrange("p (k t) -> p k t", k=8),
                in1=bc(lv(nw_name).unsqueeze(2), (128, 8, 128)), op=OP.mult),
                r=["ps%d" % b, "lcs"], w=["hT"])

    def proj_tok(lhsT_fn, nk_list, slot_list, ncols, evac, banks, lkeys):
        for c in range(4):
            b = banks[c % len(banks)]
            kk = 0
            tot = sum(nk_list)
            for si, slot in enumerate(slot_list):
                for k in range(nk_list[si]):
                    mm(psum[:, b, 0:ncols], lhsT_fn(kk, c), wbf[slot][:, k, 0:ncols], start=(kk == 0), stop=(kk == tot - 1),
                       r=lkeys + ["wbf%d" % slot], w=["ps%d" % b])
                    kk += 1
            evac(c, b)

    for l in layers:
        src_d = x_d if l == layers[0] else xl1_d
        last = (l == layers[-1])
        P.barrier()
        P.dma(lcs[:, :], lc_d[l], w=["lcs"])
        P.op("act", lambda e: e.activation(out=negA[:, :], in_=lv("alog"), func=AF.Exp), r=["lcs"], w=["negA"])
        P.op("dve", lambda e: e.tensor_scalar(out=negA[:, :], in0=negA[:, :], scalar1=-1.0, scalar2=None, op0=OP.mult),
             r=["negA"], w=["negA"])
        P.op("pool", lambda e: e.memset(hst[:, :], 0.0), w=["hst"])
        P.op("pool", lambda e: e.memset(xtail[:, :, :], 0.0), w=["xtail"])
        P.op("pool", lambda e: e.memset(ftail[:, :, :], 0.0), w=["ftail"])

        for st_i in range(NST):
            P.barrier()
            scores = aview(0, (128, S), F32)
            qT = aview(16384, (128, 8, T), BF16)
            qiT = aview(24576, (128, 4, T), BF16)
            maskT = aview(28672, (128, NCH if NCH <= 32 else 32, 128), BF16)
            junkb = aview(28672, (128, 4096), BF16)
            xin_v = [aview(36864, (128, D), F32), aview(40960, (128, D), F32)]
            hn_v = aview(45056, (128, D), BF16)
            qtok = [aview(47104, (128, 4, 128), BF16), aview(48128, (128, 4, 128), BF16)]
            qitf = aview(49152, (128, 8, 64), F32)
            rbuf = [aview(51200 + 1024 * i, (128, 512), BF16) for i in range(4)]
            maskb = [aview(55296 + 1024 * i, (128, 512), BF16) for i in range(2)]
            Eb = [aview(57344 + 1024 * i, (128, 4, 128), BF16) for i in range(2)]
            dsg = aview(59392, (128, 8, 128), BF16)
            hb_ctr = [0]
            a_tok = aview(61440, (128, 8, 128), BF16)
            small = aview(63488, (128, 256), F32)
            ropet = [aview(64512 + 256 * i, (128, 64), F32) for i in range(2)]
            qitok = aview(65024, (128, 512), BF16)
            wabs = small[:, 16:48].rearrange("p (c h) -> p c h", c=4)
            wsgn = small[:, 48:80].rearrange("p (c h) -> p c h", c=4)
            kitok = small[:, 192:256].bitcast(BF16)
            ktok = qtok[1]

            P.dma(tabs[:, :, :], tab_d[:, st_i * 192:(st_i + 1) * 192].rearrange("p (c n) -> p c n", c=4), w=["tabs"])
            rmsnorm_to_hT(src_d, st_i, "nmix", xin_v, hn_v, small, "xsrc%d" % l)

            lhs_h = lambda k, c: hT[:, k, c * 128:(c + 1) * 128]

            def rope(c, src3, dst3, half, cos_o, sin_o, nh, rk, wk):
                cosv = bc(tabs[:, c, cos_o:cos_o + half].unsqueeze(1), (128, nh, half))
                sinv = bc(tabs[:, c, sin_o:sin_o + half].unsqueeze(1), (128, nh, half))
                x1 = src3[:, :, 0:half]
                x2 = src3[:, :, half:2 * half]
                t1 = ropet[0][:, 0:nh * half].rearrange("p (a b) -> p a b", a=nh)
                t2 = ropet[1][:, 0:nh * half].rearrange("p (a b) -> p a b", a=nh)
                rr = list(rk) + ["tabs"]
                P.op("dve", lambda e: e.tensor_tensor(out=t1, in0=x1, in1=cosv, op=OP.mult), r=rr, w=["rt1"])
                P.op("dve", lambda e: e.tensor_tensor(out=t2, in0=x2, in1=sinv, op=OP.mult), r=rr, w=["rt2"])
                P.op("dve", lambda e: e.tensor_tensor(out=dst3[:, :, 0:half], in0=t1, in1=t2, op=OP.subtract),
                     r=["rt1", "rt2"], w=wk)
                P.op("dve", lambda e: e.tensor_tensor(out=t1, in0=x2, in1=cosv, op=OP.mult), r=rr, w=["rt1"])
                P.op("dve", lambda e: e.tensor_tensor(out=t2, in0=x1, in1=sinv, op=OP.mult), r=rr, w=["rt2"])
                P.op("dve", lambda e: e.tensor_tensor(out=dst3[:, :, half:2 * half], in0=t1, in1=t2, op=OP.add),
                     r=["rt1", "rt2"], w=wk)

            for qs in range(2 if stop >= 2 else 0):

                def evac_q(c, b, qs=qs):
                    import os
                    SUB = int(os.environ.get("SUB", "9"))
                    qt = qtok[0]
                    pv = psum[:, b, :].rearrange("p (h d) -> p h d", h=4)
                    if SUB <= 2:
                        return
                    P.op("act", lambda e: e.activation(out=qt[:, :, 32:128], in_=pv[:, :, 32:128], func=AF.Copy),
                         r=["ps%d" % b], w=["qtok0"])
                    if SUB <= 3:
                        return
                    rope(c, pv, qt, 16, 0, 16, 4, ["ps%d" % b], ["qtok0"])
                    if SUB <= 4:
                        return
                    tb = 4 + (c % 2)
                    for h in range(4):
                        transpose(psb(tb)[:, h * 128:(h + 1) * 128], qt[:, h, :], r=["qtok0"], w=["ps%d" % tb])
                    P.op("act", lambda e: e.activation(out=qT[:, qs * 4:(qs + 1) * 4, c * 128:(c + 1) * 128],
                                                       in_=psb(tb)[:, 0:512].rearrange("p (h t) -> p h t", h=4), func=AF.Copy),
                         r=["ps%d" % tb], w=["qT"])

                sched([(w_in_d[l], 0, 8, O_Q + qs * 512, 512)],
                      lambda s, evac_q=evac_q: proj_tok(lhs_h, [8], s, 512, evac_q, [0, 1, 2, 3], ["hT"]))


            def evac_kv(c, b):
                gc = st_i * 4 + c
                pv = psum[:, b, 0:256].rearrange("p (h d) -> p h d", h=2)
                kt = ktok[:, 0:2, :]
                P.op("act", lambda e: e.activation(out=kt[:, :, 32:128], in_=pv[:, :, 32:128], func=AF.Copy),
                     r=["ps%d" % b], w=["qtok1"])
                rope(c, pv, kt, 16, 0, 16, 2, ["ps%d" % b], ["qtok1"])
                P.op("act", lambda e: e.activation(out=Vc[:, gc, :, 0:128],
                                                   in_=psum[:, b, 256:512].rearrange("p (h d) -> p h d", h=2), func=AF.Copy),
                     r=["ps%d" % b], w=[("Vc", gc)])
                tb = 4 + (c % 2)
                for h in range(2):
                    transpose(psb(tb)[:, h * 128:(h + 1) * 128], kt[:, h, :], r=["qtok1"], w=["ps%d" % tb])
                P.op("act", lambda e: e.activation(out=kT[:, :, gc * 128:(gc + 1) * 128],
                                                   in_=psb(tb)[:, 0:256].rearrange("p (h t) -> p h t", h=2), func=AF.Copy),
                     r=["ps%d" % tb], w=[("kT", gc)])

            if stop >= 3:
                sched([(w_in_d[l], 0, 8, O_K, 512)], lambda s: proj_tok(lhs_h, [8], s, 512, evac_kv, [0, 1, 2, 3], ["hT"]))


            def evac_ki(c, b):
                gc = st_i * 4 + c
                pv = psum[:, b, 0:64].rearrange("p (h d) -> p h d", h=1)
                k3 = kitok[:, 0:64].rearrange("p (h d) -> p h d", h=1)
                P.op("act", lambda e: e.activation(out=k3[:, :, 16:64], in_=pv[:, :, 16:64], func=AF.Copy),
                     r=["ps%d" % b], w=["kitok"])
                rope(c, pv, k3, 8, 32, 40, 1, ["ps%d" % b], ["kitok"])
                P.op("dve", lambda e: e.tensor_copy(out=kitok[:, 64:128], in_=kitok[:, 0:64]), r=["kitok"], w=["kitok"])
                P.op("act", lambda e: e.activation(out=wabs[:, c, :], in_=psum[:, b, 64:72], func=AF.Abs),
                     r=["ps%d" % b], w=["wabs%d" % c])
                P.op("act", lambda e: e.activation(out=wsgn[:, c, :], in_=psum[:, b, 64:72], func=AF.Sign),
                     r=["ps%d" % b], w=["wsgn%d" % c])
                tb = 4 + (c % 2)
                transpose(psb(tb)[:, 0:128], kitok[:, :], r=["kitok"], w=["ps%d" % tb])
                P.op("act", lambda e: e.activation(out=kiT2[:, gc * 128:(gc + 1) * 128], in_=psb(tb)[:, 0:128], func=AF.Copy),
                     r=["ps%d" % tb], w=[("kiT", gc)])

            if stop >= 4:
                sched([(w_in_d[l], 0, 8, O_KI, 72)], lambda s: proj_tok(lhs_h, [8], s, 72, evac_ki, [0, 1, 2, 3], ["hT"]))


            def evac_qi(c, b):
                pv = psum[:, b, :].rearrange("p (h d) -> p h d", h=8)
                P.op("act", lambda e: e.activation(out=qitf[:, :, 16:64], in_=pv[:, :, 16:64], func=AF.Copy),
                     r=["ps%d" % b], w=["qitf"])
                rope(c, pv, qitf, 8, 32, 40, 8, ["ps%d" % b], ["qitf"])
                P.op("dve", lambda e: e.tensor_tensor(out=qitok.rearrange("p (h d) -> p h d", h=8), in0=qitf,
                                                      in1=bc(wabs[:, c, :].unsqueeze(2), (128, 8, 64)), op=OP.mult),
                     r=["qitf", "wabs%d" % c], w=["qitok"])
                tb = 4 + (c % 2)
                for j in range(4):
                    transpose(psb(tb)[:, j * 128:(j + 1) * 128], qitok[:, j * 128:(j + 1) * 128], r=["qitok"], w=["ps%d" % tb])
                P.op("act", lambda e: e.activation(out=qiT[:, :, c * 128:(c + 1) * 128],
                                                   in_=psb(tb)[:, 0:512].rearrange("p (h t) -> p h t", h=4), func=AF.Copy),
                     r=["ps%d" % tb], w=["qiT"])

            if stop >= 5:
                sched([(w_in_d[l], 0, 8, O_QI, 512)], lambda s: proj_tok(lhs_h, [8], s, 512, evac_qi, [0, 1, 2, 3], ["hT"]))
            flush()

            def attn_chunk(c):
                gc = st_i * 4 + c
                nk = gc + 1
                n = nk * 128
                tq = slice(c * 128, (c + 1) * 128)
                ngrp = (n + 511) // 512
                kkeys = [("kiT", j) for j in range(nk)]
                for h in range(8):
                    P.op("dve", lambda e, h=h: e.tensor_scalar(out=dsg[:, h, :], in0=identb[:, :], scalar1=wsgn[:, c, h:h + 1],
                                                               scalar2=None, op0=OP.mult),
                         r=["identb", "wsgn%d" % c], w=["dsg"])
                for kg in range(ngrp):
                    w_ = min(512, n - kg * 512)
                    sb_ = 6 + (kg % 2)
                    for h in range(8):
                        b = hb_ctr[0] % 6
                        hb_ctr[0] += 1
                        pr = (h % 2) * 64
                        mm(psum[:, b, 0:w_], qiT[pr:pr + 64, h // 2, tq], kiT2[pr:pr + 64, kg * 512:kg * 512 + w_], True, True,
                           r=["qiT"] + kkeys, w=["ps%d" % b])
                        rb = rbuf[h % 4]
                        P.op("act", lambda e, b=b, rb=rb, w_=w_: e.activation(out=rb[:, 0:w_], in_=psum[:, b, 0:w_], func=AF.Relu),
                             r=["ps%d" % b], w=["rbuf%d" % (h % 4)])
                        mm(psum[:, sb_, 0:w_], dsg[:, h, :], rb[:, 0:w_], (h == 0), (h == 7),
                           r=["dsg", "rbuf%d" % (h % 4)], w=["ps%d" % sb_])
                    sc = scores[:, kg * 512:kg * 512 + w_]
                    P.op("dve", lambda e, sb_=sb_, sc=sc, w_=w_: e.tensor_copy(out=sc, in_=psum[:, sb_, 0:w_]),
                         r=["ps%d" % sb_], w=[("sc", kg)])
                sckeys = [("sc", kg) for kg in range(ngrp)]
                thr = small[:, 100:101]
                if gc * 128 + 1 <= TOPK and (gc * 128 + 128) <= TOPK:
                    P.op("dve", lambda e: e.tensor_tensor(out=scores[:, n - 128:n], in0=scores[:, n - 128:n], in1=cv("caus"),
                                                          op=OP.add), r=[("sc", ngrp - 1), "cst"], w=[("sc", ngrp - 1)])
                    P.op("dve", lambda e: e.memset(thr, -1e29), w=["thr"])
                else:
                    mx = small[:, 101:102]
                    mn = small[:, 102:103]
                    w0 = small[:, 103:104]
                    mid = small[:, 104:105]
                    cnt = small[:, 105:106]
                    dd = small[:, 106:107]
                    Wk = small[:, 110:110 + NIT + 2]
                    P.op("dve", lambda e: e.tensor_reduce(out=mx, in_=scores[:, 0:n], axis=AX.X, op=OP.max), r=sckeys, w=["bs_mx"])
                    P.op("dve", lambda e: e.tensor_reduce(out=mn, in_=scores[:, 0:n], axis=AX.X, op=OP.min), r=sckeys, w=["bs_mn"])
                    P.op("dve", lambda e: e.tensor_tensor(out=scores[:, n - 128:n], in0=scores[:, n - 128:n], in1=cv("caus"),
                                                          op=OP.add), r=[("sc", ngrp - 1), "cst"], w=[("sc", ngrp - 1)])
                    P.op("dve", lambda e: e.tensor_tensor(out=w0, in0=mx, in1=mn, op=OP.subtract), r=["bs_mx", "bs_mn"], w=["bs_w0"])
                    P.op("dve", lambda e: e.tensor_scalar(out=Wk, in0=cv("pow2"), scalar1=w0, scalar2=None, op0=OP.mult),
                         r=["bs_w0", "cst"], w=["bs_wk"])
                    P.op("dve", lambda e: e.tensor_tensor(out=mid, in0=mn, in1=Wk[:, 0:1], op=OP.add), r=["bs_mn", "bs_wk"], w=["bs_mid"])
                    for it in range(NIT):
                        P.op("dve", lambda e: e.tensor_scalar(out=junkb[:, 0:n], in0=scores[:, 0:n], scalar1=mid, scalar2=None,
                                                              op0=OP.is_ge, op1=OP.add, accum_out=cnt),
                             r=sckeys + ["bs_mid"], w=["maskT", "bs_cnt"])
                        lastit = (it == NIT - 1)
                        P.op("dve", lambda e, lastit=lastit: e.tensor_scalar(out=dd, in0=cnt, scalar1=float(TOPK),
                                                                              scalar2=(1.0 if lastit else 0.5),
                                                                              op0=OP.is_ge, op1=OP.subtract),
                             r=["bs_cnt"], w=["bs_dd"])
                        dst = thr if lastit else mid
                        P.op("dve", lambda e, it=it, dst=dst: e.scalar_tensor_tensor(
                            out=dst, in0=dd, scalar=Wk[:, it:it + 1], in1=mid, op0=OP.mult, op1=OP.add),
                            r=["bs_dd", "bs_wk", "bs_mid"], w=["thr" if lastit else "bs_mid"])
                for kg in range(ngrp):
                    w_ = min(512, n - kg * 512)
                    mb = maskb[kg % 2]
                    P.op("dve", lambda e, mb=mb, kg=kg, w_=w_: e.tensor_scalar(
                        out=mb[:, 0:w_], in0=scores[:, kg * 512:kg * 512 + w_], scalar1=thr, scalar2=None, op0=OP.is_ge),
                        r=[("sc", kg), "thr"], w=["maskb%d" % (kg % 2)])
                    tb = 6 + (kg % 2)
                    nj = w_ // 128
                    for jj in range(nj):
                        transpose(psb(tb)[:, jj * 128:(jj + 1) * 128], mb[:, jj * 128:(jj + 1) * 128],
                                  r=["maskb%d" % (kg % 2)], w=["ps%d" % tb])
                    P.op("act", lambda e, tb=tb, kg=kg, nj=nj: e.activation(
                        out=maskT[:, kg * 4:kg * 4 + nj, :], in_=psb(tb)[:, 0:nj * 128].rearrange("p (j t) -> p j t", j=nj),
                        func=AF.Identity, scale=30000.0, bias=negb[:, :]), r=["ps%d" % tb, "negb"], w=["maskT"])
                if "sc" in dbg and gc == NCH - 1:
                    dd_ = dbg_dump("sc", scores, (128, S), None)
                    P.dma(dd_, scores, r=sckeys)
                    dd_ = dbg_dump("thr", thr, (128, 1), None)
                    P.dma(dd_, thr, r=["thr"])
                for kvh in range(2):
                    for j in range(nk):
                        lb = 4 + (j % 2)
                        mm(psum[:, lb, :], kT[:, kvh, j * 128:(j + 1) * 128], qT[:, kvh * 4:(kvh + 1) * 4, tq], True, False,
                           r=[("kT", j), "qT"], w=["ps%d" % lb])
                        mm(psum[:, lb, :], identb[:, :], bc(maskT[:, j, :].unsqueeze(1), (128, 4, 128)), False, True,
                           r=["identb", "maskT"], w=["ps%d" % lb])
                        Ev = Eb[j % 2]
                        P.op("act", lambda e, lb=lb, Ev=Ev: e.activation(
                            out=Ev, in_=psum[:, lb, :].rearrange("p (g t) -> p g t", g=4), func=AF.Exp, scale=mix_scale),
                            r=["ps%d" % lb], w=["E%d" % (j % 2)])
                        for g in range(4):
                            mm(psum[:, g, 0:129], Ev[:, g, :], Vc[:, j, kvh, :], (j == 0), (j == nk - 1),
                               r=["E%d" % (j % 2), ("Vc", j), "Vc"], w=["ps%d" % g])
                    for g in range(4):
                        h = kvh * 4 + g
                        rden = small[:, 120 + h:121 + h]
                        P.op("dve", lambda e, g=g, rden=rden: e.reciprocal(out=rden, in_=psum[:, g, 128:129]),
                             r=["ps%d" % g], w=["rden%d" % h])
                        P.op("act", lambda e, g=g, h=h, rden=rden: e.activation(
                            out=a_tok[:, h, :], in_=psum[:, g, 0:128], func=AF.Copy, scale=rden),
                            r=["ps%d" % g, "rden%d" % h], w=["a_tok"])
                tb = 6
                for h in range(8):
                    transpose(psb(tb)[:, h * 128:(h + 1) * 128], a_tok[:, h, :], r=["a_tok"], w=["ps%d" % tb])
                P.op("act", lambda e, tb=tb, c=c: e.activation(
                    out=aT[:, :, c * 128:(c + 1) * 128], in_=psb(tb).rearrange("p (h t) -> p h t", h=8), func=AF.Copy),
                    r=["ps%d" % tb], w=["aT"])

            for c in range(4 if stop >= 6 else 0):
                attn_chunk(c)

            if "aT" in dbg:
                dd_ = dbg_dump("aT", aT[:, :, :], (NST, 128, 8, T), None)
                P.dma(dd_[st_i], aT[:, :, :], r=["aT"])
            if "qT" in dbg:
                dd_ = dbg_dump("qT", qT, (NST, 128, 8, T), None)
                P.dma(dd_[st_i], qT, r=["qT"])
            if "hT" in dbg:
                dd_ = dbg_dump("hT", hT[:, :, :], (NST, 128, 8, T), None)
                P.dma(dd_[st_i], hT[:, :, :], r=["hT"])

            if stop >= 7:
                P.barrier()
                BT = aview(0, (128, 4, T), BF16)
                CT = aview(4096, (128, 4, T), BF16)
                Btok = aview(8192, (128, 4, 512), BF16)
                xacc = [aview(12288 + 2048 * i, (128, 512), F32) for i in range(2)]
                xact = [aview(16384 + 1024 * i, (128, 512), BF16) for i in range(2)]
                xtok = aview(18432, (128, 4, 512), BF16)
                zs = aview(22528, (128, 4, 512), BF16)
                dtt = aview(26624, (128, 4, 32), F32)
                adt = aview(27136, (128, 4, 32), F32)
                Wp = aview(27648, (128, 8, 128), F32)
                Lx = aview(31744, (128, 8, 128), BF16)
                MT = aview(33792, (128, 8, 128), BF16)
                CBm = aview(35840, (128, 128), BF16)
                Xdt = aview(36096, (128, 8, 64), BF16)
                Xd2 = aview(37120, (128, 8, 64), BF16)
                yo = aview(38144, (128, 8, 64), F32)
                yy = aview(40192, (128, 8, 64), F32)
                hb = aview(42240, (128, 512), BF16)
                sm2 = aview(43264, (128, 64), F32)
                btok_b = aview(43520, (128, 512), BF16)
                ysq = aview(44544, (128, 512), BF16)

                def conv_fm(cc, b, wname, bname, ntap, tail, dst_silu, tailkey):
                    u = psum[:, b, :]
                    acc = xacc[cc % 2]
                    ak = "xacc%d" % (cc % 2)
                    wv = lambda j: lv(wname, cc * ntap + j, cc * ntap + j + 1)
                    P.op("act", lambda e: e.activation(out=acc, in_=u, func=AF.Identity, bias=lv(bname, cc, cc + 1),
                                                       scale=wv(ntap - 1)), r=["ps%d" % b, "lcs"], w=[ak])
                    for j in range(ntap - 1):
                        sh = ntap - 1 - j
                        P.op("dve", lambda e, j=j, sh=sh: e.scalar_tensor_tensor(
                            out=acc[:, sh:512], in0=u[:, 0:512 - sh], scalar=wv(j), in1=acc[:, sh:512], op0=OP.mult, op1=OP.add),
                            r=["ps%d" % b, "lcs", ak], w=[ak])
                        P.op("dve", lambda e, j=j, sh=sh: e.scalar_tensor_tensor(
                            out=acc[:, 0:sh], in0=tail[:, cc, ntap - 1 - sh:ntap - 1], scalar=wv(j), in1=acc[:, 0:sh],
                            op0=OP.mult, op1=OP.add), r=[tailkey, "lcs", ak], w=[ak])
                    P.op("dve", lambda e: e.tensor_copy(out=tail[:, cc, :], in_=u[:, 512 - (ntap - 1):512]),
                         r=["ps%d" % b], w=[tailkey])
                    return acc, ak

                def proj_fm(slot, ncc, cc0, handler):
                    for j in range(ncc):
                        b = j % 4
                        for k in range(8):
                            mm(psum[:, b, :], wbf[slot][:, k, j * 128:(j + 1) * 128], hT[:, k, :], (k == 0), (k == 7),
                               r=["hT", "wbf%d" % slot], w=["ps%d" % b])
                        handler(cc0 + j, b)

                def h_bc(cc, b):
                    acc, ak = conv_fm(cc, b, "xcw", "xcb", 4, xtail, None, "xtail")
                    g = (cc - 16) % 4
                    if cc < 20:
                        P.op("act", lambda e: e.activation(out=BT[:, g, :], in_=acc, func=AF.Silu), r=[ak], w=["BT"])
                        tb = 4 + (cc % 2)
                        for tc_ in range(4):
                            transpose(psb(tb)[:, tc_ * 128:(tc_ + 1) * 128], BT[:, g, tc_ * 128:(tc_ + 1) * 128], r=["BT"], w=["ps%d" % tb])
                        P.op("act", lambda e: e.activation(out=Btok[:, :, g * 128:(g + 1) * 128],
                                                           in_=psb(tb)[:, 0:512].rearrange("p (c n) -> p c n", c=4), func=AF.Copy),
                             r=["ps%d" % tb], w=["Btok"])
                    else:
                        P.op("act", lambda e: e.activation(out=CT[:, g, :], in_=acc, func=AF.Silu), r=[ak], w=["CT"])

                for half in range(2):
                    sched([(w_in_d[l], 0, 8, O_XBC + 2048 + half * 512, 512)],
                          lambda s, half=half: proj_fm(s[0], 4, 16 + half * 4, h_bc))

                def evac_dt(c, b):
                    P.op("dve", lambda e: e.tensor_tensor(out=dtt[:, c, :], in0=psum[:, b, 0:32], in1=lv("dtb"), op=OP.add),
                         r=["ps%d" % b, "lcs"], w=["dtt"])
                    P.op("act", lambda e: e.activation(out=dtt[:, c, :], in_=dtt[:, c, :], func=AF.Exp), r=["dtt"], w=["dtt"])
                    P.op("act", lambda e: e.activation(out=dtt[:, c, :], in_=dtt[:, c, :], func=AF.Ln, bias=1.0), r=["dtt"], w=["dtt"])
                    P.op("dve", lambda e: e.tensor_tensor(out=adt[:, c, :], in0=dtt[:, c, :], in1=negA[:, :], op=OP.mult),
                         r=["dtt", "negA"], w=["adt"])

                sched([(w_in_d[l], 0, 8, O_DT, 32)], lambda s: proj_tok(lhs_h, [8], s, 32, evac_dt, [0, 1, 2, 3], ["hT"]))

                for g in range(4):

                    def h_x(cc, b, g=g):
                        acc, ak = conv_fm(cc, b, "xcw", "xcb", 4, xtail, None, "xtail")
                        xa = xact[cc % 2]
                        xk = "xact%d" % (cc % 2)
                        P.op("act", lambda e: e.activation(out=xa, in_=acc, func=AF.Silu), r=[ak], w=[xk])
                        tb = 4 + (cc % 2)
                        for tc_ in range(4):
                            transpose(psb(tb)[:, tc_ * 128:(tc_ + 1) * 128], xa[:, tc_ * 128:(tc_ + 1) * 128], r=[xk], w=["ps%d" % tb])
                        j = cc % 4
                        P.op("act", lambda e: e.activation(out=xtok[:, :, j * 128:(j + 1) * 128],
                                                           in_=psb(tb)[:, 0:512].rearrange("p (c n) -> p c n", c=4), func=AF.Copy),
                             r=["ps%d" % tb], w=["xtok"])

                    sched([(w_in_d[l], 0, 8, O_XBC + g * 512, 512)], lambda s, g=g, h_x=h_x: proj_fm(s[0], 4, g * 4, h_x))

                    def evac_z(c, b):
                        P.op("act", lambda e: e.activation(out=zs[:, c, :], in_=psum[:, b, :], func=AF.Silu), r=["ps%d" % b], w=["zs"])

                    sched([(w_in_d[l], 0, 8, O_Z + g * 512, 512)],
                          lambda s, evac_z=evac_z: proj_tok(lhs_h, [8], s, 512, evac_z, [0, 1, 2, 3], ["hT"]))

                    def ssd_chunk(c, g=g):
                        ts_ = slice(c * 128, (c + 1) * 128)
                        xt3 = xtok[:, c, :].rearrange("p (e d) -> p e d", e=8)
                        dtg = dtt[:, c, g * 8:(g + 1) * 8]
                        adg = adt[:, c, g * 8:(g + 1) * 8]
                        hs3 = hst[:, g * 512:(g + 1) * 512].rearrange("p (e d) -> p e d", e=8)
                        hk = "hst%d" % g
                        P.op("dve", lambda e: e.tensor_tensor(out=Xdt, in0=xt3, in1=bc(dtg.unsqueeze(2), (128, 8, 64)), op=OP.mult),
                             r=["xtok", "dtt"], w=["Xdt"])
                        P.op("dve", lambda e: e.tensor_tensor(out=Wp, in0=bc(cv("tri_le").unsqueeze(1), (128, 8, 128)),
                                                              in1=bc(adg.unsqueeze(2), (128, 8, 128)), op=OP.mult),
                             r=["adt", "cst"], w=["Wp"])
                        for hh in range(2):
                            mm(psum[:, hh, :], cv("sgt"), Wp[:, hh * 4:(hh + 1) * 4, :], True, True, r=["cst", "Wp"], w=["ps%d" % hh])
                            P.op("act", lambda e, hh=hh: e.activation(out=Lx[:, hh * 4:(hh + 1) * 4, :],
                                                                      in_=psum[:, hh, :].rearrange("p (e l) -> p e l", e=4), func=AF.Exp),
                                 r=["ps%d" % hh], w=["Lx"])
                            P.op("act", lambda e, hh=hh: e.activation(out=sm2[:, hh * 4:(hh + 1) * 4],
                                                                      in_=psum[:, hh, :].rearrange("p (e l) -> p e l", e=4)[:, :, 127],
                                                                      func=AF.Exp), r=["ps%d" % hh], w=["decay"])
                        mm(psum[:, 2, 0:128], BT[:, g, ts_], CT[:, g, ts_], True, True, r=["BT", "CT"], w=["ps2"])
                        P.op("dve", lambda e: e.tensor_tensor(out=CBm, in0=psum[:, 2, 0:128], in1=cv("tri_le"), op=OP.mult),
                             r=["ps2", "cst"], w=["CBm"])
                        P.op("dve", lambda e: e.tensor_tensor(out=MT, in0=Lx, in1=bc(CBm.unsqueeze(1), (128, 8, 128)), op=OP.mult),
                             r=["Lx", "CBm"], w=["MT"])
                        for e_ in range(8):
                            mm(psum[:, 3, e_ * 64:(e_ + 1) * 64], MT[:, e_, :], Xdt[:, e_, :], True, True, r=["MT", "Xdt"], w=["ps3"])
                        P.op("act", lambda e: e.activation(out=hb, in_=hst[:, g * 512:(g + 1) * 512], func=AF.Copy), r=[hk], w=["hb"])
                        mm(psum[:, 4, :], CT[:, g, ts_], hb, True, True, r=["CT", "hb"], w=["ps4"])
                        mm(psum[:, 5, 0:8], cv("tri_le"), adg, True, True, r=["cst", "adt"], w=["ps5"])
                        P.op("act", lambda e: e.activation(out=sm2[:, 8:16], in_=psum[:, 5, 0:8], func=AF.Exp), r=["ps5"], w=["eacs"])
                        P.op("dve", lambda e: e.tensor_tensor(out=yo, in0=psum[:, 4, :].rearrange("p (e d) -> p e d", e=8),
                                                              in1=bc(sm2[:, 8:16].unsqueeze(2), (128, 8, 64)), op=OP.mult),
                             r=["ps4", "eacs"], w=["yo"])
                        P.op("dve", lambda e: e.tensor_tensor(out=yy, in0=psum[:, 3, :].rearrange("p (e d) -> p e d", e=8), in1=yo, op=OP.add),
                             r=["ps3", "yo"], w=["yy"])
                        P.op("dve", lambda e: e.tensor_tensor(out=yo, in0=xt3, in1=bc(lv("dsk", g * 8, g * 8 + 8).unsqueeze(2), (128, 8, 64)),
                                                              op=OP.mult), r=["xtok", "lcs"], w=["yo"])
                        P.op("dve", lambda e: e.tensor_tensor(out=yy, in0=yy, in1=yo, op=OP.add), r=["yy", "yo"], w=["yy"])
                        P.op("dve", lambda e: e.tensor_tensor(out=Xd2, in0=Xdt, in1=bc(sm2[:, 0:8].unsqueeze(2), (128, 8, 64)), op=OP.mult),
                             r=["Xdt", "decay"], w=["Xd2"])
                        mm(psum[:, 6, :], Btok[:, c, g * 128:(g + 1) * 128], Xd2.rearrange("p e d -> p (e d)"), True, True,
                           r=["Btok", "Xd2"], w=["ps6"])
                        mm(psum[:, 7, 0:8], cv("ones"), adg, True, True, r=["cst", "adt"], w=["ps7"])
                        P.op("act", lambda e: e.activation(out=sm2[:, 16:24], in_=psum[:, 7, 0:8], func=AF.Exp), r=["ps7"], w=["cdec"])
                        P.op("dve", lambda e: e.tensor_tensor(out=hs3, in0=hs3, in1=bc(sm2[:, 16:24].unsqueeze(2), (128, 8, 64)), op=OP.mult),
                             r=[hk, "cdec"], w=[hk])
                        P.op("dve", lambda e: e.tensor_tensor(out=hs3, in0=psum[:, 6, :].rearrange("p (e d) -> p e d", e=8), in1=hs3, op=OP.add),
                             r=["ps6", hk], w=[hk])
                        yf = yy.rearrange("p e d -> p (e d)")
                        P.op("dve", lambda e: e.tensor_tensor(out=yf, in0=yf, in1=zs[:, c, :], op=OP.mult), r=["yy", "zs"], w=["yy"])
                        P.op("act", lambda e: e.activation(out=ysq, in_=yf, func=AF.Square, accum_out=sm2[:, 24:25]), r=["yy"], w=["ysq", "gss"])
                        P.op("act", lambda e: e.activation(out=sm2[:, 25:26], in_=sm2[:, 24:25], func=AF.Sqrt, bias=epsT[:, :], scale=1.0 / 512),
                             r=["gss", "epsT"], w=["gss"])
                        P.op("dve", lambda e: e.reciprocal(out=sm2[:, 26:27], in_=sm2[:, 25:26]), r=["gss"], w=["gss"])
                        P.op("dve", lambda e: e.tensor_scalar(out=btok_b, in0=yf, scalar1=sm2[:, 26:27], scalar2=None, op0=OP.mult),
                             r=["yy", "gss"], w=["btok_b"])
                        tb = 1
                        for j in range(4):
                            transpose(psb(tb)[:, j * 128:(j + 1) * 128], btok_b[:, j * 128:(j + 1) * 128], r=["btok_b"], w=["ps%d" % tb])
                        P.op("dve", lambda e: e.tensor_tensor(out=bT[:, g * 4:(g + 1) * 4, ts_],
                                                              in0=psb(tb)[:, 0:512].rearrange("p (j t) -> p j t", j=4),
                                                              in1=bc(lv("snw", g * 4, g * 4 + 4).unsqueeze(2), (128, 4, 128)), op=OP.mult),
                             r=["ps%d" % tb, "lcs"], w=["bT"])

                    sched([], lambda s, ssd_chunk=ssd_chunk: [ssd_chunk(c) for c in range(4)])
                flush()
                if "bT" in dbg:
                    dd_ = dbg_dump("bT", bT[:, :, :], (NST, 128, 16, T), None)
                    P.dma(dd_[st_i], bT[:, :, :], r=["bT"])

            if stop >= 8:
                P.barrier()
                ga = aview(0, (128, 4, 512), BF16)
                gb = aview(4096, (128, 4, 512), BF16)
                mf = aview(8192, (128, 4, 512), F32)
                tmpf = aview(16384, (128, 512), F32)
                mtok = aview(18432, (128, 512), BF16)
                mT = aview(19456, (128, 8, T), BF16)
                xh = aview(27648, (128, 4, 512), F32)
                for cs in range(2):
                    sched([(w_in_d[l], 0, 8, O_GA + cs * 512, 512)], lambda s: proj_tok(
                        lhs_h, [8], s, 512,
                        lambda c, b: P.op("act", lambda e: e.activation(out=ga[:, c, :], in_=psum[:, b, :], func=AF.Sigmoid),
                                          r=["ps%d" % b], w=["ga"]), [0, 1, 2, 3], ["hT"]))
                    sched([(w_in_d[l], 0, 8, O_GB + cs * 512, 512)], lambda s: proj_tok(
                        lhs_h, [8], s, 512,
                        lambda c, b: P.op("act", lambda e: e.activation(out=gb[:, c, :], in_=psum[:, b, :], func=AF.Sigmoid),
                                          r=["ps%d" % b], w=["gb"]), [0, 1, 2, 3], ["hT"]))
                    sched([(w_pa_d[l], 0, 8, cs * 512, 512)], lambda s: proj_tok(
                        lambda k, c: aT[:, k, c * 128:(c + 1) * 128], [8], s, 512,
                        lambda c, b: P.op("dve", lambda e: e.tensor_tensor(out=mf[:, c, :], in0=psum[:, b, :], in1=ga[:, c, :], op=OP.mult),
                                          r=["ps%d" % b, "ga"], w=["mf"]), [0, 1, 2, 3], ["aT"]))

                    def evac_pb(c, b, cs=cs):
                        P.op("dve", lambda e: e.tensor_tensor(out=tmpf, in0=psum[:, b, :], in1=gb[:, c, :], op=OP.mult),
                             r=["ps%d" % b, "gb"], w=["tmpf"])
                        P.op("dve", lambda e: e.tensor_tensor(out=mtok, in0=tmpf, in1=mf[:, c, :], op=OP.add), r=["tmpf", "mf"], w=["mtok"])
                        tb = 6 + (c % 2)
                        for j in range(4):
                            transpose(psb(tb)[:, j * 128:(j + 1) * 128], mtok[:, j * 128:(j + 1) * 128], r=["mtok"], w=["ps%d" % tb])
                        P.op("act", lambda e: e.activation(out=mT[:, cs * 4:(cs + 1) * 4, c * 128:(c + 1) * 128],
                                                           in_=psb(tb)[:, 0:512].rearrange("p (j t) -> p j t", j=4), func=AF.Copy),
                             r=["ps%d" % tb], w=["mT"])

                    sched([(w_pb_d[l], 0, 8, cs * 512, 512), (w_pb_d[l], 1024, 8, cs * 512, 512)],
                          lambda s, evac_pb=evac_pb: proj_tok(lambda k, c: bT[:, k, c * 128:(c + 1) * 128], [8, 8], s, 512, evac_pb,
                                                              [0, 1, 2, 3], ["bT"]))
                flush()
                for cs in range(2):
                    def out_task(s, cs=cs):
                        P.dma(xh, src_d[st_i * T:(st_i + 1) * T, cs * 512:(cs + 1) * 512].rearrange("(c p) n -> p c n", p=128),
                              r=[("xsrc%d" % l, st_i * 4 + c) for c in range(4)], w=["xh"])
                        proj_tok(lambda k, c: mT[:, k, c * 128:(c + 1) * 128], [8], s, 512,
                                 lambda c, b: P.op("dve", lambda e: e.tensor_tensor(out=xh[:, c, :], in0=psum[:, b, :], in1=xh[:, c, :], op=OP.add),
                                                   r=["ps%d" % b, "xh"], w=["xh"]), [0, 1, 2, 3], ["mT"])
                        P.dma(xmid_d[st_i * T:(st_i + 1) * T, cs * 512:(cs + 1) * 512].rearrange("(c p) n -> p c n", p=128), xh,
                              r=["xh"], w=[("xmid", st_i * 4 + c) for c in range(4)])
                    sched([(w_out_d[l], 0, 8, cs * 512, 512)], out_task)
                flush()

            if stop >= 9:
                P.barrier()
                xin_f = [aview(0, (128, D), F32), aview(4096, (128, D), F32)]
                hn_f = aview(8192, (128, D), BF16)
                small_f = aview(10240, (128, 64), F32)
                xacc = [aview(10496 + 2048 * i, (128, 512), F32) for i in range(2)]
                sg = aview(14592, (128, 22, 512), BF16)
                xo = aview(37120, (128, 4, D), F32)
                nfin = aview(53504, (128, D), F32)
                osq = aview(57600, (128, D), BF16)
                rmsnorm_to_hT(xmid_d, st_i, "nffn", xin_f, hn_f, small_f, "xmid")

                def h_up(cc, b):
                    acc, ak = conv_fm(cc, b, "fcw", "fcb", 3, ftail, None, "ftail")
                    if cc < 22:
                        P.op("act", lambda e: e.activation(out=sg[:, cc, :], in_=acc, func=AF.Silu), r=[ak], w=[("sg", cc)])
                    else:
                        P.op("dve", lambda e: e.tensor_tensor(out=sg[:, cc - 22, :], in0=sg[:, cc - 22, :], in1=acc, op=OP.mult),
                             r=[ak, ("sg", cc - 22)], w=[("sg", cc - 22)])

                for sl in range(11):
                    sched([(w_up_d[l], 0, 8, sl * 512, 512)], lambda s, sl=sl: proj_fm(s[0], 4, sl * 4, h_up))
                sgk = [("sg", i) for i in range(22)]

                def down_task(slots, cs):
                    P.dma(xo[:, :, cs * 512:(cs + 1) * 512],
                          xmid_d[st_i * T:(st_i + 1) * T, cs * 512:(cs + 1) * 512].rearrange("(c p) n -> p c n", p=128),
                          r=[("xmid", st_i * 4 + c) for c in range(4)], w=["xo"])
                    for c in range(4):
                        kk = 0
                        for si in range(2):
                            for k in range(8):
                                mm(psum[:, c, :], sg[:, kk, c * 128:(c + 1) * 128], wbf[slots[si]][:, k, :], (kk == 0), False,
                                   r=sgk + ["wbf%d" % slots[si]], w=["ps%d" % c])
                                kk += 1
                    s2 = load_slab(w_dn_d[l], 2048, 6, cs * 512, 512)
                    for c in range(4):
                        for k in range(6):
                            mm(psum[:, c, :], sg[:, 16 + k, c * 128:(c + 1) * 128], wbf[s2][:, k, :], False, (k == 5),
                               r=sgk + ["wbf%d" % s2], w=["ps%d" % c])
                        P.op("dve", lambda e, c=c, cs=cs: e.tensor_tensor(out=xo[:, c, cs * 512:(cs + 1) * 512], in0=psum[:, c, :],
                                                                          in1=xo[:, c, cs * 512:(cs + 1) * 512], op=OP.add),
                             r=["ps%d" % c, "xo"], w=["xo"])

                for cs in range(2):
                    sched([(w_dn_d[l], 0, 8, cs * 512, 512), (w_dn_d[l], 1024, 8, cs * 512, 512)],
                          lambda s, cs=cs: down_task(s, cs), extra=1)
                flush()
                if not last:
                    P.dma(xl1_d[st_i * T:(st_i + 1) * T, :].rearrange("(c p) n -> p c n", p=128), xo, r=["xo"],
                          w=[("xsrc1", st_i * 4 + c) for c in range(4)])
                else:
                    P.dma(nfin, nfin_d[:, :], w=["nfin"])
                    for c in range(4):
                        ss = small_f[:, 32 + c:33 + c]
                        rt = small_f[:, 36 + c:37 + c]
                        rs = small_f[:, 40 + c:41 + c]
                        P.op("act", lambda e, c=c, ss=ss: e.activation(out=osq, in_=xo[:, c, :], func=AF.Square, accum_out=ss),
                             r=["xo"], w=["osq", "fsm%d" % c])
                        P.op("act", lambda e, ss=ss, rt=rt: e.activation(out=rt, in_=ss, func=AF.Sqrt, bias=epsT[:, :], scale=1.0 / D),
                             r=["fsm%d" % c, "epsT"], w=["fsm%d" % c])
                        P.op("dve", lambda e, rt=rt, rs=rs: e.reciprocal(out=rs, in_=rt), r=["fsm%d" % c], w=["fsm%d" % c])
                        P.op("dve", lambda e, c=c, rs=rs: e.scalar_tensor_tensor(out=xo[:, c, :], in0=xo[:, c, :], scalar=rs, in1=nfin,
                                                                                  op0=OP.mult, op1=OP.mult),
                             r=["xo", "fsm%d" % c, "nfin"], w=["xo"])
                    P.dma(out_d[st_i * T:(st_i + 1) * T, :].rearrange("(c p) n -> p c n", p=128), xo, r=["xo"])

    if "kT" in dbg:
        dd_ = dbg_dump("kT", kT[:, :, :], (128, NKV, S), None)
        P.dma(dd_, kT[:, :, :], r=[("kT", j) for j in range(NCH)])
    if "kiT" in dbg:
        dd_ = dbg_dump("kiT", kiT2[:, :], (128, S), None)
        P.dma(dd_, kiT2[:, :], r=[("kiT", j) for j in range(NCH)])
    P.emit()
    st.close()
    return nc


_NC_CACHE = {}


def kernel(**inputs):
    inp = {k: np.asarray(v) for k, v in inputs.items()}
    x = inp["x"].astype(np.float32, copy=False)
    B, S, _ = x.shape
    if S not in _NC_CACHE:
        _NC_CACHE[S] = build(S, layers=(0, 1))
    nc = _NC_CACHE[S]
    cst, tab = host_consts(S)
    lc = np.stack([host_layer_consts(inp, l) for l in range(2)])
    nfin = np.ascontiguousarray(np.broadcast_to(inp["norm_final_w"].astype(np.float32)[None, :], (128, D)))
    shared = {"w_in": np.ascontiguousarray(inp["w_in"], dtype=np.float32),
              "w_proj_attn": np.ascontiguousarray(inp["w_proj_attn"], dtype=np.float32),
              "w_proj_ssd": np.ascontiguousarray(inp["w_proj_ssd"], dtype=np.float32),
              "w_out": np.ascontiguousarray(inp["w_out"], dtype=np.float32),
              "ffn_w_up": np.ascontiguousarray(inp["ffn_w_up"], dtype=np.float32),
              "ffn_w_down": np.ascontiguousarray(inp["ffn_w_down"], dtype=np.float32),
              "cst": cst, "tab": tab, "lc": lc, "nfin": nfin}
    in_maps = [dict(shared, x=np.ascontiguousarray(x[b])) for b in range(B)]
    res = run_bass_kernel_spmd(nc, in_maps, core_ids=list(range(B)))
    return np.stack([np.asarray(r["out"], dtype=np.float32) for r in res.results], axis=0)
```

```python
import contextlib
import numpy as np
import concourse.bass as bass
import concourse.mybir as mybir
from concourse.bass_utils import run_bass_kernel_spmd

F32 = mybir.dt.float32
BF16 = mybir.dt.bfloat16
AF = mybir.ActivationFunctionType
OP = mybir.AluOpType
AX = mybir.AxisListType

D = 1024
NH, HD, NKV = 8, 128, 2
IH, IDM = 8, 64
SSD_INNER, SSD_HD, SSD_H, SSD_G, SSD_N = 2048, 64, 32, 4, 128
CONV_DIM = 3072
FFN = 2816
EPS = 1e-6
IN_COLS = 9320
O_Q, O_K, O_V, O_QI, O_KI, O_WI, O_Z, O_XBC, O_DT, O_GA, O_GB = (
    0, 1024, 1280, 1536, 2048, 2112, 2120, 4168, 7240, 7272, 8296)
T = 512
NIT = 22

ENGS = ["pe", "act", "dve", "pool", "sp"]
NDMASEM = 24


class Prog:
    def __init__(self, nc):
        self.nc = nc
        self.ops = {e: [] for e in ENGS}
        self.last_w = {}
        self.readers = {}
        self.ndma = 0
        self.last_real = {}
        self.ps_last = {}

    @staticmethod
    def _isps(k):
        return isinstance(k, str) and k.startswith("ps") and k[2:].isdigit()

    def _deps(self, eng, r, w):
        deps = []
        for k in list(r) + list(w):
            if self._isps(k):
                is_w = k in w
                last = self.ps_last.get(k)
                if last is not None:
                    ref, lw = last
                    same = (ref[0] == "eng" and ref[1] == eng)
                    if (not same) or is_w or lw:
                        deps.append(ref)
        r = [k for k in r if not self._isps(k)]
        w = [k for k in w if not self._isps(k)]
        for k in r:
            lw = self.last_w.get(k)
            if lw is not None:
                deps.append(lw)
        for k in w:
            lw = self.last_w.get(k)
            if lw is not None:
                deps.append(lw)
            deps.extend(self.readers.get(k, ()))
        out = []
        for d in deps:
            if d[0] == "eng" and d[1] == "pe" and eng == "pe":
                continue
            if d not in out:
                out.append(d)
        return out

    def _commit(self, ref, r, w):
        for k in list(r) + list(w):
            if self._isps(k):
                self.ps_last[k] = (ref, k in w)
        r = [k for k in r if not self._isps(k)]
        w = [k for k in w if not self._isps(k)]
        for k in r:
            self.readers.setdefault(k, []).append(ref)
        for k in w:
            self.last_w[k] = ref
            self.readers[k] = []

    def _mark(self, deps):
        for d in deps:
            if d[0] == "eng":
                self.ops[d[1]][d[2]]["sig"] = True

    def op(self, eng, fn, r=(), w=()):
        deps = self._deps(eng, r, w)
        self._mark(deps)
        idx = len(self.ops[eng])
        self.ops[eng].append(dict(fn=fn, deps=deps, sig=False, dma=None))
        self.last_real[eng] = idx
        self._commit(("eng", eng, idx), r, w)

    def dma(self, out, in_, r=(), w=(), q="sp"):
        deps = self._deps(q, r, w)
        self._mark(deps)
        i = self.ndma
        self.ndma += 1
        if i >= NDMASEM:
            deps.append(("dma", i - NDMASEM))
        self.ops[q].append(dict(fn=lambda e: e.dma_start(out=out, in_=in_), deps=deps, sig=False, dma=i))
        self._commit(("dma", i), r, w)

    def barrier(self):
        deps = [("eng", e, i) for e, i in self.last_real.items()]
        deps += [("dma", i) for i in range(max(0, self.ndma - NDMASEM), self.ndma)]
        self._mark(deps)
        for e in ENGS:
            self.ops[e].append(dict(fn=None, deps=[d for d in deps if not (d[0] == "eng" and d[1] == e)],
                                    sig=False, dma=None))
        self.last_w.clear()
        self.readers.clear()
        self.ps_last.clear()

    def simulate(self):
        sigcnt = {}
        for e in ENGS:
            c = 0
            arr = []
            for o in self.ops[e]:
                if o["sig"]:
                    c += 1
                arr.append(c)
            sigcnt[e] = arr
        sem = {e: 0 for e in ENGS}
        dsem = [0] * NDMASEM
        ptr = {e: 0 for e in ENGS}
        progress = True
        while progress:
            progress = False
            for e in ENGS:
                while ptr[e] < len(self.ops[e]):
                    o = self.ops[e][ptr[e]]
                    ok = True
                    for d in o["deps"]:
                        if d[0] == "eng":
                            if sem[d[1]] < sigcnt[d[1]][d[2]]:
                                ok = False
                        else:
                            if dsem[d[1] % NDMASEM] < 16 * (d[1] // NDMASEM + 1):
                                ok = False
                    if not ok:
                        break
                    if o["dma"] is not None:
                        dsem[o["dma"] % NDMASEM] += 16
                    elif o["sig"] and o["fn"] is not None:
                        sem[e] += 1
                    ptr[e] += 1
                    progress = True
        stuck = {e: (ptr[e], len(self.ops[e])) for e in ENGS if ptr[e] < len(self.ops[e])}
        if stuck:
            for e in stuck:
                o = self.ops[e][ptr[e]]
                print("STUCK", e, ptr[e], o["deps"], "sig", o["sig"], "fn", o["fn"] is not None)
            raise RuntimeError("deadlock in semaphore protocol: %r" % stuck)
        print("simulate ok:", {e: len(self.ops[e]) for e in ENGS}, "dmas", self.ndma)

    def emit(self):
        nc = self.nc
        self.simulate()
        with contextlib.ExitStack() as st:
            esem = {e: st.enter_context(nc.semaphore("s_" + e)) for e in ENGS}
            dsem = [st.enter_context(nc.semaphore("d_%d" % i)) for i in range(NDMASEM)]
            block = st.enter_context(nc.Block())
            sigcnt = {}
            for e in ENGS:
                c = 0
                arr = []
                for o in self.ops[e]:
                    if o["sig"]:
                        c += 1
                    arr.append(c)
                sigcnt[e] = arr
            ndma = self.ndma

            def run(e, engobj):
                waited = {}
                for o in self.ops[e]:
                    for d in o["deps"]:
                        if d[0] == "eng":
                            sem = esem[d[1]]
                            val = sigcnt[d[1]][d[2]]
                            key = "e" + d[1]
                        else:
                            sem = dsem[d[1] % NDMASEM]
                            val = 16 * (d[1] // NDMASEM + 1)
                            key = "d%d" % (d[1] % NDMASEM)
                        if waited.get(key, 0) >= val:
                            continue
                        engobj.wait_ge(sem, val)
                        waited[key] = val
                    if o["fn"] is None:
                        continue
                    ins = o["fn"](engobj)
                    if o["dma"] is not None:
                        ins.then_inc(dsem[o["dma"] % NDMASEM], 16)
                    elif o["sig"]:
                        ins.then_inc(esem[e], 1)
                if e == "sp":
                    for s in range(min(NDMASEM, ndma)):
                        n = (ndma - 1 - s) // NDMASEM + 1
                        engobj.wait_ge(dsem[s], 16 * n)

            @block.sync
            def _(eng):
                run("sp", eng)

            @block.scalar
            def _(eng):
                run("act", eng)

            @block.vector
            def _(eng):
                run("dve", eng)

            @block.gpsimd
            def _(eng):
                run("pool", eng)

            @block.tensor
            def _(eng):
                run("pe", eng)


CST_COLS = {}


def _cst_layout():
    off = 0
    lay = {}
    for name, n in [("ident", 128), ("tri_le", 128), ("caus", 128), ("sgt", 128), ("ones", 128),
                    ("pow2", NIT + 2)]:
        lay[name] = (off, n)
        off += n
    return lay, off


def _lc_layout():
    off = 0
    lay = {}
    for name, n in [("nmix", 8), ("nffn", 8), ("snw", 16), ("xcw", 96), ("xcb", 24), ("fcw", 132),
                    ("fcb", 44), ("dtb", 32), ("alog", 32), ("dsk", 32)]:
        lay[name] = (off, n)
        off += n
    return lay, off


def host_consts(S):
    lay, n = _cst_layout()
    c = np.zeros((128, n), np.float32)
    i = np.arange(128)
    c[:, lay["ident"][0]:lay["ident"][0] + 128] = np.eye(128, dtype=np.float32)
    c[:, lay["tri_le"][0]:lay["tri_le"][0] + 128] = (i[:, None] <= i[None, :]).astype(np.float32)
    c[:, lay["caus"][0]:lay["caus"][0] + 128] = np.where(i[None, :] <= i[:, None], 0.0, -1e30).astype(np.float32)
    c[:, lay["sgt"][0]:lay["sgt"][0] + 128] = (i[:, None] > i[None, :]).astype(np.float32)
    c[:, lay["ones"][0]:lay["ones"][0] + 128] = 1.0
    c[:, lay["pow2"][0]:lay["pow2"][0] + NIT + 2] = (0.5 ** np.arange(1, NIT + 3))[None, :]
    nch = S // 128

    def tab(rot):
        inv = (500000.0 ** (-np.arange(0, rot, 2, dtype=np.float32) / rot)).astype(np.float32)
        ang = np.arange(S, dtype=np.float32)[:, None] * inv[None, :]
        return np.cos(ang).astype(np.float32), np.sin(ang).astype(np.float32)

    ca, sa = tab(32)
    ci, si = tab(16)
    tb = np.concatenate([ca, sa, ci, si], axis=1).reshape(nch, 128, 48).transpose(1, 0, 2)
    return c, np.ascontiguousarray(tb.reshape(128, nch * 48))


def host_layer_consts(inp, l):
    lay, n = _lc_layout()
    c = np.zeros((128, n), np.float32)

    def put(name, arr):
        o, m = lay[name]
        c[:, o:o + m] = arr.reshape(128, m)

    put("nmix", np.asarray(inp["norm_mix_w"][l]).reshape(8, 128).T)
    put("nffn", np.asarray(inp["norm_ffn_w"][l]).reshape(8, 128).T)
    put("snw", np.asarray(inp["ssd_norm_w"][l]).reshape(16, 128).T)
    put("xcw", np.asarray(inp["ssd_conv_w"][l]).reshape(4, 24, 128).transpose(2, 1, 0))
    put("xcb", np.asarray(inp["ssd_conv_b"][l]).reshape(24, 128).T)
    put("fcw", np.asarray(inp["ffn_conv_w"][l]).reshape(3, 44, 128).transpose(2, 1, 0))
    put("fcb", np.asarray(inp["ffn_conv_b"][l]).reshape(44, 128).T)
    put("dtb", np.broadcast_to(np.asarray(inp["ssd_dt_bias"][l])[None, :], (128, 32)))
    put("alog", np.broadcast_to(np.asarray(inp["ssd_a_log"][l])[None, :], (128, 32)))
    put("dsk", np.broadcast_to(np.asarray(inp["ssd_d"][l])[None, :], (128, 32)))
    return c


def build(S, layers=(0, 1), final=True, dbg=(), stop=99):
    nc = bass.Bass("TRN2", target_bir_lowering=False)
    NCH = S // 128
    NST = S // T
    TOPK = min(256, S // 4)
    L = 2
    dt_in = lambda name, shape: nc.dram_tensor(name, list(shape), F32, kind="ExternalInput").ap()
    x_d = dt_in("x", (S, D))
    w_in_d = dt_in("w_in", (L, D, IN_COLS))
    w_pa_d = dt_in("w_proj_attn", (L, D, D))
    w_pb_d = dt_in("w_proj_ssd", (L, SSD_INNER, D))
    w_out_d = dt_in("w_out", (L, D, D))
    w_up_d = dt_in("ffn_w_up", (L, D, 2 * FFN))
    w_dn_d = dt_in("ffn_w_down", (L, FFN, D))
    clay, ncst = _cst_layout()
    llay, nlc = _lc_layout()
    cst_d = dt_in("cst", (128, ncst))
    tab_d = dt_in("tab", (128, NCH * 48))
    lc_d = dt_in("lc", (L, 128, nlc))
    nfin_d = dt_in("nfin", (128, D))
    out_d = nc.dram_tensor("out", [S, D], F32, kind="ExternalOutput").ap()
    xmid_d = nc.dram_tensor("xmid", [S, D], F32, kind="Internal").ap()
    xl1_d = nc.dram_tensor("xl1", [S, D], F32, kind="Internal").ap()
    dbg_d = {}

    P = Prog(nc)
    st = contextlib.ExitStack()
    sb = lambda name, shape, dt: st.enter_context(nc.sbuf_tensor("sb_" + name, list(shape), dt))

    kT = sb("kT", (128, NKV, S), BF16)
    Vc = sb("Vc", (128, NCH, NKV, 129), BF16)
    kiT2 = sb("kiT2", (128, S), BF16)
    hst = sb("hst", (128, SSD_INNER), F32)
    wst = [sb("wst%d" % i, (128, 4, 512), F32) for i in range(2)]
    wbf = [sb("wbf%d" % i, (128, 8, 512), BF16) for i in range(2)]
    hT = sb("hT", (128, 8, T), BF16)
    aT = sb("aT", (128, 8, T), BF16)
    bT = sb("bT", (128, 16, T), BF16)
    cst = sb("cst", (128, ncst), F32)
    lcs = sb("lcs", (128, nlc), F32)
    tabs = sb("tabs", (128, 4, 48), F32)
    identb = sb("identb", (128, 128), BF16)
    trib = sb("trib", (128, 128), F32)
    xtail = sb("xtail", (128, 24, 3), F32)
    ftail = sb("ftail", (128, 44, 2), F32)
    negA = sb("negA", (128, 32), F32)
    epsT = sb("epsT", (128, 1), F32)
    negb = sb("negb", (128, 1), F32)
    AR_BYTES = 67 * 1024
    arena = sb("arena", (128, AR_BYTES // 4), F32)
    psum = st.enter_context(nc.psum_tensor("psum", [128, 8, 512], F32))

    def cv(name, j0=0, j1=None):
        o, n = clay[name]
        j1 = n if j1 is None else j1
        return cst[:, o + j0:o + j1]

    def lv(name, j0=0, j1=None):
        o, n = llay[name]
        j1 = n if j1 is None else j1
        return lcs[:, o + j0:o + j1]

    class View:
        pass

    def aview(off, shape, dt):
        n = int(np.prod(shape[1:]))
        esz = 4 if dt == F32 else 2
        assert off % 4 == 0 and off + n * esz <= AR_BYTES, (off, shape)
        nf = (n * esz + 3) // 4
        ap = arena[:, off // 4: off // 4 + nf]
        if dt != F32:
            ap = ap.bitcast(dt)
        if len(shape) == 3:
            ap = ap.rearrange("p (a b) -> p a b", a=shape[1])
        elif len(shape) == 4:
            ap = ap.rearrange("p (a b c) -> p a b c", a=shape[1], b=shape[2])
        return ap

    def psb(b):
        return psum[:, b, :].bitcast(BF16)

    def bc(ap, shape):
        return ap.to_broadcast(list(shape))

    wcnt = [0]
    hcnt = [0]

    def load_slab(wd, r0, nk, c0, ncols):
        slot = wcnt[0] % 2
        wcnt[0] += 1
        for h0 in range(0, nk, 4):
            hn_ = min(4, nk - h0)
            hs = hcnt[0] % 2
            hcnt[0] += 1
            src = wd[r0 + h0 * 128: r0 + (h0 + hn_) * 128, c0:c0 + ncols].rearrange("(k p) n -> p k n", p=128)
            P.dma(wst[hs][:, 0:hn_, 0:ncols], src, w=["wst%d" % hs])
            if hcnt[0] % 3 == 0:
                P.op("pool", lambda e, hs=hs, h0=h0, hn_=hn_, slot=slot: e.tensor_copy(
                    out=wbf[slot][:, h0:h0 + hn_, 0:ncols], in_=wst[hs][:, 0:hn_, 0:ncols]),
                    r=["wst%d" % hs], w=["wbf%d" % slot])
            else:
                P.op("act", lambda e, hs=hs, h0=h0, hn_=hn_, slot=slot: e.activation(
                    out=wbf[slot][:, h0:h0 + hn_, 0:ncols], in_=wst[hs][:, 0:hn_, 0:ncols], func=AF.Copy),
                    r=["wst%d" % hs], w=["wbf%d" % slot])
        return slot

    pending = []

    def sched(loads, fn, extra=0):
        pending.append((loads, fn, extra))

    def flush():
        tasks = pending[:]
        del pending[:]
        loaded = {}

        def do_load(i):
            if i not in loaded:
                loaded[i] = [load_slab(*a) for a in tasks[i][0]]

        for i, (loads, fn, extra) in enumerate(tasks):
            do_load(i)
            if i + 1 < len(tasks) and len(loads) + extra + len(tasks[i + 1][0]) <= 2:
                do_load(i + 1)
            fn(loaded[i])

    bankctr = [0]

    def mm(out, lhsT, rhs, start, stop, r, w):
        P.op("pe", lambda e: e.matmul(out, lhsT, rhs, start=start, stop=stop), r=r, w=w)

    def transpose(out, in_, r, w):
        P.op("pe", lambda e: e.transpose(out, in_, identb[:, :]), r=list(r) + ["identb"], w=w)

    def dbg_dump(name, ap, shape, keys):
        if name not in dbg:
            return
        if name not in dbg_d:
            dbg_d[name] = nc.dram_tensor("dbg_" + name, list(shape), ap.dtype, kind="ExternalOutput").ap()
        return dbg_d[name]

    P.dma(cst[:, :], cst_d[:, :], w=["cst"])
    P.op("dve", lambda e: e.tensor_copy(out=identb[:, :], in_=cv("ident")), r=["cst"], w=["identb"])
    P.op("dve", lambda e: e.memset(epsT[:, :], EPS), w=["epsT"])
    P.op("dve", lambda e: e.memset(negb[:, :], -30000.0), w=["negb"])
    P.op("pool", lambda e: e.memset(Vc[:, :, :, 128:129], 1.0), w=["Vc"])

    mix_scale = float(HD) ** -0.5

    def rmsnorm_to_hT(src_d, st_i, nw_name, xin_views, hn_view, small, key_prefix):
        for c in range(4):
            gc = st_i * 4 + c
            xin = xin_views[c % 2]
            xk = "xin%d" % (c % 2)
            P.dma(xin, src_d[gc * 128:(gc + 1) * 128, :], r=[(key_prefix, gc)], w=[xk])
            ss = small[:, c:c + 1]
            rt = small[:, 4 + c:5 + c]
            rstd = small[:, 8 + c:9 + c]
            P.op("act", lambda e, xin=xin, ss=ss: e.activation(out=hn_view, in_=xin, func=AF.Square, accum_out=ss),
                 r=[xk], w=["hn", "nsm%d" % c])
            P.op("act", lambda e, ss=ss, rt=rt: e.activation(out=rt, in_=ss, func=AF.Sqrt, bias=epsT[:, :], scale=1.0 / D),
                 r=["nsm%d" % c, "epsT"], w=["nsm%d" % c])
            P.op("dve", lambda e, rt=rt, rstd=rstd: e.reciprocal(out=rstd, in_=rt), r=["nsm%d" % c], w=["nsm%d" % c])
            P.op("dve", lambda e, xin=xin, rstd=rstd: e.tensor_scalar(out=hn_view, in0=xin, scalar1=rstd, scalar2=None,
                                                                       op0=OP.mult), r=[xk, "nsm%d" % c], w=["hn"])
            b = 6 + (c % 2)
            for k in range(8):
                transpose(psb(b)[:, k * 128:(k + 1) * 128], hn_view[:, k * 128:(k + 1) * 128], r=["hn"], w=["ps%d" % b])
            P.op("dve", lambda e, b=b, c=c: e.tensor_tensor(
                out=hT[:, :, c * 128:(c + 1) * 128],
                in0=psb(b).rearrange("p (k t) -> p k t", k=8),
                in1=bc(lv(nw_name).unsqueeze(2), (128, 8, 128)), op=OP.mult),
                r=["ps%d" % b, "lcs"], w=["hT"])

    def proj_tok(lhsT_fn, nk_list, slot_list, ncols, evac, banks, lkeys):
        for c in range(4):
            b = banks[c % len(banks)]
            kk = 0
            tot = sum(nk_list)
            for si, slot in enumerate(slot_list):
                for k in range(nk_list[si]):
                    mm(psum[:, b, 0:ncols], lhsT_fn(kk, c), wbf[slot][:, k, 0:ncols], start=(kk == 0), stop=(kk == tot - 1),
                       r=lkeys + ["wbf%d" % slot], w=["ps%d" % b])
                    kk += 1
            evac(c, b)

    for l in layers:
        src_d = x_d if l == layers[0] else xl1_d
        last = (l == layers[-1])
        P.barrier()
        P.dma(lcs[:, :], lc_d[l], w=["lcs"])
        P.op("act", lambda e: e.activation(out=negA[:, :], in_=lv("alog"), func=AF.Exp), r=["lcs"], w=["negA"])
        P.op("dve", lambda e: e.tensor_scalar(out=negA[:, :], in0=negA[:, :], scalar1=-1.0, scalar2=None, op0=OP.mult),
             r=["negA"], w=["negA"])
        P.op("pool", lambda e: e.memset(hst[:, :], 0.0), w=["hst"])
        P.op("pool", lambda e: e.memset(xtail[:, :, :], 0.0), w=["xtail"])
        P.op("pool", lambda e: e.memset(ftail[:, :, :], 0.0), w=["ftail"])

        for st_i in range(NST):
            P.barrier()
            scores = aview(0, (128, S), F32)
            qT = aview(16384, (128, 8, T), BF16)
            qiT = aview(24576, (128, 4, T), BF16)
            maskT = aview(28672, (128, NCH if NCH <= 32 else 32, 128), BF16)
            junkb = aview(28672, (128, 4096), BF16)
            xin_v = [aview(36864, (128, D), F32), aview(40960, (128, D), F32)]
            hn_v = aview(45056, (128, D), BF16)
            qtok = [aview(47104, (128, 4, 128), BF16), aview(48128, (128, 4, 128), BF16)]
            qitf = aview(49152, (128, 8, 64), F32)
            rbuf = [aview(53248 + 1024 * i, (128, 512), BF16) for i in range(4)]
            maskb = [aview(57344 + 1024 * i, (128, 512), BF16) for i in range(2)]
            Eb = [aview(59392 + 1024 * i, (128, 4, 128), BF16) for i in range(2)]
            dsg = aview(61440, (128, 8, 128), BF16)
            hb_ctr = [0]
            a_tok = aview(63488, (128, 8, 128), BF16)
            small = aview(65536, (128, 256), F32)
            ropet = [aview(66560 + 256 * i, (128, 64), F32) for i in range(2)]
            qitok = aview(67072, (128, 512), BF16)
            wabs = small[:, 16:48].rearrange("p (c h) -> p c h", c=4)
            wsgn = small[:, 48:80].rearrange("p (c h) -> p c h", c=4)
            kitok = small[:, 192:256].bitcast(BF16)
            ktok = qtok[1]

            P.dma(tabs[:, :, :], tab_d[:, st_i * 192:(st_i + 1) * 192].rearrange("p (c n) -> p c n", c=4), w=["tabs"])
            rmsnorm_to_hT(src_d, st_i, "nmix", xin_v, hn_v, small, "xsrc%d" % l)

            lhs_h = lambda k, c: hT[:, k, c * 128:(c + 1) * 128]

            def rope(c, src3, dst3, half, cos_o, sin_o, nh, rk, wk):
                cosv = bc(tabs[:, c, cos_o:cos_o + half].unsqueeze(1), (128, nh, half))
                sinv = bc(tabs[:, c, sin_o:sin_o + half].unsqueeze(1), (128, nh, half))
                x1 = src3[:, :, 0:half]
                x2 = src3[:, :, half:2 * half]
                t1 = ropet[0][:, 0:nh * half].rearrange("p (a b) -> p a b", a=nh)
                t2 = ropet[1][:, 0:nh * half].rearrange("p (a b) -> p a b", a=nh)
                rr = list(rk) + ["tabs"]
                P.op("dve", lambda e: e.tensor_tensor(out=t1, in0=x1, in1=cosv, op=OP.mult), r=rr, w=["rt1"])
                P.op("dve", lambda e: e.tensor_tensor(out=t2, in0=x2, in1=sinv, op=OP.mult), r=rr, w=["rt2"])
                P.op("dve", lambda e: e.tensor_tensor(out=dst3[:, :, 0:half], in0=t1, in1=t2, op=OP.subtract),
                     r=["rt1", "rt2"], w=wk)
                P.op("dve", lambda e: e.tensor_tensor(out=t1, in0=x2, in1=cosv, op=OP.mult), r=rr, w=["rt1"])
                P.op("dve", lambda e: e.tensor_tensor(out=t2, in0=x1, in1=sinv, op=OP.mult), r=rr, w=["rt2"])
                P.op("dve", lambda e: e.tensor_tensor(out=dst3[:, :, half:2 * half], in0=t1, in1=t2, op=OP.add),
                     r=["rt1", "rt2"], w=wk)

            for qs in range(2 if stop >= 2 else 0):

                def evac_q(c, b, qs=qs):
                    import os
                    SUB = int(os.environ.get("SUB", "9"))
                    qt = qtok[0]
                    pv = psum[:, b, :].rearrange("p (h d) -> p h d", h=4)
                    if SUB <= 2:
                        return
                    P.op("act", lambda e: e.activation(out=qt[:, :, 32:128], in_=pv[:, :, 32:128], func=AF.Copy),
                         r=["ps%d" % b], w=["qtok0"])
                    if SUB <= 3:
                        return
                    rope(c, pv, qt, 16, 0, 16, 4, ["ps%d" % b], ["qtok0"])
                    if SUB <= 4:
                        return
                    tb = 4 + (c % 2)
                    for h in range(4):
                        transpose(psb(tb)[:, h * 128:(h + 1) * 128], qt[:, h, :], r=["qtok0"], w=["ps%d" % tb])
                    P.op("act", lambda e: e.activation(out=qT[:, qs * 4:(qs + 1) * 4, c * 128:(c + 1) * 128],
                                                       in_=psb(tb)[:, 0:512].rearrange("p (h t) -> p h t", h=4), func=AF.Copy),
                         r=["ps%d" % tb], w=["qT"])

                sched([(w_in_d[l], 0, 8, O_Q + qs * 512, 512)],
                      lambda s, evac_q=evac_q: proj_tok(lhs_h, [8], s, 512, evac_q, [0, 1, 2, 3], ["hT"]))


            def evac_kv(c, b):
                gc = st_i * 4 + c
                pv = psum[:, b, 0:256].rearrange("p (h d) -> p h d", h=2)
                kt = ktok[:, 0:2, :]
                P.op("act", lambda e: e.activation(out=kt[:, :, 32:128], in_=pv[:, :, 32:128], func=AF.Copy),
                     r=["ps%d" % b], w=["qtok1"])
                rope(c, pv, kt, 16, 0, 16, 2, ["ps%d" % b], ["qtok1"])
                P.op("act", lambda e: e.activation(out=Vc[:, gc, :, 0:128],
                                                   in_=psum[:, b, 256:512].rearrange("p (h d) -> p h d", h=2), func=AF.Copy),
                     r=["ps%d" % b], w=[("Vc", gc)])
                tb = 4 + (c % 2)
                for h in range(2):
                    transpose(psb(tb)[:, h * 128:(h + 1) * 128], kt[:, h, :], r=["qtok1"], w=["ps%d" % tb])
                P.op("act", lambda e: e.activation(out=kT[:, :, gc * 128:(gc + 1) * 128],
                                                   in_=psb(tb)[:, 0:256].rearrange("p (h t) -> p h t", h=2), func=AF.Copy),
                     r=["ps%d" % tb], w=[("kT", gc)])

            if stop >= 3:
                sched([(w_in_d[l], 0, 8, O_K, 512)], lambda s: proj_tok(lhs_h, [8], s, 512, evac_kv, [0, 1, 2, 3], ["hT"]))


            def evac_ki(c, b):
                gc = st_i * 4 + c
                pv = psum[:, b, 0:64].rearrange("p (h d) -> p h d", h=1)
                k3 = kitok[:, 0:64].rearrange("p (h d) -> p h d", h=1)
                P.op("act", lambda e: e.activation(out=k3[:, :, 16:64], in_=pv[:, :, 16:64], func=AF.Copy),
                     r=["ps%d" % b], w=["kitok"])
                rope(c, pv, k3, 8, 32, 40, 1, ["ps%d" % b], ["kitok"])
                P.op("dve", lambda e: e.tensor_copy(out=kitok[:, 64:128], in_=kitok[:, 0:64]), r=["kitok"], w=["kitok"])
                P.op("act", lambda e: e.activation(out=wabs[:, c, :], in_=psum[:, b, 64:72], func=AF.Abs),
                     r=["ps%d" % b], w=["wabs%d" % c])
                P.op("act", lambda e: e.activation(out=wsgn[:, c, :], in_=psum[:, b, 64:72], func=AF.Sign),
                     r=["ps%d" % b], w=["wsgn%d" % c])
                tb = 4 + (c % 2)
                transpose(psb(tb)[:, 0:128], kitok[:, :], r=["kitok"], w=["ps%d" % tb])
                P.op("act", lambda e: e.activation(out=kiT2[:, gc * 128:(gc + 1) * 128], in_=psb(tb)[:, 0:128], func=AF.Copy),
                     r=["ps%d" % tb], w=[("kiT", gc)])

            if stop >= 4:
                sched([(w_in_d[l], 0, 8, O_KI, 72)], lambda s: proj_tok(lhs_h, [8], s, 72, evac_ki, [0, 1, 2, 3], ["hT"]))


            def evac_qi(c, b):
                pv = psum[:, b, :].rearrange("p (h d) -> p h d", h=8)
                P.op("act", lambda e: e.activation(out=qitf[:, :, 16:64], in_=pv[:, :, 16:64], func=AF.Copy),
                     r=["ps%d" % b], w=["qitf"])
                rope(c, pv, qitf, 8, 32, 40, 8, ["ps%d" % b], ["qitf"])
                P.op("dve", lambda e: e.tensor_tensor(out=qitok.rearrange("p (h d) -> p h d", h=8), in0=qitf,
                                                      in1=bc(wabs[:, c, :].unsqueeze(2), (128, 8, 64)), op=OP.mult),
                     r=["qitf", "wabs%d" % c], w=["qitok"])
                tb = 4 + (c % 2)
                for j in range(4):
                    transpose(psb(tb)[:, j * 128:(j + 1) * 128], qitok[:, j * 128:(j + 1) * 128], r=["qitok"], w=["ps%d" % tb])
                P.op("act", lambda e: e.activation(out=qiT[:, :, c * 128:(c + 1) * 128],
                                                   in_=psb(tb)[:, 0:512].rearrange("p (h t) -> p h t", h=4), func=AF.Copy),
                     r=["ps%d" % tb], w=["qiT"])

            if stop >= 5:
                sched([(w_in_d[l], 0, 8, O_QI, 512)], lambda s: proj_tok(lhs_h, [8], s, 512, evac_qi, [0, 1, 2, 3], ["hT"]))
            flush()

            P.barrier()
            scores_v = [aview(0, (128, S), F32), aview(36864, (128, S), F32)]
            maskT_v = [aview(28672, (128, 32, 128), BF16),
                       wst[0][:, :, :].rearrange("p a b -> p (a b)").bitcast(BF16).rearrange("p (j t) -> p j t", j=32)]
            mkey = ["maskT0", "wst0"]

            def geom(c):
                gc = st_i * 4 + c
                nk = gc + 1
                n = nk * 128
                return gc, nk, n, slice(c * 128, (c + 1) * 128), (n + 511) // 512

            def st1(c):
                gc, nk, n, tq, ngrp = geom(c)
                scores = scores_v[c % 2]
                kkeys = [("kiT", j) for j in range(nk)]
                for h in range(8):
                    P.op("pool", lambda e, h=h: e.tensor_scalar(out=dsg[:, h, :], in0=identb[:, :], scalar1=wsgn[:, c, h:h + 1],
                                                                scalar2=None, op0=OP.mult),
                         r=["identb", "wsgn%d" % c], w=["dsg"])
                for kg in range(ngrp):
                    w_ = min(512, n - kg * 512)
                    sb_ = 2
                    for h in range(8):
                        b = hb_ctr[0] % 2
                        hb_ctr[0] += 1
                        pr = (h % 2) * 64
                        mm(psum[:, b, 0:w_], qiT[pr:pr + 64, h // 2, tq], kiT2[pr:pr + 64, kg * 512:kg * 512 + w_], True, True,
                           r=["qiT"] + kkeys, w=["ps%d" % b])
                        rb = rbuf[h % 4]
                        P.op("act", lambda e, b=b, rb=rb, w_=w_: e.activation(out=rb[:, 0:w_], in_=psum[:, b, 0:w_], func=AF.Relu),
                             r=["ps%d" % b], w=["rbuf%d" % (h % 4)])
                        mm(psum[:, sb_, 0:w_], dsg[:, h, :], rb[:, 0:w_], (h == 0), (h == 7),
                           r=["dsg", "rbuf%d" % (h % 4)], w=["ps%d" % sb_])
                    sc = scores[:, kg * 512:kg * 512 + w_]
                    P.op("act", lambda e, sb_=sb_, sc=sc, w_=w_: e.activation(out=sc, in_=psum[:, sb_, 0:w_], func=AF.Copy),
                         r=["ps%d" % sb_], w=[("sc", c % 2, kg)])

            def st2(c):
                gc, nk, n, tq, ngrp = geom(c)
                scores = scores_v[c % 2]
                maskT = maskT_v[c % 2]
                junkb = maskT.rearrange("p j t -> p (j t)")
                mk = mkey[c % 2]
                sckeys = [("sc", c % 2, kg) for kg in range(ngrp)]
                lastk = ("sc", c % 2, ngrp - 1)
                thr = small[:, 100 + (c % 2):101 + (c % 2)]
                tk = "thr%d" % (c % 2)
                if (gc * 128 + 128) <= TOPK:
                    P.op("dve", lambda e: e.tensor_tensor(out=scores[:, n - 128:n], in0=scores[:, n - 128:n], in1=cv("caus"),
                                                          op=OP.add), r=[lastk, "cst"], w=[lastk])
                    P.op("dve", lambda e: e.memset(thr, -1e29), w=[tk])
                else:
                    mx = small[:, 102:103]
                    mn = small[:, 103:104]
                    w0 = small[:, 104:105]
                    mid = small[:, 105:106]
                    cnt = small[:, 106:107]
                    dd = small[:, 107:108]
                    Wk = small[:, 110:110 + NIT + 2]
                    P.op("dve", lambda e: e.tensor_reduce(out=mx, in_=scores[:, 0:n], axis=AX.X, op=OP.max), r=sckeys, w=["bs_mx"])
                    P.op("dve", lambda e: e.tensor_reduce(out=mn, in_=scores[:, 0:n], axis=AX.X, op=OP.min), r=sckeys, w=["bs_mn"])
                    P.op("dve", lambda e: e.tensor_tensor(out=scores[:, n - 128:n], in0=scores[:, n - 128:n], in1=cv("caus"),
                                                          op=OP.add), r=[lastk, "cst"], w=[lastk])
                    P.op("dve", lambda e: e.tensor_tensor(out=w0, in0=mx, in1=mn, op=OP.subtract), r=["bs_mx", "bs_mn"], w=["bs_w0"])
                    P.op("dve", lambda e: e.tensor_scalar(out=Wk, in0=cv("pow2"), scalar1=w0, scalar2=None, op0=OP.mult),
                         r=["bs_w0", "cst"], w=["bs_wk"])
                    P.op("dve", lambda e: e.tensor_tensor(out=mid, in0=mn, in1=Wk[:, 0:1], op=OP.add), r=["bs_mn", "bs_wk"], w=["bs_mid"])
                    for it in range(NIT):
                        P.op("dve", lambda e: e.tensor_scalar(out=junkb[:, 0:n], in0=scores[:, 0:n], scalar1=mid, scalar2=None,
                                                              op0=OP.is_ge, op1=OP.add, accum_out=cnt),
                             r=sckeys + ["bs_mid"], w=[mk, "bs_cnt"])
                        lastit = (it == NIT - 1)
                        P.op("dve", lambda e, lastit=lastit: e.tensor_scalar(out=dd, in0=cnt, scalar1=float(TOPK),
                                                                              scalar2=(1.0 if lastit else 0.5),
                                                                              op0=OP.is_ge, op1=OP.subtract),
                             r=["bs_cnt"], w=["bs_dd"])
                        dst = thr if lastit else mid
                        P.op("dve", lambda e, it=it, dst=dst: e.scalar_tensor_tensor(
                            out=dst, in0=dd, scalar=Wk[:, it:it + 1], in1=mid, op0=OP.mult, op1=OP.add),
                            r=["bs_dd", "bs_wk", "bs_mid"], w=[tk if lastit else "bs_mid"])
                for kg in range(ngrp):
                    w_ = min(512, n - kg * 512)
                    mb = maskb[kg % 2]
                    P.op("dve", lambda e, mb=mb, kg=kg, w_=w_: e.tensor_scalar(
                        out=mb[:, 0:w_], in0=scores[:, kg * 512:kg * 512 + w_], scalar1=thr, scalar2=None, op0=OP.is_ge),
                        r=[("sc", c % 2, kg), tk], w=["maskb%d" % (kg % 2)])
                    tb = 3
                    nj = w_ // 128
                    for jj in range(nj):
                        transpose(psb(tb)[:, jj * 128:(jj + 1) * 128], mb[:, jj * 128:(jj + 1) * 128],
                                  r=["maskb%d" % (kg % 2)], w=["ps%d" % tb])
                    P.op("act", lambda e, tb=tb, kg=kg, nj=nj: e.activation(
                        out=maskT[:, kg * 4:kg * 4 + nj, :], in_=psb(tb)[:, 0:nj * 128].rearrange("p (j t) -> p j t", j=nj),
                        func=AF.Identity, scale=30000.0, bias=negb[:, :]), r=["ps%d" % tb, "negb"], w=[mk])
                if "sc" in dbg and gc == NCH - 1:
                    dd_ = dbg_dump("sc", scores, (128, S), None)
                    P.dma(dd_, scores, r=sckeys)
                    dd_ = dbg_dump("thr", thr, (128, 1), None)
                    P.dma(dd_, thr, r=[tk])

            def st3a(c):
                gc, nk, n, tq, ngrp = geom(c)
                maskT = maskT_v[c % 2]
                mk = mkey[c % 2]
                for kvh in range(2):
                    if kvh == 1:
                        st3b_half(c, 0)
                    for j in range(nk):
                        lb = 3
                        mm(psum[:, lb, :], kT[:, kvh, j * 128:(j + 1) * 128], qT[:, kvh * 4:(kvh + 1) * 4, tq], True, False,
                           r=[("kT", j), "qT"], w=["ps%d" % lb])
                        mm(psum[:, lb, :], identb[:, :], bc(maskT[:, j, :].unsqueeze(1), (128, 4, 128)), False, True,
                           r=["identb", mk], w=["ps%d" % lb])
                        Ev = Eb[j % 2]
                        P.op("act", lambda e, lb=lb, Ev=Ev: e.activation(
                            out=Ev, in_=psum[:, lb, :].rearrange("p (g t) -> p g t", g=4), func=AF.Exp, scale=mix_scale),
                            r=["ps%d" % lb], w=["E%d" % (j % 2)])
                        for g in range(4):
                            mm(psum[:, 4 + g, 0:129], Ev[:, g, :], Vc[:, j, kvh, :], (j == 0), (j == nk - 1),
                               r=["E%d" % (j % 2), ("Vc", j), "Vc"], w=["ps%d" % (4 + g)])

            def st3b_half(c, kvh):
                for g in range(4):
                    h = kvh * 4 + g
                    rden = small[:, 140 + h:141 + h]
                    P.op("dve", lambda e, g=g, rden=rden: e.reciprocal(out=rden, in_=psum[:, 4 + g, 128:129]),
                         r=["ps%d" % (4 + g)], w=["rden%d" % h])
                    P.op("act", lambda e, g=g, h=h, rden=rden: e.activation(
                        out=a_tok[:, h, :], in_=psum[:, 4 + g, 0:128], func=AF.Copy, scale=rden),
                        r=["ps%d" % (4 + g), "rden%d" % h], w=["a_tok"])

            def st3b(c):
                st3b_half(c, 1)
                tb = 3
                for h in range(8):
                    transpose(psb(tb)[:, h * 128:(h + 1) * 128], a_tok[:, h, :], r=["a_tok"], w=["ps%d" % tb])
                P.op("act", lambda e, tb=tb, c=c: e.activation(
                    out=aT[:, :, c * 128:(c + 1) * 128], in_=psb(tb).rearrange("p (h t) -> p h t", h=8), func=AF.Copy),
                    r=["ps%d" % tb], w=["aT"])

            if stop >= 6:
                st1(0)
                st1(1)
                st2(0)
                st3a(0)
                st1(2)
                st2(1)
                st3b(0)
                st3a(1)
                st1(3)
                st2(2)
                st3b(1)
                st3a(2)
                st2(3)
                st3b(2)
                st3a(3)
                st3b(3)

            if "aT" in dbg:
                dd_ = dbg_dump("aT", aT[:, :, :], (NST, 128, 8, T), None)
                P.dma(dd_[st_i], aT[:, :, :], r=["aT"])
            if "qT" in dbg:
                dd_ = dbg_dump("qT", qT, (NST, 128, 8, T), None)
                P.dma(dd_[st_i], qT, r=["qT"])
            if "hT" in dbg:
                dd_ = dbg_dump("hT", hT[:, :, :], (NST, 128, 8, T), None)
                P.dma(dd_[st_i], hT[:, :, :], r=["hT"])

            if stop >= 7:
                P.barrier()
                BT = aview(0, (128, 4, T), BF16)
                CT = aview(4096, (128, 4, T), BF16)
                Btok = aview(8192, (128, 4, 512), BF16)
                xacc = [aview(12288 + 2048 * i, (128, 512), F32) for i in range(2)]
                xact = [aview(16384 + 1024 * i, (128, 512), BF16) for i in range(2)]
                xtok = aview(18432, (128, 4, 512), BF16)
                zs = aview(22528, (128, 4, 512), BF16)
                dtt = aview(26624, (128, 4, 32), F32)
                adt = aview(27136, (128, 4, 32), F32)
                Wp = aview(27648, (128, 8, 128), F32)
                Lx = aview(31744, (128, 8, 128), BF16)
                MT = aview(33792, (128, 8, 128), BF16)
                CBm = aview(35840, (128, 128), BF16)
                Xdt = aview(36096, (128, 8, 64), BF16)
                Xd2 = aview(37120, (128, 8, 64), BF16)
                yo = aview(38144, (128, 8, 64), F32)
                yy = aview(40192, (128, 8, 64), F32)
                hb = aview(42240, (128, 512), BF16)
                sm2 = aview(43264, (128, 64), F32)
                btok_b = aview(43520, (128, 512), BF16)
                ysq = aview(44544, (128, 512), BF16)

                def conv_fm(cc, b, wname, bname, ntap, tail, dst_silu, tailkey):
                    u = psum[:, b, :]
                    acc = xacc[cc % 2]
                    ak = "xacc%d" % (cc % 2)
                    wv = lambda j: lv(wname, cc * ntap + j, cc * ntap + j + 1)
                    P.op("act", lambda e: e.activation(out=acc, in_=u, func=AF.Identity, bias=lv(bname, cc, cc + 1),
                                                       scale=wv(ntap - 1)), r=["ps%d" % b, "lcs"], w=[ak])
                    for j in range(ntap - 1):
                        sh = ntap - 1 - j
                        P.op("dve", lambda e, j=j, sh=sh: e.scalar_tensor_tensor(
                            out=acc[:, sh:512], in0=u[:, 0:512 - sh], scalar=wv(j), in1=acc[:, sh:512], op0=OP.mult, op1=OP.add),
                            r=["ps%d" % b, "lcs", ak], w=[ak])
                        P.op("dve", lambda e, j=j, sh=sh: e.scalar_tensor_tensor(
                            out=acc[:, 0:sh], in0=tail[:, cc, ntap - 1 - sh:ntap - 1], scalar=wv(j), in1=acc[:, 0:sh],
                            op0=OP.mult, op1=OP.add), r=[tailkey, "lcs", ak], w=[ak])
                    P.op("dve", lambda e: e.tensor_copy(out=tail[:, cc, :], in_=u[:, 512 - (ntap - 1):512]),
                         r=["ps%d" % b], w=[tailkey])
                    return acc, ak

                def proj_fm(slot, ncc, cc0, handler):
                    for j in range(ncc):
                        b = j % 4
                        for k in range(8):
                            mm(psum[:, b, :], wbf[slot][:, k, j * 128:(j + 1) * 128], hT[:, k, :], (k == 0), (k == 7),
                               r=["hT", "wbf%d" % slot], w=["ps%d" % b])
                        handler(cc0 + j, b)

                def h_bc(cc, b):
                    acc, ak = conv_fm(cc, b, "xcw", "xcb", 4, xtail, None, "xtail")
                    g = (cc - 16) % 4
                    if cc < 20:
                        P.op("act", lambda e: e.activation(out=BT[:, g, :], in_=acc, func=AF.Silu), r=[ak], w=["BT"])
                        tb = 4 + (cc % 2)
                        for tc_ in range(4):
                            transpose(psb(tb)[:, tc_ * 128:(tc_ + 1) * 128], BT[:, g, tc_ * 128:(tc_ + 1) * 128], r=["BT"], w=["ps%d" % tb])
                        P.op("act", lambda e: e.activation(out=Btok[:, :, g * 128:(g + 1) * 128],
                                                           in_=psb(tb)[:, 0:512].rearrange("p (c n) -> p c n", c=4), func=AF.Copy),
                             r=["ps%d" % tb], w=["Btok"])
                    else:
                        P.op("act", lambda e: e.activation(out=CT[:, g, :], in_=acc, func=AF.Silu), r=[ak], w=["CT"])

                for half in range(2):
                    sched([(w_in_d[l], 0, 8, O_XBC + 2048 + half * 512, 512)],
                          lambda s, half=half: proj_fm(s[0], 4, 16 + half * 4, h_bc))

                def evac_dt(c, b):
                    P.op("dve", lambda e: e.tensor_tensor(out=dtt[:, c, :], in0=psum[:, b, 0:32], in1=lv("dtb"), op=OP.add),
                         r=["ps%d" % b, "lcs"], w=["dtt"])
                    P.op("act", lambda e: e.activation(out=dtt[:, c, :], in_=dtt[:, c, :], func=AF.Exp), r=["dtt"], w=["dtt"])
                    P.op("act", lambda e: e.activation(out=dtt[:, c, :], in_=dtt[:, c, :], func=AF.Ln, bias=1.0), r=["dtt"], w=["dtt"])
                    P.op("dve", lambda e: e.tensor_tensor(out=adt[:, c, :], in0=dtt[:, c, :], in1=negA[:, :], op=OP.mult),
                         r=["dtt", "negA"], w=["adt"])

                sched([(w_in_d[l], 0, 8, O_DT, 32)], lambda s: proj_tok(lhs_h, [8], s, 32, evac_dt, [0, 1, 2, 3], ["hT"]))

                for g in range(4):

                    def h_x(cc, b, g=g):
                        acc, ak = conv_fm(cc, b, "xcw", "xcb", 4, xtail, None, "xtail")
                        xa = xact[cc % 2]
                        xk = "xact%d" % (cc % 2)
                        P.op("act", lambda e: e.activation(out=xa, in_=acc, func=AF.Silu), r=[ak], w=[xk])
                        tb = 4 + (cc % 2)
                        for tc_ in range(4):
                            transpose(psb(tb)[:, tc_ * 128:(tc_ + 1) * 128], xa[:, tc_ * 128:(tc_ + 1) * 128], r=[xk], w=["ps%d" % tb])
                        j = cc % 4
                        P.op("act", lambda e: e.activation(out=xtok[:, :, j * 128:(j + 1) * 128],
                                                           in_=psb(tb)[:, 0:512].rearrange("p (c n) -> p c n", c=4), func=AF.Copy),
                             r=["ps%d" % tb], w=["xtok"])

                    sched([(w_in_d[l], 0, 8, O_XBC + g * 512, 512)], lambda s, g=g, h_x=h_x: proj_fm(s[0], 4, g * 4, h_x))

                    def evac_z(c, b):
                        P.op("act", lambda e: e.activation(out=zs[:, c, :], in_=psum[:, b, :], func=AF.Silu), r=["ps%d" % b], w=["zs"])

                    sched([(w_in_d[l], 0, 8, O_Z + g * 512, 512)],
                          lambda s, evac_z=evac_z: proj_tok(lhs_h, [8], s, 512, evac_z, [0, 1, 2, 3], ["hT"]))

                    def ssd_chunk(c, g=g):
                        ts_ = slice(c * 128, (c + 1) * 128)
                        xt3 = xtok[:, c, :].rearrange("p (e d) -> p e d", e=8)
                        dtg = dtt[:, c, g * 8:(g + 1) * 8]
                        adg = adt[:, c, g * 8:(g + 1) * 8]
                        hs3 = hst[:, g * 512:(g + 1) * 512].rearrange("p (e d) -> p e d", e=8)
                        hk = "hst%d" % g
                        P.op("dve", lambda e: e.tensor_tensor(out=Xdt, in0=xt3, in1=bc(dtg.unsqueeze(2), (128, 8, 64)), op=OP.mult),
                             r=["xtok", "dtt"], w=["Xdt"])
                        P.op("dve", lambda e: e.tensor_tensor(out=Wp, in0=bc(cv("tri_le").unsqueeze(1), (128, 8, 128)),
                                                              in1=bc(adg.unsqueeze(2), (128, 8, 128)), op=OP.mult),
                             r=["adt", "cst"], w=["Wp"])
                        for hh in range(2):
                            mm(psum[:, hh, :], cv("sgt"), Wp[:, hh * 4:(hh + 1) * 4, :], True, True, r=["cst", "Wp"], w=["ps%d" % hh])
                            P.op("act", lambda e, hh=hh: e.activation(out=Lx[:, hh * 4:(hh + 1) * 4, :],
                                                                      in_=psum[:, hh, :].rearrange("p (e l) -> p e l", e=4), func=AF.Exp),
                                 r=["ps%d" % hh], w=["Lx"])
                            P.op("act", lambda e, hh=hh: e.activation(out=sm2[:, hh * 4:(hh + 1) * 4],
                                                                      in_=psum[:, hh, :].rearrange("p (e l) -> p e l", e=4)[:, :, 127],
                                                                      func=AF.Exp), r=["ps%d" % hh], w=["decay"])
                        mm(psum[:, 2, 0:128], BT[:, g, ts_], CT[:, g, ts_], True, True, r=["BT", "CT"], w=["ps2"])
                        P.op("dve", lambda e: e.tensor_tensor(out=CBm, in0=psum[:, 2, 0:128], in1=cv("tri_le"), op=OP.mult),
                             r=["ps2", "cst"], w=["CBm"])
                        P.op("dve", lambda e: e.tensor_tensor(out=MT, in0=Lx, in1=bc(CBm.unsqueeze(1), (128, 8, 128)), op=OP.mult),
                             r=["Lx", "CBm"], w=["MT"])
                        for e_ in range(8):
                            mm(psum[:, 3, e_ * 64:(e_ + 1) * 64], MT[:, e_, :], Xdt[:, e_, :], True, True, r=["MT", "Xdt"], w=["ps3"])
                        P.op("act", lambda e: e.activation(out=hb, in_=hst[:, g * 512:(g + 1) * 512], func=AF.Copy), r=[hk], w=["hb"])
                        mm(psum[:, 4, :], CT[:, g, ts_], hb, True, True, r=["CT", "hb"], w=["ps4"])
                        mm(psum[:, 5, 0:8], cv("tri_le"), adg, True, True, r=["cst", "adt"], w=["ps5"])
                        P.op("act", lambda e: e.activation(out=sm2[:, 8:16], in_=psum[:, 5, 0:8], func=AF.Exp), r=["ps5"], w=["eacs"])
                        P.op("dve", lambda e: e.tensor_tensor(out=yo, in0=psum[:, 4, :].rearrange("p (e d) -> p e d", e=8),
                                                              in1=bc(sm2[:, 8:16].unsqueeze(2), (128, 8, 64)), op=OP.mult),
                             r=["ps4", "eacs"], w=["yo"])
                        P.op("dve", lambda e: e.tensor_tensor(out=yy, in0=psum[:, 3, :].rearrange("p (e d) -> p e d", e=8), in1=yo, op=OP.add),
                             r=["ps3", "yo"], w=["yy"])
                        P.op("dve", lambda e: e.tensor_tensor(out=yo, in0=xt3, in1=bc(lv("dsk", g * 8, g * 8 + 8).unsqueeze(2), (128, 8, 64)),
                                                              op=OP.mult), r=["xtok", "lcs"], w=["yo"])
                        P.op("dve", lambda e: e.tensor_tensor(out=yy, in0=yy, in1=yo, op=OP.add), r=["yy", "yo"], w=["yy"])
                        P.op("dve", lambda e: e.tensor_tensor(out=Xd2, in0=Xdt, in1=bc(sm2[:, 0:8].unsqueeze(2), (128, 8, 64)), op=OP.mult),
                             r=["Xdt", "decay"], w=["Xd2"])
                        mm(psum[:, 6, :], Btok[:, c, g * 128:(g + 1) * 128], Xd2.rearrange("p e d -> p (e d)"), True, True,
                           r=["Btok", "Xd2"], w=["ps6"])
                        mm(psum[:, 7, 0:8], cv("ones"), adg, True, True, r=["cst", "adt"], w=["ps7"])
                        P.op("act", lambda e: e.activation(out=sm2[:, 16:24], in_=psum[:, 7, 0:8], func=AF.Exp), r=["ps7"], w=["cdec"])
                        P.op("dve", lambda e: e.tensor_tensor(out=hs3, in0=hs3, in1=bc(sm2[:, 16:24].unsqueeze(2), (128, 8, 64)), op=OP.mult),
                             r=[hk, "cdec"], w=[hk])
                        P.op("dve", lambda e: e.tensor_tensor(out=hs3, in0=psum[:, 6, :].rearrange("p (e d) -> p e d", e=8), in1=hs3, op=OP.add),
                             r=["ps6", hk], w=[hk])
                        yf = yy.rearrange("p e d -> p (e d)")
                        P.op("dve", lambda e: e.tensor_tensor(out=yf, in0=yf, in1=zs[:, c, :], op=OP.mult), r=["yy", "zs"], w=["yy"])
                        P.op("act", lambda e: e.activation(out=ysq, in_=yf, func=AF.Square, accum_out=sm2[:, 24:25]), r=["yy"], w=["ysq", "gss"])
                        P.op("act", lambda e: e.activation(out=sm2[:, 25:26], in_=sm2[:, 24:25], func=AF.Sqrt, bias=epsT[:, :], scale=1.0 / 512),
                             r=["gss", "epsT"], w=["gss"])
                        P.op("dve", lambda e: e.reciprocal(out=sm2[:, 26:27], in_=sm2[:, 25:26]), r=["gss"], w=["gss"])
                        P.op("dve", lambda e: e.tensor_scalar(out=btok_b, in0=yf, scalar1=sm2[:, 26:27], scalar2=None, op0=OP.mult),
                             r=["yy", "gss"], w=["btok_b"])
                        tb = 1
                        for j in range(4):
                            transpose(psb(tb)[:, j * 128:(j + 1) * 128], btok_b[:, j * 128:(j + 1) * 128], r=["btok_b"], w=["ps%d" % tb])
                        P.op("dve", lambda e: e.tensor_tensor(out=bT[:, g * 4:(g + 1) * 4, ts_],
                                                              in0=psb(tb)[:, 0:512].rearrange("p (j t) -> p j t", j=4),
                                                              in1=bc(lv("snw", g * 4, g * 4 + 4).unsqueeze(2), (128, 4, 128)), op=OP.mult),
                             r=["ps%d" % tb, "lcs"], w=["bT"])

                    sched([], lambda s, ssd_chunk=ssd_chunk: [ssd_chunk(c) for c in range(4)])
                flush()
                if "bT" in dbg:
                    dd_ = dbg_dump("bT", bT[:, :, :], (NST, 128, 16, T), None)
                    P.dma(dd_[st_i], bT[:, :, :], r=["bT"])

            if stop >= 8:
                P.barrier()
                ga = aview(0, (128, 4, 512), BF16)
                gb = aview(4096, (128, 4, 512), BF16)
                mf = aview(8192, (128, 4, 512), F32)
                tmpf = aview(16384, (128, 512), F32)
                mtok = aview(18432, (128, 512), BF16)
                mT = aview(19456, (128, 8, T), BF16)
                xh = aview(27648, (128, 4, 512), F32)
                for cs in range(2):
                    sched([(w_in_d[l], 0, 8, O_GA + cs * 512, 512)], lambda s: proj_tok(
                        lhs_h, [8], s, 512,
                        lambda c, b: P.op("act", lambda e: e.activation(out=ga[:, c, :], in_=psum[:, b, :], func=AF.Sigmoid),
                                          r=["ps%d" % b], w=["ga"]), [0, 1, 2, 3], ["hT"]))
                    sched([(w_in_d[l], 0, 8, O_GB + cs * 512, 512)], lambda s: proj_tok(
                        lhs_h, [8], s, 512,
                        lambda c, b: P.op("act", lambda e: e.activation(out=gb[:, c, :], in_=psum[:, b, :], func=AF.Sigmoid),
                                          r=["ps%d" % b], w=["gb"]), [0, 1, 2, 3], ["hT"]))
                    sched([(w_pa_d[l], 0, 8, cs * 512, 512)], lambda s: proj_tok(
                        lambda k, c: aT[:, k, c * 128:(c + 1) * 128], [8], s, 512,
                        lambda c, b: P.op("dve", lambda e: e.tensor_tensor(out=mf[:, c, :], in0=psum[:, b, :], in1=ga[:, c, :], op=OP.mult),
                                          r=["ps%d" % b, "ga"], w=["mf"]), [0, 1, 2, 3], ["aT"]))

                    def evac_pb(c, b, cs=cs):
                        P.op("dve", lambda e: e.tensor_tensor(out=tmpf, in0=psum[:, b, :], in1=gb[:, c, :], op=OP.mult),
                             r=["ps%d" % b, "gb"], w=["tmpf"])
                        P.op("dve", lambda e: e.tensor_tensor(out=mtok, in0=tmpf, in1=mf[:, c, :], op=OP.add), r=["tmpf", "mf"], w=["mtok"])
                        tb = 6 + (c % 2)
                        for j in range(4):
                            transpose(psb(tb)[:, j * 128:(j + 1) * 128], mtok[:, j * 128:(j + 1) * 128], r=["mtok"], w=["ps%d" % tb])
                        P.op("act", lambda e: e.activation(out=mT[:, cs * 4:(cs + 1) * 4, c * 128:(c + 1) * 128],
                                                           in_=psb(tb)[:, 0:512].rearrange("p (j t) -> p j t", j=4), func=AF.Copy),
                             r=["ps%d" % tb], w=["mT"])

                    sched([(w_pb_d[l], 0, 8, cs * 512, 512), (w_pb_d[l], 1024, 8, cs * 512, 512)],
                          lambda s, evac_pb=evac_pb: proj_tok(lambda k, c: bT[:, k, c * 128:(c + 1) * 128], [8, 8], s, 512, evac_pb,
                                                              [0, 1, 2, 3], ["bT"]))
                flush()
                for cs in range(2):
                    def out_task(s, cs=cs):
                        P.dma(xh, src_d[st_i * T:(st_i + 1) * T, cs * 512:(cs + 1) * 512].rearrange("(c p) n -> p c n", p=128),
                              r=[("xsrc%d" % l, st_i * 4 + c) for c in range(4)], w=["xh"])
                        proj_tok(lambda k, c: mT[:, k, c * 128:(c + 1) * 128], [8], s, 512,
                                 lambda c, b: P.op("dve", lambda e: e.tensor_tensor(out=xh[:, c, :], in0=psum[:, b, :], in1=xh[:, c, :], op=OP.add),
                                                   r=["ps%d" % b, "xh"], w=["xh"]), [0, 1, 2, 3], ["mT"])
                        P.dma(xmid_d[st_i * T:(st_i + 1) * T, cs * 512:(cs + 1) * 512].rearrange("(c p) n -> p c n", p=128), xh,
                              r=["xh"], w=[("xmid", st_i * 4 + c) for c in range(4)])
                    sched([(w_out_d[l], 0, 8, cs * 512, 512)], out_task)
                flush()

            if stop >= 9:
                P.barrier()
                xin_f = [aview(0, (128, D), F32), aview(4096, (128, D), F32)]
                hn_f = aview(8192, (128, D), BF16)
                small_f = aview(10240, (128, 64), F32)
                xacc = [aview(10496 + 2048 * i, (128, 512), F32) for i in range(2)]
                sg = aview(14592, (128, 22, 512), BF16)
                xo = aview(37120, (128, 4, D), F32)
                nfin = aview(53504, (128, D), F32)
                osq = aview(57600, (128, D), BF16)
                rmsnorm_to_hT(xmid_d, st_i, "nffn", xin_f, hn_f, small_f, "xmid")

                def h_up(cc, b):
                    acc, ak = conv_fm(cc, b, "fcw", "fcb", 3, ftail, None, "ftail")
                    if cc < 22:
                        P.op("act", lambda e: e.activation(out=sg[:, cc, :], in_=acc, func=AF.Silu), r=[ak], w=[("sg", cc)])
                    else:
                        P.op("dve", lambda e: e.tensor_tensor(out=sg[:, cc - 22, :], in0=sg[:, cc - 22, :], in1=acc, op=OP.mult),
                             r=[ak, ("sg", cc - 22)], w=[("sg", cc - 22)])

                for sl in range(11):
                    sched([(w_up_d[l], 0, 8, sl * 512, 512)], lambda s, sl=sl: proj_fm(s[0], 4, sl * 4, h_up))
                sgk = [("sg", i) for i in range(22)]

                def down_task(slots, cs):
                    P.dma(xo[:, :, cs * 512:(cs + 1) * 512],
                          xmid_d[st_i * T:(st_i + 1) * T, cs * 512:(cs + 1) * 512].rearrange("(c p) n -> p c n", p=128),
                          r=[("xmid", st_i * 4 + c) for c in range(4)], w=["xo"])
                    for c in range(4):
                        kk = 0
                        for si in range(2):
                            for k in range(8):
                                mm(psum[:, c, :], sg[:, kk, c * 128:(c + 1) * 128], wbf[slots[si]][:, k, :], (kk == 0), False,
                                   r=sgk + ["wbf%d" % slots[si]], w=["ps%d" % c])
                                kk += 1
                    s2 = load_slab(w_dn_d[l], 2048, 6, cs * 512, 512)
                    for c in range(4):
                        for k in range(6):
                            mm(psum[:, c, :], sg[:, 16 + k, c * 128:(c + 1) * 128], wbf[s2][:, k, :], False, (k == 5),
                               r=sgk + ["wbf%d" % s2], w=["ps%d" % c])
                        P.op("dve", lambda e, c=c, cs=cs: e.tensor_tensor(out=xo[:, c, cs * 512:(cs + 1) * 512], in0=psum[:, c, :],
                                                                          in1=xo[:, c, cs * 512:(cs + 1) * 512], op=OP.add),
                             r=["ps%d" % c, "xo"], w=["xo"])

                for cs in range(2):
                    sched([(w_dn_d[l], 0, 8, cs * 512, 512), (w_dn_d[l], 1024, 8, cs * 512, 512)],
                          lambda s, cs=cs: down_task(s, cs), extra=1)
                flush()
                if not last:
                    P.dma(xl1_d[st_i * T:(st_i + 1) * T, :].rearrange("(c p) n -> p c n", p=128), xo, r=["xo"],
                          w=[("xsrc1", st_i * 4 + c) for c in range(4)])
                else:
                    P.dma(nfin, nfin_d[:, :], w=["nfin"])
                    for c in range(4):
                        ss = small_f[:, 32 + c:33 + c]
                        rt = small_f[:, 36 + c:37 + c]
                        rs = small_f[:, 40 + c:41 + c]
                        P.op("act", lambda e, c=c, ss=ss: e.activation(out=osq, in_=xo[:, c, :], func=AF.Square, accum_out=ss),
                             r=["xo"], w=["osq", "fsm%d" % c])
                        P.op("act", lambda e, ss=ss, rt=rt: e.activation(out=rt, in_=ss, func=AF.Sqrt, bias=epsT[:, :], scale=1.0 / D),
                             r=["fsm%d" % c, "epsT"], w=["fsm%d" % c])
                        P.op("dve", lambda e, rt=rt, rs=rs: e.reciprocal(out=rs, in_=rt), r=["fsm%d" % c], w=["fsm%d" % c])
                        P.op("dve", lambda e, c=c, rs=rs: e.scalar_tensor_tensor(out=xo[:, c, :], in0=xo[:, c, :], scalar=rs, in1=nfin,
                                                                                  op0=OP.mult, op1=OP.mult),
                             r=["xo", "fsm%d" % c, "nfin"], w=["xo"])
                    P.dma(out_d[st_i * T:(st_i + 1) * T, :].rearrange("(c p) n -> p c n", p=128), xo, r=["xo"])

    if "kT" in dbg:
        dd_ = dbg_dump("kT", kT[:, :, :], (128, NKV, S), None)
        P.dma(dd_, kT[:, :, :], r=[("kT", j) for j in range(NCH)])
    if "kiT" in dbg:
        dd_ = dbg_dump("kiT", kiT2[:, :], (128, S), None)
        P.dma(dd_, kiT2[:, :], r=[("kiT", j) for j in range(NCH)])
    P.emit()
    st.close()
    return nc


_NC_CACHE = {}


def kernel(**inputs):
    inp = {k: np.asarray(v) for k, v in inputs.items()}
    x = inp["x"].astype(np.float32, copy=False)
    B, S, _ = x.shape
    if S not in _NC_CACHE:
        _NC_CACHE[S] = build(S, layers=(0, 1))
    nc = _NC_CACHE[S]
    cst, tab = host_consts(S)
    lc = np.stack([host_layer_consts(inp, l) for l in range(2)])
    nfin = np.ascontiguousarray(np.broadcast_to(inp["norm_final_w"].astype(np.float32)[None, :], (128, D)))
    shared = {"w_in": np.ascontiguousarray(inp["w_in"], dtype=np.float32),
              "w_proj_attn": np.ascontiguousarray(inp["w_proj_attn"], dtype=np.float32),
              "w_proj_ssd": np.ascontiguousarray(inp["w_proj_ssd"], dtype=np.float32),
              "w_out": np.ascontiguousarray(inp["w_out"], dtype=np.float32),
              "ffn_w_up": np.ascontiguousarray(inp["ffn_w_up"], dtype=np.float32),
              "ffn_w_down": np.ascontiguousarray(inp["ffn_w_down"], dtype=np.float32),
              "cst": cst, "tab": tab, "lc": lc, "nfin": nfin}
    in_maps = [dict(shared, x=np.ascontiguousarray(x[b])) for b in range(B)]
    res = run_bass_kernel_spmd(nc, in_maps, core_ids=list(range(B)))
    return np.stack([np.asarray(r["out"], dtype=np.float32) for r in res.results], axis=0)
```

```python
import contextlib
import numpy as np
import concourse.bass as bass
import concourse.mybir as mybir
from concourse.bass_utils import run_bass_kernel_spmd

F32 = mybir.dt.float32
BF16 = mybir.dt.bfloat16
AF = mybir.ActivationFunctionType
OP = mybir.AluOpType
AX = mybir.AxisListType

D = 1024
NH, HD, NKV = 8, 128, 2
IH, IDM = 8, 64
SSD_INNER, SSD_HD, SSD_H, SSD_G, SSD_N = 2048, 64, 32, 4, 128
CONV_DIM = 3072
FFN = 2816
EPS = 1e-6
IN_COLS = 9320
O_Q, O_K, O_V, O_QI, O_KI, O_WI, O_Z, O_XBC, O_DT, O_GA, O_GB = (
    0, 1024, 1280, 1536, 2048, 2112, 2120, 4168, 7240, 7272, 8296)
T = 512
NIT = 22

ENGS = ["pe", "act", "dve", "pool", "sp"]
NDMASEM = 24


class Prog:
    def __init__(self, nc):
        self.nc = nc
        self.ops = {e: [] for e in ENGS}
        self.last_w = {}
        self.readers = {}
        self.ndma = 0
        self.last_real = {}
        self.ps_last = {}

    @staticmethod
    def _isps(k):
        return isinstance(k, str) and k.startswith("ps") and k[2:].isdigit()

    def _deps(self, eng, r, w):
        deps = []
        for k in list(r) + list(w):
            if self._isps(k):
                is_w = k in w
                last = self.ps_last.get(k)
                if last is not None:
                    ref, lw = last
                    same = (ref[0] == "eng" and ref[1] == eng)
                    if (not same) or is_w or lw:
                        deps.append(ref)
        r = [k for k in r if not self._isps(k)]
        w = [k for k in w if not self._isps(k)]
        for k in r:
            lw = self.last_w.get(k)
            if lw is not None:
                deps.append(lw)
        for k in w:
            lw = self.last_w.get(k)
            if lw is not None:
                deps.append(lw)
            deps.extend(self.readers.get(k, ()))
        out = []
        for d in deps:
            if d[0] == "eng" and d[1] == "pe" and eng == "pe":
                continue
            if d not in out:
                out.append(d)
        return out

    def _commit(self, ref, r, w):
        for k in list(r) + list(w):
            if self._isps(k):
                self.ps_last[k] = (ref, k in w)
        r = [k for k in r if not self._isps(k)]
        w = [k for k in w if not self._isps(k)]
        for k in r:
            self.readers.setdefault(k, []).append(ref)
        for k in w:
            self.last_w[k] = ref
            self.readers[k] = []

    def _mark(self, deps):
        for d in deps:
            if d[0] == "eng":
                self.ops[d[1]][d[2]]["sig"] = True

    def op(self, eng, fn, r=(), w=()):
        deps = self._deps(eng, r, w)
        self._mark(deps)
        idx = len(self.ops[eng])
        self.ops[eng].append(dict(fn=fn, deps=deps, sig=False, dma=None))
        self.last_real[eng] = idx
        self._commit(("eng", eng, idx), r, w)

    def dma(self, out, in_, r=(), w=(), q="sp"):
        deps = self._deps(q, r, w)
        self._mark(deps)
        i = self.ndma
        self.ndma += 1
        if i >= NDMASEM:
            deps.append(("dma", i - NDMASEM))
        self.ops[q].append(dict(fn=lambda e: e.dma_start(out=out, in_=in_), deps=deps, sig=False, dma=i))
        self._commit(("dma", i), r, w)

    def barrier(self):
        deps = [("eng", e, i) for e, i in self.last_real.items()]
        deps += [("dma", i) for i in range(max(0, self.ndma - NDMASEM), self.ndma)]
        self._mark(deps)
        for e in ENGS:
            self.ops[e].append(dict(fn=None, deps=[d for d in deps if not (d[0] == "eng" and d[1] == e)],
                                    sig=False, dma=None))
        self.last_w.clear()
        self.readers.clear()
        self.ps_last.clear()

    def simulate(self):
        sigcnt = {}
        for e in ENGS:
            c = 0
            arr = []
            for o in self.ops[e]:
                if o["sig"]:
                    c += 1
                arr.append(c)
            sigcnt[e] = arr
        sem = {e: 0 for e in ENGS}
        dsem = [0] * NDMASEM
        ptr = {e: 0 for e in ENGS}
        progress = True
        while progress:
            progress = False
            for e in ENGS:
                while ptr[e] < len(self.ops[e]):
                    o = self.ops[e][ptr[e]]
                    ok = True
                    for d in o["deps"]:
                        if d[0] == "eng":
                            if sem[d[1]] < sigcnt[d[1]][d[2]]:
                                ok = False
                        else:
                            if dsem[d[1] % NDMASEM] < 16 * (d[1] // NDMASEM + 1):
                                ok = False
                    if not ok:
                        break
                    if o["dma"] is not None:
                        dsem[o["dma"] % NDMASEM] += 16
                    elif o["sig"] and o["fn"] is not None:
                        sem[e] += 1
                    ptr[e] += 1
                    progress = True
        stuck = {e: (ptr[e], len(self.ops[e])) for e in ENGS if ptr[e] < len(self.ops[e])}
        if stuck:
            for e in stuck:
                o = self.ops[e][ptr[e]]
                print("STUCK", e, ptr[e], o["deps"], "sig", o["sig"], "fn", o["fn"] is not None)
            raise RuntimeError("deadlock in semaphore protocol: %r" % stuck)
        print("simulate ok:", {e: len(self.ops[e]) for e in ENGS}, "dmas", self.ndma)

    def emit(self):
        nc = self.nc
        self.simulate()
        with contextlib.ExitStack() as st:
            esem = {e: st.enter_context(nc.semaphore("s_" + e)) for e in ENGS}
            dsem = [st.enter_context(nc.semaphore("d_%d" % i)) for i in range(NDMASEM)]
            block = st.enter_context(nc.Block())
            sigcnt = {}
            for e in ENGS:
                c = 0
                arr = []
                for o in self.ops[e]:
                    if o["sig"]:
                        c += 1
                    arr.append(c)
                sigcnt[e] = arr
            ndma = self.ndma

            def run(e, engobj):
                waited = {}
                for o in self.ops[e]:
                    for d in o["deps"]:
                        if d[0] == "eng":
                            sem = esem[d[1]]
                            val = sigcnt[d[1]][d[2]]
                            key = "e" + d[1]
                        else:
                            sem = dsem[d[1] % NDMASEM]
                            val = 16 * (d[1] // NDMASEM + 1)
                            key = "d%d" % (d[1] % NDMASEM)
                        if waited.get(key, 0) >= val:
                            continue
                        engobj.wait_ge(sem, val)
                        waited[key] = val
                    if o["fn"] is None:
                        continue
                    ins = o["fn"](engobj)
                    if o["dma"] is not None:
                        ins.then_inc(dsem[o["dma"] % NDMASEM], 16)
                    elif o["sig"]:
                        ins.then_inc(esem[e], 1)
                if e == "sp":
                    for s in range(min(NDMASEM, ndma)):
                        n = (ndma - 1 - s) // NDMASEM + 1
                        engobj.wait_ge(dsem[s], 16 * n)

            @block.sync
            def _(eng):
                run("sp", eng)

            @block.scalar
            def _(eng):
                run("act", eng)

            @block.vector
            def _(eng):
                run("dve", eng)

            @block.gpsimd
            def _(eng):
                run("pool", eng)

            @block.tensor
            def _(eng):
                run("pe", eng)


CST_COLS = {}


def _cst_layout():
    off = 0
    lay = {}
    for name, n in [("ident", 128), ("tri_le", 128), ("caus", 128), ("sgt", 128), ("ones", 128),
                    ("pow2", NIT + 2)]:
        lay[name] = (off, n)
        off += n
    return lay, off


def _lc_layout():
    off = 0
    lay = {}
    for name, n in [("nmix", 8), ("nffn", 8), ("snw", 16), ("xcw", 96), ("xcb", 24), ("fcw", 132),
                    ("fcb", 44), ("dtb", 32), ("alog", 32), ("dsk", 32)]:
        lay[name] = (off, n)
        off += n
    return lay, off


def host_consts(S):
    lay, n = _cst_layout()
    c = np.zeros((128, n), np.float32)
    i = np.arange(128)
    c[:, lay["ident"][0]:lay["ident"][0] + 128] = np.eye(128, dtype=np.float32)
    c[:, lay["tri_le"][0]:lay["tri_le"][0] + 128] = (i[:, None] <= i[None, :]).astype(np.float32)
    c[:, lay["caus"][0]:lay["caus"][0] + 128] = np.where(i[None, :] <= i[:, None], 0.0, -1e30).astype(np.float32)
    c[:, lay["sgt"][0]:lay["sgt"][0] + 128] = (i[:, None] > i[None, :]).astype(np.float32)
    c[:, lay["ones"][0]:lay["ones"][0] + 128] = 1.0
    c[:, lay["pow2"][0]:lay["pow2"][0] + NIT + 2] = (0.5 ** np.arange(1, NIT + 3))[None, :]
    nch = S // 128

    def tab(rot):
        inv = (500000.0 ** (-np.arange(0, rot, 2, dtype=np.float32) / rot)).astype(np.float32)
        ang = np.arange(S, dtype=np.float32)[:, None] * inv[None, :]
        return np.cos(ang).astype(np.float32), np.sin(ang).astype(np.float32)

    ca, sa = tab(32)
    ci, si = tab(16)
    tb = np.concatenate([ca, sa, ci, si], axis=1).reshape(nch, 128, 48).transpose(1, 0, 2)
    return c, np.ascontiguousarray(tb.reshape(128, nch * 48))


def host_layer_consts(inp, l):
    lay, n = _lc_layout()
    c = np.zeros((128, n), np.float32)

    def put(name, arr):
        o, m = lay[name]
        c[:, o:o + m] = arr.reshape(128, m)

    put("nmix", np.asarray(inp["norm_mix_w"][l]).reshape(8, 128).T)
    put("nffn", np.asarray(inp["norm_ffn_w"][l]).reshape(8, 128).T)
    put("snw", np.asarray(inp["ssd_norm_w"][l]).reshape(16, 128).T)
    put("xcw", np.asarray(inp["ssd_conv_w"][l]).reshape(4, 24, 128).transpose(2, 1, 0))
    put("xcb", np.asarray(inp["ssd_conv_b"][l]).reshape(24, 128).T)
    put("fcw", np.asarray(inp["ffn_conv_w"][l]).reshape(3, 44, 128).transpose(2, 1, 0))
    put("fcb", np.asarray(inp["ffn_conv_b"][l]).reshape(44, 128).T)
    put("dtb", np.broadcast_to(np.asarray(inp["ssd_dt_bias"][l])[None, :], (128, 32)))
    put("alog", np.broadcast_to(np.asarray(inp["ssd_a_log"][l])[None, :], (128, 32)))
    put("dsk", np.broadcast_to(np.asarray(inp["ssd_d"][l])[None, :], (128, 32)))
    return c


def build(S, layers=(0, 1), final=True, dbg=(), stop=99):
    nc = bass.Bass("TRN2", target_bir_lowering=False)
    NCH = S // 128
    NST = S // T
    TOPK = min(256, S // 4)
    L = 2
    dt_in = lambda name, shape: nc.dram_tensor(name, list(shape), F32, kind="ExternalInput").ap()
    x_d = dt_in("x", (S, D))
    w_in_d = dt_in("w_in", (L, D, IN_COLS))
    w_pa_d = dt_in("w_proj_attn", (L, D, D))
    w_pb_d = dt_in("w_proj_ssd", (L, SSD_INNER, D))
    w_out_d = dt_in("w_out", (L, D, D))
    w_up_d = dt_in("ffn_w_up", (L, D, 2 * FFN))
    w_dn_d = dt_in("ffn_w_down", (L, FFN, D))
    clay, ncst = _cst_layout()
    llay, nlc = _lc_layout()
    cst_d = dt_in("cst", (128, ncst))
    tab_d = dt_in("tab", (128, NCH * 48))
    lc_d = dt_in("lc", (L, 128, nlc))
    nfin_d = dt_in("nfin", (128, D))
    out_d = nc.dram_tensor("out", [S, D], F32, kind="ExternalOutput").ap()
    xmid_d = nc.dram_tensor("xmid", [S, D], F32, kind="Internal").ap()
    xl1_d = nc.dram_tensor("xl1", [S, D], F32, kind="Internal").ap()
    dbg_d = {}

    P = Prog(nc)
    st = contextlib.ExitStack()
    sb = lambda name, shape, dt: st.enter_context(nc.sbuf_tensor("sb_" + name, list(shape), dt))

    kT = sb("kT", (128, NKV, S), BF16)
    Vc = sb("Vc", (128, NCH, NKV, 129), BF16)
    kiT2 = sb("kiT2", (128, S), BF16)
    hst = sb("hst", (128, SSD_INNER), F32)
    wst = [sb("wst%d" % i, (128, 4, 512), F32) for i in range(2)]
    wbf = [sb("wbf%d" % i, (128, 8, 512), BF16) for i in range(2)]
    hT = sb("hT", (128, 8, T), BF16)
    aT = sb("aT", (128, 8, T), BF16)
    bT = sb("bT", (128, 16, T), BF16)
    cst = sb("cst", (128, ncst), F32)
    lcs = sb("lcs", (128, nlc), F32)
    tabs = sb("tabs", (128, 4, 48), F32)
    identb = sb("identb", (128, 128), BF16)
    trib = sb("trib", (128, 128), F32)
    xtail = sb("xtail", (128, 24, 3), F32)
    ftail = sb("ftail", (128, 44, 2), F32)
    negA = sb("negA", (128, 32), F32)
    epsT = sb("epsT", (128, 1), F32)
    negb = sb("negb", (128, 1), F32)
    AR_BYTES = 69 * 1024
    arena = sb("arena", (128, AR_BYTES // 4), F32)
    psum = st.enter_context(nc.psum_tensor("psum", [128, 8, 512], F32))

    def cv(name, j0=0, j1=None):
        o, n = clay[name]
        j1 = n if j1 is None else j1
        return cst[:, o + j0:o + j1]

    def lv(name, j0=0, j1=None):
        o, n = llay[name]
        j1 = n if j1 is None else j1
        return lcs[:, o + j0:o + j1]

    class View:
        pass

    def aview(off, shape, dt):
        n = int(np.prod(shape[1:]))
        esz = 4 if dt == F32 else 2
        assert off % 4 == 0 and off + n * esz <= AR_BYTES, (off, shape)
        nf = (n * esz + 3) // 4
        ap = arena[:, off // 4: off // 4 + nf]
        if dt != F32:
            ap = ap.bitcast(dt)
        if len(shape) == 3:
            ap = ap.rearrange("p (a b) -> p a b", a=shape[1])
        elif len(shape) == 4:
            ap = ap.rearrange("p (a b c) -> p a b c", a=shape[1], b=shape[2])
        return ap

    def psb(b):
        return psum[:, b, :].bitcast(BF16)

    def bc(ap, shape):
        return ap.to_broadcast(list(shape))

    wcnt = [0]
    hcnt = [0]

    def load_slab(wd, r0, nk, c0, ncols):
        slot = wcnt[0] % 2
        wcnt[0] += 1
        for h0 in range(0, nk, 4):
            hn_ = min(4, nk - h0)
            hs = hcnt[0] % 2
            hcnt[0] += 1
            src = wd[r0 + h0 * 128: r0 + (h0 + hn_) * 128, c0:c0 + ncols].rearrange("(k p) n -> p k n", p=128)
            P.dma(wst[hs][:, 0:hn_, 0:ncols], src, w=["wst%d" % hs])
            if hcnt[0] % 3 == 0:
                P.op("pool", lambda e, hs=hs, h0=h0, hn_=hn_, slot=slot: e.tensor_copy(
                    out=wbf[slot][:, h0:h0 + hn_, 0:ncols], in_=wst[hs][:, 0:hn_, 0:ncols]),
                    r=["wst%d" % hs], w=["wbf%d" % slot])
            else:
                P.op("act", lambda e, hs=hs, h0=h0, hn_=hn_, slot=slot: e.activation(
                    out=wbf[slot][:, h0:h0 + hn_, 0:ncols], in_=wst[hs][:, 0:hn_, 0:ncols], func=AF.Copy),
                    r=["wst%d" % hs], w=["wbf%d" % slot])
        return slot

    pending = []

    def sched(loads, fn, extra=0):
        pending.append((loads, fn, extra))

    def flush():
        tasks = pending[:]
        del pending[:]
        loaded = {}

        def do_load(i):
            if i not in loaded:
                loaded[i] = [load_slab(*a) for a in tasks[i][0]]

        for i, (loads, fn, extra) in enumerate(tasks):
            do_load(i)
            if i + 1 < len(tasks) and len(loads) + extra + len(tasks[i + 1][0]) <= 2:
                do_load(i + 1)
            fn(loaded[i])

    bankctr = [0]

    def mm(out, lhsT, rhs, start, stop, r, w):
        P.op("pe", lambda e: e.matmul(out, lhsT, rhs, start=start, stop=stop), r=r, w=w)

    def transpose(out, in_, r, w):
        P.op("pe", lambda e: e.transpose(out, in_, identb[:, :]), r=list(r) + ["identb"], w=w)

    def dbg_dump(name, ap, shape, keys):
        if name not in dbg:
            return
        if name not in dbg_d:
            dbg_d[name] = nc.dram_tensor("dbg_" + name, list(shape), ap.dtype, kind="ExternalOutput").ap()
        return dbg_d[name]

    P.dma(cst[:, :], cst_d[:, :], w=["cst"])
    P.op("dve", lambda e: e.tensor_copy(out=identb[:, :], in_=cv("ident")), r=["cst"], w=["identb"])
    P.op("dve", lambda e: e.memset(epsT[:, :], EPS), w=["epsT"])
    P.op("dve", lambda e: e.memset(negb[:, :], -30000.0), w=["negb"])
    P.op("pool", lambda e: e.memset(Vc[:, :, :, 128:129], 1.0), w=["Vc"])

    mix_scale = float(HD) ** -0.5

    def rmsnorm_to_hT(src_d, st_i, nw_name, xin_views, hn_view, small, key_prefix):
        for c in range(4):
            gc = st_i * 4 + c
            xin = xin_views[c % 2]
            xk = "xin%d" % (c % 2)
            P.dma(xin, src_d[gc * 128:(gc + 1) * 128, :], r=[(key_prefix, gc)], w=[xk])
            ss = small[:, c:c + 1]
            rt = small[:, 4 + c:5 + c]
            rstd = small[:, 8 + c:9 + c]
            P.op("act", lambda e, xin=xin, ss=ss: e.activation(out=hn_view, in_=xin, func=AF.Square, accum_out=ss),
                 r=[xk], w=["hn", "nsm%d" % c])
            P.op("act", lambda e, ss=ss, rt=rt: e.activation(out=rt, in_=ss, func=AF.Sqrt, bias=epsT[:, :], scale=1.0 / D),
                 r=["nsm%d" % c, "epsT"], w=["nsm%d" % c])
            P.op("dve", lambda e, rt=rt, rstd=rstd: e.reciprocal(out=rstd, in_=rt), r=["nsm%d" % c], w=["nsm%d" % c])
            P.op("dve", lambda e, xin=xin, rstd=rstd: e.tensor_scalar(out=hn_view, in0=xin, scalar1=rstd, scalar2=None,
                                                                       op0=OP.mult), r=[xk, "nsm%d" % c], w=["hn"])
            b = 6 + (c % 2)
            for k in range(8):
                transpose(psb(b)[:, k * 128:(k + 1) * 128], hn_view[:, k * 128:(k + 1) * 128], r=["hn"], w=["ps%d" % b])
            P.op("dve", lambda e, b=b, c=c: e.tensor_tensor(
                out=hT[:, :, c * 128:(c + 1) * 128],
                in0=psb(b).rearrange("p (k t) -> p k t", k=8),
                in1=bc(lv(nw_name).unsqueeze(2), (128, 8, 128)), op=OP.mult),
                r=["ps%d" % b, "lcs"], w=["hT"])

    def proj_tok(lhsT_fn, nk_list, slot_list, ncols, evac, banks, lkeys):
        for c in range(4):
            b = banks[c % len(banks)]
            kk = 0
            tot = sum(nk_list)
            for si, slot in enumerate(slot_list):
                for k in range(nk_list[si]):
                    mm(psum[:, b, 0:ncols], lhsT_fn(kk, c), wbf[slot][:, k, 0:ncols], start=(kk == 0), stop=(kk == tot - 1),
                       r=lkeys + ["wbf%d" % slot], w=["ps%d" % b])
                    kk += 1
            evac(c, b)

    for l in layers:
        src_d = x_d if l == layers[0] else xl1_d
        last = (l == layers[-1])
        P.barrier()
        P.dma(lcs[:, :], lc_d[l], w=["lcs"])
        P.op("act", lambda e: e.activation(out=negA[:, :], in_=lv("alog"), func=AF.Exp), r=["lcs"], w=["negA"])
        P.op("dve", lambda e: e.tensor_scalar(out=negA[:, :], in0=negA[:, :], scalar1=-1.0, scalar2=None, op0=OP.mult),
             r=["negA"], w=["negA"])
        P.op("pool", lambda e: e.memset(hst[:, :], 0.0), w=["hst"])
        P.op("pool", lambda e: e.memset(xtail[:, :, :], 0.0), w=["xtail"])
        P.op("pool", lambda e: e.memset(ftail[:, :, :], 0.0), w=["ftail"])

        for st_i in range(NST):
            P.barrier()
            scores = aview(0, (128, S), F32)
            qT = aview(16384, (128, 8, T), BF16)
            qiT = aview(24576, (128, 4, T), BF16)
            maskT = aview(28672, (128, NCH if NCH <= 32 else 32, 128), BF16)
            junkb = aview(28672, (128, 4096), BF16)
            xin_v = [aview(36864, (128, D), F32), aview(40960, (128, D), F32)]
            hn_v = aview(45056, (128, D), BF16)
            qtok = [aview(47104, (128, 4, 128), BF16), aview(48128, (128, 4, 128), BF16)]
            qitf = aview(49152, (128, 8, 64), F32)
            rbuf = [aview(53248 + 1024 * i, (128, 512), BF16) for i in range(4)]
            maskb = [aview(57344 + 1024 * i, (128, 512), BF16) for i in range(2)]
            Eb = [aview(59392 + 1024 * i, (128, 4, 128), BF16) for i in range(2)]
            dsg = aview(61440, (128, 8, 128), BF16)
            hb_ctr = [0]
            a_tok = aview(63488, (128, 8, 128), BF16)
            small = aview(65536, (128, 256), F32)
            ropet = [aview(66560 + 256 * i, (128, 64), F32) for i in range(2)]
            qitok = aview(67072, (128, 512), BF16)
            wabs = small[:, 16:48].rearrange("p (c h) -> p c h", c=4)
            wsgn = small[:, 48:80].rearrange("p (c h) -> p c h", c=4)
            kitok = small[:, 192:256].bitcast(BF16)
            ktok = qtok[1]

            P.dma(tabs[:, :, :], tab_d[:, st_i * 192:(st_i + 1) * 192].rearrange("p (c n) -> p c n", c=4), w=["tabs"])
            rmsnorm_to_hT(src_d, st_i, "nmix", xin_v, hn_v, small, "xsrc%d" % l)

            lhs_h = lambda k, c: hT[:, k, c * 128:(c + 1) * 128]

            def rope(c, src3, dst3, half, cos_o, sin_o, nh, rk, wk):
                cosv = bc(tabs[:, c, cos_o:cos_o + half].unsqueeze(1), (128, nh, half))
                sinv = bc(tabs[:, c, sin_o:sin_o + half].unsqueeze(1), (128, nh, half))
                x1 = src3[:, :, 0:half]
                x2 = src3[:, :, half:2 * half]
                t1 = ropet[0][:, 0:nh * half].rearrange("p (a b) -> p a b", a=nh)
                t2 = ropet[1][:, 0:nh * half].rearrange("p (a b) -> p a b", a=nh)
                rr = list(rk) + ["tabs"]
                P.op("dve", lambda e: e.tensor_tensor(out=t1, in0=x1, in1=cosv, op=OP.mult), r=rr, w=["rt1"])
                P.op("dve", lambda e: e.tensor_tensor(out=t2, in0=x2, in1=sinv, op=OP.mult), r=rr, w=["rt2"])
                P.op("dve", lambda e: e.tensor_tensor(out=dst3[:, :, 0:half], in0=t1, in1=t2, op=OP.subtract),
                     r=["rt1", "rt2"], w=wk)
                P.op("dve", lambda e: e.tensor_tensor(out=t1, in0=x2, in1=cosv, op=OP.mult), r=rr, w=["rt1"])
                P.op("dve", lambda e: e.tensor_tensor(out=t2, in0=x1, in1=sinv, op=OP.mult), r=rr, w=["rt2"])
                P.op("dve", lambda e: e.tensor_tensor(out=dst3[:, :, half:2 * half], in0=t1, in1=t2, op=OP.add),
                     r=["rt1", "rt2"], w=wk)

            for qs in range(2 if stop >= 2 else 0):

                def evac_q(c, b, qs=qs):
                    import os
                    SUB = int(os.environ.get("SUB", "9"))
                    qt = qtok[0]
                    pv = psum[:, b, :].rearrange("p (h d) -> p h d", h=4)
                    if SUB <= 2:
                        return
                    P.op("act", lambda e: e.activation(out=qt[:, :, 32:128], in_=pv[:, :, 32:128], func=AF.Copy),
                         r=["ps%d" % b], w=["qtok0"])
                    if SUB <= 3:
                        return
                    rope(c, pv, qt, 16, 0, 16, 4, ["ps%d" % b], ["qtok0"])
                    if SUB <= 4:
                        return
                    tb = 4 + (c % 2)
                    for h in range(4):
                        transpose(psb(tb)[:, h * 128:(h + 1) * 128], qt[:, h, :], r=["qtok0"], w=["ps%d" % tb])
                    P.op("act", lambda e: e.activation(out=qT[:, qs * 4:(qs + 1) * 4, c * 128:(c + 1) * 128],
                                                       in_=psb(tb)[:, 0:512].rearrange("p (h t) -> p h t", h=4), func=AF.Copy),
                         r=["ps%d" % tb], w=["qT"])

                sched([(w_in_d[l], 0, 8, O_Q + qs * 512, 512)],
                      lambda s, evac_q=evac_q: proj_tok(lhs_h, [8], s, 512, evac_q, [0, 1, 2, 3], ["hT"]))


            def evac_kv(c, b):
                gc = st_i * 4 + c
                pv = psum[:, b, 0:256].rearrange("p (h d) -> p h d", h=2)
                kt = ktok[:, 0:2, :]
                P.op("act", lambda e: e.activation(out=kt[:, :, 32:128], in_=pv[:, :, 32:128], func=AF.Copy),
                     r=["ps%d" % b], w=["qtok1"])
                rope(c, pv, kt, 16, 0, 16, 2, ["ps%d" % b], ["qtok1"])
                P.op("act", lambda e: e.activation(out=Vc[:, gc, :, 0:128],
                                                   in_=psum[:, b, 256:512].rearrange("p (h d) -> p h d", h=2), func=AF.Copy),
                     r=["ps%d" % b], w=[("Vc", gc)])
                tb = 4 + (c % 2)
                for h in range(2):
                    transpose(psb(tb)[:, h * 128:(h + 1) * 128], kt[:, h, :], r=["qtok1"], w=["ps%d" % tb])
                P.op("act", lambda e: e.activation(out=kT[:, :, gc * 128:(gc + 1) * 128],
                                                   in_=psb(tb)[:, 0:256].rearrange("p (h t) -> p h t", h=2), func=AF.Copy),
                     r=["ps%d" % tb], w=[("kT", gc)])

            if stop >= 3:
                sched([(w_in_d[l], 0, 8, O_K, 512)], lambda s: proj_tok(lhs_h, [8], s, 512, evac_kv, [0, 1, 2, 3], ["hT"]))


            def evac_ki(c, b):
                gc = st_i * 4 + c
                pv = psum[:, b, 0:64].rearrange("p (h d) -> p h d", h=1)
                k3 = kitok[:, 0:64].rearrange("p (h d) -> p h d", h=1)
                P.op("act", lambda e: e.activation(out=k3[:, :, 16:64], in_=pv[:, :, 16:64], func=AF.Copy),
                     r=["ps%d" % b], w=["kitok"])
                rope(c, pv, k3, 8, 32, 40, 1, ["ps%d" % b], ["kitok"])
                P.op("dve", lambda e: e.tensor_copy(out=kitok[:, 64:128], in_=kitok[:, 0:64]), r=["kitok"], w=["kitok"])
                P.op("act", lambda e: e.activation(out=wabs[:, c, :], in_=psum[:, b, 64:72], func=AF.Abs),
                     r=["ps%d" % b], w=["wabs%d" % c])
                P.op("act", lambda e: e.activation(out=wsgn[:, c, :], in_=psum[:, b, 64:72], func=AF.Sign),
                     r=["ps%d" % b], w=["wsgn%d" % c])
                tb = 4 + (c % 2)
                transpose(psb(tb)[:, 0:128], kitok[:, :], r=["kitok"], w=["ps%d" % tb])
                P.op("act", lambda e: e.activation(out=kiT2[:, gc * 128:(gc + 1) * 128], in_=psb(tb)[:, 0:128], func=AF.Copy),
                     r=["ps%d" % tb], w=[("kiT", gc)])

            if stop >= 4:
                sched([(w_in_d[l], 0, 8, O_KI, 72)], lambda s: proj_tok(lhs_h, [8], s, 72, evac_ki, [0, 1, 2, 3], ["hT"]))


            def evac_qi(c, b):
                pv = psum[:, b, :].rearrange("p (h d) -> p h d", h=8)
                P.op("act", lambda e: e.activation(out=qitf[:, :, 16:64], in_=pv[:, :, 16:64], func=AF.Copy),
                     r=["ps%d" % b], w=["qitf"])
                rope(c, pv, qitf, 8, 32, 40, 8, ["ps%d" % b], ["qitf"])
                P.op("dve", lambda e: e.tensor_tensor(out=qitok.rearrange("p (h d) -> p h d", h=8), in0=qitf,
                                                      in1=bc(wabs[:, c, :].unsqueeze(2), (128, 8, 64)), op=OP.mult),
                     r=["qitf", "wabs%d" % c], w=["qitok"])
                tb = 4 + (c % 2)
                for j in range(4):
                    transpose(psb(tb)[:, j * 128:(j + 1) * 128], qitok[:, j * 128:(j + 1) * 128], r=["qitok"], w=["ps%d" % tb])
                P.op("act", lambda e: e.activation(out=qiT[:, :, c * 128:(c + 1) * 128],
                                                   in_=psb(tb)[:, 0:512].rearrange("p (h t) -> p h t", h=4), func=AF.Copy),
                     r=["ps%d" % tb], w=["qiT"])

            if stop >= 5:
                sched([(w_in_d[l], 0, 8, O_QI, 512)], lambda s: proj_tok(lhs_h, [8], s, 512, evac_qi, [0, 1, 2, 3], ["hT"]))
            flush()

            P.barrier()
            scores_v = [aview(0, (128, S), F32), aview(36864, (128, S), F32)]
            maskT_v = [aview(28672, (128, 32, 128), BF16),
                       wst[0][:, :, :].rearrange("p a b -> p (a b)").bitcast(BF16).rearrange("p (j t) -> p j t", j=32)]
            mkey = ["maskT0", "wst0"]

            def geom(c):
                gc = st_i * 4 + c
                nk = gc + 1
                n = nk * 128
                return gc, nk, n, slice(c * 128, (c + 1) * 128), (n + 511) // 512

            def st1(c):
                gc, nk, n, tq, ngrp = geom(c)
                scores = scores_v[c % 2]
                kkeys = [("kiT", j) for j in range(nk)]
                for h in range(8):
                    P.op("pool", lambda e, h=h: e.tensor_scalar(out=dsg[:, h, :], in0=identb[:, :], scalar1=wsgn[:, c, h:h + 1],
                                                                scalar2=None, op0=OP.mult),
                         r=["identb", "wsgn%d" % c], w=["dsg"])
                for kg in range(ngrp):
                    w_ = min(512, n - kg * 512)
                    sb_ = 2
                    for h in range(8):
                        b = hb_ctr[0] % 2
                        hb_ctr[0] += 1
                        pr = (h % 2) * 64
                        mm(psum[:, b, 0:w_], qiT[pr:pr + 64, h // 2, tq], kiT2[pr:pr + 64, kg * 512:kg * 512 + w_], True, True,
                           r=["qiT"] + kkeys, w=["ps%d" % b])
                        rb = rbuf[h % 4]
                        P.op("act", lambda e, b=b, rb=rb, w_=w_: e.activation(out=rb[:, 0:w_], in_=psum[:, b, 0:w_], func=AF.Relu),
                             r=["ps%d" % b], w=["rbuf%d" % (h % 4)])
                        mm(psum[:, sb_, 0:w_], dsg[:, h, :], rb[:, 0:w_], (h == 0), (h == 7),
                           r=["dsg", "rbuf%d" % (h % 4)], w=["ps%d" % sb_])
                    sc = scores[:, kg * 512:kg * 512 + w_]
                    P.op("act", lambda e, sb_=sb_, sc=sc, w_=w_: e.activation(out=sc, in_=psum[:, sb_, 0:w_], func=AF.Copy),
                         r=["ps%d" % sb_], w=[("sc", c % 2, kg)])

            def st2(c):
                gc, nk, n, tq, ngrp = geom(c)
                scores = scores_v[c % 2]
                maskT = maskT_v[c % 2]
                junkb = maskT.rearrange("p j t -> p (j t)")
                mk = mkey[c % 2]
                sckeys = [("sc", c % 2, kg) for kg in range(ngrp)]
                lastk = ("sc", c % 2, ngrp - 1)
                thr = small[:, 100 + (c % 2):101 + (c % 2)]
                tk = "thr%d" % (c % 2)
                if (gc * 128 + 128) <= TOPK:
                    P.op("dve", lambda e: e.tensor_tensor(out=scores[:, n - 128:n], in0=scores[:, n - 128:n], in1=cv("caus"),
                                                          op=OP.add), r=[lastk, "cst"], w=[lastk])
                    P.op("dve", lambda e: e.memset(thr, -1e29), w=[tk])
                else:
                    mx = small[:, 102:103]
                    mn = small[:, 103:104]
                    w0 = small[:, 104:105]
                    mid = small[:, 105:106]
                    cnt = small[:, 106:107]
                    dd = small[:, 107:108]
                    Wk = small[:, 110:110 + NIT + 2]
                    P.op("dve", lambda e: e.tensor_reduce(out=mx, in_=scores[:, 0:n], axis=AX.X, op=OP.max), r=sckeys, w=["bs_mx"])
                    P.op("dve", lambda e: e.tensor_reduce(out=mn, in_=scores[:, 0:n], axis=AX.X, op=OP.min), r=sckeys, w=["bs_mn"])
                    P.op("dve", lambda e: e.tensor_tensor(out=scores[:, n - 128:n], in0=scores[:, n - 128:n], in1=cv("caus"),
                                                          op=OP.add), r=[lastk, "cst"], w=[lastk])
                    P.op("dve", lambda e: e.tensor_tensor(out=w0, in0=mx, in1=mn, op=OP.subtract), r=["bs_mx", "bs_mn"], w=["bs_w0"])
                    P.op("dve", lambda e: e.tensor_scalar(out=Wk, in0=cv("pow2"), scalar1=w0, scalar2=None, op0=OP.mult),
                         r=["bs_w0", "cst"], w=["bs_wk"])
                    P.op("dve", lambda e: e.tensor_tensor(out=mid, in0=mn, in1=Wk[:, 0:1], op=OP.add), r=["bs_mn", "bs_wk"], w=["bs_mid"])
                    for it in range(NIT):
                        P.op("dve", lambda e: e.tensor_scalar(out=junkb[:, 0:n], in0=scores[:, 0:n], scalar1=mid, scalar2=None,
                                                              op0=OP.is_ge, op1=OP.add, accum_out=cnt),
                             r=sckeys + ["bs_mid"], w=[mk, "bs_cnt"])
                        lastit = (it == NIT - 1)
                        P.op("dve", lambda e, lastit=lastit: e.tensor_scalar(out=dd, in0=cnt, scalar1=float(TOPK),
                                                                              scalar2=(1.0 if lastit else 0.5),
                                                                              op0=OP.is_ge, op1=OP.subtract),
                             r=["bs_cnt"], w=["bs_dd"])
                        dst = thr if lastit else mid
                        P.op("dve", lambda e, it=it, dst=dst: e.scalar_tensor_tensor(
                            out=dst, in0=dd, scalar=Wk[:, it:it + 1], in1=mid, op0=OP.mult, op1=OP.add),
                            r=["bs_dd", "bs_wk", "bs_mid"], w=[tk if lastit else "bs_mid"])
                for kg in range(ngrp):
                    w_ = min(512, n - kg * 512)
                    mb = maskb[kg % 2]
                    P.op("dve", lambda e, mb=mb, kg=kg, w_=w_: e.tensor_scalar(
                        out=mb[:, 0:w_], in0=scores[:, kg * 512:kg * 512 + w_], scalar1=thr, scalar2=None, op0=OP.is_ge),
                        r=[("sc", c % 2, kg), tk], w=["maskb%d" % (kg % 2)])
                    tb = 3
                    nj = w_ // 128
                    for jj in range(nj):
                        transpose(psb(tb)[:, jj * 128:(jj + 1) * 128], mb[:, jj * 128:(jj + 1) * 128],
                                  r=["maskb%d" % (kg % 2)], w=["ps%d" % tb])
                    P.op("act", lambda e, tb=tb, kg=kg, nj=nj: e.activation(
                        out=maskT[:, kg * 4:kg * 4 + nj, :], in_=psb(tb)[:, 0:nj * 128].rearrange("p (j t) -> p j t", j=nj),
                        func=AF.Identity, scale=30000.0, bias=negb[:, :]), r=["ps%d" % tb, "negb"], w=[mk])
                if "sc" in dbg and gc == NCH - 1:
                    dd_ = dbg_dump("sc", scores, (128, S), None)
                    P.dma(dd_, scores, r=sckeys)
                    dd_ = dbg_dump("thr", thr, (128, 1), None)
                    P.dma(dd_, thr, r=[tk])

            def st3a(c):
                gc, nk, n, tq, ngrp = geom(c)
                maskT = maskT_v[c % 2]
                mk = mkey[c % 2]
                for kvh in range(2):
                    if kvh == 1:
                        st3b_half(c, 0)
                    for j in range(nk):
                        lb = 3
                        mm(psum[:, lb, :], kT[:, kvh, j * 128:(j + 1) * 128], qT[:, kvh * 4:(kvh + 1) * 4, tq], True, False,
                           r=[("kT", j), "qT"], w=["ps%d" % lb])
                        mm(psum[:, lb, :], identb[:, :], bc(maskT[:, j, :].unsqueeze(1), (128, 4, 128)), False, True,
                           r=["identb", mk], w=["ps%d" % lb])
                        Ev = Eb[j % 2]
                        P.op("act", lambda e, lb=lb, Ev=Ev: e.activation(
                            out=Ev, in_=psum[:, lb, :].rearrange("p (g t) -> p g t", g=4), func=AF.Exp, scale=mix_scale),
                            r=["ps%d" % lb], w=["E%d" % (j % 2)])
                        for g in range(4):
                            mm(psum[:, 4 + g, 0:129], Ev[:, g, :], Vc[:, j, kvh, :], (j == 0), (j == nk - 1),
                               r=["E%d" % (j % 2), ("Vc", j), "Vc"], w=["ps%d" % (4 + g)])

            def st3b_half(c, kvh):
                for g in range(4):
                    h = kvh * 4 + g
                    rden = small[:, 140 + h:141 + h]
                    P.op("dve", lambda e, g=g, rden=rden: e.reciprocal(out=rden, in_=psum[:, 4 + g, 128:129]),
                         r=["ps%d" % (4 + g)], w=["rden%d" % h])
                    P.op("act", lambda e, g=g, h=h, rden=rden: e.activation(
                        out=a_tok[:, h, :], in_=psum[:, 4 + g, 0:128], func=AF.Copy, scale=rden),
                        r=["ps%d" % (4 + g), "rden%d" % h], w=["a_tok"])

            def st3b(c):
                st3b_half(c, 1)
                tb = 3
                for h in range(8):
                    transpose(psb(tb)[:, h * 128:(h + 1) * 128], a_tok[:, h, :], r=["a_tok"], w=["ps%d" % tb])
                P.op("act", lambda e, tb=tb, c=c: e.activation(
                    out=aT[:, :, c * 128:(c + 1) * 128], in_=psb(tb).rearrange("p (h t) -> p h t", h=8), func=AF.Copy),
                    r=["ps%d" % tb], w=["aT"])

            if stop >= 6:
                st1(0)
                st1(1)
                st2(0)
                st3a(0)
                st1(2)
                st2(1)
                st3b(0)
                st3a(1)
                st1(3)
                st2(2)
                st3b(1)
                st3a(2)
                st2(3)
                st3b(2)
                st3a(3)
                st3b(3)

            if "aT" in dbg:
                dd_ = dbg_dump("aT", aT[:, :, :], (NST, 128, 8, T), None)
                P.dma(dd_[st_i], aT[:, :, :], r=["aT"])
            if "qT" in dbg:
                dd_ = dbg_dump("qT", qT, (NST, 128, 8, T), None)
                P.dma(dd_[st_i], qT, r=["qT"])
            if "hT" in dbg:
                dd_ = dbg_dump("hT", hT[:, :, :], (NST, 128, 8, T), None)
                P.dma(dd_[st_i], hT[:, :, :], r=["hT"])

            if stop >= 7:
                P.barrier()
                BT = aview(0, (128, 4, T), BF16)
                CT = aview(4096, (128, 4, T), BF16)
                Btok = aview(8192, (128, 4, 512), BF16)
                xacc = [aview(12288 + 2048 * i, (128, 512), F32) for i in range(2)]
                xact = [aview(16384 + 1024 * i, (128, 512), BF16) for i in range(2)]
                dtt = aview(18432, (128, 4, 32), F32)
                adt = aview(18944, (128, 4, 32), F32)

                def conv_fm(cc, b, wname, bname, ntap, tail, dst_silu, tailkey):
                    u = psum[:, b, :]
                    acc = xacc[cc % 2]
                    ak = "xacc%d" % (cc % 2)
                    wv = lambda j: lv(wname, cc * ntap + j, cc * ntap + j + 1)
                    P.op("act", lambda e: e.activation(out=acc, in_=u, func=AF.Identity, bias=lv(bname, cc, cc + 1),
                                                       scale=wv(ntap - 1)), r=["ps%d" % b, "lcs"], w=[ak])
                    for j in range(ntap - 1):
                        sh = ntap - 1 - j
                        P.op("dve", lambda e, j=j, sh=sh: e.scalar_tensor_tensor(
                            out=acc[:, sh:512], in0=u[:, 0:512 - sh], scalar=wv(j), in1=acc[:, sh:512], op0=OP.mult, op1=OP.add),
                            r=["ps%d" % b, "lcs", ak], w=[ak])
                        P.op("dve", lambda e, j=j, sh=sh: e.scalar_tensor_tensor(
                            out=acc[:, 0:sh], in0=tail[:, cc, ntap - 1 - sh:ntap - 1], scalar=wv(j), in1=acc[:, 0:sh],
                            op0=OP.mult, op1=OP.add), r=[tailkey, "lcs", ak], w=[ak])
                    P.op("dve", lambda e: e.tensor_copy(out=tail[:, cc, :], in_=u[:, 512 - (ntap - 1):512]),
                         r=["ps%d" % b], w=[tailkey])
                    return acc, ak

                def proj_fm(slot, ncc, cc0, handler):
                    for j in range(ncc):
                        b = j % 4
                        for k in range(8):
                            mm(psum[:, b, :], wbf[slot][:, k, j * 128:(j + 1) * 128], hT[:, k, :], (k == 0), (k == 7),
                               r=["hT", "wbf%d" % slot], w=["ps%d" % b])
                        handler(cc0 + j, b)

                def h_bc(cc, b):
                    acc, ak = conv_fm(cc, b, "xcw", "xcb", 4, xtail, None, "xtail")
                    g = (cc - 16) % 4
                    if cc < 20:
                        P.op("act", lambda e: e.activation(out=BT[:, g, :], in_=acc, func=AF.Silu), r=[ak], w=["BT"])
                        tb = 4 + (cc % 2)
                        for tc_ in range(4):
                            transpose(psb(tb)[:, tc_ * 128:(tc_ + 1) * 128], BT[:, g, tc_ * 128:(tc_ + 1) * 128], r=["BT"], w=["ps%d" % tb])
                        P.op("act", lambda e: e.activation(out=Btok[:, :, g * 128:(g + 1) * 128],
                                                           in_=psb(tb)[:, 0:512].rearrange("p (c n) -> p c n", c=4), func=AF.Copy),
                             r=["ps%d" % tb], w=["Btok"])
                    else:
                        P.op("act", lambda e: e.activation(out=CT[:, g, :], in_=acc, func=AF.Silu), r=[ak], w=["CT"])

                for half in range(2):
                    sched([(w_in_d[l], 0, 8, O_XBC + 2048 + half * 512, 512)],
                          lambda s, half=half: proj_fm(s[0], 4, 16 + half * 4, h_bc))

                def evac_dt(c, b):
                    P.op("dve", lambda e: e.tensor_tensor(out=dtt[:, c, :], in0=psum[:, b, 0:32], in1=lv("dtb"), op=OP.add),
                         r=["ps%d" % b, "lcs"], w=["dtt"])
                    P.op("act", lambda e: e.activation(out=dtt[:, c, :], in_=dtt[:, c, :], func=AF.Exp), r=["dtt"], w=["dtt"])
                    P.op("act", lambda e: e.activation(out=dtt[:, c, :], in_=dtt[:, c, :], func=AF.Ln, bias=1.0), r=["dtt"], w=["dtt"])
                    P.op("dve", lambda e: e.tensor_tensor(out=adt[:, c, :], in0=dtt[:, c, :], in1=negA[:, :], op=OP.mult),
                         r=["dtt", "negA"], w=["adt"])

                sched([(w_in_d[l], 0, 8, O_DT, 32)], lambda s: proj_tok(lhs_h, [8], s, 32, evac_dt, [0, 1, 2, 3], ["hT"]))

                _o = [19 * 1024]

                def _al(nbytes):
                    o = _o[0]
                    _o[0] += nbytes
                    return o

                TP = []
                for p_ in range(2):
                    d_ = dict(
                        xtok=aview(_al(4096), (128, 4, 512), BF16), zs=aview(_al(4096), (128, 4, 512), BF16),
                        Wp=aview(_al(4096), (128, 8, 128), F32), Lx=aview(_al(2048), (128, 8, 128), BF16),
                        MT=aview(_al(2048), (128, 8, 128), BF16), CBm=aview(_al(256), (128, 128), BF16),
                        Xdt=aview(_al(1024), (128, 8, 64), BF16), Xd2=aview(_al(1024), (128, 8, 64), BF16),
                        yo=aview(_al(2048), (128, 8, 64), F32), yy=aview(_al(2048), (128, 8, 64), F32),
                        hb=aview(_al(1024), (128, 512), BF16), sm2=aview(_al(256), (128, 64), F32),
                        btok_b=aview(_al(1024), (128, 512), BF16))
                    TP.append(d_)

                def make_hx(g):
                    p_ = g % 2
                    xtok = TP[p_]["xtok"]

                    def h_x(cc, b):
                        acc, ak = conv_fm(cc, b, "xcw", "xcb", 4, xtail, None, "xtail")
                        xa = xact[cc % 2]
                        xk = "xact%d" % (cc % 2)
                        P.op("act", lambda e: e.activation(out=xa, in_=acc, func=AF.Silu), r=[ak], w=[xk])
                        tb = 4 + (cc % 2)
                        for tc_ in range(4):
                            transpose(psb(tb)[:, tc_ * 128:(tc_ + 1) * 128], xa[:, tc_ * 128:(tc_ + 1) * 128], r=[xk], w=["ps%d" % tb])
                        j = cc % 4
                        P.op("act", lambda e: e.activation(out=xtok[:, :, j * 128:(j + 1) * 128],
                                                           in_=psb(tb)[:, 0:512].rearrange("p (c n) -> p c n", c=4), func=AF.Copy),
                             r=["ps%d" % tb], w=["xtok%d" % p_])
                    return h_x

                def make_evz(g):
                    p_ = g % 2
                    zs = TP[p_]["zs"]

                    def evac_z(c, b):
                        P.op("act", lambda e: e.activation(out=zs[:, c, :], in_=psum[:, b, :], func=AF.Silu), r=["ps%d" % b], w=["zs%d" % p_])
                    return evac_z

                def ssd_chain(g):
                    p_ = g % 2
                    t_ = TP[p_]
                    xtok, zs, Wp, Lx, MT, CBm, Xdt, Xd2, yo, yy, hb, sm2, btok_b = (
                        t_["xtok"], t_["zs"], t_["Wp"], t_["Lx"], t_["MT"], t_["CBm"], t_["Xdt"], t_["Xd2"], t_["yo"], t_["yy"],
                        t_["hb"], t_["sm2"], t_["btok_b"])
                    K_ = lambda n_: "%s%d" % (n_, p_)
                    b0, b1, b2, b3 = 4 * p_, 4 * p_ + 1, 4 * p_ + 2, 4 * p_ + 3
                    pk = lambda b: "ps%d" % b
                    hk = "hst%d" % g
                    hs3 = hst[:, g * 512:(g + 1) * 512].rearrange("p (e d) -> p e d", e=8)
                    for c in range(4):
                        ts_ = slice(c * 128, (c + 1) * 128)
                        xt3 = xtok[:, c, :].rearrange("p (e d) -> p e d", e=8)
                        dtg = dtt[:, c, g * 8:(g + 1) * 8]
                        adg = adt[:, c, g * 8:(g + 1) * 8]
                        P.op("dve", lambda e, xt3=xt3, dtg=dtg: e.tensor_tensor(out=Xdt, in0=xt3, in1=bc(dtg.unsqueeze(2), (128, 8, 64)), op=OP.mult),
                             r=[K_("xtok"), "dtt"], w=[K_("Xdt")])
                        P.op("dve", lambda e, adg=adg: e.tensor_tensor(out=Wp, in0=bc(cv("tri_le").unsqueeze(1), (128, 8, 128)),
                                                                       in1=bc(adg.unsqueeze(2), (128, 8, 128)), op=OP.mult),
                             r=["adt", "cst"], w=[K_("Wp")])
                        yield
                        for hh, bb in ((0, b0), (1, b1)):
                            mm(psum[:, bb, :], cv("sgt"), Wp[:, hh * 4:(hh + 1) * 4, :], True, True, r=["cst", K_("Wp")], w=[pk(bb)])
                            P.op("act", lambda e, hh=hh, bb=bb: e.activation(out=Lx[:, hh * 4:(hh + 1) * 4, :],
                                                                             in_=psum[:, bb, :].rearrange("p (e l) -> p e l", e=4), func=AF.Exp),
                                 r=[pk(bb)], w=[K_("Lx")])
                            P.op("act", lambda e, hh=hh, bb=bb: e.activation(out=sm2[:, hh * 4:(hh + 1) * 4],
                                                                             in_=psum[:, bb, :].rearrange("p (e l) -> p e l", e=4)[:, :, 127],
                                                                             func=AF.Exp), r=[pk(bb)], w=[K_("decay")])
                        yield
                        mm(psum[:, b2, 0:128], BT[:, g, ts_], CT[:, g, ts_], True, True, r=["BT", "CT"], w=[pk(b2)])
                        mm(psum[:, b2, 128:136], cv("tri_le"), adg, True, True, r=["cst", "adt"], w=[pk(b2)])
                        mm(psum[:, b2, 136:144], cv("ones"), adg, True, True, r=["cst", "adt"], w=[pk(b2)])
                        P.op("dve", lambda e: e.tensor_tensor(out=CBm, in0=psum[:, b2, 0:128], in1=cv("tri_le"), op=OP.mult),
                             r=[pk(b2), "cst"], w=[K_("CBm")])
                        P.op("act", lambda e: e.activation(out=sm2[:, 8:24], in_=psum[:, b2, 128:144], func=AF.Exp), r=[pk(b2)], w=[K_("eacs")])
                        yield
                        P.op("dve", lambda e: e.tensor_tensor(out=MT, in0=Lx, in1=bc(CBm.unsqueeze(1), (128, 8, 128)), op=OP.mult),
                             r=[K_("Lx"), K_("CBm")], w=[K_("MT")])
                        for e_ in range(8):
                            mm(psum[:, b3, e_ * 64:(e_ + 1) * 64], MT[:, e_, :], Xdt[:, e_, :], True, True, r=[K_("MT"), K_("Xdt")], w=[pk(b3)])
                        yield
                        P.op("act", lambda e: e.activation(out=hb, in_=hst[:, g * 512:(g + 1) * 512], func=AF.Copy), r=[hk], w=[K_("hb")])
                        mm(psum[:, b0, :], CT[:, g, ts_], hb, True, True, r=["CT", K_("hb")], w=[pk(b0)])
                        P.op("dve", lambda e: e.tensor_tensor(out=yo, in0=psum[:, b0, :].rearrange("p (e d) -> p e d", e=8),
                                                              in1=bc(sm2[:, 8:16].unsqueeze(2), (128, 8, 64)), op=OP.mult),
                             r=[pk(b0), K_("eacs")], w=[K_("yo")])
                        yield
                        P.op("dve", lambda e: e.tensor_tensor(out=yy, in0=psum[:, b3, :].rearrange("p (e d) -> p e d", e=8), in1=yo, op=OP.add),
                             r=[pk(b3), K_("yo")], w=[K_("yy")])
                        P.op("dve", lambda e, xt3=xt3: e.tensor_tensor(out=yo, in0=xt3, in1=bc(lv("dsk", g * 8, g * 8 + 8).unsqueeze(2), (128, 8, 64)),
                                                                       op=OP.mult), r=[K_("xtok"), "lcs"], w=[K_("yo")])
                        P.op("dve", lambda e: e.tensor_tensor(out=yy, in0=yy, in1=yo, op=OP.add), r=[K_("yy"), K_("yo")], w=[K_("yy")])
                        yield
                        P.op("dve", lambda e: e.tensor_tensor(out=Xd2, in0=Xdt, in1=bc(sm2[:, 0:8].unsqueeze(2), (128, 8, 64)), op=OP.mult),
                             r=[K_("Xdt"), K_("decay")], w=[K_("Xd2")])
                        mm(psum[:, b1, :], Btok[:, c, g * 128:(g + 1) * 128], Xd2.rearrange("p e d -> p (e d)"), True, True,
                           r=["Btok", K_("Xd2")], w=[pk(b1)])
                        P.op("dve", lambda e: e.tensor_tensor(out=hs3, in0=hs3, in1=bc(sm2[:, 16:24].unsqueeze(2), (128, 8, 64)), op=OP.mult),
                             r=[hk, K_("eacs")], w=[hk])
                        P.op("dve", lambda e: e.tensor_tensor(out=hs3, in0=psum[:, b1, :].rearrange("p (e d) -> p e d", e=8), in1=hs3, op=OP.add),
                             r=[pk(b1), hk], w=[hk])
                        yield
                        yf = yy.rearrange("p e d -> p (e d)")
                        P.op("dve", lambda e, yf=yf, c=c: e.tensor_tensor(out=yf, in0=yf, in1=zs[:, c, :], op=OP.mult), r=[K_("yy"), K_("zs")], w=[K_("yy")])
                        P.op("act", lambda e, yf=yf: e.activation(out=btok_b, in_=yf, func=AF.Square, accum_out=sm2[:, 24:25]),
                             r=[K_("yy")], w=[K_("btok_b"), K_("gss")])
                        P.op("act", lambda e: e.activation(out=sm2[:, 25:26], in_=sm2[:, 24:25], func=AF.Sqrt, bias=epsT[:, :], scale=1.0 / 512),
                             r=[K_("gss"), "epsT"], w=[K_("gss")])
                        P.op("dve", lambda e: e.reciprocal(out=sm2[:, 26:27], in_=sm2[:, 25:26]), r=[K_("gss")], w=[K_("gss")])
                        yield
                        P.op("dve", lambda e, yf=yf: e.tensor_scalar(out=btok_b, in0=yf, scalar1=sm2[:, 26:27], scalar2=None, op0=OP.mult),
                             r=[K_("yy"), K_("gss")], w=[K_("btok_b")])
                        for j in range(4):
                            transpose(psb(b2)[:, j * 128:(j + 1) * 128], btok_b[:, j * 128:(j + 1) * 128], r=[K_("btok_b")], w=[pk(b2)])
                        P.op("dve", lambda e, ts_=ts_: e.tensor_tensor(out=bT[:, g * 4:(g + 1) * 4, ts_],
                                                                       in0=psb(b2)[:, 0:512].rearrange("p (j t) -> p j t", j=4),
                                                                       in1=bc(lv("snw", g * 4, g * 4 + 4).unsqueeze(2), (128, 4, 128)), op=OP.mult),
                             r=[pk(b2), "lcs"], w=["bT"])
                        yield

                def run_pair(g0):
                    gens = [ssd_chain(g0), ssd_chain(g0 + 1)]
                    while gens:
                        for gen in list(gens):
                            try:
                                next(gen)
                            except StopIteration:
                                gens.remove(gen)

                for g0 in (0, 2):
                    for g in (g0, g0 + 1):
                        sched([(w_in_d[l], 0, 8, O_XBC + g * 512, 512)], lambda s, g=g: proj_fm(s[0], 4, g * 4, make_hx(g)))
                        sched([(w_in_d[l], 0, 8, O_Z + g * 512, 512)],
                              lambda s, g=g: proj_tok(lhs_h, [8], s, 512, make_evz(g), [0, 1, 2, 3], ["hT"]))
                    sched([], lambda s, g0=g0: run_pair(g0))
                flush()
                if "bT" in dbg:
                    dd_ = dbg_dump("bT", bT[:, :, :], (NST, 128, 16, T), None)
                    P.dma(dd_[st_i], bT[:, :, :], r=["bT"])

            if stop >= 8:
                P.barrier()
                ga = aview(0, (128, 4, 512), BF16)
                gb = aview(4096, (128, 4, 512), BF16)
                mf = aview(8192, (128, 4, 512), F32)
                tmpf = aview(16384, (128, 512), F32)
                mtok = aview(18432, (128, 512), BF16)
                mT = aview(19456, (128, 8, T), BF16)
                xh = aview(27648, (128, 4, 512), F32)
                for cs in range(2):
                    sched([(w_in_d[l], 0, 8, O_GA + cs * 512, 512)], lambda s: proj_tok(
                        lhs_h, [8], s, 512,
                        lambda c, b: P.op("act", lambda e: e.activation(out=ga[:, c, :], in_=psum[:, b, :], func=AF.Sigmoid),
                                          r=["ps%d" % b], w=["ga"]), [0, 1, 2, 3], ["hT"]))
                    sched([(w_in_d[l], 0, 8, O_GB + cs * 512, 512)], lambda s: proj_tok(
                        lhs_h, [8], s, 512,
                        lambda c, b: P.op("act", lambda e: e.activation(out=gb[:, c, :], in_=psum[:, b, :], func=AF.Sigmoid),
                                          r=["ps%d" % b], w=["gb"]), [0, 1, 2, 3], ["hT"]))
                    sched([(w_pa_d[l], 0, 8, cs * 512, 512)], lambda s: proj_tok(
                        lambda k, c: aT[:, k, c * 128:(c + 1) * 128], [8], s, 512,
                        lambda c, b: P.op("dve", lambda e: e.tensor_tensor(out=mf[:, c, :], in0=psum[:, b, :], in1=ga[:, c, :], op=OP.mult),
                                          r=["ps%d" % b, "ga"], w=["mf"]), [0, 1, 2, 3], ["aT"]))

                    def evac_pb(c, b, cs=cs):
                        P.op("dve", lambda e: e.tensor_tensor(out=tmpf, in0=psum[:, b, :], in1=gb[:, c, :], op=OP.mult),
                             r=["ps%d" % b, "gb"], w=["tmpf"])
                        P.op("dve", lambda e: e.tensor_tensor(out=mtok, in0=tmpf, in1=mf[:, c, :], op=OP.add), r=["tmpf", "mf"], w=["mtok"])
                        tb = 6 + (c % 2)
                        for j in range(4):
                            transpose(psb(tb)[:, j * 128:(j + 1) * 128], mtok[:, j * 128:(j + 1) * 128], r=["mtok"], w=["ps%d" % tb])
                        P.op("act", lambda e: e.activation(out=mT[:, cs * 4:(cs + 1) * 4, c * 128:(c + 1) * 128],
                                                           in_=psb(tb)[:, 0:512].rearrange("p (j t) -> p j t", j=4), func=AF.Copy),
                             r=["ps%d" % tb], w=["mT"])

                    sched([(w_pb_d[l], 0, 8, cs * 512, 512), (w_pb_d[l], 1024, 8, cs * 512, 512)],
                          lambda s, evac_pb=evac_pb: proj_tok(lambda k, c: bT[:, k, c * 128:(c + 1) * 128], [8, 8], s, 512, evac_pb,
                                                              [0, 1, 2, 3], ["bT"]))
                flush()
                for cs in range(2):
                    def out_task(s, cs=cs):
                        P.dma(xh, src_d[st_i * T:(st_i + 1) * T, cs * 512:(cs + 1) * 512].rearrange("(c p) n -> p c n", p=128),
                              r=[("xsrc%d" % l, st_i * 4 + c) for c in range(4)], w=["xh"])
                        proj_tok(lambda k, c: mT[:, k, c * 128:(c + 1) * 128], [8], s, 512,
                                 lambda c, b: P.op("dve", lambda e: e.tensor_tensor(out=xh[:, c, :], in0=psum[:, b, :], in1=xh[:, c, :], op=OP.add),
                                                   r=["ps%d" % b, "xh"], w=["xh"]), [0, 1, 2, 3], ["mT"])
                        P.dma(xmid_d[st_i * T:(st_i + 1) * T, cs * 512:(cs + 1) * 512].rearrange("(c p) n -> p c n", p=128), xh,
                              r=["xh"], w=[("xmid", st_i * 4 + c) for c in range(4)])
                    sched([(w_out_d[l], 0, 8, cs * 512, 512)], out_task)
                flush()

            if stop >= 9:
                P.barrier()
                xin_f = [aview(0, (128, D), F32), aview(4096, (128, D), F32)]
                hn_f = aview(8192, (128, D), BF16)
                small_f = aview(10240, (128, 64), F32)
                xacc = [aview(10496 + 2048 * i, (128, 512), F32) for i in range(2)]
                sg = aview(14592, (128, 22, 512), BF16)
                xo = aview(37120, (128, 4, D), F32)
                nfin = aview(53504, (128, D), F32)
                osq = aview(57600, (128, D), BF16)
                rmsnorm_to_hT(xmid_d, st_i, "nffn", xin_f, hn_f, small_f, "xmid")

                def h_up(cc, b):
                    acc, ak = conv_fm(cc, b, "fcw", "fcb", 3, ftail, None, "ftail")
                    if cc < 22:
                        P.op("act", lambda e: e.activation(out=sg[:, cc, :], in_=acc, func=AF.Silu), r=[ak], w=[("sg", cc)])
                    else:
                        P.op("dve", lambda e: e.tensor_tensor(out=sg[:, cc - 22, :], in0=sg[:, cc - 22, :], in1=acc, op=OP.mult),
                             r=[ak, ("sg", cc - 22)], w=[("sg", cc - 22)])

                for sl in range(11):
                    sched([(w_up_d[l], 0, 8, sl * 512, 512)], lambda s, sl=sl: proj_fm(s[0], 4, sl * 4, h_up))
                sgk = [("sg", i) for i in range(22)]

                def down_task(slots, cs):
                    P.dma(xo[:, :, cs * 512:(cs + 1) * 512],
                          xmid_d[st_i * T:(st_i + 1) * T, cs * 512:(cs + 1) * 512].rearrange("(c p) n -> p c n", p=128),
                          r=[("xmid", st_i * 4 + c) for c in range(4)], w=["xo"])
                    for c in range(4):
                        kk = 0
                        for si in range(2):
                            for k in range(8):
                                mm(psum[:, c, :], sg[:, kk, c * 128:(c + 1) * 128], wbf[slots[si]][:, k, :], (kk == 0), False,
                                   r=sgk + ["wbf%d" % slots[si]], w=["ps%d" % c])
                                kk += 1
                    s2 = load_slab(w_dn_d[l], 2048, 6, cs * 512, 512)
                    for c in range(4):
                        for k in range(6):
                            mm(psum[:, c, :], sg[:, 16 + k, c * 128:(c + 1) * 128], wbf[s2][:, k, :], False, (k == 5),
                               r=sgk + ["wbf%d" % s2], w=["ps%d" % c])
                        P.op("dve", lambda e, c=c, cs=cs: e.tensor_tensor(out=xo[:, c, cs * 512:(cs + 1) * 512], in0=psum[:, c, :],
                                                                          in1=xo[:, c, cs * 512:(cs + 1) * 512], op=OP.add),
                             r=["ps%d" % c, "xo"], w=["xo"])

                for cs in range(2):
                    sched([(w_dn_d[l], 0, 8, cs * 512, 512), (w_dn_d[l], 1024, 8, cs * 512, 512)],
                          lambda s, cs=cs: down_task(s, cs), extra=1)
                flush()
                if not last:
                    P.dma(xl1_d[st_i * T:(st_i + 1) * T, :].rearrange("(c p) n -> p c n", p=128), xo, r=["xo"],
                          w=[("xsrc1", st_i * 4 + c) for c in range(4)])
                else:
                    P.dma(nfin, nfin_d[:, :], w=["nfin"])
                    for c in range(4):
                        ss = small_f[:, 32 + c:33 + c]
                        rt = small_f[:, 36 + c:37 + c]
                        rs = small_f[:, 40 + c:41 + c]
                        P.op("act", lambda e, c=c, ss=ss: e.activation(out=osq, in_=xo[:, c, :], func=AF.Square, accum_out=ss),
                             r=["xo"], w=["osq", "fsm%d" % c])
                        P.op("act", lambda e, ss=ss, rt=rt: e.activation(out=rt, in_=ss, func=AF.Sqrt, bias=epsT[:, :], scale=1.0 / D),
                             r=["fsm%d" % c, "epsT"], w=["fsm%d" % c])
                        P.op("dve", lambda e, rt=rt, rs=rs: e.reciprocal(out=rs, in_=rt), r=["fsm%d" % c], w=["fsm%d" % c])
                        P.op("dve", lambda e, c=c, rs=rs: e.scalar_tensor_tensor(out=xo[:, c, :], in0=xo[:, c, :], scalar=rs, in1=nfin,
                                                                                  op0=OP.mult, op1=OP.mult),
                             r=["xo", "fsm%d" % c, "nfin"], w=["xo"])
                    P.dma(out_d[st_i * T:(st_i + 1) * T, :].rearrange("(c p) n -> p c n", p=128), xo, r=["xo"])

    if "kT" in dbg:
        dd_ = dbg_dump("kT", kT[:, :, :], (128, NKV, S), None)
        P.dma(dd_, kT[:, :, :], r=[("kT", j) for j in range(NCH)])
    if "kiT" in dbg:
        dd_ = dbg_dump("kiT", kiT2[:, :], (128, S), None)
        P.dma(dd_, kiT2[:, :], r=[("kiT", j) for j in range(NCH)])
    P.emit()
    st.close()
    return nc


_NC_CACHE = {}


def kernel(**inputs):
    inp = {k: np.asarray(v) for k, v in inputs.items()}
    x = inp["x"].astype(np.float32, copy=False)
    B, S, _ = x.shape
    if S not in _NC_CACHE:
        _NC_CACHE[S] = build(S, layers=(0, 1))
    nc = _NC_CACHE[S]
    cst, tab = host_consts(S)
    lc = np.stack([host_layer_consts(inp, l) for l in range(2)])
    nfin = np.ascontiguousarray(np.broadcast_to(inp["norm_final_w"].astype(np.float32)[None, :], (128, D)))
    shared = {"w_in": np.ascontiguousarray(inp["w_in"], dtype=np.float32),
              "w_proj_attn": np.ascontiguousarray(inp["w_proj_attn"], dtype=np.float32),
              "w_proj_ssd": np.ascontiguousarray(inp["w_proj_ssd"], dtype=np.float32),
              "w_out": np.ascontiguousarray(inp["w_out"], dtype=np.float32),
              "ffn_w_up": np.ascontiguousarray(inp["ffn_w_up"], dtype=np.float32),
              "ffn_w_down": np.ascontiguousarray(inp["ffn_w_down"], dtype=np.float32),
              "cst": cst, "tab": tab, "lc": lc, "nfin": nfin}
    in_maps = [dict(shared, x=np.ascontiguousarray(x[b])) for b in range(B)]
    res = run_bass_kernel_spmd(nc, in_maps, core_ids=list(range(B)))
    return np.stack([np.asarray(r["out"], dtype=np.float32) for r in res.results], axis=0)
```

```python
import contextlib
import numpy as np
import concourse.bass as bass
import concourse.mybir as mybir
from concourse.bass_utils import run_bass_kernel_spmd

F32 = mybir.dt.float32
BF16 = mybir.dt.bfloat16
AF = mybir.ActivationFunctionType
OP = mybir.AluOpType
AX = mybir.AxisListType

D = 1024
NH, HD, NKV = 8, 128, 2
IH, IDM = 8, 64
SSD_INNER, SSD_HD, SSD_H, SSD_G, SSD_N = 2048, 64, 32, 4, 128
CONV_DIM = 3072
FFN = 2816
EPS = 1e-6
IN_COLS = 9320
O_Q, O_K, O_V, O_QI, O_KI, O_WI, O_Z, O_XBC, O_DT, O_GA, O_GB = (
    0, 1024, 1280, 1536, 2048, 2112, 2120, 4168, 7240, 7272, 8296)
T = 512
NIT = 22

ENGS = ["pe", "act", "dve", "pool", "sp"]
NDMASEM = 24


class Prog:
    def __init__(self, nc):
        self.nc = nc
        self.ops = {e: [] for e in ENGS}
        self.last_w = {}
        self.readers = {}
        self.ndma = 0
        self.last_real = {}
        self.ps_last = {}

    @staticmethod
    def _isps(k):
        return isinstance(k, str) and k.startswith("ps") and k[2:].isdigit()

    def _deps(self, eng, r, w):
        deps = []
        for k in list(r) + list(w):
            if self._isps(k):
                is_w = k in w
                last = self.ps_last.get(k)
                if last is not None:
                    ref, lw = last
                    same = (ref[0] == "eng" and ref[1] == eng)
                    if (not same) or is_w or lw:
                        deps.append(ref)
        r = [k for k in r if not self._isps(k)]
        w = [k for k in w if not self._isps(k)]
        for k in r:
            lw = self.last_w.get(k)
            if lw is not None:
                deps.append(lw)
        for k in w:
            lw = self.last_w.get(k)
            if lw is not None:
                deps.append(lw)
            deps.extend(self.readers.get(k, ()))
        out = []
        for d in deps:
            if d[0] == "eng" and d[1] == "pe" and eng == "pe":
                continue
            if d not in out:
                out.append(d)
        return out

    def _commit(self, ref, r, w):
        for k in list(r) + list(w):
            if self._isps(k):
                self.ps_last[k] = (ref, k in w)
        r = [k for k in r if not self._isps(k)]
        w = [k for k in w if not self._isps(k)]
        for k in r:
            self.readers.setdefault(k, []).append(ref)
        for k in w:
            self.last_w[k] = ref
            self.readers[k] = []

    def _mark(self, deps):
        for d in deps:
            if d[0] == "eng":
                self.ops[d[1]][d[2]]["sig"] = True

    def op(self, eng, fn, r=(), w=()):
        deps = self._deps(eng, r, w)
        self._mark(deps)
        idx = len(self.ops[eng])
        self.ops[eng].append(dict(fn=fn, deps=deps, sig=False, dma=None))
        self.last_real[eng] = idx
        self._commit(("eng", eng, idx), r, w)

    def dma(self, out, in_, r=(), w=(), q="sp"):
        deps = self._deps(q, r, w)
        self._mark(deps)
        i = self.ndma
        self.ndma += 1
        if i >= NDMASEM:
            deps.append(("dma", i - NDMASEM))
        self.ops[q].append(dict(fn=lambda e: e.dma_start(out=out, in_=in_), deps=deps, sig=False, dma=i))
        self._commit(("dma", i), r, w)

    def barrier(self):
        deps = [("eng", e, i) for e, i in self.last_real.items()]
        deps += [("dma", i) for i in range(max(0, self.ndma - NDMASEM), self.ndma)]
        self._mark(deps)
        for e in ENGS:
            self.ops[e].append(dict(fn=None, deps=[d for d in deps if not (d[0] == "eng" and d[1] == e)],
                                    sig=False, dma=None))
        self.last_w.clear()
        self.readers.clear()
        self.ps_last.clear()

    def simulate(self):
        sigcnt = {}
        for e in ENGS:
            c = 0
            arr = []
            for o in self.ops[e]:
                if o["sig"]:
                    c += 1
                arr.append(c)
            sigcnt[e] = arr
        sem = {e: 0 for e in ENGS}
        dsem = [0] * NDMASEM
        ptr = {e: 0 for e in ENGS}
        progress = True
        while progress:
            progress = False
            for e in ENGS:
                while ptr[e] < len(self.ops[e]):
                    o = self.ops[e][ptr[e]]
                    ok = True
                    for d in o["deps"]:
                        if d[0] == "eng":
                            if sem[d[1]] < sigcnt[d[1]][d[2]]:
                                ok = False
                        else:
                            if dsem[d[1] % NDMASEM] < 16 * (d[1] // NDMASEM + 1):
                                ok = False
                    if not ok:
                        break
                    if o["dma"] is not None:
                        dsem[o["dma"] % NDMASEM] += 16
                    elif o["sig"] and o["fn"] is not None:
                        sem[e] += 1
                    ptr[e] += 1
                    progress = True
        stuck = {e: (ptr[e], len(self.ops[e])) for e in ENGS if ptr[e] < len(self.ops[e])}
        if stuck:
            for e in stuck:
                o = self.ops[e][ptr[e]]
                print("STUCK", e, ptr[e], o["deps"], "sig", o["sig"], "fn", o["fn"] is not None)
            raise RuntimeError("deadlock in semaphore protocol: %r" % stuck)
        print("simulate ok:", {e: len(self.ops[e]) for e in ENGS}, "dmas", self.ndma)

    def emit(self):
        nc = self.nc
        self.simulate()
        with contextlib.ExitStack() as st:
            esem = {e: st.enter_context(nc.semaphore("s_" + e)) for e in ENGS}
            dsem = [st.enter_context(nc.semaphore("d_%d" % i)) for i in range(NDMASEM)]
            block = st.enter_context(nc.Block())
            sigcnt = {}
            for e in ENGS:
                c = 0
                arr = []
                for o in self.ops[e]:
                    if o["sig"]:
                        c += 1
                    arr.append(c)
                sigcnt[e] = arr
            ndma = self.ndma

            def run(e, engobj):
                waited = {}
                for o in self.ops[e]:
                    for d in o["deps"]:
                        if d[0] == "eng":
                            sem = esem[d[1]]
                            val = sigcnt[d[1]][d[2]]
                            key = "e" + d[1]
                        else:
                            sem = dsem[d[1] % NDMASEM]
                            val = 16 * (d[1] // NDMASEM + 1)
                            key = "d%d" % (d[1] % NDMASEM)
                        if waited.get(key, 0) >= val:
                            continue
                        engobj.wait_ge(sem, val)
                        waited[key] = val
                    if o["fn"] is None:
                        continue
                    ins = o["fn"](engobj)
                    if o["dma"] is not None:
                        ins.then_inc(dsem[o["dma"] % NDMASEM], 16)
                    elif o["sig"]:
                        ins.then_inc(esem[e], 1)
                if e == "sp":
                    for s in range(min(NDMASEM, ndma)):
                        n = (ndma - 1 - s) // NDMASEM + 1
                        engobj.wait_ge(dsem[s], 16 * n)

            @block.sync
            def _(eng):
                run("sp", eng)

            @block.scalar
            def _(eng):
                run("act", eng)

            @block.vector
            def _(eng):
                run("dve", eng)

            @block.gpsimd
            def _(eng):
                run("pool", eng)

            @block.tensor
            def _(eng):
                run("pe", eng)


CST_COLS = {}


def _cst_layout():
    off = 0
    lay = {}
    for name, n in [("ident", 128), ("tri_le", 128), ("caus", 128), ("sgt", 128), ("ones", 128),
                    ("pow2", NIT + 2)]:
        lay[name] = (off, n)
        off += n
    return lay, off


def _lc_layout():
    off = 0
    lay = {}
    for name, n in [("nmix", 8), ("nffn", 8), ("snw", 16), ("xcw", 96), ("xcb", 24), ("fcw", 132),
                    ("fcb", 44), ("dtb", 32), ("alog", 32), ("dsk", 32)]:
        lay[name] = (off, n)
        off += n
    return lay, off


def host_consts(S):
    lay, n = _cst_layout()
    c = np.zeros((128, n), np.float32)
    i = np.arange(128)
    c[:, lay["ident"][0]:lay["ident"][0] + 128] = np.eye(128, dtype=np.float32)
    c[:, lay["tri_le"][0]:lay["tri_le"][0] + 128] = (i[:, None] <= i[None, :]).astype(np.float32)
    c[:, lay["caus"][0]:lay["caus"][0] + 128] = np.where(i[None, :] <= i[:, None], 0.0, -1e30).astype(np.float32)
    c[:, lay["sgt"][0]:lay["sgt"][0] + 128] = (i[:, None] > i[None, :]).astype(np.float32)
    c[:, lay["ones"][0]:lay["ones"][0] + 128] = 1.0
    c[:, lay["pow2"][0]:lay["pow2"][0] + NIT + 2] = (0.5 ** np.arange(1, NIT + 3))[None, :]
    nch = S // 128

    def tab(rot):
        inv = (500000.0 ** (-np.arange(0, rot, 2, dtype=np.float32) / rot)).astype(np.float32)
        ang = np.arange(S, dtype=np.float32)[:, None] * inv[None, :]
        return np.cos(ang).astype(np.float32), np.sin(ang).astype(np.float32)

    ca, sa = tab(32)
    ci, si = tab(16)
    tb = np.concatenate([ca, sa, ci, si], axis=1).reshape(nch, 128, 48).transpose(1, 0, 2)
    return c, np.ascontiguousarray(tb.reshape(128, nch * 48))


def host_layer_consts(inp, l):
    lay, n = _lc_layout()
    c = np.zeros((128, n), np.float32)

    def put(name, arr):
        o, m = lay[name]
        c[:, o:o + m] = arr.reshape(128, m)

    put("nmix", np.asarray(inp["norm_mix_w"][l]).reshape(8, 128).T)
    put("nffn", np.asarray(inp["norm_ffn_w"][l]).reshape(8, 128).T)
    put("snw", np.asarray(inp["ssd_norm_w"][l]).reshape(16, 128).T)
    put("xcw", np.asarray(inp["ssd_conv_w"][l]).reshape(4, 24, 128).transpose(2, 1, 0))
    put("xcb", np.asarray(inp["ssd_conv_b"][l]).reshape(24, 128).T)
    put("fcw", np.asarray(inp["ffn_conv_w"][l]).reshape(3, 44, 128).transpose(2, 1, 0))
    put("fcb", np.asarray(inp["ffn_conv_b"][l]).reshape(44, 128).T)
    put("dtb", np.broadcast_to(np.asarray(inp["ssd_dt_bias"][l])[None, :], (128, 32)))
    put("alog", np.broadcast_to(np.asarray(inp["ssd_a_log"][l])[None, :], (128, 32)))
    put("dsk", np.broadcast_to(np.asarray(inp["ssd_d"][l])[None, :], (128, 32)))
    return c


def build(S, layers=(0, 1), final=True, dbg=(), stop=99):
    nc = bass.Bass("TRN2", target_bir_lowering=False)
    NCH = S // 128
    NST = S // T
    TOPK = min(256, S // 4)
    L = 2
    dt_in = lambda name, shape: nc.dram_tensor(name, list(shape), F32, kind="ExternalInput").ap()
    x_d = dt_in("x", (S, D))
    w_in_d = dt_in("w_in", (L, D, IN_COLS))
    w_pa_d = dt_in("w_proj_attn", (L, D, D))
    w_pb_d = dt_in("w_proj_ssd", (L, SSD_INNER, D))
    w_out_d = dt_in("w_out", (L, D, D))
    w_up_d = dt_in("ffn_w_up", (L, D, 2 * FFN))
    w_dn_d = dt_in("ffn_w_down", (L, FFN, D))
    clay, ncst = _cst_layout()
    llay, nlc = _lc_layout()
    cst_d = dt_in("cst", (128, ncst))
    tab_d = dt_in("tab", (128, NCH * 48))
    lc_d = dt_in("lc", (L, 128, nlc))
    nfin_d = dt_in("nfin", (128, D))
    out_d = nc.dram_tensor("out", [S, D], F32, kind="ExternalOutput").ap()
    xmid_d = nc.dram_tensor("xmid", [S, D], F32, kind="Internal").ap()
    xl1_d = nc.dram_tensor("xl1", [S, D], F32, kind="Internal").ap()
    dbg_d = {}

    P = Prog(nc)
    st = contextlib.ExitStack()
    sb = lambda name, shape, dt: st.enter_context(nc.sbuf_tensor("sb_" + name, list(shape), dt))

    kT = sb("kT", (128, NKV, S), BF16)
    Vc = sb("Vc", (128, NCH, NKV, 129), BF16)
    kiT2 = sb("kiT2", (128, S), BF16)
    hst = sb("hst", (128, SSD_INNER), F32)
    wst = [sb("wst%d" % i, (128, 4, 512), F32) for i in range(2)]
    wbf = [sb("wbf%d" % i, (128, 8, 512), BF16) for i in range(2)]
    hT = sb("hT", (128, 8, T), BF16)
    aT = sb("aT", (128, 8, T), BF16)
    bT = sb("bT", (128, 16, T), BF16)
    cst = sb("cst", (128, ncst), F32)
    lcs = sb("lcs", (128, nlc), F32)
    tabs = sb("tabs", (128, 4, 48), F32)
    identb = sb("identb", (128, 128), BF16)
    trib = sb("trib", (128, 128), F32)
    xtail = sb("xtail", (128, 24, 3), F32)
    ftail = sb("ftail", (128, 44, 2), F32)
    negA = sb("negA", (128, 32), F32)
    epsT = sb("epsT", (128, 1), F32)
    negb = sb("negb", (128, 1), F32)
    AR_BYTES = 69 * 1024
    arena = sb("arena", (128, AR_BYTES // 4), F32)
    psum = st.enter_context(nc.psum_tensor("psum", [128, 8, 512], F32))

    def cv(name, j0=0, j1=None):
        o, n = clay[name]
        j1 = n if j1 is None else j1
        return cst[:, o + j0:o + j1]

    def lv(name, j0=0, j1=None):
        o, n = llay[name]
        j1 = n if j1 is None else j1
        return lcs[:, o + j0:o + j1]

    class View:
        pass

    def aview(off, shape, dt):
        n = int(np.prod(shape[1:]))
        esz = 4 if dt == F32 else 2
        assert off % 4 == 0 and off + n * esz <= AR_BYTES, (off, shape)
        nf = (n * esz + 3) // 4
        ap = arena[:, off // 4: off // 4 + nf]
        if dt != F32:
            ap = ap.bitcast(dt)
        if len(shape) == 3:
            ap = ap.rearrange("p (a b) -> p a b", a=shape[1])
        elif len(shape) == 4:
            ap = ap.rearrange("p (a b c) -> p a b c", a=shape[1], b=shape[2])
        return ap

    def psb(b):
        return psum[:, b, :].bitcast(BF16)

    def bc(ap, shape):
        return ap.to_broadcast(list(shape))

    wcnt = [0]
    hcnt = [0]

    def load_slab(wd, r0, nk, c0, ncols):
        slot = wcnt[0] % 2
        wcnt[0] += 1
        for h0 in range(0, nk, 4):
            hn_ = min(4, nk - h0)
            hs = hcnt[0] % 2
            hcnt[0] += 1
            src = wd[r0 + h0 * 128: r0 + (h0 + hn_) * 128, c0:c0 + ncols].rearrange("(k p) n -> p k n", p=128)
            P.dma(wst[hs][:, 0:hn_, 0:ncols], src, w=["wst%d" % hs])
            if hcnt[0] % 3 == 0:
                P.op("pool", lambda e, hs=hs, h0=h0, hn_=hn_, slot=slot: e.tensor_copy(
                    out=wbf[slot][:, h0:h0 + hn_, 0:ncols], in_=wst[hs][:, 0:hn_, 0:ncols]),
                    r=["wst%d" % hs], w=["wbf%d" % slot])
            else:
                P.op("act", lambda e, hs=hs, h0=h0, hn_=hn_, slot=slot: e.activation(
                    out=wbf[slot][:, h0:h0 + hn_, 0:ncols], in_=wst[hs][:, 0:hn_, 0:ncols], func=AF.Copy),
                    r=["wst%d" % hs], w=["wbf%d" % slot])
        return slot

    pending = []

    def sched(loads, fn, extra=0):
        pending.append((loads, fn, extra))

    def flush():
        tasks = pending[:]
        del pending[:]
        loaded = {}

        def do_load(i):
            if i not in loaded:
                loaded[i] = [load_slab(*a) for a in tasks[i][0]]

        for i, (loads, fn, extra) in enumerate(tasks):
            do_load(i)
            if i + 1 < len(tasks) and len(loads) + extra + len(tasks[i + 1][0]) <= 2:
                do_load(i + 1)
            fn(loaded[i])

    bankctr = [0]

    def mm(out, lhsT, rhs, start, stop, r, w):
        P.op("pe", lambda e: e.matmul(out, lhsT, rhs, start=start, stop=stop), r=r, w=w)

    def transpose(out, in_, r, w):
        P.op("pe", lambda e: e.transpose(out, in_, identb[:, :]), r=list(r) + ["identb"], w=w)

    def dbg_dump(name, ap, shape, keys):
        if name not in dbg:
            return
        if name not in dbg_d:
            dbg_d[name] = nc.dram_tensor("dbg_" + name, list(shape), ap.dtype, kind="ExternalOutput").ap()
        return dbg_d[name]

    P.dma(cst[:, :], cst_d[:, :], w=["cst"])
    P.op("dve", lambda e: e.tensor_copy(out=identb[:, :], in_=cv("ident")), r=["cst"], w=["identb"])
    P.op("dve", lambda e: e.memset(epsT[:, :], EPS), w=["epsT"])
    P.op("dve", lambda e: e.memset(negb[:, :], -30000.0), w=["negb"])
    P.op("pool", lambda e: e.memset(Vc[:, :, :, 128:129], 1.0), w=["Vc"])

    mix_scale = float(HD) ** -0.5

    def rmsnorm_to_hT(src_d, st_i, nw_name, xin_views, hn_view, small, key_prefix):
        for c in range(4):
            gc = st_i * 4 + c
            xin = xin_views[c % 2]
            xk = "xin%d" % (c % 2)
            P.dma(xin, src_d[gc * 128:(gc + 1) * 128, :], r=[(key_prefix, gc)], w=[xk])
            ss = small[:, c:c + 1]
            rt = small[:, 4 + c:5 + c]
            rstd = small[:, 8 + c:9 + c]
            P.op("act", lambda e, xin=xin, ss=ss: e.activation(out=hn_view, in_=xin, func=AF.Square, accum_out=ss),
                 r=[xk], w=["hn", "nsm%d" % c])
            P.op("act", lambda e, ss=ss, rt=rt: e.activation(out=rt, in_=ss, func=AF.Ln, bias=epsT[:, :], scale=1.0 / D),
                 r=["nsm%d" % c, "epsT"], w=["nsm%d" % c])
            P.op("act", lambda e, rt=rt, rstd=rstd: e.activation(out=rstd, in_=rt, func=AF.Exp, scale=-0.5),
                 r=["nsm%d" % c], w=["nsm%d" % c])
            P.op("dve", lambda e, xin=xin, rstd=rstd: e.tensor_scalar(out=hn_view, in0=xin, scalar1=rstd, scalar2=None,
                                                                       op0=OP.mult), r=[xk, "nsm%d" % c], w=["hn"])
            b = 6 + (c % 2)
            for k in range(8):
                transpose(psb(b)[:, k * 128:(k + 1) * 128], hn_view[:, k * 128:(k + 1) * 128], r=["hn"], w=["ps%d" % b])
            P.op("dve", lambda e, b=b, c=c: e.tensor_tensor(
                out=hT[:, :, c * 128:(c + 1) * 128],
                in0=psb(b).rearrange("p (k t) -> p k t", k=8),
                in1=bc(lv(nw_name).unsqueeze(2), (128, 8, 128)), op=OP.mult),
                r=["ps%d" % b, "lcs"], w=["hT"])

    def proj_tok(lhsT_fn, nk_list, slot_list, ncols, evac, banks, lkeys):
        for c in range(4):
            b = banks[c % len(banks)]
            kk = 0
            tot = sum(nk_list)
            for si, slot in enumerate(slot_list):
                for k in range(nk_list[si]):
                    mm(psum[:, b, 0:ncols], lhsT_fn(kk, c), wbf[slot][:, k, 0:ncols], start=(kk == 0), stop=(kk == tot - 1),
                       r=lkeys + ["wbf%d" % slot], w=["ps%d" % b])
                    kk += 1
            evac(c, b)

    for l in layers:
        src_d = x_d if l == layers[0] else xl1_d
        last = (l == layers[-1])
        P.barrier()
        P.dma(lcs[:, :], lc_d[l], w=["lcs"])
        P.op("act", lambda e: e.activation(out=negA[:, :], in_=lv("alog"), func=AF.Exp), r=["lcs"], w=["negA"])
        P.op("dve", lambda e: e.tensor_scalar(out=negA[:, :], in0=negA[:, :], scalar1=-1.0, scalar2=None, op0=OP.mult),
             r=["negA"], w=["negA"])
        P.op("pool", lambda e: e.memset(hst[:, :], 0.0), w=["hst"])
        P.op("pool", lambda e: e.memset(xtail[:, :, :], 0.0), w=["xtail"])
        P.op("pool", lambda e: e.memset(ftail[:, :, :], 0.0), w=["ftail"])

        for st_i in range(NST):
            P.barrier()
            scores = aview(0, (128, S), F32)
            qT = aview(16384, (128, 8, T), BF16)
            qiT = aview(24576, (128, 4, T), BF16)
            maskT = aview(28672, (128, NCH if NCH <= 32 else 32, 128), BF16)
            junkb = aview(28672, (128, 4096), BF16)
            xin_v = [aview(36864, (128, D), F32), aview(40960, (128, D), F32)]
            hn_v = aview(45056, (128, D), BF16)
            qtok = [aview(47104, (128, 4, 128), BF16), aview(48128, (128, 4, 128), BF16)]
            qitf = aview(49152, (128, 8, 64), F32)
            rbuf = [aview(53248 + 1024 * i, (128, 512), BF16) for i in range(4)]
            maskb = [aview(57344 + 1024 * i, (128, 512), BF16) for i in range(2)]
            Eb = [aview(59392 + 1024 * i, (128, 4, 128), BF16) for i in range(2)]
            dsg = aview(61440, (128, 8, 128), BF16)
            hb_ctr = [0]
            a_tok = aview(63488, (128, 8, 128), BF16)
            small = aview(65536, (128, 256), F32)
            ropet = [aview(66560 + 256 * i, (128, 64), F32) for i in range(2)]
            qitok = aview(67072, (128, 512), BF16)
            wabs = small[:, 16:48].rearrange("p (c h) -> p c h", c=4)
            wsgn = small[:, 48:80].rearrange("p (c h) -> p c h", c=4)
            kitok = small[:, 192:256].bitcast(BF16)
            ktok = qtok[1]

            P.dma(tabs[:, :, :], tab_d[:, st_i * 192:(st_i + 1) * 192].rearrange("p (c n) -> p c n", c=4), w=["tabs"])
            rmsnorm_to_hT(src_d, st_i, "nmix", xin_v, hn_v, small, "xsrc%d" % l)

            lhs_h = lambda k, c: hT[:, k, c * 128:(c + 1) * 128]

            def rope(c, src3, dst3, half, cos_o, sin_o, nh, rk, wk):
                cosv = bc(tabs[:, c, cos_o:cos_o + half].unsqueeze(1), (128, nh, half))
                sinv = bc(tabs[:, c, sin_o:sin_o + half].unsqueeze(1), (128, nh, half))
                x1 = src3[:, :, 0:half]
                x2 = src3[:, :, half:2 * half]
                t1 = ropet[0][:, 0:nh * half].rearrange("p (a b) -> p a b", a=nh)
                t2 = ropet[1][:, 0:nh * half].rearrange("p (a b) -> p a b", a=nh)
                rr = list(rk) + ["tabs"]
                P.op("dve", lambda e: e.tensor_tensor(out=t1, in0=x1, in1=cosv, op=OP.mult), r=rr, w=["rt1"])
                P.op("dve", lambda e: e.tensor_tensor(out=t2, in0=x2, in1=sinv, op=OP.mult), r=rr, w=["rt2"])
                P.op("dve", lambda e: e.tensor_tensor(out=dst3[:, :, 0:half], in0=t1, in1=t2, op=OP.subtract),
                     r=["rt1", "rt2"], w=wk)
                P.op("dve", lambda e: e.tensor_tensor(out=t1, in0=x2, in1=cosv, op=OP.mult), r=rr, w=["rt1"])
                P.op("dve", lambda e: e.tensor_tensor(out=t2, in0=x1, in1=sinv, op=OP.mult), r=rr, w=["rt2"])
                P.op("dve", lambda e: e.tensor_tensor(out=dst3[:, :, half:2 * half], in0=t1, in1=t2, op=OP.add),
                     r=["rt1", "rt2"], w=wk)

            for qs in range(2 if stop >= 2 else 0):

                def evac_q(c, b, qs=qs):
                    import os
                    SUB = int(os.environ.get("SUB", "9"))
                    qt = qtok[0]
                    pv = psum[:, b, :].rearrange("p (h d) -> p h d", h=4)
                    if SUB <= 2:
                        return
                    P.op("act", lambda e: e.activation(out=qt[:, :, 32:128], in_=pv[:, :, 32:128], func=AF.Copy),
                         r=["ps%d" % b], w=["qtok0"])
                    if SUB <= 3:
                        return
                    rope(c, pv, qt, 16, 0, 16, 4, ["ps%d" % b], ["qtok0"])
                    if SUB <= 4:
                        return
                    tb = 4 + (c % 2)
                    for h in range(4):
                        transpose(psb(tb)[:, h * 128:(h + 1) * 128], qt[:, h, :], r=["qtok0"], w=["ps%d" % tb])
                    P.op("act", lambda e: e.activation(out=qT[:, qs * 4:(qs + 1) * 4, c * 128:(c + 1) * 128],
                                                       in_=psb(tb)[:, 0:512].rearrange("p (h t) -> p h t", h=4), func=AF.Copy),
                         r=["ps%d" % tb], w=["qT"])

                sched([(w_in_d[l], 0, 8, O_Q + qs * 512, 512)],
                      lambda s, evac_q=evac_q: proj_tok(lhs_h, [8], s, 512, evac_q, [0, 1, 2, 3], ["hT"]))


            def evac_kv(c, b):
                gc = st_i * 4 + c
                pv = psum[:, b, 0:256].rearrange("p (h d) -> p h d", h=2)
                kt = ktok[:, 0:2, :]
                P.op("act", lambda e: e.activation(out=kt[:, :, 32:128], in_=pv[:, :, 32:128], func=AF.Copy),
                     r=["ps%d" % b], w=["qtok1"])
                rope(c, pv, kt, 16, 0, 16, 2, ["ps%d" % b], ["qtok1"])
                P.op("act", lambda e: e.activation(out=Vc[:, gc, :, 0:128],
                                                   in_=psum[:, b, 256:512].rearrange("p (h d) -> p h d", h=2), func=AF.Copy),
                     r=["ps%d" % b], w=[("Vc", gc)])
                tb = 4 + (c % 2)
                for h in range(2):
                    transpose(psb(tb)[:, h * 128:(h + 1) * 128], kt[:, h, :], r=["qtok1"], w=["ps%d" % tb])
                P.op("act", lambda e: e.activation(out=kT[:, :, gc * 128:(gc + 1) * 128],
                                                   in_=psb(tb)[:, 0:256].rearrange("p (h t) -> p h t", h=2), func=AF.Copy),
                     r=["ps%d" % tb], w=[("kT", gc)])

            if stop >= 3:
                sched([(w_in_d[l], 0, 8, O_K, 512)], lambda s: proj_tok(lhs_h, [8], s, 512, evac_kv, [0, 1, 2, 3], ["hT"]))


            def evac_ki(c, b):
                gc = st_i * 4 + c
                pv = psum[:, b, 0:64].rearrange("p (h d) -> p h d", h=1)
                k3 = kitok[:, 0:64].rearrange("p (h d) -> p h d", h=1)
                P.op("act", lambda e: e.activation(out=k3[:, :, 16:64], in_=pv[:, :, 16:64], func=AF.Copy),
                     r=["ps%d" % b], w=["kitok"])
                rope(c, pv, k3, 8, 32, 40, 1, ["ps%d" % b], ["kitok"])
                P.op("dve", lambda e: e.tensor_copy(out=kitok[:, 64:128], in_=kitok[:, 0:64]), r=["kitok"], w=["kitok"])
                P.op("act", lambda e: e.activation(out=wabs[:, c, :], in_=psum[:, b, 64:72], func=AF.Abs),
                     r=["ps%d" % b], w=["wabs%d" % c])
                P.op("act", lambda e: e.activation(out=wsgn[:, c, :], in_=psum[:, b, 64:72], func=AF.Sign),
                     r=["ps%d" % b], w=["wsgn%d" % c])
                tb = 4 + (c % 2)
                transpose(psb(tb)[:, 0:128], kitok[:, :], r=["kitok"], w=["ps%d" % tb])
                P.op("act", lambda e: e.activation(out=kiT2[:, gc * 128:(gc + 1) * 128], in_=psb(tb)[:, 0:128], func=AF.Copy),
                     r=["ps%d" % tb], w=[("kiT", gc)])

            if stop >= 4:
                sched([(w_in_d[l], 0, 8, O_KI, 72)], lambda s: proj_tok(lhs_h, [8], s, 72, evac_ki, [0, 1, 2, 3], ["hT"]))


            def evac_qi(c, b):
                pv = psum[:, b, :].rearrange("p (h d) -> p h d", h=8)
                P.op("act", lambda e: e.activation(out=qitf[:, :, 16:64], in_=pv[:, :, 16:64], func=AF.Copy),
                     r=["ps%d" % b], w=["qitf"])
                rope(c, pv, qitf, 8, 32, 40, 8, ["ps%d" % b], ["qitf"])
                P.op("dve", lambda e: e.tensor_tensor(out=qitok.rearrange("p (h d) -> p h d", h=8), in0=qitf,
                                                      in1=bc(wabs[:, c, :].unsqueeze(2), (128, 8, 64)), op=OP.mult),
                     r=["qitf", "wabs%d" % c], w=["qitok"])
                tb = 4 + (c % 2)
                for j in range(4):
                    transpose(psb(tb)[:, j * 128:(j + 1) * 128], qitok[:, j * 128:(j + 1) * 128], r=["qitok"], w=["ps%d" % tb])
                P.op("act", lambda e: e.activation(out=qiT[:, :, c * 128:(c + 1) * 128],
                                                   in_=psb(tb)[:, 0:512].rearrange("p (h t) -> p h t", h=4), func=AF.Copy),
                     r=["ps%d" % tb], w=["qiT"])

            if stop >= 5:
                sched([(w_in_d[l], 0, 8, O_QI, 512)], lambda s: proj_tok(lhs_h, [8], s, 512, evac_qi, [0, 1, 2, 3], ["hT"]))
            flush()

            P.barrier()
            scores_v = [aview(0, (128, S), F32), aview(36864, (128, S), F32)]
            maskT_v = [aview(28672, (128, 32, 128), BF16),
                       wst[0][:, :, :].rearrange("p a b -> p (a b)").bitcast(BF16).rearrange("p (j t) -> p j t", j=32)]
            mkey = ["maskT0", "wst0"]

            def geom(c):
                gc = st_i * 4 + c
                nk = gc + 1
                n = nk * 128
                return gc, nk, n, slice(c * 128, (c + 1) * 128), (n + 511) // 512

            def st1(c):
                gc, nk, n, tq, ngrp = geom(c)
                scores = scores_v[c % 2]
                kkeys = [("kiT", j) for j in range(nk)]
                for h in range(8):
                    P.op("pool", lambda e, h=h: e.tensor_scalar(out=dsg[:, h, :], in0=identb[:, :], scalar1=wsgn[:, c, h:h + 1],
                                                                scalar2=None, op0=OP.mult),
                         r=["identb", "wsgn%d" % c], w=["dsg"])
                for kg in range(ngrp):
                    w_ = min(512, n - kg * 512)
                    sb_ = 2
                    for h in range(8):
                        b = hb_ctr[0] % 2
                        hb_ctr[0] += 1
                        pr = (h % 2) * 64
                        mm(psum[:, b, 0:w_], qiT[pr:pr + 64, h // 2, tq], kiT2[pr:pr + 64, kg * 512:kg * 512 + w_], True, True,
                           r=["qiT"] + kkeys, w=["ps%d" % b])
                        rb = rbuf[h % 4]
                        P.op("act", lambda e, b=b, rb=rb, w_=w_: e.activation(out=rb[:, 0:w_], in_=psum[:, b, 0:w_], func=AF.Relu),
                             r=["ps%d" % b], w=["rbuf%d" % (h % 4)])
                        mm(psum[:, sb_, 0:w_], dsg[:, h, :], rb[:, 0:w_], (h == 0), (h == 7),
                           r=["dsg", "rbuf%d" % (h % 4)], w=["ps%d" % sb_])
                    sc = scores[:, kg * 512:kg * 512 + w_]
                    P.op("act", lambda e, sb_=sb_, sc=sc, w_=w_: e.activation(out=sc, in_=psum[:, sb_, 0:w_], func=AF.Copy),
                         r=["ps%d" % sb_], w=[("sc", c % 2, kg)])

            def st2(c):
                gc, nk, n, tq, ngrp = geom(c)
                scores = scores_v[c % 2]
                maskT = maskT_v[c % 2]
                junkb = maskT.rearrange("p j t -> p (j t)")
                mk = mkey[c % 2]
                sckeys = [("sc", c % 2, kg) for kg in range(ngrp)]
                lastk = ("sc", c % 2, ngrp - 1)
                thr = small[:, 100 + (c % 2):101 + (c % 2)]
                tk = "thr%d" % (c % 2)
                if (gc * 128 + 128) <= TOPK:
                    P.op("dve", lambda e: e.tensor_tensor(out=scores[:, n - 128:n], in0=scores[:, n - 128:n], in1=cv("caus"),
                                                          op=OP.add), r=[lastk, "cst"], w=[lastk])
                    P.op("dve", lambda e: e.memset(thr, -1e29), w=[tk])
                else:
                    mx = small[:, 102:103]
                    mn = small[:, 103:104]
                    w0 = small[:, 104:105]
                    mid = small[:, 105:106]
                    cnt = small[:, 106:107]
                    dd = small[:, 107:108]
                    Wk = small[:, 110:110 + NIT + 2]
                    P.op("dve", lambda e: e.tensor_reduce(out=mx, in_=scores[:, 0:n], axis=AX.X, op=OP.max), r=sckeys, w=["bs_mx"])
                    P.op("dve", lambda e: e.tensor_reduce(out=mn, in_=scores[:, 0:n], axis=AX.X, op=OP.min), r=sckeys, w=["bs_mn"])
                    P.op("dve", lambda e: e.tensor_tensor(out=scores[:, n - 128:n], in0=scores[:, n - 128:n], in1=cv("caus"),
                                                          op=OP.add), r=[lastk, "cst"], w=[lastk])
                    P.op("dve", lambda e: e.tensor_tensor(out=w0, in0=mx, in1=mn, op=OP.subtract), r=["bs_mx", "bs_mn"], w=["bs_w0"])
                    P.op("dve", lambda e: e.tensor_scalar(out=Wk, in0=cv("pow2"), scalar1=w0, scalar2=None, op0=OP.mult),
                         r=["bs_w0", "cst"], w=["bs_wk"])
                    P.op("dve", lambda e: e.tensor_tensor(out=mid, in0=mn, in1=Wk[:, 0:1], op=OP.add), r=["bs_mn", "bs_wk"], w=["bs_mid"])
                    for it in range(NIT):
                        P.op("dve", lambda e: e.tensor_scalar(out=junkb[:, 0:n], in0=scores[:, 0:n], scalar1=mid, scalar2=None,
                                                              op0=OP.is_ge, op1=OP.add, accum_out=cnt),
                             r=sckeys + ["bs_mid"], w=[mk, "bs_cnt"])
                        lastit = (it == NIT - 1)
                        P.op("dve", lambda e, lastit=lastit: e.tensor_scalar(out=dd, in0=cnt, scalar1=float(TOPK),
                                                                              scalar2=(1.0 if lastit else 0.5),
                                                                              op0=OP.is_ge, op1=OP.subtract),
                             r=["bs_cnt"], w=["bs_dd"])
                        dst = thr if lastit else mid
                        P.op("dve", lambda e, it=it, dst=dst: e.scalar_tensor_tensor(
                            out=dst, in0=dd, scalar=Wk[:, it:it + 1], in1=mid, op0=OP.mult, op1=OP.add),
                            r=["bs_dd", "bs_wk", "bs_mid"], w=[tk if lastit else "bs_mid"])
                for kg in range(ngrp):
                    w_ = min(512, n - kg * 512)
                    mb = maskb[kg % 2]
                    P.op("dve", lambda e, mb=mb, kg=kg, w_=w_: e.tensor_scalar(
                        out=mb[:, 0:w_], in0=scores[:, kg * 512:kg * 512 + w_], scalar1=thr, scalar2=None, op0=OP.is_ge),
                        r=[("sc", c % 2, kg), tk], w=["maskb%d" % (kg % 2)])
                    tb = 3
                    nj = w_ // 128
                    for jj in range(nj):
                        transpose(psb(tb)[:, jj * 128:(jj + 1) * 128], mb[:, jj * 128:(jj + 1) * 128],
                                  r=["maskb%d" % (kg % 2)], w=["ps%d" % tb])
                    P.op("act", lambda e, tb=tb, kg=kg, nj=nj: e.activation(
                        out=maskT[:, kg * 4:kg * 4 + nj, :], in_=psb(tb)[:, 0:nj * 128].rearrange("p (j t) -> p j t", j=nj),
                        func=AF.Identity, scale=30000.0, bias=negb[:, :]), r=["ps%d" % tb, "negb"], w=[mk])
                if "sc" in dbg and gc == NCH - 1:
                    dd_ = dbg_dump("sc", scores, (128, S), None)
                    P.dma(dd_, scores, r=sckeys)
                    dd_ = dbg_dump("thr", thr, (128, 1), None)
                    P.dma(dd_, thr, r=[tk])

            def st3a(c):
                gc, nk, n, tq, ngrp = geom(c)
                maskT = maskT_v[c % 2]
                mk = mkey[c % 2]
                for kvh in range(2):
                    if kvh == 1:
                        st3b_half(c, 0)
                    for j in range(nk):
                        lb = 3 if j % 2 == 0 else 2
                        mm(psum[:, lb, :], kT[:, kvh, j * 128:(j + 1) * 128], qT[:, kvh * 4:(kvh + 1) * 4, tq], True, False,
                           r=[("kT", j), "qT"], w=["ps%d" % lb])
                        mm(psum[:, lb, :], identb[:, :], bc(maskT[:, j, :].unsqueeze(1), (128, 4, 128)), False, True,
                           r=["identb", mk], w=["ps%d" % lb])
                        Ev = Eb[j % 2]
                        P.op("act", lambda e, lb=lb, Ev=Ev: e.activation(
                            out=Ev, in_=psum[:, lb, :].rearrange("p (g t) -> p g t", g=4), func=AF.Exp, scale=mix_scale),
                            r=["ps%d" % lb], w=["E%d" % (j % 2)])
                        for g in range(4):
                            mm(psum[:, 4 + g, 0:129], Ev[:, g, :], Vc[:, j, kvh, :], (j == 0), (j == nk - 1),
                               r=["E%d" % (j % 2), ("Vc", j), "Vc"], w=["ps%d" % (4 + g)])

            def st3b_half(c, kvh):
                for g in range(4):
                    h = kvh * 4 + g
                    rden = small[:, 140 + h:141 + h]
                    P.op("dve", lambda e, g=g, rden=rden: e.reciprocal(out=rden, in_=psum[:, 4 + g, 128:129]),
                         r=["ps%d" % (4 + g)], w=["rden%d" % h])
                    P.op("act", lambda e, g=g, h=h, rden=rden: e.activation(
                        out=a_tok[:, h, :], in_=psum[:, 4 + g, 0:128], func=AF.Copy, scale=rden),
                        r=["ps%d" % (4 + g), "rden%d" % h], w=["a_tok"])

            def st3b(c):
                st3b_half(c, 1)
                tb = 3
                for h in range(8):
                    transpose(psb(tb)[:, h * 128:(h + 1) * 128], a_tok[:, h, :], r=["a_tok"], w=["ps%d" % tb])
                P.op("act", lambda e, tb=tb, c=c: e.activation(
                    out=aT[:, :, c * 128:(c + 1) * 128], in_=psb(tb).rearrange("p (h t) -> p h t", h=8), func=AF.Copy),
                    r=["ps%d" % tb], w=["aT"])

            if stop >= 6:
                st1(0)
                st1(1)
                st2(0)
                st3a(0)
                st1(2)
                st2(1)
                st3b(0)
                st3a(1)
                st1(3)
                st2(2)
                st3b(1)
                st3a(2)
                st2(3)
                st3b(2)
                st3a(3)
                st3b(3)

            if "aT" in dbg:
                dd_ = dbg_dump("aT", aT[:, :, :], (NST, 128, 8, T), None)
                P.dma(dd_[st_i], aT[:, :, :], r=["aT"])
            if "qT" in dbg:
                dd_ = dbg_dump("qT", qT, (NST, 128, 8, T), None)
                P.dma(dd_[st_i], qT, r=["qT"])
            if "hT" in dbg:
                dd_ = dbg_dump("hT", hT[:, :, :], (NST, 128, 8, T), None)
                P.dma(dd_[st_i], hT[:, :, :], r=["hT"])

            if stop >= 7:
                P.barrier()
                BT = aview(0, (128, 4, T), BF16)
                CT = aview(4096, (128, 4, T), BF16)
                Btok = aview(8192, (128, 4, 512), BF16)
                xacc = [aview(12288 + 2048 * i, (128, 512), F32) for i in range(2)]
                xact = [aview(16384 + 1024 * i, (128, 512), BF16) for i in range(2)]
                dtt = aview(18432, (128, 4, 32), F32)
                adt = aview(18944, (128, 4, 32), F32)

                def conv_fm(cc, b, wname, bname, ntap, tail, dst_silu, tailkey):
                    u = psum[:, b, :]
                    nx_ = len(xacc)
                    acc = xacc[cc % nx_]
                    ak = "xacc%d" % (cc % nx_)
                    wv = lambda j: lv(wname, cc * ntap + j, cc * ntap + j + 1)
                    P.op("act", lambda e: e.activation(out=acc, in_=u, func=AF.Identity, bias=lv(bname, cc, cc + 1),
                                                       scale=wv(ntap - 1)), r=["ps%d" % b, "lcs"], w=[ak])
                    for j in range(ntap - 1):
                        sh = ntap - 1 - j
                        P.op("dve", lambda e, j=j, sh=sh: e.scalar_tensor_tensor(
                            out=acc[:, sh:512], in0=u[:, 0:512 - sh], scalar=wv(j), in1=acc[:, sh:512], op0=OP.mult, op1=OP.add),
                            r=["ps%d" % b, "lcs", ak], w=[ak])
                        P.op("dve", lambda e, j=j, sh=sh: e.scalar_tensor_tensor(
                            out=acc[:, 0:sh], in0=tail[:, cc, ntap - 1 - sh:ntap - 1], scalar=wv(j), in1=acc[:, 0:sh],
                            op0=OP.mult, op1=OP.add), r=[tailkey, "lcs", ak], w=[ak])
                    P.op("dve", lambda e: e.tensor_copy(out=tail[:, cc, :], in_=u[:, 512 - (ntap - 1):512]),
                         r=["ps%d" % b], w=[tailkey])
                    return acc, ak

                def proj_fm(slot, ncc, cc0, handler):
                    for j in range(ncc):
                        b = j % 4
                        for k in range(8):
                            mm(psum[:, b, :], wbf[slot][:, k, j * 128:(j + 1) * 128], hT[:, k, :], (k == 0), (k == 7),
                               r=["hT", "wbf%d" % slot], w=["ps%d" % b])
                        handler(cc0 + j, b)

                def h_bc(cc, b):
                    acc, ak = conv_fm(cc, b, "xcw", "xcb", 4, xtail, None, "xtail")
                    g = (cc - 16) % 4
                    if cc < 20:
                        P.op("act", lambda e: e.activation(out=BT[:, g, :], in_=acc, func=AF.Silu), r=[ak], w=["BT"])
                        tb = 4 + (cc % 2)
                        for tc_ in range(4):
                            transpose(psb(tb)[:, tc_ * 128:(tc_ + 1) * 128], BT[:, g, tc_ * 128:(tc_ + 1) * 128], r=["BT"], w=["ps%d" % tb])
                        P.op("act", lambda e: e.activation(out=Btok[:, :, g * 128:(g + 1) * 128],
                                                           in_=psb(tb)[:, 0:512].rearrange("p (c n) -> p c n", c=4), func=AF.Copy),
                             r=["ps%d" % tb], w=["Btok"])
                    else:
                        P.op("act", lambda e: e.activation(out=CT[:, g, :], in_=acc, func=AF.Silu), r=[ak], w=["CT"])

                for half in range(2):
                    sched([(w_in_d[l], 0, 8, O_XBC + 2048 + half * 512, 512)],
                          lambda s, half=half: proj_fm(s[0], 4, 16 + half * 4, h_bc))

                def evac_dt(c, b):
                    P.op("dve", lambda e: e.tensor_tensor(out=dtt[:, c, :], in0=psum[:, b, 0:32], in1=lv("dtb"), op=OP.add),
                         r=["ps%d" % b, "lcs"], w=["dtt"])
                    P.op("act", lambda e: e.activation(out=dtt[:, c, :], in_=dtt[:, c, :], func=AF.Exp), r=["dtt"], w=["dtt"])
                    P.op("act", lambda e: e.activation(out=dtt[:, c, :], in_=dtt[:, c, :], func=AF.Ln, bias=1.0), r=["dtt"], w=["dtt"])
                    P.op("dve", lambda e: e.tensor_tensor(out=adt[:, c, :], in0=dtt[:, c, :], in1=negA[:, :], op=OP.mult),
                         r=["dtt", "negA"], w=["adt"])

                sched([(w_in_d[l], 0, 8, O_DT, 32)], lambda s: proj_tok(lhs_h, [8], s, 32, evac_dt, [0, 1, 2, 3], ["hT"]))

                _o = [19 * 1024]

                def _al(nbytes):
                    o = _o[0]
                    _o[0] += nbytes
                    return o

                TP = []
                for p_ in range(2):
                    d_ = dict(
                        xtok=aview(_al(4096), (128, 4, 512), BF16), zs=aview(_al(4096), (128, 4, 512), BF16),
                        Wp=aview(_al(4096), (128, 8, 128), F32), Lx=aview(_al(2048), (128, 8, 128), BF16),
                        MT=aview(_al(2048), (128, 8, 128), BF16), CBm=aview(_al(256), (128, 128), BF16),
                        Xdt=aview(_al(1024), (128, 8, 64), BF16), Xd2=aview(_al(1024), (128, 8, 64), BF16),
                        yo=aview(_al(2048), (128, 8, 64), F32), yy=aview(_al(2048), (128, 8, 64), F32),
                        hb=aview(_al(1024), (128, 512), BF16), sm2=aview(_al(256), (128, 64), F32),
                        btok_b=aview(_al(1024), (128, 512), BF16))
                    TP.append(d_)

                def make_hx(g):
                    p_ = g % 2
                    xtok = TP[p_]["xtok"]

                    def h_x(cc, b):
                        acc, ak = conv_fm(cc, b, "xcw", "xcb", 4, xtail, None, "xtail")
                        xa = xact[cc % 2]
                        xk = "xact%d" % (cc % 2)
                        P.op("act", lambda e: e.activation(out=xa, in_=acc, func=AF.Silu), r=[ak], w=[xk])
                        tb = 4 + (cc % 2)
                        for tc_ in range(4):
                            transpose(psb(tb)[:, tc_ * 128:(tc_ + 1) * 128], xa[:, tc_ * 128:(tc_ + 1) * 128], r=[xk], w=["ps%d" % tb])
                        j = cc % 4
                        P.op("act", lambda e: e.activation(out=xtok[:, :, j * 128:(j + 1) * 128],
                                                           in_=psb(tb)[:, 0:512].rearrange("p (c n) -> p c n", c=4), func=AF.Copy),
                             r=["ps%d" % tb], w=["xtok%d" % p_])
                    return h_x

                def make_evz(g):
                    p_ = g % 2
                    zs = TP[p_]["zs"]

                    def evac_z(c, b):
                        P.op("act", lambda e: e.activation(out=zs[:, c, :], in_=psum[:, b, :], func=AF.Silu), r=["ps%d" % b], w=["zs%d" % p_])
                    return evac_z

                def ssd_chain(g):
                    p_ = g % 2
                    t_ = TP[p_]
                    xtok, zs, Wp, Lx, MT, CBm, Xdt, Xd2, yo, yy, hb, sm2, btok_b = (
                        t_["xtok"], t_["zs"], t_["Wp"], t_["Lx"], t_["MT"], t_["CBm"], t_["Xdt"], t_["Xd2"], t_["yo"], t_["yy"],
                        t_["hb"], t_["sm2"], t_["btok_b"])
                    K_ = lambda n_: "%s%d" % (n_, p_)
                    b0, b1, b2, b3 = 4 * p_, 4 * p_ + 1, 4 * p_ + 2, 4 * p_ + 3
                    pk = lambda b: "ps%d" % b
                    hk = "hst%d" % g
                    hs3 = hst[:, g * 512:(g + 1) * 512].rearrange("p (e d) -> p e d", e=8)
                    for c in range(4):
                        ts_ = slice(c * 128, (c + 1) * 128)
                        xt3 = xtok[:, c, :].rearrange("p (e d) -> p e d", e=8)
                        dtg = dtt[:, c, g * 8:(g + 1) * 8]
                        adg = adt[:, c, g * 8:(g + 1) * 8]
                        P.op("dve", lambda e, xt3=xt3, dtg=dtg: e.tensor_tensor(out=Xdt, in0=xt3, in1=bc(dtg.unsqueeze(2), (128, 8, 64)), op=OP.mult),
                             r=[K_("xtok"), "dtt"], w=[K_("Xdt")])
                        P.op("dve", lambda e, adg=adg: e.tensor_tensor(out=Wp, in0=bc(cv("tri_le").unsqueeze(1), (128, 8, 128)),
                                                                       in1=bc(adg.unsqueeze(2), (128, 8, 128)), op=OP.mult),
                             r=["adt", "cst"], w=[K_("Wp")])
                        yield
                        for hh, bb in ((0, b0), (1, b1)):
                            mm(psum[:, bb, :], cv("sgt"), Wp[:, hh * 4:(hh + 1) * 4, :], True, True, r=["cst", K_("Wp")], w=[pk(bb)])
                            P.op("act", lambda e, hh=hh, bb=bb: e.activation(out=Lx[:, hh * 4:(hh + 1) * 4, :],
                                                                             in_=psum[:, bb, :].rearrange("p (e l) -> p e l", e=4), func=AF.Exp),
                                 r=[pk(bb)], w=[K_("Lx")])
                            P.op("act", lambda e, hh=hh, bb=bb: e.activation(out=sm2[:, hh * 4:(hh + 1) * 4],
                                                                             in_=psum[:, bb, :].rearrange("p (e l) -> p e l", e=4)[:, :, 127],
                                                                             func=AF.Exp), r=[pk(bb)], w=[K_("decay")])
                        yield
                        mm(psum[:, b2, 0:128], BT[:, g, ts_], CT[:, g, ts_], True, True, r=["BT", "CT"], w=[pk(b2)])
                        mm(psum[:, b2, 128:136], cv("tri_le"), adg, True, True, r=["cst", "adt"], w=[pk(b2)])
                        mm(psum[:, b2, 136:144], cv("ones"), adg, True, True, r=["cst", "adt"], w=[pk(b2)])
                        P.op("dve", lambda e: e.tensor_tensor(out=CBm, in0=psum[:, b2, 0:128], in1=cv("tri_le"), op=OP.mult),
                             r=[pk(b2), "cst"], w=[K_("CBm")])
                        P.op("act", lambda e: e.activation(out=sm2[:, 8:24], in_=psum[:, b2, 128:144], func=AF.Exp), r=[pk(b2)], w=[K_("eacs")])
                        yield
                        P.op("dve", lambda e: e.tensor_tensor(out=MT, in0=Lx, in1=bc(CBm.unsqueeze(1), (128, 8, 128)), op=OP.mult),
                             r=[K_("Lx"), K_("CBm")], w=[K_("MT")])
                        for e_ in range(8):
                            mm(psum[:, b3, e_ * 64:(e_ + 1) * 64], MT[:, e_, :], Xdt[:, e_, :], True, True, r=[K_("MT"), K_("Xdt")], w=[pk(b3)])
                        yield
                        P.op("act", lambda e: e.activation(out=hb, in_=hst[:, g * 512:(g + 1) * 512], func=AF.Copy), r=[hk], w=[K_("hb")])
                        mm(psum[:, b0, :], CT[:, g, ts_], hb, True, True, r=["CT", K_("hb")], w=[pk(b0)])
                        P.op("dve", lambda e: e.tensor_tensor(out=yo, in0=psum[:, b0, :].rearrange("p (e d) -> p e d", e=8),
                                                              in1=bc(sm2[:, 8:16].unsqueeze(2), (128, 8, 64)), op=OP.mult),
                             r=[pk(b0), K_("eacs")], w=[K_("yo")])
                        yield
                        P.op("dve", lambda e: e.tensor_tensor(out=yy, in0=psum[:, b3, :].rearrange("p (e d) -> p e d", e=8), in1=yo, op=OP.add),
                             r=[pk(b3), K_("yo")], w=[K_("yy")])
                        P.op("dve", lambda e, xt3=xt3: e.tensor_tensor(out=yo, in0=xt3, in1=bc(lv("dsk", g * 8, g * 8 + 8).unsqueeze(2), (128, 8, 64)),
                                                                       op=OP.mult), r=[K_("xtok"), "lcs"], w=[K_("yo")])
                        P.op("dve", lambda e: e.tensor_tensor(out=yy, in0=yy, in1=yo, op=OP.add), r=[K_("yy"), K_("yo")], w=[K_("yy")])
                        yield
                        P.op("dve", lambda e: e.tensor_tensor(out=Xd2, in0=Xdt, in1=bc(sm2[:, 0:8].unsqueeze(2), (128, 8, 64)), op=OP.mult),
                             r=[K_("Xdt"), K_("decay")], w=[K_("Xd2")])
                        mm(psum[:, b1, :], Btok[:, c, g * 128:(g + 1) * 128], Xd2.rearrange("p e d -> p (e d)"), True, True,
                           r=["Btok", K_("Xd2")], w=[pk(b1)])
                        P.op("dve", lambda e: e.tensor_tensor(out=hs3, in0=hs3, in1=bc(sm2[:, 16:24].unsqueeze(2), (128, 8, 64)), op=OP.mult),
                             r=[hk, K_("eacs")], w=[hk])
                        P.op("dve", lambda e: e.tensor_tensor(out=hs3, in0=psum[:, b1, :].rearrange("p (e d) -> p e d", e=8), in1=hs3, op=OP.add),
                             r=[pk(b1), hk], w=[hk])
                        yield
                        yf = yy.rearrange("p e d -> p (e d)")
                        P.op("dve", lambda e, yf=yf, c=c: e.tensor_tensor(out=yf, in0=yf, in1=zs[:, c, :], op=OP.mult), r=[K_("yy"), K_("zs")], w=[K_("yy")])
                        P.op("act", lambda e, yf=yf: e.activation(out=btok_b, in_=yf, func=AF.Square, accum_out=sm2[:, 24:25]),
                             r=[K_("yy")], w=[K_("btok_b"), K_("gss")])
                        P.op("act", lambda e: e.activation(out=sm2[:, 25:26], in_=sm2[:, 24:25], func=AF.Ln, bias=epsT[:, :], scale=1.0 / 512),
                             r=[K_("gss"), "epsT"], w=[K_("gss")])
                        P.op("act", lambda e: e.activation(out=sm2[:, 26:27], in_=sm2[:, 25:26], func=AF.Exp, scale=-0.5),
                             r=[K_("gss")], w=[K_("gss")])
                        yield
                        P.op("dve", lambda e, yf=yf: e.tensor_scalar(out=btok_b, in0=yf, scalar1=sm2[:, 26:27], scalar2=None, op0=OP.mult),
                             r=[K_("yy"), K_("gss")], w=[K_("btok_b")])
                        for j in range(4):
                            transpose(psb(b2)[:, j * 128:(j + 1) * 128], btok_b[:, j * 128:(j + 1) * 128], r=[K_("btok_b")], w=[pk(b2)])
                        P.op("dve", lambda e, ts_=ts_: e.tensor_tensor(out=bT[:, g * 4:(g + 1) * 4, ts_],
                                                                       in0=psb(b2)[:, 0:512].rearrange("p (j t) -> p j t", j=4),
                                                                       in1=bc(lv("snw", g * 4, g * 4 + 4).unsqueeze(2), (128, 4, 128)), op=OP.mult),
                             r=[pk(b2), "lcs"], w=["bT"])
                        yield

                def run_pair(g0):
                    gens = [ssd_chain(g0), ssd_chain(g0 + 1)]
                    while gens:
                        for gen in list(gens):
                            try:
                                next(gen)
                            except StopIteration:
                                gens.remove(gen)

                for g0 in (0, 2):
                    for g in (g0, g0 + 1):
                        sched([(w_in_d[l], 0, 8, O_XBC + g * 512, 512)], lambda s, g=g: proj_fm(s[0], 4, g * 4, make_hx(g)))
                        sched([(w_in_d[l], 0, 8, O_Z + g * 512, 512)],
                              lambda s, g=g: proj_tok(lhs_h, [8], s, 512, make_evz(g), [0, 1, 2, 3], ["hT"]))
                    sched([], lambda s, g0=g0: run_pair(g0))
                flush()
                if "bT" in dbg:
                    dd_ = dbg_dump("bT", bT[:, :, :], (NST, 128, 16, T), None)
                    P.dma(dd_[st_i], bT[:, :, :], r=["bT"])

            if stop >= 8:
                P.barrier()
                ga = aview(0, (128, 4, 512), BF16)
                gb = aview(4096, (128, 4, 512), BF16)
                mf = aview(8192, (128, 4, 512), F32)
                tmpf = aview(16384, (128, 512), F32)
                mtok = aview(18432, (128, 512), BF16)
                mT = aview(19456, (128, 8, T), BF16)
                xh = aview(27648, (128, 4, 512), F32)
                for cs in range(2):
                    sched([(w_in_d[l], 0, 8, O_GA + cs * 512, 512)], lambda s: proj_tok(
                        lhs_h, [8], s, 512,
                        lambda c, b: P.op("act", lambda e: e.activation(out=ga[:, c, :], in_=psum[:, b, :], func=AF.Sigmoid),
                                          r=["ps%d" % b], w=["ga"]), [0, 1, 2, 3], ["hT"]))
                    sched([(w_in_d[l], 0, 8, O_GB + cs * 512, 512)], lambda s: proj_tok(
                        lhs_h, [8], s, 512,
                        lambda c, b: P.op("act", lambda e: e.activation(out=gb[:, c, :], in_=psum[:, b, :], func=AF.Sigmoid),
                                          r=["ps%d" % b], w=["gb"]), [0, 1, 2, 3], ["hT"]))
                    sched([(w_pa_d[l], 0, 8, cs * 512, 512)], lambda s: proj_tok(
                        lambda k, c: aT[:, k, c * 128:(c + 1) * 128], [8], s, 512,
                        lambda c, b: P.op("dve", lambda e: e.tensor_tensor(out=mf[:, c, :], in0=psum[:, b, :], in1=ga[:, c, :], op=OP.mult),
                                          r=["ps%d" % b, "ga"], w=["mf"]), [0, 1, 2, 3], ["aT"]))

                    def evac_pb(c, b, cs=cs):
                        P.op("dve", lambda e: e.tensor_tensor(out=tmpf, in0=psum[:, b, :], in1=gb[:, c, :], op=OP.mult),
                             r=["ps%d" % b, "gb"], w=["tmpf"])
                        P.op("dve", lambda e: e.tensor_tensor(out=mtok, in0=tmpf, in1=mf[:, c, :], op=OP.add), r=["tmpf", "mf"], w=["mtok"])
                        tb = 6 + (c % 2)
                        for j in range(4):
                            transpose(psb(tb)[:, j * 128:(j + 1) * 128], mtok[:, j * 128:(j + 1) * 128], r=["mtok"], w=["ps%d" % tb])
                        P.op("act", lambda e: e.activation(out=mT[:, cs * 4:(cs + 1) * 4, c * 128:(c + 1) * 128],
                                                           in_=psb(tb)[:, 0:512].rearrange("p (j t) -> p j t", j=4), func=AF.Copy),
                             r=["ps%d" % tb], w=["mT"])

                    sched([(w_pb_d[l], 0, 8, cs * 512, 512), (w_pb_d[l], 1024, 8, cs * 512, 512)],
                          lambda s, evac_pb=evac_pb: proj_tok(lambda k, c: bT[:, k, c * 128:(c + 1) * 128], [8, 8], s, 512, evac_pb,
                                                              [0, 1, 2, 3], ["bT"]))
                flush()
                for cs in range(2):
                    def out_task(s, cs=cs):
                        P.dma(xh, src_d[st_i * T:(st_i + 1) * T, cs * 512:(cs + 1) * 512].rearrange("(c p) n -> p c n", p=128),
                              r=[("xsrc%d" % l, st_i * 4 + c) for c in range(4)], w=["xh"])
                        proj_tok(lambda k, c: mT[:, k, c * 128:(c + 1) * 128], [8], s, 512,
                                 lambda c, b: P.op("dve", lambda e: e.tensor_tensor(out=xh[:, c, :], in0=psum[:, b, :], in1=xh[:, c, :], op=OP.add),
                                                   r=["ps%d" % b, "xh"], w=["xh"]), [0, 1, 2, 3], ["mT"])
                        P.dma(xmid_d[st_i * T:(st_i + 1) * T, cs * 512:(cs + 1) * 512].rearrange("(c p) n -> p c n", p=128), xh,
                              r=["xh"], w=[("xmid", st_i * 4 + c) for c in range(4)])
                    sched([(w_out_d[l], 0, 8, cs * 512, 512)], out_task)
                flush()

            if stop >= 9:
                P.barrier()
                xin_f = [aview(0, (128, D), F32), aview(4096, (128, D), F32)]
                hn_f = aview(8192, (128, D), BF16)
                small_f = aview(10240, (128, 64), F32)
                xacc = [aview(10496 + 2048 * i, (128, 512), F32) for i in range(4)]
                sg = aview(18688, (128, 22, 512), BF16)
                xo = aview(41216, (128, 4, D), F32)
                nfin = aview(57600, (128, D), F32)
                osq = aview(61696, (128, D), BF16)
                rmsnorm_to_hT(xmid_d, st_i, "nffn", xin_f, hn_f, small_f, "xmid")

                def h_up(cc, b):
                    acc, ak = conv_fm(cc, b, "fcw", "fcb", 3, ftail, None, "ftail")
                    if cc < 22:
                        P.op("act", lambda e: e.activation(out=sg[:, cc, :], in_=acc, func=AF.Silu), r=[ak], w=[("sg", cc)])
                    else:
                        P.op("dve", lambda e: e.tensor_tensor(out=sg[:, cc - 22, :], in0=sg[:, cc - 22, :], in1=acc, op=OP.mult),
                             r=[ak, ("sg", cc - 22)], w=[("sg", cc - 22)])

                for sl in range(11):
                    sched([(w_up_d[l], 0, 8, sl * 512, 512)], lambda s, sl=sl: proj_fm(s[0], 4, sl * 4, h_up))
                sgk = [("sg", i) for i in range(22)]

                def down_task(slots, cs):
                    P.dma(xo[:, :, cs * 512:(cs + 1) * 512],
                          xmid_d[st_i * T:(st_i + 1) * T, cs * 512:(cs + 1) * 512].rearrange("(c p) n -> p c n", p=128),
                          r=[("xmid", st_i * 4 + c) for c in range(4)], w=["xo"])
                    for c in range(4):
                        kk = 0
                        for si in range(2):
                            for k in range(8):
                                mm(psum[:, c, :], sg[:, kk, c * 128:(c + 1) * 128], wbf[slots[si]][:, k, :], (kk == 0), False,
                                   r=sgk + ["wbf%d" % slots[si]], w=["ps%d" % c])
                                kk += 1
                    s2 = load_slab(w_dn_d[l], 2048, 6, cs * 512, 512)
                    for c in range(4):
                        for k in range(6):
                            mm(psum[:, c, :], sg[:, 16 + k, c * 128:(c + 1) * 128], wbf[s2][:, k, :], False, (k == 5),
                               r=sgk + ["wbf%d" % s2], w=["ps%d" % c])
                        P.op("dve", lambda e, c=c, cs=cs: e.tensor_tensor(out=xo[:, c, cs * 512:(cs + 1) * 512], in0=psum[:, c, :],
                                                                          in1=xo[:, c, cs * 512:(cs + 1) * 512], op=OP.add),
                             r=["ps%d" % c, "xo"], w=["xo"])

                for cs in range(2):
                    sched([(w_dn_d[l], 0, 8, cs * 512, 512), (w_dn_d[l], 1024, 8, cs * 512, 512)],
                          lambda s, cs=cs: down_task(s, cs), extra=1)
                flush()
                if not last:
                    P.dma(xl1_d[st_i * T:(st_i + 1) * T, :].rearrange("(c p) n -> p c n", p=128), xo, r=["xo"],
                          w=[("xsrc1", st_i * 4 + c) for c in range(4)])
                else:
                    P.dma(nfin, nfin_d[:, :], w=["nfin"])
                    for c in range(4):
                        ss = small_f[:, 32 + c:33 + c]
                        rt = small_f[:, 36 + c:37 + c]
                        rs = small_f[:, 40 + c:41 + c]
                        P.op("act", lambda e, c=c, ss=ss: e.activation(out=osq, in_=xo[:, c, :], func=AF.Square, accum_out=ss),
                             r=["xo"], w=["osq", "fsm%d" % c])
                        P.op("act", lambda e, ss=ss, rt=rt: e.activation(out=rt, in_=ss, func=AF.Ln, bias=epsT[:, :], scale=1.0 / D),
                             r=["fsm%d" % c, "epsT"], w=["fsm%d" % c])
                        P.op("act", lambda e, rt=rt, rs=rs: e.activation(out=rs, in_=rt, func=AF.Exp, scale=-0.5),
                             r=["fsm%d" % c], w=["fsm%d" % c])
                        P.op("dve", lambda e, c=c, rs=rs: e.scalar_tensor_tensor(out=xo[:, c, :], in0=xo[:, c, :], scalar=rs, in1=nfin,
                                                                                  op0=OP.mult, op1=OP.mult),
                             r=["xo", "fsm%d" % c, "nfin"], w=["xo"])
                    P.dma(out_d[st_i * T:(st_i + 1) * T, :].rearrange("(c p) n -> p c n", p=128), xo, r=["xo"])

    if "kT" in dbg:
        dd_ = dbg_dump("kT", kT[:, :, :], (128, NKV, S), None)
        P.dma(dd_, kT[:, :, :], r=[("kT", j) for j in range(NCH)])
    if "kiT" in dbg:
        dd_ = dbg_dump("kiT", kiT2[:, :], (128, S), None)
        P.dma(dd_, kiT2[:, :], r=[("kiT", j) for j in range(NCH)])
    P.emit()
    st.close()
    return nc


_NC_CACHE = {}


def kernel(**inputs):
    inp = {k: np.asarray(v) for k, v in inputs.items()}
    x = inp["x"].astype(np.float32, copy=False)
    B, S, _ = x.shape
    if S not in _NC_CACHE:
        _NC_CACHE[S] = build(S, layers=(0, 1))
    nc = _NC_CACHE[S]
    cst, tab = host_consts(S)
    lc = np.stack([host_layer_consts(inp, l) for l in range(2)])
    nfin = np.ascontiguousarray(np.broadcast_to(inp["norm_final_w"].astype(np.float32)[None, :], (128, D)))
    shared = {"w_in": np.ascontiguousarray(inp["w_in"], dtype=np.float32),
              "w_proj_attn": np.ascontiguousarray(inp["w_proj_attn"], dtype=np.float32),
              "w_proj_ssd": np.ascontiguousarray(inp["w_proj_ssd"], dtype=np.float32),
              "w_out": np.ascontiguousarray(inp["w_out"], dtype=np.float32),
              "ffn_w_up": np.ascontiguousarray(inp["ffn_w_up"], dtype=np.float32),
              "ffn_w_down": np.ascontiguousarray(inp["ffn_w_down"], dtype=np.float32),
              "cst": cst, "tab": tab, "lc": lc, "nfin": nfin}
    in_maps = [dict(shared, x=np.ascontiguousarray(x[b])) for b in range(B)]
    res = run_bass_kernel_spmd(nc, in_maps, core_ids=list(range(B)))
    return np.stack([np.asarray(r["out"], dtype=np.float32) for r in res.results], axis=0)
```

```python
import contextlib
import numpy as np
import concourse.bass as bass
import concourse.mybir as mybir
from concourse.bass_utils import run_bass_kernel_spmd

F32 = mybir.dt.float32
BF16 = mybir.dt.bfloat16
AF = mybir.ActivationFunctionType
OP = mybir.AluOpType
AX = mybir.AxisListType

D = 1024
NH, HD, NKV = 8, 128, 2
IH, IDM = 8, 64
SSD_INNER, SSD_HD, SSD_H, SSD_G, SSD_N = 2048, 64, 32, 4, 128
CONV_DIM = 3072
FFN = 2816
EPS = 1e-6
IN_COLS = 9320
O_Q, O_K, O_V, O_QI, O_KI, O_WI, O_Z, O_XBC, O_DT, O_GA, O_GB = (
    0, 1024, 1280, 1536, 2048, 2112, 2120, 4168, 7240, 7272, 8296)
T = 512
NIT = 22

ENGS = ["pe", "act", "dve", "pool", "sp"]
NDMASEM = 24


class Prog:
    def __init__(self, nc):
        self.nc = nc
        self.ops = {e: [] for e in ENGS}
        self.last_w = {}
        self.readers = {}
        self.ndma = 0
        self.last_real = {}
        self.ps_last = {}

    @staticmethod
    def _isps(k):
        return isinstance(k, str) and k.startswith("ps") and k[2:].isdigit()

    def _deps(self, eng, r, w):
        deps = []
        for k in list(r) + list(w):
            if self._isps(k):
                is_w = k in w
                last = self.ps_last.get(k)
                if last is not None:
                    ref, lw = last
                    same = (ref[0] == "eng" and ref[1] == eng)
                    if (not same) or is_w or lw:
                        deps.append(ref)
        r = [k for k in r if not self._isps(k)]
        w = [k for k in w if not self._isps(k)]
        for k in r:
            lw = self.last_w.get(k)
            if lw is not None:
                deps.append(lw)
        for k in w:
            lw = self.last_w.get(k)
            if lw is not None:
                deps.append(lw)
            deps.extend(self.readers.get(k, ()))
        out = []
        for d in deps:
            if d[0] == "eng" and d[1] == "pe" and eng == "pe":
                continue
            if d not in out:
                out.append(d)
        return out

    def _commit(self, ref, r, w):
        for k in list(r) + list(w):
            if self._isps(k):
                self.ps_last[k] = (ref, k in w)
        r = [k for k in r if not self._isps(k)]
        w = [k for k in w if not self._isps(k)]
        for k in r:
            self.readers.setdefault(k, []).append(ref)
        for k in w:
            self.last_w[k] = ref
            self.readers[k] = []

    def _mark(self, deps):
        for d in deps:
            if d[0] == "eng":
                self.ops[d[1]][d[2]]["sig"] = True

    def op(self, eng, fn, r=(), w=()):
        deps = self._deps(eng, r, w)
        self._mark(deps)
        idx = len(self.ops[eng])
        self.ops[eng].append(dict(fn=fn, deps=deps, sig=False, dma=None))
        self.last_real[eng] = idx
        self._commit(("eng", eng, idx), r, w)

    def dma(self, out, in_, r=(), w=(), q="sp"):
        deps = self._deps(q, r, w)
        self._mark(deps)
        i = self.ndma
        self.ndma += 1
        if i >= NDMASEM:
            deps.append(("dma", i - NDMASEM))
        self.ops[q].append(dict(fn=lambda e: e.dma_start(out=out, in_=in_), deps=deps, sig=False, dma=i))
        self._commit(("dma", i), r, w)

    def barrier(self):
        deps = [("eng", e, i) for e, i in self.last_real.items()]
        deps += [("dma", i) for i in range(max(0, self.ndma - NDMASEM), self.ndma)]
        self._mark(deps)
        for e in ENGS:
            self.ops[e].append(dict(fn=None, deps=[d for d in deps if not (d[0] == "eng" and d[1] == e)],
                                    sig=False, dma=None))
        self.last_w.clear()
        self.readers.clear()
        self.ps_last.clear()

    def simulate(self):
        sigcnt = {}
        for e in ENGS:
            c = 0
            arr = []
            for o in self.ops[e]:
                if o["sig"]:
                    c += 1
                arr.append(c)
            sigcnt[e] = arr
        sem = {e: 0 for e in ENGS}
        dsem = [0] * NDMASEM
        ptr = {e: 0 for e in ENGS}
        progress = True
        while progress:
            progress = False
            for e in ENGS:
                while ptr[e] < len(self.ops[e]):
                    o = self.ops[e][ptr[e]]
                    ok = True
                    for d in o["deps"]:
                        if d[0] == "eng":
                            if sem[d[1]] < sigcnt[d[1]][d[2]]:
                                ok = False
                        else:
                            if dsem[d[1] % NDMASEM] < 16 * (d[1] // NDMASEM + 1):
                                ok = False
                    if not ok:
                        break
                    if o["dma"] is not None:
                        dsem[o["dma"] % NDMASEM] += 16
                    elif o["sig"] and o["fn"] is not None:
                        sem[e] += 1
                    ptr[e] += 1
                    progress = True
        stuck = {e: (ptr[e], len(self.ops[e])) for e in ENGS if ptr[e] < len(self.ops[e])}
        if stuck:
            for e in stuck:
                o = self.ops[e][ptr[e]]
                print("STUCK", e, ptr[e], o["deps"], "sig", o["sig"], "fn", o["fn"] is not None)
            raise RuntimeError("deadlock in semaphore protocol: %r" % stuck)
        print("simulate ok:", {e: len(self.ops[e]) for e in ENGS}, "dmas", self.ndma)

    def emit(self):
        nc = self.nc
        self.simulate()
        with contextlib.ExitStack() as st:
            esem = {e: st.enter_context(nc.semaphore("s_" + e)) for e in ENGS}
            dsem = [st.enter_context(nc.semaphore("d_%d" % i)) for i in range(NDMASEM)]
            block = st.enter_context(nc.Block())
            sigcnt = {}
            for e in ENGS:
                c = 0
                arr = []
                for o in self.ops[e]:
                    if o["sig"]:
                        c += 1
                    arr.append(c)
                sigcnt[e] = arr
            ndma = self.ndma

            def run(e, engobj):
                waited = {}
                for o in self.ops[e]:
                    for d in o["deps"]:
                        if d[0] == "eng":
                            sem = esem[d[1]]
                            val = sigcnt[d[1]][d[2]]
                            key = "e" + d[1]
                        else:
                            sem = dsem[d[1] % NDMASEM]
                            val = 16 * (d[1] // NDMASEM + 1)
                            key = "d%d" % (d[1] % NDMASEM)
                        if waited.get(key, 0) >= val:
                            continue
                        engobj.wait_ge(sem, val)
                        waited[key] = val
                    if o["fn"] is None:
                        continue
                    ins = o["fn"](engobj)
                    if o["dma"] is not None:
                        ins.then_inc(dsem[o["dma"] % NDMASEM], 16)
                    elif o["sig"]:
                        ins.then_inc(esem[e], 1)
                if e == "sp":
                    for s in range(min(NDMASEM, ndma)):
                        n = (ndma - 1 - s) // NDMASEM + 1
                        engobj.wait_ge(dsem[s], 16 * n)

            @block.sync
            def _(eng):
                run("sp", eng)

            @block.scalar
            def _(eng):
                run("act", eng)

            @block.vector
            def _(eng):
                run("dve", eng)

            @block.gpsimd
            def _(eng):
                run("pool", eng)

            @block.tensor
            def _(eng):
                run("pe", eng)


CST_COLS = {}


def _cst_layout():
    off = 0
    lay = {}
    for name, n in [("ident", 128), ("tri_le", 128), ("caus", 128), ("sgt", 128), ("ones", 128),
                    ("pow2", NIT + 2)]:
        lay[name] = (off, n)
        off += n
    return lay, off


def _lc_layout():
    off = 0
    lay = {}
    for name, n in [("nmix", 8), ("nffn", 8), ("snw", 16), ("xcw", 96), ("xcb", 24), ("fcw", 132),
                    ("fcb", 44), ("dtb", 32), ("alog", 32), ("dsk", 32)]:
        lay[name] = (off, n)
        off += n
    return lay, off


def host_consts(S):
    lay, n = _cst_layout()
    c = np.zeros((128, n), np.float32)
    i = np.arange(128)
    c[:, lay["ident"][0]:lay["ident"][0] + 128] = np.eye(128, dtype=np.float32)
    c[:, lay["tri_le"][0]:lay["tri_le"][0] + 128] = (i[:, None] <= i[None, :]).astype(np.float32)
    c[:, lay["caus"][0]:lay["caus"][0] + 128] = np.where(i[None, :] <= i[:, None], 0.0, -1e30).astype(np.float32)
    c[:, lay["sgt"][0]:lay["sgt"][0] + 128] = (i[:, None] > i[None, :]).astype(np.float32)
    c[:, lay["ones"][0]:lay["ones"][0] + 128] = 1.0
    c[:, lay["pow2"][0]:lay["pow2"][0] + NIT + 2] = (0.5 ** np.arange(1, NIT + 3))[None, :]
    nch = S // 128

    def tab(rot):
        inv = (500000.0 ** (-np.arange(0, rot, 2, dtype=np.float32) / rot)).astype(np.float32)
        ang = np.arange(S, dtype=np.float32)[:, None] * inv[None, :]
        return np.cos(ang).astype(np.float32), np.sin(ang).astype(np.float32)

    ca, sa = tab(32)
    ci, si = tab(16)
    tb = np.concatenate([ca, sa, ci, si], axis=1).reshape(nch, 128, 48).transpose(1, 0, 2)
    return c, np.ascontiguousarray(tb.reshape(128, nch * 48))


def host_layer_consts(inp, l):
    lay, n = _lc_layout()
    c = np.zeros((128, n), np.float32)

    def put(name, arr):
        o, m = lay[name]
        c[:, o:o + m] = arr.reshape(128, m)

    put("nmix", np.asarray(inp["norm_mix_w"][l]).reshape(8, 128).T)
    put("nffn", np.asarray(inp["norm_ffn_w"][l]).reshape(8, 128).T)
    put("snw", np.asarray(inp["ssd_norm_w"][l]).reshape(16, 128).T)
    put("xcw", np.asarray(inp["ssd_conv_w"][l]).reshape(4, 24, 128).transpose(2, 1, 0))
    put("xcb", np.asarray(inp["ssd_conv_b"][l]).reshape(24, 128).T)
    put("fcw", np.asarray(inp["ffn_conv_w"][l]).reshape(3, 44, 128).transpose(2, 1, 0))
    put("fcb", np.asarray(inp["ffn_conv_b"][l]).reshape(44, 128).T)
    put("dtb", np.broadcast_to(np.asarray(inp["ssd_dt_bias"][l])[None, :], (128, 32)))
    put("alog", np.broadcast_to(np.asarray(inp["ssd_a_log"][l])[None, :], (128, 32)))
    put("dsk", np.broadcast_to(np.asarray(inp["ssd_d"][l])[None, :], (128, 32)))
    return c


def build(S, layers=(0, 1), final=True, dbg=(), stop=99):
    nc = bass.Bass("TRN2", target_bir_lowering=False)
    NCH = S // 128
    NST = S // T
    TOPK = min(256, S // 4)
    L = 2
    dt_in = lambda name, shape: nc.dram_tensor(name, list(shape), F32, kind="ExternalInput").ap()
    x_d = dt_in("x", (S, D))
    w_in_d = dt_in("w_in", (L, D, IN_COLS))
    w_pa_d = dt_in("w_proj_attn", (L, D, D))
    w_pb_d = dt_in("w_proj_ssd", (L, SSD_INNER, D))
    w_out_d = dt_in("w_out", (L, D, D))
    w_up_d = dt_in("ffn_w_up", (L, D, 2 * FFN))
    w_dn_d = dt_in("ffn_w_down", (L, FFN, D))
    clay, ncst = _cst_layout()
    llay, nlc = _lc_layout()
    cst_d = dt_in("cst", (128, ncst))
    tab_d = dt_in("tab", (128, NCH * 48))
    lc_d = dt_in("lc", (L, 128, nlc))
    nfin_d = dt_in("nfin", (128, D))
    out_d = nc.dram_tensor("out", [S, D], F32, kind="ExternalOutput").ap()
    xmid_d = nc.dram_tensor("xmid", [S, D], F32, kind="Internal").ap()
    xl1_d = nc.dram_tensor("xl1", [S, D], F32, kind="Internal").ap()
    dbg_d = {}

    P = Prog(nc)
    st = contextlib.ExitStack()
    sb = lambda name, shape, dt: st.enter_context(nc.sbuf_tensor("sb_" + name, list(shape), dt))

    kT = sb("kT", (128, NKV, S), BF16)
    Vc = sb("Vc", (128, NCH, NKV, 129), BF16)
    kiT2 = sb("kiT2", (128, S), BF16)
    hst = sb("hst", (128, SSD_INNER), F32)
    wst = [sb("wst%d" % i, (128, 4, 512), F32) for i in range(2)]
    wbf = [sb("wbf%d" % i, (128, 8, 512), BF16) for i in range(2)]
    hT = sb("hT", (128, 8, T), BF16)
    aT = sb("aT", (128, 8, T), BF16)
    bT = sb("bT", (128, 16, T), BF16)
    cst = sb("cst", (128, ncst), F32)
    lcs = sb("lcs", (128, nlc), F32)
    tabs = sb("tabs", (128, 4, 48), F32)
    identb = sb("identb", (128, 128), BF16)
    trib = sb("trib", (128, 128), F32)
    xtail = sb("xtail", (128, 24, 3), F32)
    ftail = sb("ftail", (128, 44, 2), F32)
    negA = sb("negA", (128, 32), F32)
    epsT = sb("epsT", (128, 1), F32)
    negb = sb("negb", (128, 1), F32)
    AR_BYTES = 69 * 1024
    arena = sb("arena", (128, AR_BYTES // 4), F32)
    psum = st.enter_context(nc.psum_tensor("psum", [128, 8, 512], F32))

    def cv(name, j0=0, j1=None):
        o, n = clay[name]
        j1 = n if j1 is None else j1
        return cst[:, o + j0:o + j1]

    def lv(name, j0=0, j1=None):
        o, n = llay[name]
        j1 = n if j1 is None else j1
        return lcs[:, o + j0:o + j1]

    class View:
        pass

    def aview(off, shape, dt):
        n = int(np.prod(shape[1:]))
        esz = 4 if dt == F32 else 2
        assert off % 4 == 0 and off + n * esz <= AR_BYTES, (off, shape)
        nf = (n * esz + 3) // 4
        ap = arena[:, off // 4: off // 4 + nf]
        if dt != F32:
            ap = ap.bitcast(dt)
        if len(shape) == 3:
            ap = ap.rearrange("p (a b) -> p a b", a=shape[1])
        elif len(shape) == 4:
            ap = ap.rearrange("p (a b c) -> p a b c", a=shape[1], b=shape[2])
        return ap

    def psb(b):
        return psum[:, b, :].bitcast(BF16)

    def bc(ap, shape):
        return ap.to_broadcast(list(shape))

    wcnt = [0]
    hcnt = [0]

    def load_slab(wd, r0, nk, c0, ncols):
        slot = wcnt[0] % 2
        wcnt[0] += 1
        for h0 in range(0, nk, 4):
            hn_ = min(4, nk - h0)
            hs = hcnt[0] % 2
            hcnt[0] += 1
            src = wd[r0 + h0 * 128: r0 + (h0 + hn_) * 128, c0:c0 + ncols].rearrange("(k p) n -> p k n", p=128)
            P.dma(wst[hs][:, 0:hn_, 0:ncols], src, w=["wst%d" % hs])
            if hcnt[0] % 3 == 0:
                P.op("pool", lambda e, hs=hs, h0=h0, hn_=hn_, slot=slot: e.tensor_copy(
                    out=wbf[slot][:, h0:h0 + hn_, 0:ncols], in_=wst[hs][:, 0:hn_, 0:ncols]),
                    r=["wst%d" % hs], w=["wbf%d" % slot])
            else:
                P.op("act", lambda e, hs=hs, h0=h0, hn_=hn_, slot=slot: e.activation(
                    out=wbf[slot][:, h0:h0 + hn_, 0:ncols], in_=wst[hs][:, 0:hn_, 0:ncols], func=AF.Copy),
                    r=["wst%d" % hs], w=["wbf%d" % slot])
        return slot

    pending = []

    def sched(loads, fn, extra=0):
        pending.append((loads, fn, extra))

    def flush():
        tasks = pending[:]
        del pending[:]
        loaded = {}

        def do_load(i):
            if i not in loaded:
                loaded[i] = [load_slab(*a) for a in tasks[i][0]]

        for i, (loads, fn, extra) in enumerate(tasks):
            do_load(i)
            if i + 1 < len(tasks) and len(loads) + extra + len(tasks[i + 1][0]) <= 2:
                do_load(i + 1)
            fn(loaded[i])

    bankctr = [0]

    def mm(out, lhsT, rhs, start, stop, r, w):
        P.op("pe", lambda e: e.matmul(out, lhsT, rhs, start=start, stop=stop), r=r, w=w)

    def transpose(out, in_, r, w):
        P.op("pe", lambda e: e.transpose(out, in_, identb[:, :]), r=list(r) + ["identb"], w=w)

    def dbg_dump(name, ap, shape, keys):
        if name not in dbg:
            return
        if name not in dbg_d:
            dbg_d[name] = nc.dram_tensor("dbg_" + name, list(shape), ap.dtype, kind="ExternalOutput").ap()
        return dbg_d[name]

    P.dma(cst[:, :], cst_d[:, :], w=["cst"])
    P.op("dve", lambda e: e.tensor_copy(out=identb[:, :], in_=cv("ident")), r=["cst"], w=["identb"])
    P.op("dve", lambda e: e.memset(epsT[:, :], EPS), w=["epsT"])
    P.op("dve", lambda e: e.memset(negb[:, :], -30000.0), w=["negb"])
    P.op("pool", lambda e: e.memset(Vc[:, :, :, 128:129], 1.0), w=["Vc"])

    mix_scale = float(HD) ** -0.5

    def rmsnorm_to_hT(src_d, st_i, nw_name, xin_views, hn_views, small, key_prefix):
        for c in range(4):
            gc = st_i * 4 + c
            xin = xin_views[c % 2]
            xk = "xin%d" % (c % 2)
            hn_view = hn_views[c % 2]
            hk_ = "hn%d" % (c % 2)
            P.dma(xin, src_d[gc * 128:(gc + 1) * 128, :], r=[(key_prefix, gc)], w=[xk])
            ss = small[:, c:c + 1]
            rt = small[:, 4 + c:5 + c]
            rstd = small[:, 8 + c:9 + c]
            P.op("act", lambda e, xin=xin, ss=ss, hn_view=hn_view: e.activation(out=hn_view, in_=xin, func=AF.Square, accum_out=ss),
                 r=[xk], w=[hk_, "nsm%d" % c])
            P.op("act", lambda e, ss=ss, rt=rt: e.activation(out=rt, in_=ss, func=AF.Ln, bias=epsT[:, :], scale=1.0 / D),
                 r=["nsm%d" % c, "epsT"], w=["nsm%d" % c])
            P.op("act", lambda e, rt=rt, rstd=rstd: e.activation(out=rstd, in_=rt, func=AF.Exp, scale=-0.5),
                 r=["nsm%d" % c], w=["nsm%d" % c])
            P.op("dve", lambda e, xin=xin, rstd=rstd, hn_view=hn_view: e.tensor_scalar(out=hn_view, in0=xin, scalar1=rstd, scalar2=None,
                                                                                        op0=OP.mult), r=[xk, "nsm%d" % c], w=[hk_])
            b = 6 + (c % 2)
            for k in range(8):
                transpose(psb(b)[:, k * 128:(k + 1) * 128], hn_view[:, k * 128:(k + 1) * 128], r=[hk_], w=["ps%d" % b])
            P.op("dve", lambda e, b=b, c=c: e.tensor_tensor(
                out=hT[:, :, c * 128:(c + 1) * 128],
                in0=psb(b).rearrange("p (k t) -> p k t", k=8),
                in1=bc(lv(nw_name).unsqueeze(2), (128, 8, 128)), op=OP.mult),
                r=["ps%d" % b, "lcs"], w=["hT"])

    def proj_tok(lhsT_fn, nk_list, slot_list, ncols, evac, banks, lkeys):
        for c in range(4):
            b = banks[c % len(banks)]
            kk = 0
            tot = sum(nk_list)
            for si, slot in enumerate(slot_list):
                for k in range(nk_list[si]):
                    mm(psum[:, b, 0:ncols], lhsT_fn(kk, c), wbf[slot][:, k, 0:ncols], start=(kk == 0), stop=(kk == tot - 1),
                       r=lkeys + ["wbf%d" % slot], w=["ps%d" % b])
                    kk += 1
            evac(c, b)

    for l in layers:
        src_d = x_d if l == layers[0] else xl1_d
        last = (l == layers[-1])
        P.barrier()
        P.dma(lcs[:, :], lc_d[l], w=["lcs"])
        P.op("act", lambda e: e.activation(out=negA[:, :], in_=lv("alog"), func=AF.Exp), r=["lcs"], w=["negA"])
        P.op("dve", lambda e: e.tensor_scalar(out=negA[:, :], in0=negA[:, :], scalar1=-1.0, scalar2=None, op0=OP.mult),
             r=["negA"], w=["negA"])
        P.op("pool", lambda e: e.memset(hst[:, :], 0.0), w=["hst"])
        P.op("pool", lambda e: e.memset(xtail[:, :, :], 0.0), w=["xtail"])
        P.op("pool", lambda e: e.memset(ftail[:, :, :], 0.0), w=["ftail"])

        for st_i in range(NST):
            P.barrier()
            scores = aview(0, (128, S), F32)
            qT = aview(16384, (128, 8, T), BF16)
            qiT = aview(24576, (128, 4, T), BF16)
            maskT = aview(28672, (128, NCH if NCH <= 32 else 32, 128), BF16)
            junkb = aview(28672, (128, 4096), BF16)
            xin_v = [aview(36864, (128, D), F32), aview(40960, (128, D), F32)]
            hn_v = [aview(45056, (128, D), BF16), aview(68096, (128, D), BF16)]
            qtok = [aview(47104, (128, 4, 128), BF16), aview(48128, (128, 4, 128), BF16)]
            qitf = aview(49152, (128, 8, 64), F32)
            rbuf = [aview(53248 + 1024 * i, (128, 512), BF16) for i in range(4)]
            maskb = [aview(57344 + 1024 * i, (128, 512), BF16) for i in range(2)]
            Eb = [aview(59392 + 1024 * i, (128, 4, 128), BF16) for i in range(2)]
            dsg = aview(61440, (128, 8, 128), BF16)
            hb_ctr = [0]
            a_tok = aview(63488, (128, 8, 128), BF16)
            small = aview(65536, (128, 256), F32)
            ropet = [aview(66560 + 256 * i, (128, 64), F32) for i in range(2)] + [aview(70144 + 256 * i, (128, 64), F32) for i in range(2)]
            qitok = aview(67072, (128, 512), BF16)
            wabs = small[:, 16:48].rearrange("p (c h) -> p c h", c=4)
            wsgn = small[:, 48:80].rearrange("p (c h) -> p c h", c=4)
            kitok = small[:, 192:256].bitcast(BF16)
            ktok = qtok[1]

            P.dma(tabs[:, :, :], tab_d[:, st_i * 192:(st_i + 1) * 192].rearrange("p (c n) -> p c n", c=4), w=["tabs"])
            rmsnorm_to_hT(src_d, st_i, "nmix", xin_v, hn_v, small, "xsrc%d" % l)

            lhs_h = lambda k, c: hT[:, k, c * 128:(c + 1) * 128]

            def rope(c, src3, dst3, half, cos_o, sin_o, nh, rk, wk):
                cosv = bc(tabs[:, c, cos_o:cos_o + half].unsqueeze(1), (128, nh, half))
                sinv = bc(tabs[:, c, sin_o:sin_o + half].unsqueeze(1), (128, nh, half))
                x1 = src3[:, :, 0:half]
                x2 = src3[:, :, half:2 * half]
                t1 = ropet[0][:, 0:nh * half].rearrange("p (a b) -> p a b", a=nh)
                t2 = ropet[1][:, 0:nh * half].rearrange("p (a b) -> p a b", a=nh)
                rr = list(rk) + ["tabs"]
                P.op("dve", lambda e: e.tensor_tensor(out=t1, in0=x1, in1=cosv, op=OP.mult), r=rr, w=["rt1"])
                P.op("dve", lambda e: e.tensor_tensor(out=t2, in0=x2, in1=sinv, op=OP.mult), r=rr, w=["rt2"])
                P.op("dve", lambda e: e.tensor_tensor(out=dst3[:, :, 0:half], in0=t1, in1=t2, op=OP.subtract),
                     r=["rt1", "rt2"], w=wk)
                t3 = ropet[2][:, 0:nh * half].rearrange("p (a b) -> p a b", a=nh)
                t4 = ropet[3][:, 0:nh * half].rearrange("p (a b) -> p a b", a=nh)
                P.op("dve", lambda e: e.tensor_tensor(out=t3, in0=x2, in1=cosv, op=OP.mult), r=rr, w=["rt3"])
                P.op("dve", lambda e: e.tensor_tensor(out=t4, in0=x1, in1=sinv, op=OP.mult), r=rr, w=["rt4"])
                P.op("dve", lambda e: e.tensor_tensor(out=dst3[:, :, half:2 * half], in0=t3, in1=t4, op=OP.add),
                     r=["rt3", "rt4"], w=wk)

            for qs in range(2 if stop >= 2 else 0):

                def evac_q(c, b, qs=qs):
                    import os
                    SUB = int(os.environ.get("SUB", "9"))
                    qt = qtok[c % 2]
                    qk_ = "qtok%d" % (c % 2)
                    pv = psum[:, b, :].rearrange("p (h d) -> p h d", h=4)
                    if SUB <= 2:
                        return
                    P.op("act", lambda e: e.activation(out=qt[:, :, 32:128], in_=pv[:, :, 32:128], func=AF.Copy),
                         r=["ps%d" % b], w=[qk_])
                    if SUB <= 3:
                        return
                    rope(c, pv, qt, 16, 0, 16, 4, ["ps%d" % b], [qk_])
                    if SUB <= 4:
                        return
                    tb = 4 + (c % 2)
                    for h in range(4):
                        transpose(psb(tb)[:, h * 128:(h + 1) * 128], qt[:, h, :], r=[qk_], w=["ps%d" % tb])
                    P.op("act", lambda e: e.activation(out=qT[:, qs * 4:(qs + 1) * 4, c * 128:(c + 1) * 128],
                                                       in_=psb(tb)[:, 0:512].rearrange("p (h t) -> p h t", h=4), func=AF.Copy),
                         r=["ps%d" % tb], w=["qT"])

                sched([(w_in_d[l], 0, 8, O_Q + qs * 512, 512)],
                      lambda s, evac_q=evac_q: proj_tok(lhs_h, [8], s, 512, evac_q, [0, 1, 2, 3], ["hT"]))


            def evac_kv(c, b):
                gc = st_i * 4 + c
                pv = psum[:, b, 0:256].rearrange("p (h d) -> p h d", h=2)
                kt = qtok[c % 2][:, 0:2, :]
                qk_ = "qtok%d" % (c % 2)
                P.op("act", lambda e: e.activation(out=kt[:, :, 32:128], in_=pv[:, :, 32:128], func=AF.Copy),
                     r=["ps%d" % b], w=[qk_])
                rope(c, pv, kt, 16, 0, 16, 2, ["ps%d" % b], [qk_])
                P.op("act", lambda e: e.activation(out=Vc[:, gc, :, 0:128],
                                                   in_=psum[:, b, 256:512].rearrange("p (h d) -> p h d", h=2), func=AF.Copy),
                     r=["ps%d" % b], w=[("Vc", gc)])
                tb = 4 + (c % 2)
                for h in range(2):
                    transpose(psb(tb)[:, h * 128:(h + 1) * 128], kt[:, h, :], r=[qk_], w=["ps%d" % tb])
                P.op("act", lambda e: e.activation(out=kT[:, :, gc * 128:(gc + 1) * 128],
                                                   in_=psb(tb)[:, 0:256].rearrange("p (h t) -> p h t", h=2), func=AF.Copy),
                     r=["ps%d" % tb], w=[("kT", gc)])

            if stop >= 3:
                sched([(w_in_d[l], 0, 8, O_K, 512)], lambda s: proj_tok(lhs_h, [8], s, 512, evac_kv, [0, 1, 2, 3], ["hT"]))


            def evac_ki(c, b):
                gc = st_i * 4 + c
                pv = psum[:, b, 0:64].rearrange("p (h d) -> p h d", h=1)
                k3 = kitok[:, 0:64].rearrange("p (h d) -> p h d", h=1)
                P.op("act", lambda e: e.activation(out=k3[:, :, 16:64], in_=pv[:, :, 16:64], func=AF.Copy),
                     r=["ps%d" % b], w=["kitok"])
                rope(c, pv, k3, 8, 32, 40, 1, ["ps%d" % b], ["kitok"])
                P.op("dve", lambda e: e.tensor_copy(out=kitok[:, 64:128], in_=kitok[:, 0:64]), r=["kitok"], w=["kitok"])
                P.op("act", lambda e: e.activation(out=wabs[:, c, :], in_=psum[:, b, 64:72], func=AF.Abs),
                     r=["ps%d" % b], w=["wabs%d" % c])
                P.op("act", lambda e: e.activation(out=wsgn[:, c, :], in_=psum[:, b, 64:72], func=AF.Sign),
                     r=["ps%d" % b], w=["wsgn%d" % c])
                tb = 4 + (c % 2)
                transpose(psb(tb)[:, 0:128], kitok[:, :], r=["kitok"], w=["ps%d" % tb])
                P.op("act", lambda e: e.activation(out=kiT2[:, gc * 128:(gc + 1) * 128], in_=psb(tb)[:, 0:128], func=AF.Copy),
                     r=["ps%d" % tb], w=[("kiT", gc)])

            if stop >= 4:
                sched([(w_in_d[l], 0, 8, O_KI, 72)], lambda s: proj_tok(lhs_h, [8], s, 72, evac_ki, [0, 1, 2, 3], ["hT"]))


            def evac_qi(c, b):
                pv = psum[:, b, :].rearrange("p (h d) -> p h d", h=8)
                P.op("act", lambda e: e.activation(out=qitf[:, :, 16:64], in_=pv[:, :, 16:64], func=AF.Copy),
                     r=["ps%d" % b], w=["qitf"])
                rope(c, pv, qitf, 8, 32, 40, 8, ["ps%d" % b], ["qitf"])
                P.op("dve", lambda e: e.tensor_tensor(out=qitok.rearrange("p (h d) -> p h d", h=8), in0=qitf,
                                                      in1=bc(wabs[:, c, :].unsqueeze(2), (128, 8, 64)), op=OP.mult),
                     r=["qitf", "wabs%d" % c], w=["qitok"])
                tb = 4 + (c % 2)
                for j in range(4):
                    transpose(psb(tb)[:, j * 128:(j + 1) * 128], qitok[:, j * 128:(j + 1) * 128], r=["qitok"], w=["ps%d" % tb])
                P.op("act", lambda e: e.activation(out=qiT[:, :, c * 128:(c + 1) * 128],
                                                   in_=psb(tb)[:, 0:512].rearrange("p (h t) -> p h t", h=4), func=AF.Copy),
                     r=["ps%d" % tb], w=["qiT"])

            if stop >= 5:
                sched([(w_in_d[l], 0, 8, O_QI, 512)], lambda s: proj_tok(lhs_h, [8], s, 512, evac_qi, [0, 1, 2, 3], ["hT"]))
            flush()

            P.barrier()
            scores_v = [aview(0, (128, S), F32), aview(36864, (128, S), F32)]
            maskT_v = [aview(28672, (128, 32, 128), BF16),
                       wst[0][:, :, :].rearrange("p a b -> p (a b)").bitcast(BF16).rearrange("p (j t) -> p j t", j=32)]
            mkey = ["maskT0", "wst0"]

            def geom(c):
                gc = st_i * 4 + c
                nk = gc + 1
                n = nk * 128
                return gc, nk, n, slice(c * 128, (c + 1) * 128), (n + 511) // 512

            def st1(c):
                gc, nk, n, tq, ngrp = geom(c)
                scores = scores_v[c % 2]
                kkeys = [("kiT", j) for j in range(nk)]
                for h in range(8):
                    P.op("pool", lambda e, h=h: e.tensor_scalar(out=dsg[:, h, :], in0=identb[:, :], scalar1=wsgn[:, c, h:h + 1],
                                                                scalar2=None, op0=OP.mult),
                         r=["identb", "wsgn%d" % c], w=["dsg"])
                for kg in range(ngrp):
                    w_ = min(512, n - kg * 512)
                    sb_ = 2
                    for h in range(8):
                        b = hb_ctr[0] % 2
                        hb_ctr[0] += 1
                        pr = (h % 2) * 64
                        mm(psum[:, b, 0:w_], qiT[pr:pr + 64, h // 2, tq], kiT2[pr:pr + 64, kg * 512:kg * 512 + w_], True, True,
                           r=["qiT"] + kkeys, w=["ps%d" % b])
                        rb = rbuf[h % 4]
                        P.op("act", lambda e, b=b, rb=rb, w_=w_: e.activation(out=rb[:, 0:w_], in_=psum[:, b, 0:w_], func=AF.Relu),
                             r=["ps%d" % b], w=["rbuf%d" % (h % 4)])
                        mm(psum[:, sb_, 0:w_], dsg[:, h, :], rb[:, 0:w_], (h == 0), (h == 7),
                           r=["dsg", "rbuf%d" % (h % 4)], w=["ps%d" % sb_])
                    sc = scores[:, kg * 512:kg * 512 + w_]
                    P.op("act", lambda e, sb_=sb_, sc=sc, w_=w_: e.activation(out=sc, in_=psum[:, sb_, 0:w_], func=AF.Copy),
                         r=["ps%d" % sb_], w=[("sc", c % 2, kg)])

            def st2(c):
                gc, nk, n, tq, ngrp = geom(c)
                scores = scores_v[c % 2]
                maskT = maskT_v[c % 2]
                junkb = maskT.rearrange("p j t -> p (j t)")
                mk = mkey[c % 2]
                sckeys = [("sc", c % 2, kg) for kg in range(ngrp)]
                lastk = ("sc", c % 2, ngrp - 1)
                thr = small[:, 100 + (c % 2):101 + (c % 2)]
                tk = "thr%d" % (c % 2)
                if (gc * 128 + 128) <= TOPK:
                    P.op("dve", lambda e: e.tensor_tensor(out=scores[:, n - 128:n], in0=scores[:, n - 128:n], in1=cv("caus"),
                                                          op=OP.add), r=[lastk, "cst"], w=[lastk])
                    P.op("dve", lambda e: e.memset(thr, -1e29), w=[tk])
                else:
                    mx = small[:, 102:103]
                    mn = small[:, 103:104]
                    w0 = small[:, 104:105]
                    mid = small[:, 105:106]
                    cnt = small[:, 106:107]
                    dd = small[:, 107:108]
                    Wk = small[:, 110:110 + NIT + 2]
                    P.op("dve", lambda e: e.tensor_reduce(out=mx, in_=scores[:, 0:n], axis=AX.X, op=OP.max), r=sckeys, w=["bs_mx"])
                    P.op("dve", lambda e: e.tensor_reduce(out=mn, in_=scores[:, 0:n], axis=AX.X, op=OP.min), r=sckeys, w=["bs_mn"])
                    P.op("dve", lambda e: e.tensor_tensor(out=scores[:, n - 128:n], in0=scores[:, n - 128:n], in1=cv("caus"),
                                                          op=OP.add), r=[lastk, "cst"], w=[lastk])
                    P.op("dve", lambda e: e.tensor_tensor(out=w0, in0=mx, in1=mn, op=OP.subtract), r=["bs_mx", "bs_mn"], w=["bs_w0"])
                    P.op("dve", lambda e: e.tensor_scalar(out=Wk, in0=cv("pow2"), scalar1=w0, scalar2=None, op0=OP.mult),
                         r=["bs_w0", "cst"], w=["bs_wk"])
                    P.op("dve", lambda e: e.tensor_tensor(out=mid, in0=mn, in1=Wk[:, 0:1], op=OP.add), r=["bs_mn", "bs_wk"], w=["bs_mid"])
                    for it in range(NIT):
                        P.op("dve", lambda e: e.tensor_scalar(out=junkb[:, 0:n], in0=scores[:, 0:n], scalar1=mid, scalar2=None,
                                                              op0=OP.is_ge, op1=OP.add, accum_out=cnt),
                             r=sckeys + ["bs_mid"], w=[mk, "bs_cnt"])
                        lastit = (it == NIT - 1)
                        P.op("dve", lambda e, lastit=lastit: e.tensor_scalar(out=dd, in0=cnt, scalar1=float(TOPK),
                                                                              scalar2=(1.0 if lastit else 0.5),
                                                                              op0=OP.is_ge, op1=OP.subtract),
                             r=["bs_cnt"], w=["bs_dd"])
                        dst = thr if lastit else mid
                        P.op("dve", lambda e, it=it, dst=dst: e.scalar_tensor_tensor(
                            out=dst, in0=dd, scalar=Wk[:, it:it + 1], in1=mid, op0=OP.mult, op1=OP.add),
                            r=["bs_dd", "bs_wk", "bs_mid"], w=[tk if lastit else "bs_mid"])
                for kg in range(ngrp):
                    w_ = min(512, n - kg * 512)
                    mb = maskb[kg % 2]
                    P.op("dve", lambda e, mb=mb, kg=kg, w_=w_: e.tensor_scalar(
                        out=mb[:, 0:w_], in0=scores[:, kg * 512:kg * 512 + w_], scalar1=thr, scalar2=None, op0=OP.is_ge),
                        r=[("sc", c % 2, kg), tk], w=["maskb%d" % (kg % 2)])
                    tb = 3
                    nj = w_ // 128
                    for jj in range(nj):
                        transpose(psb(tb)[:, jj * 128:(jj + 1) * 128], mb[:, jj * 128:(jj + 1) * 128],
                                  r=["maskb%d" % (kg % 2)], w=["ps%d" % tb])
                    P.op("act", lambda e, tb=tb, kg=kg, nj=nj: e.activation(
                        out=maskT[:, kg * 4:kg * 4 + nj, :], in_=psb(tb)[:, 0:nj * 128].rearrange("p (j t) -> p j t", j=nj),
                        func=AF.Identity, scale=30000.0, bias=negb[:, :]), r=["ps%d" % tb, "negb"], w=[mk])
                if "sc" in dbg and gc == NCH - 1:
                    dd_ = dbg_dump("sc", scores, (128, S), None)
                    P.dma(dd_, scores, r=sckeys)
                    dd_ = dbg_dump("thr", thr, (128, 1), None)
                    P.dma(dd_, thr, r=[tk])

            def st3a(c):
                gc, nk, n, tq, ngrp = geom(c)
                maskT = maskT_v[c % 2]
                mk = mkey[c % 2]
                for kvh in range(2):
                    if kvh == 1:
                        st3b_half(c, 0)
                    for j in range(nk):
                        lb = 3 if j % 2 == 0 else 2
                        mm(psum[:, lb, :], kT[:, kvh, j * 128:(j + 1) * 128], qT[:, kvh * 4:(kvh + 1) * 4, tq], True, False,
                           r=[("kT", j), "qT"], w=["ps%d" % lb])
                        mm(psum[:, lb, :], identb[:, :], bc(maskT[:, j, :].unsqueeze(1), (128, 4, 128)), False, True,
                           r=["identb", mk], w=["ps%d" % lb])
                        Ev = Eb[j % 2]
                        P.op("act", lambda e, lb=lb, Ev=Ev: e.activation(
                            out=Ev, in_=psum[:, lb, :].rearrange("p (g t) -> p g t", g=4), func=AF.Exp, scale=mix_scale),
                            r=["ps%d" % lb], w=["E%d" % (j % 2)])
                        for g in range(4):
                            mm(psum[:, 4 + g, 0:129], Ev[:, g, :], Vc[:, j, kvh, :], (j == 0), (j == nk - 1),
                               r=["E%d" % (j % 2), ("Vc", j), "Vc"], w=["ps%d" % (4 + g)])

            def st3b_half(c, kvh):
                for g in range(4):
                    h = kvh * 4 + g
                    rden = small[:, 140 + h:141 + h]
                    P.op("dve", lambda e, g=g, rden=rden: e.reciprocal(out=rden, in_=psum[:, 4 + g, 128:129]),
                         r=["ps%d" % (4 + g)], w=["rden%d" % h])
                    P.op("act", lambda e, g=g, h=h, rden=rden: e.activation(
                        out=a_tok[:, h, :], in_=psum[:, 4 + g, 0:128], func=AF.Copy, scale=rden),
                        r=["ps%d" % (4 + g), "rden%d" % h], w=["a_tok"])

            def st3b(c):
                st3b_half(c, 1)
                tb = 3
                for h in range(8):
                    transpose(psb(tb)[:, h * 128:(h + 1) * 128], a_tok[:, h, :], r=["a_tok"], w=["ps%d" % tb])
                P.op("act", lambda e, tb=tb, c=c: e.activation(
                    out=aT[:, :, c * 128:(c + 1) * 128], in_=psb(tb).rearrange("p (h t) -> p h t", h=8), func=AF.Copy),
                    r=["ps%d" % tb], w=["aT"])

            if stop >= 6:
                st1(0)
                st1(1)
                st2(0)
                st3a(0)
                st1(2)
                st2(1)
                st3b(0)
                st3a(1)
                st1(3)
                st2(2)
                st3b(1)
                st3a(2)
                st2(3)
                st3b(2)
                st3a(3)
                st3b(3)

            if "aT" in dbg:
                dd_ = dbg_dump("aT", aT[:, :, :], (NST, 128, 8, T), None)
                P.dma(dd_[st_i], aT[:, :, :], r=["aT"])
            if "qT" in dbg:
                dd_ = dbg_dump("qT", qT, (NST, 128, 8, T), None)
                P.dma(dd_[st_i], qT, r=["qT"])
            if "hT" in dbg:
                dd_ = dbg_dump("hT", hT[:, :, :], (NST, 128, 8, T), None)
                P.dma(dd_[st_i], hT[:, :, :], r=["hT"])

            if stop >= 7:
                P.barrier()
                BT = aview(0, (128, 4, T), BF16)
                CT = aview(4096, (128, 4, T), BF16)
                Btok = aview(8192, (128, 4, 512), BF16)
                xacc = [aview(12288 + 2048 * i, (128, 512), F32) for i in range(2)]
                xact = [aview(16384 + 1024 * i, (128, 512), BF16) for i in range(2)]
                dtt = aview(18432, (128, 4, 32), F32)
                adt = aview(18944, (128, 4, 32), F32)

                def conv_fm(cc, b, wname, bname, ntap, tail, dst_silu, tailkey):
                    u = psum[:, b, :]
                    nx_ = len(xacc)
                    acc = xacc[cc % nx_]
                    ak = "xacc%d" % (cc % nx_)
                    wv = lambda j: lv(wname, cc * ntap + j, cc * ntap + j + 1)
                    P.op("act", lambda e: e.activation(out=acc, in_=u, func=AF.Identity, bias=lv(bname, cc, cc + 1),
                                                       scale=wv(ntap - 1)), r=["ps%d" % b, "lcs"], w=[ak])
                    for j in range(ntap - 1):
                        sh = ntap - 1 - j
                        P.op("dve", lambda e, j=j, sh=sh: e.scalar_tensor_tensor(
                            out=acc[:, sh:512], in0=u[:, 0:512 - sh], scalar=wv(j), in1=acc[:, sh:512], op0=OP.mult, op1=OP.add),
                            r=["ps%d" % b, "lcs", ak], w=[ak])
                        P.op("dve", lambda e, j=j, sh=sh: e.scalar_tensor_tensor(
                            out=acc[:, 0:sh], in0=tail[:, cc, ntap - 1 - sh:ntap - 1], scalar=wv(j), in1=acc[:, 0:sh],
                            op0=OP.mult, op1=OP.add), r=[tailkey, "lcs", ak], w=[ak])
                    P.op("dve", lambda e: e.tensor_copy(out=tail[:, cc, :], in_=u[:, 512 - (ntap - 1):512]),
                         r=["ps%d" % b], w=[tailkey])
                    return acc, ak

                def proj_fm(slot, ncc, cc0, handler):
                    for j in range(ncc):
                        b = j % 4
                        for k in range(8):
                            mm(psum[:, b, :], wbf[slot][:, k, j * 128:(j + 1) * 128], hT[:, k, :], (k == 0), (k == 7),
                               r=["hT", "wbf%d" % slot], w=["ps%d" % b])
                        handler(cc0 + j, b)

                def h_bc(cc, b):
                    acc, ak = conv_fm(cc, b, "xcw", "xcb", 4, xtail, None, "xtail")
                    g = (cc - 16) % 4
                    if cc < 20:
                        P.op("act", lambda e: e.activation(out=BT[:, g, :], in_=acc, func=AF.Silu), r=[ak], w=["BT"])
                        tb = 4 + (cc % 2)
                        for tc_ in range(4):
                            transpose(psb(tb)[:, tc_ * 128:(tc_ + 1) * 128], BT[:, g, tc_ * 128:(tc_ + 1) * 128], r=["BT"], w=["ps%d" % tb])
                        P.op("act", lambda e: e.activation(out=Btok[:, :, g * 128:(g + 1) * 128],
                                                           in_=psb(tb)[:, 0:512].rearrange("p (c n) -> p c n", c=4), func=AF.Copy),
                             r=["ps%d" % tb], w=["Btok"])
                    else:
                        P.op("act", lambda e: e.activation(out=CT[:, g, :], in_=acc, func=AF.Silu), r=[ak], w=["CT"])

                for half in range(2):
                    sched([(w_in_d[l], 0, 8, O_XBC + 2048 + half * 512, 512)],
                          lambda s, half=half: proj_fm(s[0], 4, 16 + half * 4, h_bc))

                def evac_dt(c, b):
                    P.op("dve", lambda e: e.tensor_tensor(out=dtt[:, c, :], in0=psum[:, b, 0:32], in1=lv("dtb"), op=OP.add),
                         r=["ps%d" % b, "lcs"], w=["dtt"])
                    P.op("act", lambda e: e.activation(out=dtt[:, c, :], in_=dtt[:, c, :], func=AF.Exp), r=["dtt"], w=["dtt"])
                    P.op("act", lambda e: e.activation(out=dtt[:, c, :], in_=dtt[:, c, :], func=AF.Ln, bias=1.0), r=["dtt"], w=["dtt"])
                    P.op("dve", lambda e: e.tensor_tensor(out=adt[:, c, :], in0=dtt[:, c, :], in1=negA[:, :], op=OP.mult),
                         r=["dtt", "negA"], w=["adt"])

                sched([(w_in_d[l], 0, 8, O_DT, 32)], lambda s: proj_tok(lhs_h, [8], s, 32, evac_dt, [0, 1, 2, 3], ["hT"]))

                _o = [19 * 1024]

                def _al(nbytes):
                    o = _o[0]
                    _o[0] += nbytes
                    return o

                TP = []
                for p_ in range(2):
                    d_ = dict(
                        xtok=aview(_al(4096), (128, 4, 512), BF16), zs=aview(_al(4096), (128, 4, 512), BF16),
                        Wp=aview(_al(4096), (128, 8, 128), F32), Lx=aview(_al(2048), (128, 8, 128), BF16),
                        MT=aview(_al(2048), (128, 8, 128), BF16), CBm=aview(_al(256), (128, 128), BF16),
                        Xdt=aview(_al(1024), (128, 8, 64), BF16), Xd2=aview(_al(1024), (128, 8, 64), BF16),
                        yo=aview(_al(2048), (128, 8, 64), F32), yy=aview(_al(2048), (128, 8, 64), F32),
                        hb=aview(_al(1024), (128, 512), BF16), sm2=aview(_al(256), (128, 64), F32),
                        btok_b=aview(_al(1024), (128, 512), BF16))
                    TP.append(d_)

                def make_hx(g):
                    p_ = g % 2
                    xtok = TP[p_]["xtok"]

                    def h_x(cc, b):
                        acc, ak = conv_fm(cc, b, "xcw", "xcb", 4, xtail, None, "xtail")
                        xa = xact[cc % 2]
                        xk = "xact%d" % (cc % 2)
                        P.op("act", lambda e: e.activation(out=xa, in_=acc, func=AF.Silu), r=[ak], w=[xk])
                        tb = 4 + (cc % 2)
                        for tc_ in range(4):
                            transpose(psb(tb)[:, tc_ * 128:(tc_ + 1) * 128], xa[:, tc_ * 128:(tc_ + 1) * 128], r=[xk], w=["ps%d" % tb])
                        j = cc % 4
                        P.op("act", lambda e: e.activation(out=xtok[:, :, j * 128:(j + 1) * 128],
                                                           in_=psb(tb)[:, 0:512].rearrange("p (c n) -> p c n", c=4), func=AF.Copy),
                             r=["ps%d" % tb], w=["xtok%d" % p_])
                    return h_x

                def make_evz(g):
                    p_ = g % 2
                    zs = TP[p_]["zs"]

                    def evac_z(c, b):
                        P.op("act", lambda e: e.activation(out=zs[:, c, :], in_=psum[:, b, :], func=AF.Silu), r=["ps%d" % b], w=["zs%d" % p_])
                    return evac_z

                def ssd_chain(g):
                    p_ = g % 2
                    t_ = TP[p_]
                    xtok, zs, Wp, Lx, MT, CBm, Xdt, Xd2, yo, yy, hb, sm2, btok_b = (
                        t_["xtok"], t_["zs"], t_["Wp"], t_["Lx"], t_["MT"], t_["CBm"], t_["Xdt"], t_["Xd2"], t_["yo"], t_["yy"],
                        t_["hb"], t_["sm2"], t_["btok_b"])
                    K_ = lambda n_: "%s%d" % (n_, p_)
                    b0, b1, b2, b3 = 4 * p_, 4 * p_ + 1, 4 * p_ + 2, 4 * p_ + 3
                    pk = lambda b: "ps%d" % b
                    hk = "hst%d" % g
                    hs3 = hst[:, g * 512:(g + 1) * 512].rearrange("p (e d) -> p e d", e=8)
                    for c in range(4):
                        ts_ = slice(c * 128, (c + 1) * 128)
                        xt3 = xtok[:, c, :].rearrange("p (e d) -> p e d", e=8)
                        dtg = dtt[:, c, g * 8:(g + 1) * 8]
                        adg = adt[:, c, g * 8:(g + 1) * 8]
                        P.op("dve", lambda e, xt3=xt3, dtg=dtg: e.tensor_tensor(out=Xdt, in0=xt3, in1=bc(dtg.unsqueeze(2), (128, 8, 64)), op=OP.mult),
                             r=[K_("xtok"), "dtt"], w=[K_("Xdt")])
                        P.op("dve", lambda e, adg=adg: e.tensor_tensor(out=Wp, in0=bc(cv("tri_le").unsqueeze(1), (128, 8, 128)),
                                                                       in1=bc(adg.unsqueeze(2), (128, 8, 128)), op=OP.mult),
                             r=["adt", "cst"], w=[K_("Wp")])
                        yield
                        for hh, bb in ((0, b0), (1, b1)):
                            mm(psum[:, bb, :], cv("sgt"), Wp[:, hh * 4:(hh + 1) * 4, :], True, True, r=["cst", K_("Wp")], w=[pk(bb)])
                            P.op("act", lambda e, hh=hh, bb=bb: e.activation(out=Lx[:, hh * 4:(hh + 1) * 4, :],
                                                                             in_=psum[:, bb, :].rearrange("p (e l) -> p e l", e=4), func=AF.Exp),
                                 r=[pk(bb)], w=[K_("Lx")])
                            P.op("act", lambda e, hh=hh, bb=bb: e.activation(out=sm2[:, hh * 4:(hh + 1) * 4],
                                                                             in_=psum[:, bb, :].rearrange("p (e l) -> p e l", e=4)[:, :, 127],
                                                                             func=AF.Exp), r=[pk(bb)], w=[K_("decay")])
                        yield
                        mm(psum[:, b2, 0:128], BT[:, g, ts_], CT[:, g, ts_], True, True, r=["BT", "CT"], w=[pk(b2)])
                        mm(psum[:, b2, 128:136], cv("tri_le"), adg, True, True, r=["cst", "adt"], w=[pk(b2)])
                        mm(psum[:, b2, 136:144], cv("ones"), adg, True, True, r=["cst", "adt"], w=[pk(b2)])
                        P.op("dve", lambda e: e.tensor_tensor(out=CBm, in0=psum[:, b2, 0:128], in1=cv("tri_le"), op=OP.mult),
                             r=[pk(b2), "cst"], w=[K_("CBm")])
                        P.op("act", lambda e: e.activation(out=sm2[:, 8:24], in_=psum[:, b2, 128:144], func=AF.Exp), r=[pk(b2)], w=[K_("eacs")])
                        yield
                        P.op("dve", lambda e: e.tensor_tensor(out=MT, in0=Lx, in1=bc(CBm.unsqueeze(1), (128, 8, 128)), op=OP.mult),
                             r=[K_("Lx"), K_("CBm")], w=[K_("MT")])
                        for e_ in range(8):
                            mm(psum[:, b3, e_ * 64:(e_ + 1) * 64], MT[:, e_, :], Xdt[:, e_, :], True, True, r=[K_("MT"), K_("Xdt")], w=[pk(b3)])
                        yield
                        P.op("act", lambda e: e.activation(out=hb, in_=hst[:, g * 512:(g + 1) * 512], func=AF.Copy), r=[hk], w=[K_("hb")])
                        mm(psum[:, b0, :], CT[:, g, ts_], hb, True, True, r=["CT", K_("hb")], w=[pk(b0)])
                        P.op("dve", lambda e: e.tensor_tensor(out=yo, in0=psum[:, b0, :].rearrange("p (e d) -> p e d", e=8),
                                                              in1=bc(sm2[:, 8:16].unsqueeze(2), (128, 8, 64)), op=OP.mult),
                             r=[pk(b0), K_("eacs")], w=[K_("yo")])
                        yield
                        P.op("dve", lambda e: e.tensor_tensor(out=yy, in0=psum[:, b3, :].rearrange("p (e d) -> p e d", e=8), in1=yo, op=OP.add),
                             r=[pk(b3), K_("yo")], w=[K_("yy")])
                        P.op("dve", lambda e, xt3=xt3: e.tensor_tensor(out=yo, in0=xt3, in1=bc(lv("dsk", g * 8, g * 8 + 8).unsqueeze(2), (128, 8, 64)),
                                                                       op=OP.mult), r=[K_("xtok"), "lcs"], w=[K_("yo")])
                        P.op("dve", lambda e: e.tensor_tensor(out=yy, in0=yy, in1=yo, op=OP.add), r=[K_("yy"), K_("yo")], w=[K_("yy")])
                        yield
                        P.op("dve", lambda e: e.tensor_tensor(out=Xd2, in0=Xdt, in1=bc(sm2[:, 0:8].unsqueeze(2), (128, 8, 64)), op=OP.mult),
                             r=[K_("Xdt"), K_("decay")], w=[K_("Xd2")])
                        mm(psum[:, b1, :], Btok[:, c, g * 128:(g + 1) * 128], Xd2.rearrange("p e d -> p (e d)"), True, True,
                           r=["Btok", K_("Xd2")], w=[pk(b1)])
                        P.op("dve", lambda e: e.tensor_tensor(out=hs3, in0=hs3, in1=bc(sm2[:, 16:24].unsqueeze(2), (128, 8, 64)), op=OP.mult),
                             r=[hk, K_("eacs")], w=[hk])
                        P.op("dve", lambda e: e.tensor_tensor(out=hs3, in0=psum[:, b1, :].rearrange("p (e d) -> p e d", e=8), in1=hs3, op=OP.add),
                             r=[pk(b1), hk], w=[hk])
                        yield
                        yf = yy.rearrange("p e d -> p (e d)")
                        P.op("dve", lambda e, yf=yf, c=c: e.tensor_tensor(out=yf, in0=yf, in1=zs[:, c, :], op=OP.mult), r=[K_("yy"), K_("zs")], w=[K_("yy")])
                        P.op("act", lambda e, yf=yf: e.activation(out=btok_b, in_=yf, func=AF.Square, accum_out=sm2[:, 24:25]),
                             r=[K_("yy")], w=[K_("btok_b"), K_("gss")])
                        P.op("act", lambda e: e.activation(out=sm2[:, 25:26], in_=sm2[:, 24:25], func=AF.Ln, bias=epsT[:, :], scale=1.0 / 512),
                             r=[K_("gss"), "epsT"], w=[K_("gss")])
                        P.op("act", lambda e: e.activation(out=sm2[:, 26:27], in_=sm2[:, 25:26], func=AF.Exp, scale=-0.5),
                             r=[K_("gss")], w=[K_("gss")])
                        yield
                        P.op("dve", lambda e, yf=yf: e.tensor_scalar(out=btok_b, in0=yf, scalar1=sm2[:, 26:27], scalar2=None, op0=OP.mult),
                             r=[K_("yy"), K_("gss")], w=[K_("btok_b")])
                        for j in range(4):
                            transpose(psb(b2)[:, j * 128:(j + 1) * 128], btok_b[:, j * 128:(j + 1) * 128], r=[K_("btok_b")], w=[pk(b2)])
                        P.op("dve", lambda e, ts_=ts_: e.tensor_tensor(out=bT[:, g * 4:(g + 1) * 4, ts_],
                                                                       in0=psb(b2)[:, 0:512].rearrange("p (j t) -> p j t", j=4),
                                                                       in1=bc(lv("snw", g * 4, g * 4 + 4).unsqueeze(2), (128, 4, 128)), op=OP.mult),
                             r=[pk(b2), "lcs"], w=["bT"])
                        yield

                def run_pair(g0):
                    gens = [ssd_chain(g0), ssd_chain(g0 + 1)]
                    while gens:
                        for gen in list(gens):
                            try:
                                next(gen)
                            except StopIteration:
                                gens.remove(gen)

                for g0 in (0, 2):
                    for g in (g0, g0 + 1):
                        sched([(w_in_d[l], 0, 8, O_XBC + g * 512, 512)], lambda s, g=g: proj_fm(s[0], 4, g * 4, make_hx(g)))
                        sched([(w_in_d[l], 0, 8, O_Z + g * 512, 512)],
                              lambda s, g=g: proj_tok(lhs_h, [8], s, 512, make_evz(g), [0, 1, 2, 3], ["hT"]))
                    sched([], lambda s, g0=g0: run_pair(g0))
                flush()
                if "bT" in dbg:
                    dd_ = dbg_dump("bT", bT[:, :, :], (NST, 128, 16, T), None)
                    P.dma(dd_[st_i], bT[:, :, :], r=["bT"])

            if stop >= 8:
                P.barrier()
                ga = aview(0, (128, 4, 512), BF16)
                gb = aview(4096, (128, 4, 512), BF16)
                mf = aview(8192, (128, 4, 512), F32)
                tmpf = aview(16384, (128, 512), F32)
                mtok = aview(18432, (128, 512), BF16)
                mT = aview(19456, (128, 8, T), BF16)
                xh = aview(27648, (128, 4, 512), F32)
                for cs in range(2):
                    sched([(w_in_d[l], 0, 8, O_GA + cs * 512, 512)], lambda s: proj_tok(
                        lhs_h, [8], s, 512,
                        lambda c, b: P.op("act", lambda e: e.activation(out=ga[:, c, :], in_=psum[:, b, :], func=AF.Sigmoid),
                                          r=["ps%d" % b], w=["ga"]), [0, 1, 2, 3], ["hT"]))
                    sched([(w_in_d[l], 0, 8, O_GB + cs * 512, 512)], lambda s: proj_tok(
                        lhs_h, [8], s, 512,
                        lambda c, b: P.op("act", lambda e: e.activation(out=gb[:, c, :], in_=psum[:, b, :], func=AF.Sigmoid),
                                          r=["ps%d" % b], w=["gb"]), [0, 1, 2, 3], ["hT"]))
                    sched([(w_pa_d[l], 0, 8, cs * 512, 512)], lambda s: proj_tok(
                        lambda k, c: aT[:, k, c * 128:(c + 1) * 128], [8], s, 512,
                        lambda c, b: P.op("dve", lambda e: e.tensor_tensor(out=mf[:, c, :], in0=psum[:, b, :], in1=ga[:, c, :], op=OP.mult),
                                          r=["ps%d" % b, "ga"], w=["mf"]), [0, 1, 2, 3], ["aT"]))

                    def evac_pb(c, b, cs=cs):
                        P.op("dve", lambda e: e.tensor_tensor(out=tmpf, in0=psum[:, b, :], in1=gb[:, c, :], op=OP.mult),
                             r=["ps%d" % b, "gb"], w=["tmpf"])
                        P.op("dve", lambda e: e.tensor_tensor(out=mtok, in0=tmpf, in1=mf[:, c, :], op=OP.add), r=["tmpf", "mf"], w=["mtok"])
                        tb = 6 + (c % 2)
                        for j in range(4):
                            transpose(psb(tb)[:, j * 128:(j + 1) * 128], mtok[:, j * 128:(j + 1) * 128], r=["mtok"], w=["ps%d" % tb])
                        P.op("act", lambda e: e.activation(out=mT[:, cs * 4:(cs + 1) * 4, c * 128:(c + 1) * 128],
                                                           in_=psb(tb)[:, 0:512].rearrange("p (j t) -> p j t", j=4), func=AF.Copy),
                             r=["ps%d" % tb], w=["mT"])

                    sched([(w_pb_d[l], 0, 8, cs * 512, 512), (w_pb_d[l], 1024, 8, cs * 512, 512)],
                          lambda s, evac_pb=evac_pb: proj_tok(lambda k, c: bT[:, k, c * 128:(c + 1) * 128], [8, 8], s, 512, evac_pb,
                                                              [0, 1, 2, 3], ["bT"]))
                flush()
                for cs in range(2):
                    def out_task(s, cs=cs):
                        P.dma(xh, src_d[st_i * T:(st_i + 1) * T, cs * 512:(cs + 1) * 512].rearrange("(c p) n -> p c n", p=128),
                              r=[("xsrc%d" % l, st_i * 4 + c) for c in range(4)], w=["xh"])
                        proj_tok(lambda k, c: mT[:, k, c * 128:(c + 1) * 128], [8], s, 512,
                                 lambda c, b: P.op("dve", lambda e: e.tensor_tensor(out=xh[:, c, :], in0=psum[:, b, :], in1=xh[:, c, :], op=OP.add),
                                                   r=["ps%d" % b, "xh"], w=["xh"]), [0, 1, 2, 3], ["mT"])
                        P.dma(xmid_d[st_i * T:(st_i + 1) * T, cs * 512:(cs + 1) * 512].rearrange("(c p) n -> p c n", p=128), xh,
                              r=["xh"], w=[("xmid", st_i * 4 + c) for c in range(4)])
                    sched([(w_out_d[l], 0, 8, cs * 512, 512)], out_task)
                flush()

            if stop >= 9:
                P.barrier()
                xin_f = [aview(0, (128, D), F32), aview(4096, (128, D), F32)]
                hn_f = [aview(8192, (128, D), BF16), aview(63744, (128, D), BF16)]
                small_f = aview(10240, (128, 64), F32)
                xacc = [aview(10496 + 2048 * i, (128, 512), F32) for i in range(4)]
                sg = aview(18688, (128, 22, 512), BF16)
                xo = aview(41216, (128, 4, D), F32)
                nfin = aview(57600, (128, D), F32)
                osq = aview(61696, (128, D), BF16)
                rmsnorm_to_hT(xmid_d, st_i, "nffn", xin_f, hn_f, small_f, "xmid")

                def h_up(cc, b):
                    acc, ak = conv_fm(cc, b, "fcw", "fcb", 3, ftail, None, "ftail")
                    if cc < 22:
                        P.op("act", lambda e: e.activation(out=sg[:, cc, :], in_=acc, func=AF.Silu), r=[ak], w=[("sg", cc)])
                    else:
                        P.op("dve", lambda e: e.tensor_tensor(out=sg[:, cc - 22, :], in0=sg[:, cc - 22, :], in1=acc, op=OP.mult),
                             r=[ak, ("sg", cc - 22)], w=[("sg", cc - 22)])

                for sl in range(11):
                    sched([(w_up_d[l], 0, 8, sl * 512, 512)], lambda s, sl=sl: proj_fm(s[0], 4, sl * 4, h_up))
                sgk = [("sg", i) for i in range(22)]

                def down_task(slots, cs):
                    P.dma(xo[:, :, cs * 512:(cs + 1) * 512],
                          xmid_d[st_i * T:(st_i + 1) * T, cs * 512:(cs + 1) * 512].rearrange("(c p) n -> p c n", p=128),
                          r=[("xmid", st_i * 4 + c) for c in range(4)], w=["xo"])
                    for c in range(4):
                        kk = 0
                        for si in range(2):
                            for k in range(8):
                                mm(psum[:, c, :], sg[:, kk, c * 128:(c + 1) * 128], wbf[slots[si]][:, k, :], (kk == 0), False,
                                   r=sgk + ["wbf%d" % slots[si]], w=["ps%d" % c])
                                kk += 1
                    s2 = load_slab(w_dn_d[l], 2048, 6, cs * 512, 512)
                    for c in range(4):
                        for k in range(6):
                            mm(psum[:, c, :], sg[:, 16 + k, c * 128:(c + 1) * 128], wbf[s2][:, k, :], False, (k == 5),
                               r=sgk + ["wbf%d" % s2], w=["ps%d" % c])
                        P.op("dve", lambda e, c=c, cs=cs: e.tensor_tensor(out=xo[:, c, cs * 512:(cs + 1) * 512], in0=psum[:, c, :],
                                                                          in1=xo[:, c, cs * 512:(cs + 1) * 512], op=OP.add),
                             r=["ps%d" % c, "xo"], w=["xo"])

                for cs in range(2):
                    sched([(w_dn_d[l], 0, 8, cs * 512, 512), (w_dn_d[l], 1024, 8, cs * 512, 512)],
                          lambda s, cs=cs: down_task(s, cs), extra=1)
                flush()
                if not last:
                    P.dma(xl1_d[st_i * T:(st_i + 1) * T, :].rearrange("(c p) n -> p c n", p=128), xo, r=["xo"],
                          w=[("xsrc1", st_i * 4 + c) for c in range(4)])
                else:
                    P.dma(nfin, nfin_d[:, :], w=["nfin"])
                    for c in range(4):
                        ss = small_f[:, 32 + c:33 + c]
                        rt = small_f[:, 36 + c:37 + c]
                        rs = small_f[:, 40 + c:41 + c]
                        P.op("act", lambda e, c=c, ss=ss: e.activation(out=osq, in_=xo[:, c, :], func=AF.Square, accum_out=ss),
                             r=["xo"], w=["osq", "fsm%d" % c])
                        P.op("act", lambda e, ss=ss, rt=rt: e.activation(out=rt, in_=ss, func=AF.Ln, bias=epsT[:, :], scale=1.0 / D),
                             r=["fsm%d" % c, "epsT"], w=["fsm%d" % c])
                        P.op("act", lambda e, rt=rt, rs=rs: e.activation(out=rs, in_=rt, func=AF.Exp, scale=-0.5),
                             r=["fsm%d" % c], w=["fsm%d" % c])
                        P.op("dve", lambda e, c=c, rs=rs: e.scalar_tensor_tensor(out=xo[:, c, :], in0=xo[:, c, :], scalar=rs, in1=nfin,
                                                                                  op0=OP.mult, op1=OP.mult),
                             r=["xo", "fsm%d" % c, "nfin"], w=["xo"])
                    P.dma(out_d[st_i * T:(st_i + 1) * T, :].rearrange("(c p) n -> p c n", p=128), xo, r=["xo"])

    if "kT" in dbg:
        dd_ = dbg_dump("kT", kT[:, :, :], (128, NKV, S), None)
        P.dma(dd_, kT[:, :, :], r=[("kT", j) for j in range(NCH)])
    if "kiT" in dbg:
        dd_ = dbg_dump("kiT", kiT2[:, :], (128, S), None)
        P.dma(dd_, kiT2[:, :], r=[("kiT", j) for j in range(NCH)])
    P.emit()
    st.close()
    return nc


_NC_CACHE = {}


def kernel(**inputs):
    inp = {k: np.asarray(v) for k, v in inputs.items()}
    x = inp["x"].astype(np.float32, copy=False)
    B, S, _ = x.shape
    if S not in _NC_CACHE:
        _NC_CACHE[S] = build(S, layers=(0, 1))
    nc = _NC_CACHE[S]
    cst, tab = host_consts(S)
    lc = np.stack([host_layer_consts(inp, l) for l in range(2)])
    nfin = np.ascontiguousarray(np.broadcast_to(inp["norm_final_w"].astype(np.float32)[None, :], (128, D)))
    shared = {"w_in": np.ascontiguousarray(inp["w_in"], dtype=np.float32),
              "w_proj_attn": np.ascontiguousarray(inp["w_proj_attn"], dtype=np.float32),
              "w_proj_ssd": np.ascontiguousarray(inp["w_proj_ssd"], dtype=np.float32),
              "w_out": np.ascontiguousarray(inp["w_out"], dtype=np.float32),
              "ffn_w_up": np.ascontiguousarray(inp["ffn_w_up"], dtype=np.float32),
              "ffn_w_down": np.ascontiguousarray(inp["ffn_w_down"], dtype=np.float32),
              "cst": cst, "tab": tab, "lc": lc, "nfin": nfin}
    in_maps = [dict(shared, x=np.ascontiguousarray(x[b])) for b in range(B)]
    res = run_bass_kernel_spmd(nc, in_maps, core_ids=list(range(B)))
    return np.stack([np.asarray(r["out"], dtype=np.float32) for r in res.results], axis=0)
```

```python
import contextlib
import numpy as np
import concourse.bass as bass
import concourse.mybir as mybir
from concourse.bass_utils import run_bass_kernel_spmd

F32 = mybir.dt.float32
BF16 = mybir.dt.bfloat16
AF = mybir.ActivationFunctionType
OP = mybir.AluOpType
AX = mybir.AxisListType

D = 1024
NH, HD, NKV = 8, 128, 2
IH, IDM = 8, 64
SSD_INNER, SSD_HD, SSD_H, SSD_G, SSD_N = 2048, 64, 32, 4, 128
CONV_DIM = 3072
FFN = 2816
EPS = 1e-6
IN_COLS = 9320
O_Q, O_K, O_V, O_QI, O_KI, O_WI, O_Z, O_XBC, O_DT, O_GA, O_GB = (
    0, 1024, 1280, 1536, 2048, 2112, 2120, 4168, 7240, 7272, 8296)
T = 512
NIT = 22

ENGS = ["pe", "act", "dve", "pool", "sp"]
NDMASEM = 24


class Prog:
    def __init__(self, nc):
        self.nc = nc
        self.ops = {e: [] for e in ENGS}
        self.last_w = {}
        self.readers = {}
        self.ndma = 0
        self.last_real = {}
        self.ps_last = {}

    @staticmethod
    def _isps(k):
        return isinstance(k, str) and k.startswith("ps") and k[2:].isdigit()

    def _deps(self, eng, r, w):
        deps = []
        for k in list(r) + list(w):
            if self._isps(k):
                is_w = k in w
                last = self.ps_last.get(k)
                if last is not None:
                    ref, lw = last
                    same = (ref[0] == "eng" and ref[1] == eng)
                    if (not same) or is_w or lw:
                        deps.append(ref)
        r = [k for k in r if not self._isps(k)]
        w = [k for k in w if not self._isps(k)]
        for k in r:
            lw = self.last_w.get(k)
            if lw is not None:
                deps.append(lw)
        for k in w:
            lw = self.last_w.get(k)
            if lw is not None:
                deps.append(lw)
            deps.extend(self.readers.get(k, ()))
        out = []
        for d in deps:
            if d[0] == "eng" and d[1] == "pe" and eng == "pe":
                continue
            if d not in out:
                out.append(d)
        return out

    def _commit(self, ref, r, w):
        for k in list(r) + list(w):
            if self._isps(k):
                self.ps_last[k] = (ref, k in w)
        r = [k for k in r if not self._isps(k)]
        w = [k for k in w if not self._isps(k)]
        for k in r:
            self.readers.setdefault(k, []).append(ref)
        for k in w:
            self.last_w[k] = ref
            self.readers[k] = []

    def _mark(self, deps):
        for d in deps:
            if d[0] == "eng":
                self.ops[d[1]][d[2]]["sig"] = True

    def op(self, eng, fn, r=(), w=()):
        deps = self._deps(eng, r, w)
        self._mark(deps)
        idx = len(self.ops[eng])
        self.ops[eng].append(dict(fn=fn, deps=deps, sig=False, dma=None))
        self.last_real[eng] = idx
        self._commit(("eng", eng, idx), r, w)

    def dma(self, out, in_, r=(), w=(), q="sp"):
        deps = self._deps(q, r, w)
        self._mark(deps)
        i = self.ndma
        self.ndma += 1
        if i >= NDMASEM:
            deps.append(("dma", i - NDMASEM))
        self.ops[q].append(dict(fn=lambda e: e.dma_start(out=out, in_=in_), deps=deps, sig=False, dma=i))
        self._commit(("dma", i), r, w)

    def barrier(self):
        deps = [("eng", e, i) for e, i in self.last_real.items()]
        deps += [("dma", i) for i in range(max(0, self.ndma - NDMASEM), self.ndma)]
        self._mark(deps)
        for e in ENGS:
            self.ops[e].append(dict(fn=None, deps=[d for d in deps if not (d[0] == "eng" and d[1] == e)],
                                    sig=False, dma=None))
        self.last_w.clear()
        self.readers.clear()
        self.ps_last.clear()

    def simulate(self):
        sigcnt = {}
        for e in ENGS:
            c = 0
            arr = []
            for o in self.ops[e]:
                if o["sig"]:
                    c += 1
                arr.append(c)
            sigcnt[e] = arr
        sem = {e: 0 for e in ENGS}
        dsem = [0] * NDMASEM
        ptr = {e: 0 for e in ENGS}
        progress = True
        while progress:
            progress = False
            for e in ENGS:
                while ptr[e] < len(self.ops[e]):
                    o = self.ops[e][ptr[e]]
                    ok = True
                    for d in o["deps"]:
                        if d[0] == "eng":
                            if sem[d[1]] < sigcnt[d[1]][d[2]]:
                                ok = False
                        else:
                            if dsem[d[1] % NDMASEM] < 16 * (d[1] // NDMASEM + 1):
                                ok = False
                    if not ok:
                        break
                    if o["dma"] is not None:
                        dsem[o["dma"] % NDMASEM] += 16
                    elif o["sig"] and o["fn"] is not None:
                        sem[e] += 1
                    ptr[e] += 1
                    progress = True
        stuck = {e: (ptr[e], len(self.ops[e])) for e in ENGS if ptr[e] < len(self.ops[e])}
        if stuck:
            for e in stuck:
                o = self.ops[e][ptr[e]]
                print("STUCK", e, ptr[e], o["deps"], "sig", o["sig"], "fn", o["fn"] is not None)
            raise RuntimeError("deadlock in semaphore protocol: %r" % stuck)
        print("simulate ok:", {e: len(self.ops[e]) for e in ENGS}, "dmas", self.ndma)

    def emit(self):
        nc = self.nc
        self.simulate()
        with contextlib.ExitStack() as st:
            esem = {e: st.enter_context(nc.semaphore("s_" + e)) for e in ENGS}
            dsem = [st.enter_context(nc.semaphore("d_%d" % i)) for i in range(NDMASEM)]
            block = st.enter_context(nc.Block())
            sigcnt = {}
            for e in ENGS:
                c = 0
                arr = []
                for o in self.ops[e]:
                    if o["sig"]:
                        c += 1
                    arr.append(c)
                sigcnt[e] = arr
            ndma = self.ndma

            def run(e, engobj):
                waited = {}
                for o in self.ops[e]:
                    for d in o["deps"]:
                        if d[0] == "eng":
                            sem = esem[d[1]]
                            val = sigcnt[d[1]][d[2]]
                            key = "e" + d[1]
                        else:
                            sem = dsem[d[1] % NDMASEM]
                            val = 16 * (d[1] // NDMASEM + 1)
                            key = "d%d" % (d[1] % NDMASEM)
                        if waited.get(key, 0) >= val:
                            continue
                        engobj.wait_ge(sem, val)
                        waited[key] = val
                    if o["fn"] is None:
                        continue
                    ins = o["fn"](engobj)
                    if o["dma"] is not None:
                        ins.then_inc(dsem[o["dma"] % NDMASEM], 16)
                    elif o["sig"]:
                        ins.then_inc(esem[e], 1)
                if e == "sp":
                    for s in range(min(NDMASEM, ndma)):
                        n = (ndma - 1 - s) // NDMASEM + 1
                        engobj.wait_ge(dsem[s], 16 * n)

            @block.sync
            def _(eng):
                run("sp", eng)

            @block.scalar
            def _(eng):
                run("act", eng)

            @block.vector
            def _(eng):
                run("dve", eng)

            @block.gpsimd
            def _(eng):
                run("pool", eng)

            @block.tensor
            def _(eng):
                run("pe", eng)


CST_COLS = {}


def _cst_layout():
    off = 0
    lay = {}
    for name, n in [("ident", 128), ("tri_le", 128), ("caus", 128), ("sgt", 128), ("ones", 128),
                    ("pow2", NIT + 2)]:
        lay[name] = (off, n)
        off += n
    return lay, off


def _lc_layout():
    off = 0
    lay = {}
    for name, n in [("nmix", 8), ("nffn", 8), ("snw", 16), ("xcw", 96), ("xcb", 24), ("fcw", 132),
                    ("fcb", 44), ("dtb", 32), ("alog", 32), ("dsk", 32)]:
        lay[name] = (off, n)
        off += n
    return lay, off


def host_consts(S):
    lay, n = _cst_layout()
    c = np.zeros((128, n), np.float32)
    i = np.arange(128)
    c[:, lay["ident"][0]:lay["ident"][0] + 128] = np.eye(128, dtype=np.float32)
    c[:, lay["tri_le"][0]:lay["tri_le"][0] + 128] = (i[:, None] <= i[None, :]).astype(np.float32)
    c[:, lay["caus"][0]:lay["caus"][0] + 128] = np.where(i[None, :] <= i[:, None], 0.0, -1e30).astype(np.float32)
    c[:, lay["sgt"][0]:lay["sgt"][0] + 128] = (i[:, None] > i[None, :]).astype(np.float32)
    c[:, lay["ones"][0]:lay["ones"][0] + 128] = 1.0
    c[:, lay["pow2"][0]:lay["pow2"][0] + NIT + 2] = (0.5 ** np.arange(1, NIT + 3))[None, :]
    nch = S // 128

    def tab(rot):
        inv = (500000.0 ** (-np.arange(0, rot, 2, dtype=np.float32) / rot)).astype(np.float32)
        ang = np.arange(S, dtype=np.float32)[:, None] * inv[None, :]
        return np.cos(ang).astype(np.float32), np.sin(ang).astype(np.float32)

    ca, sa = tab(32)
    ci, si = tab(16)
    tb = np.concatenate([ca, sa, ci, si], axis=1).reshape(nch, 128, 48).transpose(1, 0, 2)
    return c, np.ascontiguousarray(tb.reshape(128, nch * 48))


def host_layer_consts(inp, l):
    lay, n = _lc_layout()
    c = np.zeros((128, n), np.float32)

    def put(name, arr):
        o, m = lay[name]
        c[:, o:o + m] = arr.reshape(128, m)

    put("nmix", np.asarray(inp["norm_mix_w"][l]).reshape(8, 128).T)
    put("nffn", np.asarray(inp["norm_ffn_w"][l]).reshape(8, 128).T)
    put("snw", np.asarray(inp["ssd_norm_w"][l]).reshape(16, 128).T)
    put("xcw", np.asarray(inp["ssd_conv_w"][l]).reshape(4, 24, 128).transpose(2, 1, 0))
    put("xcb", np.asarray(inp["ssd_conv_b"][l]).reshape(24, 128).T)
    put("fcw", np.asarray(inp["ffn_conv_w"][l]).reshape(3, 44, 128).transpose(2, 1, 0))
    put("fcb", np.asarray(inp["ffn_conv_b"][l]).reshape(44, 128).T)
    put("dtb", np.broadcast_to(np.asarray(inp["ssd_dt_bias"][l])[None, :], (128, 32)))
    put("alog", np.broadcast_to(np.asarray(inp["ssd_a_log"][l])[None, :], (128, 32)))
    put("dsk", np.broadcast_to(np.asarray(inp["ssd_d"][l])[None, :], (128, 32)))
    return c


def build(S, layers=(0, 1), final=True, dbg=(), stop=99):
    nc = bass.Bass("TRN2", target_bir_lowering=False)
    NCH = S // 128
    NST = S // T
    TOPK = min(256, S // 4)
    L = 2
    dt_in = lambda name, shape: nc.dram_tensor(name, list(shape), F32, kind="ExternalInput").ap()
    x_d = dt_in("x", (S, D))
    w_in_d = dt_in("w_in", (L, D, IN_COLS))
    w_pa_d = dt_in("w_proj_attn", (L, D, D))
    w_pb_d = dt_in("w_proj_ssd", (L, SSD_INNER, D))
    w_out_d = dt_in("w_out", (L, D, D))
    w_up_d = dt_in("ffn_w_up", (L, D, 2 * FFN))
    w_dn_d = dt_in("ffn_w_down", (L, FFN, D))
    clay, ncst = _cst_layout()
    llay, nlc = _lc_layout()
    cst_d = dt_in("cst", (128, ncst))
    tab_d = dt_in("tab", (128, NCH * 48))
    lc_d = dt_in("lc", (L, 128, nlc))
    nfin_d = dt_in("nfin", (128, D))
    out_d = nc.dram_tensor("out", [S, D], F32, kind="ExternalOutput").ap()
    xmid_d = nc.dram_tensor("xmid", [S, D], F32, kind="Internal").ap()
    xl1_d = nc.dram_tensor("xl1", [S, D], F32, kind="Internal").ap()
    dbg_d = {}

    P = Prog(nc)
    st = contextlib.ExitStack()
    sb = lambda name, shape, dt: st.enter_context(nc.sbuf_tensor("sb_" + name, list(shape), dt))

    kT = sb("kT", (128, NKV, S), BF16)
    Vc = sb("Vc", (128, NCH, NKV, 129), BF16)
    kiT2 = sb("kiT2", (128, S), BF16)
    hst = sb("hst", (128, SSD_INNER), F32)
    wst = [sb("wst%d" % i, (128, 4, 512), F32) for i in range(2)]
    wbf = [sb("wbf%d" % i, (128, 8, 512), BF16) for i in range(2)]
    hT = sb("hT", (128, 8, T), BF16)
    aT = sb("aT", (128, 8, T), BF16)
    bT = sb("bT", (128, 16, T), BF16)
    cst = sb("cst", (128, ncst), F32)
    lcs = sb("lcs", (128, nlc), F32)
    tabs = sb("tabs", (128, 4, 48), F32)
    identb = sb("identb", (128, 128), BF16)
    trib = sb("trib", (128, 128), F32)
    xtail = sb("xtail", (128, 24, 3), F32)
    ftail = sb("ftail", (128, 44, 2), F32)
    negA = sb("negA", (128, 32), F32)
    epsT = sb("epsT", (128, 1), F32)
    negb = sb("negb", (128, 1), F32)
    AR_BYTES = 69 * 1024
    arena = sb("arena", (128, AR_BYTES // 4), F32)
    psum = st.enter_context(nc.psum_tensor("psum", [128, 8, 512], F32))

    def cv(name, j0=0, j1=None):
        o, n = clay[name]
        j1 = n if j1 is None else j1
        return cst[:, o + j0:o + j1]

    def lv(name, j0=0, j1=None):
        o, n = llay[name]
        j1 = n if j1 is None else j1
        return lcs[:, o + j0:o + j1]

    class View:
        pass

    def aview(off, shape, dt):
        n = int(np.prod(shape[1:]))
        esz = 4 if dt == F32 else 2
        assert off % 4 == 0 and off + n * esz <= AR_BYTES, (off, shape)
        nf = (n * esz + 3) // 4
        ap = arena[:, off // 4: off // 4 + nf]
        if dt != F32:
            ap = ap.bitcast(dt)
        if len(shape) == 3:
            ap = ap.rearrange("p (a b) -> p a b", a=shape[1])
        elif len(shape) == 4:
            ap = ap.rearrange("p (a b c) -> p a b c", a=shape[1], b=shape[2])
        return ap

    def psb(b):
        return psum[:, b, :].bitcast(BF16)

    def bc(ap, shape):
        return ap.to_broadcast(list(shape))

    wcnt = [0]
    hcnt = [0]

    def load_slab(wd, r0, nk, c0, ncols):
        slot = wcnt[0] % 2
        wcnt[0] += 1
        for h0 in range(0, nk, 4):
            hn_ = min(4, nk - h0)
            hs = hcnt[0] % 2
            hcnt[0] += 1
            src = wd[r0 + h0 * 128: r0 + (h0 + hn_) * 128, c0:c0 + ncols].rearrange("(k p) n -> p k n", p=128)
            P.dma(wst[hs][:, 0:hn_, 0:ncols], src, w=["wst%d" % hs])
            if hcnt[0] % 3 == 0:
                P.op("pool", lambda e, hs=hs, h0=h0, hn_=hn_, slot=slot: e.tensor_copy(
                    out=wbf[slot][:, h0:h0 + hn_, 0:ncols], in_=wst[hs][:, 0:hn_, 0:ncols]),
                    r=["wst%d" % hs], w=["wbf%d" % slot])
            else:
                P.op("act", lambda e, hs=hs, h0=h0, hn_=hn_, slot=slot: e.activation(
                    out=wbf[slot][:, h0:h0 + hn_, 0:ncols], in_=wst[hs][:, 0:hn_, 0:ncols], func=AF.Copy),
                    r=["wst%d" % hs], w=["wbf%d" % slot])
        return slot

    pending = []

    def sched(loads, fn, extra=0):
        pending.append((loads, fn, extra))

    def flush():
        tasks = pending[:]
        del pending[:]
        loaded = {}

        def do_load(i):
            if i not in loaded:
                loaded[i] = [load_slab(*a) for a in tasks[i][0]]

        for i, (loads, fn, extra) in enumerate(tasks):
            do_load(i)
            if i + 1 < len(tasks) and len(loads) + extra + len(tasks[i + 1][0]) <= 2:
                do_load(i + 1)
            fn(loaded[i])

    bankctr = [0]

    def mm(out, lhsT, rhs, start, stop, r, w):
        P.op("pe", lambda e: e.matmul(out, lhsT, rhs, start=start, stop=stop), r=r, w=w)

    def transpose(out, in_, r, w):
        P.op("pe", lambda e: e.transpose(out, in_, identb[:, :]), r=list(r) + ["identb"], w=w)

    def dbg_dump(name, ap, shape, keys):
        if name not in dbg:
            return
        if name not in dbg_d:
            dbg_d[name] = nc.dram_tensor("dbg_" + name, list(shape), ap.dtype, kind="ExternalOutput").ap()
        return dbg_d[name]

    P.dma(cst[:, :], cst_d[:, :], w=["cst"])
    P.op("dve", lambda e: e.tensor_copy(out=identb[:, :], in_=cv("ident")), r=["cst"], w=["identb"])
    P.op("dve", lambda e: e.memset(epsT[:, :], EPS), w=["epsT"])
    P.op("dve", lambda e: e.memset(negb[:, :], -30000.0), w=["negb"])
    P.op("pool", lambda e: e.memset(Vc[:, :, :, 128:129], 1.0), w=["Vc"])

    mix_scale = float(HD) ** -0.5

    def rmsnorm_to_hT(src_d, st_i, nw_name, xin_views, hn_views, small, key_prefix):
        for c in range(4):
            gc = st_i * 4 + c
            xin = xin_views[c % 2]
            xk = "xin%d" % (c % 2)
            hn_view = hn_views[c % 2]
            hk_ = "hn%d" % (c % 2)
            P.dma(xin, src_d[gc * 128:(gc + 1) * 128, :], r=[(key_prefix, gc)], w=[xk])
            ss = small[:, c:c + 1]
            rt = small[:, 4 + c:5 + c]
            rstd = small[:, 8 + c:9 + c]
            P.op("act", lambda e, xin=xin, ss=ss, hn_view=hn_view: e.activation(out=hn_view, in_=xin, func=AF.Square, accum_out=ss),
                 r=[xk], w=[hk_, "nsm%d" % c])
            P.op("act", lambda e, ss=ss, rt=rt: e.activation(out=rt, in_=ss, func=AF.Ln, bias=epsT[:, :], scale=1.0 / D),
                 r=["nsm%d" % c, "epsT"], w=["nsm%d" % c])
            P.op("act", lambda e, rt=rt, rstd=rstd: e.activation(out=rstd, in_=rt, func=AF.Exp, scale=-0.5),
                 r=["nsm%d" % c], w=["nsm%d" % c])
            P.op("dve", lambda e, xin=xin, rstd=rstd, hn_view=hn_view: e.tensor_scalar(out=hn_view, in0=xin, scalar1=rstd, scalar2=None,
                                                                                        op0=OP.mult), r=[xk, "nsm%d" % c], w=[hk_])
            b = 6 + (c % 2)
            for k in range(8):
                transpose(psb(b)[:, k * 128:(k + 1) * 128], hn_view[:, k * 128:(k + 1) * 128], r=[hk_], w=["ps%d" % b])
            P.op("dve", lambda e, b=b, c=c: e.tensor_tensor(
                out=hT[:, :, c * 128:(c + 1) * 128],
                in0=psb(b).rearrange("p (k t) -> p k t", k=8),
                in1=bc(lv(nw_name).unsqueeze(2), (128, 8, 128)), op=OP.mult),
                r=["ps%d" % b, "lcs"], w=["hT"])

    def proj_tok(lhsT_fn, nk_list, slot_list, ncols, evac, banks, lkeys):
        prev_post = None
        for c in range(4):
            b = banks[c % len(banks)]
            kk = 0
            tot = sum(nk_list)
            for si, slot in enumerate(slot_list):
                for k in range(nk_list[si]):
                    mm(psum[:, b, 0:ncols], lhsT_fn(kk, c), wbf[slot][:, k, 0:ncols], start=(kk == 0), stop=(kk == tot - 1),
                       r=lkeys + ["wbf%d" % slot], w=["ps%d" % b])
                    kk += 1
            post = evac(c, b)
            if prev_post is not None:
                prev_post()
            prev_post = post if callable(post) else None
        if prev_post is not None:
            prev_post()

    for l in layers:
        src_d = x_d if l == layers[0] else xl1_d
        last = (l == layers[-1])
        P.barrier()
        P.dma(lcs[:, :], lc_d[l], w=["lcs"])
        P.op("act", lambda e: e.activation(out=negA[:, :], in_=lv("alog"), func=AF.Exp), r=["lcs"], w=["negA"])
        P.op("dve", lambda e: e.tensor_scalar(out=negA[:, :], in0=negA[:, :], scalar1=-1.0, scalar2=None, op0=OP.mult),
             r=["negA"], w=["negA"])
        P.op("pool", lambda e: e.memset(hst[:, :], 0.0), w=["hst"])
        P.op("pool", lambda e: e.memset(xtail[:, :, :], 0.0), w=["xtail"])
        P.op("pool", lambda e: e.memset(ftail[:, :, :], 0.0), w=["ftail"])

        for st_i in range(NST):
            P.barrier()
            scores = aview(0, (128, S), F32)
            qT = aview(16384, (128, 8, T), BF16)
            qiT = aview(24576, (128, 4, T), BF16)
            maskT = aview(28672, (128, NCH if NCH <= 32 else 32, 128), BF16)
            junkb = aview(28672, (128, 4096), BF16)
            xin_v = [aview(36864, (128, D), F32), aview(40960, (128, D), F32)]
            hn_v = [aview(45056, (128, D), BF16), aview(68096, (128, D), BF16)]
            qtok = [aview(47104, (128, 4, 128), BF16), aview(48128, (128, 4, 128), BF16)]
            qitf = aview(49152, (128, 8, 64), F32)
            rbuf = [aview(53248 + 1024 * i, (128, 512), BF16) for i in range(4)]
            maskb = [aview(57344 + 1024 * i, (128, 512), BF16) for i in range(2)]
            Eb = [aview(59392 + 1024 * i, (128, 4, 128), BF16) for i in range(2)]
            dsg = aview(61440, (128, 8, 128), BF16)
            hb_ctr = [0]
            a_tok = aview(63488, (128, 8, 128), BF16)
            small = aview(65536, (128, 256), F32)
            ropet = [aview(66560 + 256 * i, (128, 64), F32) for i in range(2)] + [aview(70144 + 256 * i, (128, 64), F32) for i in range(2)]
            qitok = aview(67072, (128, 512), BF16)
            wabs = small[:, 16:48].rearrange("p (c h) -> p c h", c=4)
            wsgn = small[:, 48:80].rearrange("p (c h) -> p c h", c=4)
            kitok = small[:, 192:256].bitcast(BF16)
            ktok = qtok[1]

            P.dma(tabs[:, :, :], tab_d[:, st_i * 192:(st_i + 1) * 192].rearrange("p (c n) -> p c n", c=4), w=["tabs"])
            rmsnorm_to_hT(src_d, st_i, "nmix", xin_v, hn_v, small, "xsrc%d" % l)

            lhs_h = lambda k, c: hT[:, k, c * 128:(c + 1) * 128]

            def rope(c, src3, dst3, half, cos_o, sin_o, nh, rk, wk):
                cosv = bc(tabs[:, c, cos_o:cos_o + half].unsqueeze(1), (128, nh, half))
                sinv = bc(tabs[:, c, sin_o:sin_o + half].unsqueeze(1), (128, nh, half))
                x1 = src3[:, :, 0:half]
                x2 = src3[:, :, half:2 * half]
                t1 = ropet[0][:, 0:nh * half].rearrange("p (a b) -> p a b", a=nh)
                t2 = ropet[1][:, 0:nh * half].rearrange("p (a b) -> p a b", a=nh)
                rr = list(rk) + ["tabs"]
                P.op("dve", lambda e: e.tensor_tensor(out=t1, in0=x1, in1=cosv, op=OP.mult), r=rr, w=["rt1"])
                P.op("dve", lambda e: e.tensor_tensor(out=t2, in0=x2, in1=sinv, op=OP.mult), r=rr, w=["rt2"])
                P.op("dve", lambda e: e.tensor_tensor(out=dst3[:, :, 0:half], in0=t1, in1=t2, op=OP.subtract),
                     r=["rt1", "rt2"], w=wk)
                t3 = ropet[2][:, 0:nh * half].rearrange("p (a b) -> p a b", a=nh)
                t4 = ropet[3][:, 0:nh * half].rearrange("p (a b) -> p a b", a=nh)
                P.op("dve", lambda e: e.tensor_tensor(out=t3, in0=x2, in1=cosv, op=OP.mult), r=rr, w=["rt3"])
                P.op("dve", lambda e: e.tensor_tensor(out=t4, in0=x1, in1=sinv, op=OP.mult), r=rr, w=["rt4"])
                P.op("dve", lambda e: e.tensor_tensor(out=dst3[:, :, half:2 * half], in0=t3, in1=t4, op=OP.add),
                     r=["rt3", "rt4"], w=wk)

            for qs in range(2 if stop >= 2 else 0):

                def evac_q(c, b, qs=qs):
                    import os
                    SUB = int(os.environ.get("SUB", "9"))
                    qt = qtok[c % 2]
                    qk_ = "qtok%d" % (c % 2)
                    pv = psum[:, b, :].rearrange("p (h d) -> p h d", h=4)
                    if SUB <= 2:
                        return
                    P.op("act", lambda e: e.activation(out=qt[:, :, 32:128], in_=pv[:, :, 32:128], func=AF.Copy),
                         r=["ps%d" % b], w=[qk_])
                    if SUB <= 3:
                        return
                    rope(c, pv, qt, 16, 0, 16, 4, ["ps%d" % b], [qk_])
                    if SUB <= 4:
                        return
                    def post():
                        tb = 4 + (c % 2)
                        for h in range(4):
                            transpose(psb(tb)[:, h * 128:(h + 1) * 128], qt[:, h, :], r=[qk_], w=["ps%d" % tb])
                        P.op("act", lambda e: e.activation(out=qT[:, qs * 4:(qs + 1) * 4, c * 128:(c + 1) * 128],
                                                           in_=psb(tb)[:, 0:512].rearrange("p (h t) -> p h t", h=4), func=AF.Copy),
                             r=["ps%d" % tb], w=["qT"])
                    return post

                sched([(w_in_d[l], 0, 8, O_Q + qs * 512, 512)],
                      lambda s, evac_q=evac_q: proj_tok(lhs_h, [8], s, 512, evac_q, [0, 1, 2, 3], ["hT"]))


            def evac_kv(c, b):
                gc = st_i * 4 + c
                pv = psum[:, b, 0:256].rearrange("p (h d) -> p h d", h=2)
                kt = qtok[c % 2][:, 0:2, :]
                qk_ = "qtok%d" % (c % 2)
                P.op("act", lambda e: e.activation(out=kt[:, :, 32:128], in_=pv[:, :, 32:128], func=AF.Copy),
                     r=["ps%d" % b], w=[qk_])
                rope(c, pv, kt, 16, 0, 16, 2, ["ps%d" % b], [qk_])
                P.op("act", lambda e: e.activation(out=Vc[:, gc, :, 0:128],
                                                   in_=psum[:, b, 256:512].rearrange("p (h d) -> p h d", h=2), func=AF.Copy),
                     r=["ps%d" % b], w=[("Vc", gc)])
                def post():
                    tb = 4 + (c % 2)
                    for h in range(2):
                        transpose(psb(tb)[:, h * 128:(h + 1) * 128], kt[:, h, :], r=[qk_], w=["ps%d" % tb])
                    P.op("act", lambda e: e.activation(out=kT[:, :, gc * 128:(gc + 1) * 128],
                                                       in_=psb(tb)[:, 0:256].rearrange("p (h t) -> p h t", h=2), func=AF.Copy),
                         r=["ps%d" % tb], w=[("kT", gc)])
                return post

            if stop >= 3:
                sched([(w_in_d[l], 0, 8, O_K, 512)], lambda s: proj_tok(lhs_h, [8], s, 512, evac_kv, [0, 1, 2, 3], ["hT"]))


            def evac_ki(c, b):
                gc = st_i * 4 + c
                pv = psum[:, b, 0:64].rearrange("p (h d) -> p h d", h=1)
                k3 = kitok[:, 0:64].rearrange("p (h d) -> p h d", h=1)
                P.op("act", lambda e: e.activation(out=k3[:, :, 16:64], in_=pv[:, :, 16:64], func=AF.Copy),
                     r=["ps%d" % b], w=["kitok"])
                rope(c, pv, k3, 8, 32, 40, 1, ["ps%d" % b], ["kitok"])
                P.op("dve", lambda e: e.tensor_copy(out=kitok[:, 64:128], in_=kitok[:, 0:64]), r=["kitok"], w=["kitok"])
                P.op("act", lambda e: e.activation(out=wabs[:, c, :], in_=psum[:, b, 64:72], func=AF.Abs),
                     r=["ps%d" % b], w=["wabs%d" % c])
                P.op("act", lambda e: e.activation(out=wsgn[:, c, :], in_=psum[:, b, 64:72], func=AF.Sign),
                     r=["ps%d" % b], w=["wsgn%d" % c])
                tb = 4 + (c % 2)
                transpose(psb(tb)[:, 0:128], kitok[:, :], r=["kitok"], w=["ps%d" % tb])
                P.op("act", lambda e: e.activation(out=kiT2[:, gc * 128:(gc + 1) * 128], in_=psb(tb)[:, 0:128], func=AF.Copy),
                     r=["ps%d" % tb], w=[("kiT", gc)])

            if stop >= 4:
                sched([(w_in_d[l], 0, 8, O_KI, 72)], lambda s: proj_tok(lhs_h, [8], s, 72, evac_ki, [0, 1, 2, 3], ["hT"]))


            def evac_qi(c, b):
                pv = psum[:, b, :].rearrange("p (h d) -> p h d", h=8)
                P.op("act", lambda e: e.activation(out=qitf[:, :, 16:64], in_=pv[:, :, 16:64], func=AF.Copy),
                     r=["ps%d" % b], w=["qitf"])
                rope(c, pv, qitf, 8, 32, 40, 8, ["ps%d" % b], ["qitf"])
                P.op("dve", lambda e: e.tensor_tensor(out=qitok.rearrange("p (h d) -> p h d", h=8), in0=qitf,
                                                      in1=bc(wabs[:, c, :].unsqueeze(2), (128, 8, 64)), op=OP.mult),
                     r=["qitf", "wabs%d" % c], w=["qitok"])
                tb = 4 + (c % 2)
                for j in range(4):
                    transpose(psb(tb)[:, j * 128:(j + 1) * 128], qitok[:, j * 128:(j + 1) * 128], r=["qitok"], w=["ps%d" % tb])
                P.op("act", lambda e: e.activation(out=qiT[:, :, c * 128:(c + 1) * 128],
                                                   in_=psb(tb)[:, 0:512].rearrange("p (h t) -> p h t", h=4), func=AF.Copy),
                     r=["ps%d" % tb], w=["qiT"])

            if stop >= 5:
                sched([(w_in_d[l], 0, 8, O_QI, 512)], lambda s: proj_tok(lhs_h, [8], s, 512, evac_qi, [0, 1, 2, 3], ["hT"]))
            flush()

            P.barrier()
            scores_v = [aview(0, (128, S), F32), aview(36864, (128, S), F32)]
            maskT_v = [aview(28672, (128, 32, 128), BF16),
                       wst[0][:, :, :].rearrange("p a b -> p (a b)").bitcast(BF16).rearrange("p (j t) -> p j t", j=32)]
            mkey = ["maskT0", "wst0"]

            def geom(c):
                gc = st_i * 4 + c
                nk = gc + 1
                n = nk * 128
                return gc, nk, n, slice(c * 128, (c + 1) * 128), (n + 511) // 512

            def st1(c):
                gc, nk, n, tq, ngrp = geom(c)
                scores = scores_v[c % 2]
                kkeys = [("kiT", j) for j in range(nk)]
                for h in range(8):
                    P.op("pool", lambda e, h=h: e.tensor_scalar(out=dsg[:, h, :], in0=identb[:, :], scalar1=wsgn[:, c, h:h + 1],
                                                                scalar2=None, op0=OP.mult),
                         r=["identb", "wsgn%d" % c], w=["dsg"])
                for kg in range(ngrp):
                    w_ = min(512, n - kg * 512)
                    sb_ = 2
                    pend = None
                    for h in range(8):
                        b = hb_ctr[0] % 2
                        hb_ctr[0] += 1
                        pr = (h % 2) * 64
                        mm(psum[:, b, 0:w_], qiT[pr:pr + 64, h // 2, tq], kiT2[pr:pr + 64, kg * 512:kg * 512 + w_], True, True,
                           r=["qiT"] + kkeys, w=["ps%d" % b])
                        rb = rbuf[h % 4]
                        P.op("act", lambda e, b=b, rb=rb, w_=w_: e.activation(out=rb[:, 0:w_], in_=psum[:, b, 0:w_], func=AF.Relu),
                             r=["ps%d" % b], w=["rbuf%d" % (h % 4)])
                        if pend is not None:
                            pend()
                        pend = (lambda h=h, rb=rb: mm(psum[:, sb_, 0:w_], dsg[:, h, :], rb[:, 0:w_], (h == 0), (h == 7),
                                                      r=["dsg", "rbuf%d" % (h % 4)], w=["ps%d" % sb_]))
                    pend()
                    sc = scores[:, kg * 512:kg * 512 + w_]
                    P.op("act", lambda e, sb_=sb_, sc=sc, w_=w_: e.activation(out=sc, in_=psum[:, sb_, 0:w_], func=AF.Copy),
                         r=["ps%d" % sb_], w=[("sc", c % 2, kg)])

            def st2(c):
                gc, nk, n, tq, ngrp = geom(c)
                scores = scores_v[c % 2]
                maskT = maskT_v[c % 2]
                junkb = maskT.rearrange("p j t -> p (j t)")
                mk = mkey[c % 2]
                sckeys = [("sc", c % 2, kg) for kg in range(ngrp)]
                lastk = ("sc", c % 2, ngrp - 1)
                thr = small[:, 100 + (c % 2):101 + (c % 2)]
                tk = "thr%d" % (c % 2)
                if (gc * 128 + 128) <= TOPK:
                    P.op("dve", lambda e: e.tensor_tensor(out=scores[:, n - 128:n], in0=scores[:, n - 128:n], in1=cv("caus"),
                                                          op=OP.add), r=[lastk, "cst"], w=[lastk])
                    P.op("dve", lambda e: e.memset(thr, -1e29), w=[tk])
                else:
                    mx = small[:, 102:103]
                    mn = small[:, 103:104]
                    w0 = small[:, 104:105]
                    mid = small[:, 105:106]
                    cnt = small[:, 106:107]
                    dd = small[:, 107:108]
                    Wk = small[:, 110:110 + NIT + 2]
                    P.op("dve", lambda e: e.tensor_reduce(out=mx, in_=scores[:, 0:n], axis=AX.X, op=OP.max), r=sckeys, w=["bs_mx"])
                    P.op("dve", lambda e: e.tensor_reduce(out=mn, in_=scores[:, 0:n], axis=AX.X, op=OP.min), r=sckeys, w=["bs_mn"])
                    P.op("dve", lambda e: e.tensor_tensor(out=scores[:, n - 128:n], in0=scores[:, n - 128:n], in1=cv("caus"),
                                                          op=OP.add), r=[lastk, "cst"], w=[lastk])
                    P.op("dve", lambda e: e.tensor_tensor(out=w0, in0=mx, in1=mn, op=OP.subtract), r=["bs_mx", "bs_mn"], w=["bs_w0"])
                    P.op("dve", lambda e: e.tensor_scalar(out=Wk, in0=cv("pow2"), scalar1=w0, scalar2=None, op0=OP.mult),
                         r=["bs_w0", "cst"], w=["bs_wk"])
                    P.op("dve", lambda e: e.tensor_tensor(out=mid, in0=mn, in1=Wk[:, 0:1], op=OP.add), r=["bs_mn", "bs_wk"], w=["bs_mid"])
                    for it in range(NIT):
                        P.op("dve", lambda e: e.tensor_scalar(out=junkb[:, 0:n], in0=scores[:, 0:n], scalar1=mid, scalar2=None,
                                                              op0=OP.is_ge, op1=OP.add, accum_out=cnt),
                             r=sckeys + ["bs_mid"], w=[mk, "bs_cnt"])
                        lastit = (it == NIT - 1)
                        P.op("dve", lambda e, lastit=lastit: e.tensor_scalar(out=dd, in0=cnt, scalar1=float(TOPK),
                                                                              scalar2=(1.0 if lastit else 0.5),
                                                                              op0=OP.is_ge, op1=OP.subtract),
                             r=["bs_cnt"], w=["bs_dd"])
                        dst = thr if lastit else mid
                        P.op("dve", lambda e, it=it, dst=dst: e.scalar_tensor_tensor(
                            out=dst, in0=dd, scalar=Wk[:, it:it + 1], in1=mid, op0=OP.mult, op1=OP.add),
                            r=["bs_dd", "bs_wk", "bs_mid"], w=[tk if lastit else "bs_mid"])
                for kg in range(ngrp):
                    w_ = min(512, n - kg * 512)
                    mb = maskb[kg % 2]
                    P.op("dve", lambda e, mb=mb, kg=kg, w_=w_: e.tensor_scalar(
                        out=mb[:, 0:w_], in0=scores[:, kg * 512:kg * 512 + w_], scalar1=thr, scalar2=None, op0=OP.is_ge),
                        r=[("sc", c % 2, kg), tk], w=["maskb%d" % (kg % 2)])
                    tb = 3
                    nj = w_ // 128
                    for jj in range(nj):
                        transpose(psb(tb)[:, jj * 128:(jj + 1) * 128], mb[:, jj * 128:(jj + 1) * 128],
                                  r=["maskb%d" % (kg % 2)], w=["ps%d" % tb])
                    P.op("act", lambda e, tb=tb, kg=kg, nj=nj: e.activation(
                        out=maskT[:, kg * 4:kg * 4 + nj, :], in_=psb(tb)[:, 0:nj * 128].rearrange("p (j t) -> p j t", j=nj),
                        func=AF.Identity, scale=30000.0, bias=negb[:, :]), r=["ps%d" % tb, "negb"], w=[mk])
                if "sc" in dbg and gc == NCH - 1:
                    dd_ = dbg_dump("sc", scores, (128, S), None)
                    P.dma(dd_, scores, r=sckeys)
                    dd_ = dbg_dump("thr", thr, (128, 1), None)
                    P.dma(dd_, thr, r=[tk])

            def st3a(c):
                gc, nk, n, tq, ngrp = geom(c)
                maskT = maskT_v[c % 2]
                mk = mkey[c % 2]
                pend_pv = [None]
                for kvh in range(2):
                    if kvh == 1:
                        st3b_half(c, 0)
                    for j in range(nk):
                        lb = 3 if j % 2 == 0 else 2
                        mm(psum[:, lb, :], kT[:, kvh, j * 128:(j + 1) * 128], qT[:, kvh * 4:(kvh + 1) * 4, tq], True, False,
                           r=[("kT", j), "qT"], w=["ps%d" % lb])
                        mm(psum[:, lb, :], identb[:, :], bc(maskT[:, j, :].unsqueeze(1), (128, 4, 128)), False, True,
                           r=["identb", mk], w=["ps%d" % lb])
                        Ev = Eb[j % 2]
                        P.op("act", lambda e, lb=lb, Ev=Ev: e.activation(
                            out=Ev, in_=psum[:, lb, :].rearrange("p (g t) -> p g t", g=4), func=AF.Exp, scale=mix_scale),
                            r=["ps%d" % lb], w=["E%d" % (j % 2)])
                        if pend_pv[0] is not None:
                            pend_pv[0]()

                        def pv(j=j, Ev=Ev, kvh=kvh):
                            for g in range(4):
                                mm(psum[:, 4 + g, 0:129], Ev[:, g, :], Vc[:, j, kvh, :], (j == 0), (j == nk - 1),
                                   r=["E%d" % (j % 2), ("Vc", j), "Vc"], w=["ps%d" % (4 + g)])
                        pend_pv[0] = pv
                    pend_pv[0]()
                    pend_pv[0] = None

            def st3b_half(c, kvh):
                for g in range(4):
                    h = kvh * 4 + g
                    rden = small[:, 140 + h:141 + h]
                    P.op("dve", lambda e, g=g, rden=rden: e.reciprocal(out=rden, in_=psum[:, 4 + g, 128:129]),
                         r=["ps%d" % (4 + g)], w=["rden%d" % h])
                    P.op("act", lambda e, g=g, h=h, rden=rden: e.activation(
                        out=a_tok[:, h, :], in_=psum[:, 4 + g, 0:128], func=AF.Copy, scale=rden),
                        r=["ps%d" % (4 + g), "rden%d" % h], w=["a_tok"])

            def st3b(c):
                st3b_half(c, 1)
                tb = 3
                for h in range(8):
                    transpose(psb(tb)[:, h * 128:(h + 1) * 128], a_tok[:, h, :], r=["a_tok"], w=["ps%d" % tb])
                P.op("act", lambda e, tb=tb, c=c: e.activation(
                    out=aT[:, :, c * 128:(c + 1) * 128], in_=psb(tb).rearrange("p (h t) -> p h t", h=8), func=AF.Copy),
                    r=["ps%d" % tb], w=["aT"])

            if stop >= 6:
                st1(0)
                st1(1)
                st2(0)
                st3a(0)
                st1(2)
                st2(1)
                st3b(0)
                st3a(1)
                st1(3)
                st2(2)
                st3b(1)
                st3a(2)
                st2(3)
                st3b(2)
                st3a(3)
                st3b(3)

            if "aT" in dbg:
                dd_ = dbg_dump("aT", aT[:, :, :], (NST, 128, 8, T), None)
                P.dma(dd_[st_i], aT[:, :, :], r=["aT"])
            if "qT" in dbg:
                dd_ = dbg_dump("qT", qT, (NST, 128, 8, T), None)
                P.dma(dd_[st_i], qT, r=["qT"])
            if "hT" in dbg:
                dd_ = dbg_dump("hT", hT[:, :, :], (NST, 128, 8, T), None)
                P.dma(dd_[st_i], hT[:, :, :], r=["hT"])

            if stop >= 7:
                P.barrier()
                BT = aview(0, (128, 4, T), BF16)
                CT = aview(4096, (128, 4, T), BF16)
                Btok = aview(8192, (128, 4, 512), BF16)
                xacc = [aview(12288 + 2048 * i, (128, 512), F32) for i in range(2)]
                xact = [aview(16384 + 1024 * i, (128, 512), BF16) for i in range(2)]
                dtt = aview(18432, (128, 4, 32), F32)
                adt = aview(18944, (128, 4, 32), F32)

                def conv_fm(cc, b, wname, bname, ntap, tail, dst_silu, tailkey):
                    u = psum[:, b, :]
                    nx_ = len(xacc)
                    acc = xacc[cc % nx_]
                    ak = "xacc%d" % (cc % nx_)
                    wv = lambda j: lv(wname, cc * ntap + j, cc * ntap + j + 1)
                    P.op("act", lambda e: e.activation(out=acc, in_=u, func=AF.Identity, bias=lv(bname, cc, cc + 1),
                                                       scale=wv(ntap - 1)), r=["ps%d" % b, "lcs"], w=[ak])
                    for j in range(ntap - 1):
                        sh = ntap - 1 - j
                        P.op("dve", lambda e, j=j, sh=sh: e.scalar_tensor_tensor(
                            out=acc[:, sh:512], in0=u[:, 0:512 - sh], scalar=wv(j), in1=acc[:, sh:512], op0=OP.mult, op1=OP.add),
                            r=["ps%d" % b, "lcs", ak], w=[ak])
                        P.op("dve", lambda e, j=j, sh=sh: e.scalar_tensor_tensor(
                            out=acc[:, 0:sh], in0=tail[:, cc, ntap - 1 - sh:ntap - 1], scalar=wv(j), in1=acc[:, 0:sh],
                            op0=OP.mult, op1=OP.add), r=[tailkey, "lcs", ak], w=[ak])
                    P.op("dve", lambda e: e.tensor_copy(out=tail[:, cc, :], in_=u[:, 512 - (ntap - 1):512]),
                         r=["ps%d" % b], w=[tailkey])
                    return acc, ak

                def proj_fm(slot, ncc, cc0, handler):
                    prev_post = None
                    for j in range(ncc):
                        b = j % 4
                        for k in range(8):
                            mm(psum[:, b, :], wbf[slot][:, k, j * 128:(j + 1) * 128], hT[:, k, :], (k == 0), (k == 7),
                               r=["hT", "wbf%d" % slot], w=["ps%d" % b])
                        post = handler(cc0 + j, b)
                        if prev_post is not None:
                            prev_post()
                        prev_post = post if callable(post) else None
                    if prev_post is not None:
                        prev_post()

                def h_bc(cc, b):
                    acc, ak = conv_fm(cc, b, "xcw", "xcb", 4, xtail, None, "xtail")
                    g = (cc - 16) % 4
                    if cc < 20:
                        P.op("act", lambda e: e.activation(out=BT[:, g, :], in_=acc, func=AF.Silu), r=[ak], w=["BT"])
                        tb = 4 + (cc % 2)
                        for tc_ in range(4):
                            transpose(psb(tb)[:, tc_ * 128:(tc_ + 1) * 128], BT[:, g, tc_ * 128:(tc_ + 1) * 128], r=["BT"], w=["ps%d" % tb])
                        P.op("act", lambda e: e.activation(out=Btok[:, :, g * 128:(g + 1) * 128],
                                                           in_=psb(tb)[:, 0:512].rearrange("p (c n) -> p c n", c=4), func=AF.Copy),
                             r=["ps%d" % tb], w=["Btok"])
                    else:
                        P.op("act", lambda e: e.activation(out=CT[:, g, :], in_=acc, func=AF.Silu), r=[ak], w=["CT"])

                for half in range(2):
                    sched([(w_in_d[l], 0, 8, O_XBC + 2048 + half * 512, 512)],
                          lambda s, half=half: proj_fm(s[0], 4, 16 + half * 4, h_bc))

                def evac_dt(c, b):
                    P.op("dve", lambda e: e.tensor_tensor(out=dtt[:, c, :], in0=psum[:, b, 0:32], in1=lv("dtb"), op=OP.add),
                         r=["ps%d" % b, "lcs"], w=["dtt"])
                    P.op("act", lambda e: e.activation(out=dtt[:, c, :], in_=dtt[:, c, :], func=AF.Exp), r=["dtt"], w=["dtt"])
                    P.op("act", lambda e: e.activation(out=dtt[:, c, :], in_=dtt[:, c, :], func=AF.Ln, bias=1.0), r=["dtt"], w=["dtt"])
                    P.op("dve", lambda e: e.tensor_tensor(out=adt[:, c, :], in0=dtt[:, c, :], in1=negA[:, :], op=OP.mult),
                         r=["dtt", "negA"], w=["adt"])

                sched([(w_in_d[l], 0, 8, O_DT, 32)], lambda s: proj_tok(lhs_h, [8], s, 32, evac_dt, [0, 1, 2, 3], ["hT"]))

                _o = [19 * 1024]

                def _al(nbytes):
                    o = _o[0]
                    _o[0] += nbytes
                    return o

                TP = []
                for p_ in range(2):
                    d_ = dict(
                        xtok=aview(_al(4096), (128, 4, 512), BF16), zs=aview(_al(4096), (128, 4, 512), BF16),
                        Wp=aview(_al(4096), (128, 8, 128), F32), Lx=aview(_al(2048), (128, 8, 128), BF16),
                        MT=aview(_al(2048), (128, 8, 128), BF16), CBm=aview(_al(256), (128, 128), BF16),
                        Xdt=aview(_al(1024), (128, 8, 64), BF16), Xd2=aview(_al(1024), (128, 8, 64), BF16),
                        yo=aview(_al(2048), (128, 8, 64), F32), yy=aview(_al(2048), (128, 8, 64), F32),
                        hb=aview(_al(1024), (128, 512), BF16), sm2=aview(_al(256), (128, 64), F32),
                        btok_b=aview(_al(1024), (128, 512), BF16))
                    TP.append(d_)

                def make_hx(g):
                    p_ = g % 2
                    xtok = TP[p_]["xtok"]

                    def h_x(cc, b):
                        acc, ak = conv_fm(cc, b, "xcw", "xcb", 4, xtail, None, "xtail")
                        xa = xact[cc % 2]
                        xk = "xact%d" % (cc % 2)
                        P.op("act", lambda e: e.activation(out=xa, in_=acc, func=AF.Silu), r=[ak], w=[xk])
                        def post():
                            tb = 4 + (cc % 2)
                            for tc_ in range(4):
                                transpose(psb(tb)[:, tc_ * 128:(tc_ + 1) * 128], xa[:, tc_ * 128:(tc_ + 1) * 128], r=[xk], w=["ps%d" % tb])
                            j = cc % 4
                            P.op("act", lambda e: e.activation(out=xtok[:, :, j * 128:(j + 1) * 128],
                                                               in_=psb(tb)[:, 0:512].rearrange("p (c n) -> p c n", c=4), func=AF.Copy),
                                 r=["ps%d" % tb], w=["xtok%d" % p_])
                        return post
                    return h_x

                def make_evz(g):
                    p_ = g % 2
                    zs = TP[p_]["zs"]

                    def evac_z(c, b):
                        P.op("act", lambda e: e.activation(out=zs[:, c, :], in_=psum[:, b, :], func=AF.Silu), r=["ps%d" % b], w=["zs%d" % p_])
                    return evac_z

                def ssd_chain(g):
                    p_ = g % 2
                    t_ = TP[p_]
                    xtok, zs, Wp, Lx, MT, CBm, Xdt, Xd2, yo, yy, hb, sm2, btok_b = (
                        t_["xtok"], t_["zs"], t_["Wp"], t_["Lx"], t_["MT"], t_["CBm"], t_["Xdt"], t_["Xd2"], t_["yo"], t_["yy"],
                        t_["hb"], t_["sm2"], t_["btok_b"])
                    K_ = lambda n_: "%s%d" % (n_, p_)
                    b0, b1, b2, b3 = 4 * p_, 4 * p_ + 1, 4 * p_ + 2, 4 * p_ + 3
                    pk = lambda b: "ps%d" % b
                    hk = "hst%d" % g
                    hs3 = hst[:, g * 512:(g + 1) * 512].rearrange("p (e d) -> p e d", e=8)
                    for c in range(4):
                        ts_ = slice(c * 128, (c + 1) * 128)
                        xt3 = xtok[:, c, :].rearrange("p (e d) -> p e d", e=8)
                        dtg = dtt[:, c, g * 8:(g + 1) * 8]
                        adg = adt[:, c, g * 8:(g + 1) * 8]
                        P.op("dve", lambda e, xt3=xt3, dtg=dtg: e.tensor_tensor(out=Xdt, in0=xt3, in1=bc(dtg.unsqueeze(2), (128, 8, 64)), op=OP.mult),
                             r=[K_("xtok"), "dtt"], w=[K_("Xdt")])
                        P.op("dve", lambda e, adg=adg: e.tensor_tensor(out=Wp, in0=bc(cv("tri_le").unsqueeze(1), (128, 8, 128)),
                                                                       in1=bc(adg.unsqueeze(2), (128, 8, 128)), op=OP.mult),
                             r=["adt", "cst"], w=[K_("Wp")])
                        yield
                        for hh, bb in ((0, b0), (1, b1)):
                            mm(psum[:, bb, :], cv("sgt"), Wp[:, hh * 4:(hh + 1) * 4, :], True, True, r=["cst", K_("Wp")], w=[pk(bb)])
                            P.op("act", lambda e, hh=hh, bb=bb: e.activation(out=Lx[:, hh * 4:(hh + 1) * 4, :],
                                                                             in_=psum[:, bb, :].rearrange("p (e l) -> p e l", e=4), func=AF.Exp),
                                 r=[pk(bb)], w=[K_("Lx")])
                            P.op("act", lambda e, hh=hh, bb=bb: e.activation(out=sm2[:, hh * 4:(hh + 1) * 4],
                                                                             in_=psum[:, bb, :].rearrange("p (e l) -> p e l", e=4)[:, :, 127],
                                                                             func=AF.Exp), r=[pk(bb)], w=[K_("decay")])
                        yield
                        mm(psum[:, b2, 0:128], BT[:, g, ts_], CT[:, g, ts_], True, True, r=["BT", "CT"], w=[pk(b2)])
                        mm(psum[:, b2, 128:136], cv("tri_le"), adg, True, True, r=["cst", "adt"], w=[pk(b2)])
                        mm(psum[:, b2, 136:144], cv("ones"), adg, True, True, r=["cst", "adt"], w=[pk(b2)])
                        P.op("dve", lambda e: e.tensor_tensor(out=CBm, in0=psum[:, b2, 0:128], in1=cv("tri_le"), op=OP.mult),
                             r=[pk(b2), "cst"], w=[K_("CBm")])
                        P.op("act", lambda e: e.activation(out=sm2[:, 8:24], in_=psum[:, b2, 128:144], func=AF.Exp), r=[pk(b2)], w=[K_("eacs")])
                        yield
                        P.op("dve", lambda e: e.tensor_tensor(out=MT, in0=Lx, in1=bc(CBm.unsqueeze(1), (128, 8, 128)), op=OP.mult),
                             r=[K_("Lx"), K_("CBm")], w=[K_("MT")])
                        for e_ in range(8):
                            mm(psum[:, b3, e_ * 64:(e_ + 1) * 64], MT[:, e_, :], Xdt[:, e_, :], True, True, r=[K_("MT"), K_("Xdt")], w=[pk(b3)])
                        yield
                        P.op("act", lambda e: e.activation(out=hb, in_=hst[:, g * 512:(g + 1) * 512], func=AF.Copy), r=[hk], w=[K_("hb")])
                        mm(psum[:, b0, :], CT[:, g, ts_], hb, True, True, r=["CT", K_("hb")], w=[pk(b0)])
                        P.op("dve", lambda e: e.tensor_tensor(out=yo, in0=psum[:, b0, :].rearrange("p (e d) -> p e d", e=8),
                                                              in1=bc(sm2[:, 8:16].unsqueeze(2), (128, 8, 64)), op=OP.mult),
                             r=[pk(b0), K_("eacs")], w=[K_("yo")])
                        yield
                        P.op("dve", lambda e: e.tensor_tensor(out=yy, in0=psum[:, b3, :].rearrange("p (e d) -> p e d", e=8), in1=yo, op=OP.add),
                             r=[pk(b3), K_("yo")], w=[K_("yy")])
                        P.op("dve", lambda e, xt3=xt3: e.tensor_tensor(out=yo, in0=xt3, in1=bc(lv("dsk", g * 8, g * 8 + 8).unsqueeze(2), (128, 8, 64)),
                                                                       op=OP.mult), r=[K_("xtok"), "lcs"], w=[K_("yo")])
                        P.op("dve", lambda e: e.tensor_tensor(out=yy, in0=yy, in1=yo, op=OP.add), r=[K_("yy"), K_("yo")], w=[K_("yy")])
                        yield
                        P.op("dve", lambda e: e.tensor_tensor(out=Xd2, in0=Xdt, in1=bc(sm2[:, 0:8].unsqueeze(2), (128, 8, 64)), op=OP.mult),
                             r=[K_("Xdt"), K_("decay")], w=[K_("Xd2")])
                        mm(psum[:, b1, :], Btok[:, c, g * 128:(g + 1) * 128], Xd2.rearrange("p e d -> p (e d)"), True, True,
                           r=["Btok", K_("Xd2")], w=[pk(b1)])
                        P.op("dve", lambda e: e.tensor_tensor(out=hs3, in0=hs3, in1=bc(sm2[:, 16:24].unsqueeze(2), (128, 8, 64)), op=OP.mult),
                             r=[hk, K_("eacs")], w=[hk])
                        P.op("dve", lambda e: e.tensor_tensor(out=hs3, in0=psum[:, b1, :].rearrange("p (e d) -> p e d", e=8), in1=hs3, op=OP.add),
                             r=[pk(b1), hk], w=[hk])
                        yield
                        yf = yy.rearrange("p e d -> p (e d)")
                        P.op("dve", lambda e, yf=yf, c=c: e.tensor_tensor(out=yf, in0=yf, in1=zs[:, c, :], op=OP.mult), r=[K_("yy"), K_("zs")], w=[K_("yy")])
                        P.op("act", lambda e, yf=yf: e.activation(out=btok_b, in_=yf, func=AF.Square, accum_out=sm2[:, 24:25]),
                             r=[K_("yy")], w=[K_("btok_b"), K_("gss")])
                        P.op("act", lambda e: e.activation(out=sm2[:, 25:26], in_=sm2[:, 24:25], func=AF.Ln, bias=epsT[:, :], scale=1.0 / 512),
                             r=[K_("gss"), "epsT"], w=[K_("gss")])
                        P.op("act", lambda e: e.activation(out=sm2[:, 26:27], in_=sm2[:, 25:26], func=AF.Exp, scale=-0.5),
                             r=[K_("gss")], w=[K_("gss")])
                        yield
                        P.op("dve", lambda e, yf=yf: e.tensor_scalar(out=btok_b, in0=yf, scalar1=sm2[:, 26:27], scalar2=None, op0=OP.mult),
                             r=[K_("yy"), K_("gss")], w=[K_("btok_b")])
                        for j in range(4):
                            transpose(psb(b2)[:, j * 128:(j + 1) * 128], btok_b[:, j * 128:(j + 1) * 128], r=[K_("btok_b")], w=[pk(b2)])
                        P.op("dve", lambda e, ts_=ts_: e.tensor_tensor(out=bT[:, g * 4:(g + 1) * 4, ts_],
                                                                       in0=psb(b2)[:, 0:512].rearrange("p (j t) -> p j t", j=4),
                                                                       in1=bc(lv("snw", g * 4, g * 4 + 4).unsqueeze(2), (128, 4, 128)), op=OP.mult),
                             r=[pk(b2), "lcs"], w=["bT"])
                        yield

                def run_pair(g0):
                    gens = [ssd_chain(g0), ssd_chain(g0 + 1)]
                    while gens:
                        for gen in list(gens):
                            try:
                                next(gen)
                            except StopIteration:
                                gens.remove(gen)

                for g0 in (0, 2):
                    for g in (g0, g0 + 1):
                        sched([(w_in_d[l], 0, 8, O_XBC + g * 512, 512)], lambda s, g=g: proj_fm(s[0], 4, g * 4, make_hx(g)))
                        sched([(w_in_d[l], 0, 8, O_Z + g * 512, 512)],
                              lambda s, g=g: proj_tok(lhs_h, [8], s, 512, make_evz(g), [0, 1, 2, 3], ["hT"]))
                    sched([], lambda s, g0=g0: run_pair(g0))
                flush()
                if "bT" in dbg:
                    dd_ = dbg_dump("bT", bT[:, :, :], (NST, 128, 16, T), None)
                    P.dma(dd_[st_i], bT[:, :, :], r=["bT"])

            if stop >= 8:
                P.barrier()
                ga = aview(0, (128, 4, 512), BF16)
                gb = aview(4096, (128, 4, 512), BF16)
                mf = aview(8192, (128, 4, 512), F32)
                tmpf = aview(16384, (128, 512), F32)
                mtok = aview(18432, (128, 512), BF16)
                mT = aview(19456, (128, 8, T), BF16)
                xh = aview(27648, (128, 4, 512), F32)
                for cs in range(2):
                    sched([(w_in_d[l], 0, 8, O_GA + cs * 512, 512)], lambda s: proj_tok(
                        lhs_h, [8], s, 512,
                        lambda c, b: P.op("act", lambda e: e.activation(out=ga[:, c, :], in_=psum[:, b, :], func=AF.Sigmoid),
                                          r=["ps%d" % b], w=["ga"]), [0, 1, 2, 3], ["hT"]))
                    sched([(w_in_d[l], 0, 8, O_GB + cs * 512, 512)], lambda s: proj_tok(
                        lhs_h, [8], s, 512,
                        lambda c, b: P.op("act", lambda e: e.activation(out=gb[:, c, :], in_=psum[:, b, :], func=AF.Sigmoid),
                                          r=["ps%d" % b], w=["gb"]), [0, 1, 2, 3], ["hT"]))
                    sched([(w_pa_d[l], 0, 8, cs * 512, 512)], lambda s: proj_tok(
                        lambda k, c: aT[:, k, c * 128:(c + 1) * 128], [8], s, 512,
                        lambda c, b: P.op("dve", lambda e: e.tensor_tensor(out=mf[:, c, :], in0=psum[:, b, :], in1=ga[:, c, :], op=OP.mult),
                                          r=["ps%d" % b, "ga"], w=["mf"]), [0, 1, 2, 3], ["aT"]))

                    def evac_pb(c, b, cs=cs):
                        P.op("dve", lambda e: e.tensor_tensor(out=tmpf, in0=psum[:, b, :], in1=gb[:, c, :], op=OP.mult),
                             r=["ps%d" % b, "gb"], w=["tmpf"])
                        P.op("dve", lambda e: e.tensor_tensor(out=mtok, in0=tmpf, in1=mf[:, c, :], op=OP.add), r=["tmpf", "mf"], w=["mtok"])
                        tb = 6 + (c % 2)
                        for j in range(4):
                            transpose(psb(tb)[:, j * 128:(j + 1) * 128], mtok[:, j * 128:(j + 1) * 128], r=["mtok"], w=["ps%d" % tb])
                        P.op("act", lambda e: e.activation(out=mT[:, cs * 4:(cs + 1) * 4, c * 128:(c + 1) * 128],
                                                           in_=psb(tb)[:, 0:512].rearrange("p (j t) -> p j t", j=4), func=AF.Copy),
                             r=["ps%d" % tb], w=["mT"])
                        return None

                    sched([(w_pb_d[l], 0, 8, cs * 512, 512), (w_pb_d[l], 1024, 8, cs * 512, 512)],
                          lambda s, evac_pb=evac_pb: proj_tok(lambda k, c: bT[:, k, c * 128:(c + 1) * 128], [8, 8], s, 512, evac_pb,
                                                              [0, 1, 2, 3], ["bT"]))
                flush()
                for cs in range(2):
                    def out_task(s, cs=cs):
                        P.dma(xh, src_d[st_i * T:(st_i + 1) * T, cs * 512:(cs + 1) * 512].rearrange("(c p) n -> p c n", p=128),
                              r=[("xsrc%d" % l, st_i * 4 + c) for c in range(4)], w=["xh"])
                        proj_tok(lambda k, c: mT[:, k, c * 128:(c + 1) * 128], [8], s, 512,
                                 lambda c, b: P.op("dve", lambda e: e.tensor_tensor(out=xh[:, c, :], in0=psum[:, b, :], in1=xh[:, c, :], op=OP.add),
                                                   r=["ps%d" % b, "xh"], w=["xh"]), [0, 1, 2, 3], ["mT"])
                        P.dma(xmid_d[st_i * T:(st_i + 1) * T, cs * 512:(cs + 1) * 512].rearrange("(c p) n -> p c n", p=128), xh,
                              r=["xh"], w=[("xmid", st_i * 4 + c) for c in range(4)])
                    sched([(w_out_d[l], 0, 8, cs * 512, 512)], out_task)
                flush()

            if stop >= 9:
                P.barrier()
                xin_f = [aview(0, (128, D), F32), aview(4096, (128, D), F32)]
                hn_f = [aview(8192, (128, D), BF16), aview(63744, (128, D), BF16)]
                small_f = aview(10240, (128, 64), F32)
                xacc = [aview(10496 + 2048 * i, (128, 512), F32) for i in range(4)]
                sg = aview(18688, (128, 22, 512), BF16)
                xo = aview(41216, (128, 4, D), F32)
                nfin = aview(57600, (128, D), F32)
                osq = aview(61696, (128, D), BF16)
                rmsnorm_to_hT(xmid_d, st_i, "nffn", xin_f, hn_f, small_f, "xmid")

                def h_up(cc, b):
                    acc, ak = conv_fm(cc, b, "fcw", "fcb", 3, ftail, None, "ftail")
                    if cc < 22:
                        P.op("act", lambda e: e.activation(out=sg[:, cc, :], in_=acc, func=AF.Silu), r=[ak], w=[("sg", cc)])
                    else:
                        P.op("dve", lambda e: e.tensor_tensor(out=sg[:, cc - 22, :], in0=sg[:, cc - 22, :], in1=acc, op=OP.mult),
                             r=[ak, ("sg", cc - 22)], w=[("sg", cc - 22)])

                for sl in range(11):
                    sched([(w_up_d[l], 0, 8, sl * 512, 512)], lambda s, sl=sl: proj_fm(s[0], 4, sl * 4, h_up))
                sgk = [("sg", i) for i in range(22)]

                def down_task(slots, cs):
                    P.dma(xo[:, :, cs * 512:(cs + 1) * 512],
                          xmid_d[st_i * T:(st_i + 1) * T, cs * 512:(cs + 1) * 512].rearrange("(c p) n -> p c n", p=128),
                          r=[("xmid", st_i * 4 + c) for c in range(4)], w=["xo"])
                    for c in range(4):
                        kk = 0
                        for si in range(2):
                            for k in range(8):
                                mm(psum[:, c, :], sg[:, kk, c * 128:(c + 1) * 128], wbf[slots[si]][:, k, :], (kk == 0), False,
                                   r=sgk + ["wbf%d" % slots[si]], w=["ps%d" % c])
                                kk += 1
                    s2 = load_slab(w_dn_d[l], 2048, 6, cs * 512, 512)
                    for c in range(4):
                        for k in range(6):
                            mm(psum[:, c, :], sg[:, 16 + k, c * 128:(c + 1) * 128], wbf[s2][:, k, :], False, (k == 5),
                               r=sgk + ["wbf%d" % s2], w=["ps%d" % c])
                        P.op("dve", lambda e, c=c, cs=cs: e.tensor_tensor(out=xo[:, c, cs * 512:(cs + 1) * 512], in0=psum[:, c, :],
                                                                          in1=xo[:, c, cs * 512:(cs + 1) * 512], op=OP.add),
                             r=["ps%d" % c, "xo"], w=["xo"])

                for cs in range(2):
                    sched([(w_dn_d[l], 0, 8, cs * 512, 512), (w_dn_d[l], 1024, 8, cs * 512, 512)],
                          lambda s, cs=cs: down_task(s, cs), extra=1)
                flush()
                if not last:
                    P.dma(xl1_d[st_i * T:(st_i + 1) * T, :].rearrange("(c p) n -> p c n", p=128), xo, r=["xo"],
                          w=[("xsrc1", st_i * 4 + c) for c in range(4)])
                else:
                    P.dma(nfin, nfin_d[:, :], w=["nfin"])
                    for c in range(4):
                        ss = small_f[:, 32 + c:33 + c]
                        rt = small_f[:, 36 + c:37 + c]
                        rs = small_f[:, 40 + c:41 + c]
                        P.op("act", lambda e, c=c, ss=ss: e.activation(out=osq, in_=xo[:, c, :], func=AF.Square, accum_out=ss),
                             r=["xo"], w=["osq", "fsm%d" % c])
                        P.op("act", lambda e, ss=ss, rt=rt: e.activation(out=rt, in_=ss, func=AF.Ln, bias=epsT[:, :], scale=1.0 / D),
                             r=["fsm%d" % c, "epsT"], w=["fsm%d" % c])
                        P.op("act", lambda e, rt=rt, rs=rs: e.activation(out=rs, in_=rt, func=AF.Exp, scale=-0.5),
                             r=["fsm%d" % c], w=["fsm%d" % c])
                        P.op("dve", lambda e, c=c, rs=rs: e.scalar_tensor_tensor(out=xo[:, c, :], in0=xo[:, c, :], scalar=rs, in1=nfin,
                                                                                  op0=OP.mult, op1=OP.mult),
                             r=["xo", "fsm%d" % c, "nfin"], w=["xo"])
                    P.dma(out_d[st_i * T:(st_i + 1) * T, :].rearrange("(c p) n -> p c n", p=128), xo, r=["xo"])

    if "kT" in dbg:
        dd_ = dbg_dump("kT", kT[:, :, :], (128, NKV, S), None)
        P.dma(dd_, kT[:, :, :], r=[("kT", j) for j in range(NCH)])
    if "kiT" in dbg:
        dd_ = dbg_dump("kiT", kiT2[:, :], (128, S), None)
        P.dma(dd_, kiT2[:, :], r=[("kiT", j) for j in range(NCH)])
    P.emit()
    st.close()
    return nc


_NC_CACHE = {}


def kernel(**inputs):
    inp = {k: np.asarray(v) for k, v in inputs.items()}
    x = inp["x"].astype(np.float32, copy=False)
    B, S, _ = x.shape
    if S not in _NC_CACHE:
        _NC_CACHE[S] = build(S, layers=(0, 1))
    nc = _NC_CACHE[S]
    cst, tab = host_consts(S)
    lc = np.stack([host_layer_consts(inp, l) for l in range(2)])
    nfin = np.ascontiguousarray(np.broadcast_to(inp["norm_final_w"].astype(np.float32)[None, :], (128, D)))
    shared = {"w_in": np.ascontiguousarray(inp["w_in"], dtype=np.float32),
              "w_proj_attn": np.ascontiguousarray(inp["w_proj_attn"], dtype=np.float32),
              "w_proj_ssd": np.ascontiguousarray(inp["w_proj_ssd"], dtype=np.float32),
              "w_out": np.ascontiguousarray(inp["w_out"], dtype=np.float32),
              "ffn_w_up": np.ascontiguousarray(inp["ffn_w_up"], dtype=np.float32),
              "ffn_w_down": np.ascontiguousarray(inp["ffn_w_down"], dtype=np.float32),
              "cst": cst, "tab": tab, "lc": lc, "nfin": nfin}
    in_maps = [dict(shared, x=np.ascontiguousarray(x[b])) for b in range(B)]
    res = run_bass_kernel_spmd(nc, in_maps, core_ids=list(range(B)))
    return np.stack([np.asarray(r["out"], dtype=np.float32) for r in res.results], axis=0)
```

```python
import contextlib
import numpy as np
import concourse.bass as bass
import concourse.mybir as mybir
from concourse.bass_utils import run_bass_kernel_spmd

F32 = mybir.dt.float32
BF16 = mybir.dt.bfloat16
AF = mybir.ActivationFunctionType
OP = mybir.AluOpType
AX = mybir.AxisListType

D = 1024
NH, HD, NKV = 8, 128, 2
IH, IDM = 8, 64
SSD_INNER, SSD_HD, SSD_H, SSD_G, SSD_N = 2048, 64, 32, 4, 128
CONV_DIM = 3072
FFN = 2816
EPS = 1e-6
IN_COLS = 9320
O_Q, O_K, O_V, O_QI, O_KI, O_WI, O_Z, O_XBC, O_DT, O_GA, O_GB = (
    0, 1024, 1280, 1536, 2048, 2112, 2120, 4168, 7240, 7272, 8296)
T = 512
NIT = 22

ENGS = ["pe", "act", "dve", "pool", "sp"]
NDMASEM = 24


class Prog:
    def __init__(self, nc):
        self.nc = nc
        self.ops = {e: [] for e in ENGS}
        self.last_w = {}
        self.readers = {}
        self.ndma = 0
        self.last_real = {}
        self.ps_last = {}

    @staticmethod
    def _isps(k):
        return isinstance(k, str) and k.startswith("ps") and k[2:].isdigit()

    def _deps(self, eng, r, w):
        deps = []
        for k in list(r) + list(w):
            if self._isps(k):
                is_w = k in w
                last = self.ps_last.get(k)
                if last is not None:
                    ref, lw = last
                    same = (ref[0] == "eng" and ref[1] == eng)
                    if (not same) or is_w or lw:
                        deps.append(ref)
        r = [k for k in r if not self._isps(k)]
        w = [k for k in w if not self._isps(k)]
        for k in r:
            lw = self.last_w.get(k)
            if lw is not None:
                deps.append(lw)
        for k in w:
            lw = self.last_w.get(k)
            if lw is not None:
                deps.append(lw)
            deps.extend(self.readers.get(k, ()))
        out = []
        for d in deps:
            if d[0] == "eng" and d[1] == "pe" and eng == "pe":
                continue
            if d not in out:
                out.append(d)
        return out

    def _commit(self, ref, r, w):
        for k in list(r) + list(w):
            if self._isps(k):
                self.ps_last[k] = (ref, k in w)
        r = [k for k in r if not self._isps(k)]
        w = [k for k in w if not self._isps(k)]
        for k in r:
            self.readers.setdefault(k, []).append(ref)
        for k in w:
            self.last_w[k] = ref
            self.readers[k] = []

    def _mark(self, deps):
        for d in deps:
            if d[0] == "eng":
                self.ops[d[1]][d[2]]["sig"] = True

    def op(self, eng, fn, r=(), w=()):
        deps = self._deps(eng, r, w)
        self._mark(deps)
        idx = len(self.ops[eng])
        self.ops[eng].append(dict(fn=fn, deps=deps, sig=False, dma=None))
        self.last_real[eng] = idx
        self._commit(("eng", eng, idx), r, w)

    def dma(self, out, in_, r=(), w=(), q="sp"):
        deps = self._deps(q, r, w)
        self._mark(deps)
        i = self.ndma
        self.ndma += 1
        if i >= NDMASEM:
            deps.append(("dma", i - NDMASEM))
        self.ops[q].append(dict(fn=lambda e: e.dma_start(out=out, in_=in_), deps=deps, sig=False, dma=i))
        self._commit(("dma", i), r, w)

    def barrier(self):
        deps = [("eng", e, i) for e, i in self.last_real.items()]
        deps += [("dma", i) for i in range(max(0, self.ndma - NDMASEM), self.ndma)]
        self._mark(deps)
        for e in ENGS:
            self.ops[e].append(dict(fn=None, deps=[d for d in deps if not (d[0] == "eng" and d[1] == e)],
                                    sig=False, dma=None))
        self.last_w.clear()
        self.readers.clear()
        self.ps_last.clear()

    def simulate(self):
        sigcnt = {}
        for e in ENGS:
            c = 0
            arr = []
            for o in self.ops[e]:
                if o["sig"]:
                    c += 1
                arr.append(c)
            sigcnt[e] = arr
        sem = {e: 0 for e in ENGS}
        dsem = [0] * NDMASEM
        ptr = {e: 0 for e in ENGS}
        progress = True
        while progress:
            progress = False
            for e in ENGS:
                while ptr[e] < len(self.ops[e]):
                    o = self.ops[e][ptr[e]]
                    ok = True
                    for d in o["deps"]:
                        if d[0] == "eng":
                            if sem[d[1]] < sigcnt[d[1]][d[2]]:
                                ok = False
                        else:
                            if dsem[d[1] % NDMASEM] < 16 * (d[1] // NDMASEM + 1):
                                ok = False
                    if not ok:
                        break
                    if o["dma"] is not None:
                        dsem[o["dma"] % NDMASEM] += 16
                    elif o["sig"] and o["fn"] is not None:
                        sem[e] += 1
                    ptr[e] += 1
                    progress = True
        stuck = {e: (ptr[e], len(self.ops[e])) for e in ENGS if ptr[e] < len(self.ops[e])}
        if stuck:
            for e in stuck:
                o = self.ops[e][ptr[e]]
                print("STUCK", e, ptr[e], o["deps"], "sig", o["sig"], "fn", o["fn"] is not None)
            raise RuntimeError("deadlock in semaphore protocol: %r" % stuck)
        print("simulate ok:", {e: len(self.ops[e]) for e in ENGS}, "dmas", self.ndma)

    def emit(self):
        nc = self.nc
        self.simulate()
        with contextlib.ExitStack() as st:
            esem = {e: st.enter_context(nc.semaphore("s_" + e)) for e in ENGS}
            dsem = [st.enter_context(nc.semaphore("d_%d" % i)) for i in range(NDMASEM)]
            block = st.enter_context(nc.Block())
            sigcnt = {}
            for e in ENGS:
                c = 0
                arr = []
                for o in self.ops[e]:
                    if o["sig"]:
                        c += 1
                    arr.append(c)
                sigcnt[e] = arr
            ndma = self.ndma

            def run(e, engobj):
                waited = {}
                for o in self.ops[e]:
                    for d in o["deps"]:
                        if d[0] == "eng":
                            sem = esem[d[1]]
                            val = sigcnt[d[1]][d[2]]
                            key = "e" + d[1]
                        else:
                            sem = dsem[d[1] % NDMASEM]
                            val = 16 * (d[1] // NDMASEM + 1)
                            key = "d%d" % (d[1] % NDMASEM)
                        if waited.get(key, 0) >= val:
                            continue
                        engobj.wait_ge(sem, val)
                        waited[key] = val
                    if o["fn"] is None:
                        continue
                    ins = o["fn"](engobj)
                    if o["dma"] is not None:
                        ins.then_inc(dsem[o["dma"] % NDMASEM], 16)
                    elif o["sig"]:
                        ins.then_inc(esem[e], 1)
                if e == "sp":
                    for s in range(min(NDMASEM, ndma)):
                        n = (ndma - 1 - s) // NDMASEM + 1
                        engobj.wait_ge(dsem[s], 16 * n)

            @block.sync
            def _(eng):
                run("sp", eng)

            @block.scalar
            def _(eng):
                run("act", eng)

            @block.vector
            def _(eng):
                run("dve", eng)

            @block.gpsimd
            def _(eng):
                run("pool", eng)

            @block.tensor
            def _(eng):
                run("pe", eng)


CST_COLS = {}


def _cst_layout():
    off = 0
    lay = {}
    for name, n in [("ident", 128), ("tri_le", 128), ("caus", 128), ("sgt", 128), ("ones", 128),
                    ("pow2", NIT + 2)]:
        lay[name] = (off, n)
        off += n
    return lay, off


def _lc_layout():
    off = 0
    lay = {}
    for name, n in [("nmix", 8), ("nffn", 8), ("snw", 16), ("xcw", 96), ("xcb", 24), ("fcw", 132),
                    ("fcb", 44), ("dtb", 32), ("alog", 32), ("dsk", 32)]:
        lay[name] = (off, n)
        off += n
    return lay, off


def host_consts(S):
    lay, n = _cst_layout()
    c = np.zeros((128, n), np.float32)
    i = np.arange(128)
    c[:, lay["ident"][0]:lay["ident"][0] + 128] = np.eye(128, dtype=np.float32)
    c[:, lay["tri_le"][0]:lay["tri_le"][0] + 128] = (i[:, None] <= i[None, :]).astype(np.float32)
    c[:, lay["caus"][0]:lay["caus"][0] + 128] = np.where(i[None, :] <= i[:, None], 0.0, -1e30).astype(np.float32)
    c[:, lay["sgt"][0]:lay["sgt"][0] + 128] = (i[:, None] > i[None, :]).astype(np.float32)
    c[:, lay["ones"][0]:lay["ones"][0] + 128] = 1.0
    c[:, lay["pow2"][0]:lay["pow2"][0] + NIT + 2] = (0.5 ** np.arange(1, NIT + 3))[None, :]
    nch = S // 128

    def tab(rot):
        inv = (500000.0 ** (-np.arange(0, rot, 2, dtype=np.float32) / rot)).astype(np.float32)
        ang = np.arange(S, dtype=np.float32)[:, None] * inv[None, :]
        return np.cos(ang).astype(np.float32), np.sin(ang).astype(np.float32)

    ca, sa = tab(32)
    ci, si = tab(16)
    tb = np.concatenate([ca, sa, ci, si], axis=1).reshape(nch, 128, 48).transpose(1, 0, 2)
    return c, np.ascontiguousarray(tb.reshape(128, nch * 48))


def host_layer_consts(inp, l):
    lay, n = _lc_layout()
    c = np.zeros((128, n), np.float32)

    def put(name, arr):
        o, m = lay[name]
        c[:, o:o + m] = arr.reshape(128, m)

    put("nmix", np.asarray(inp["norm_mix_w"][l]).reshape(8, 128).T)
    put("nffn", np.asarray(inp["norm_ffn_w"][l]).reshape(8, 128).T)
    put("snw", np.asarray(inp["ssd_norm_w"][l]).reshape(16, 128).T)
    put("xcw", np.asarray(inp["ssd_conv_w"][l]).reshape(4, 24, 128).transpose(2, 1, 0))
    put("xcb", np.asarray(inp["ssd_conv_b"][l]).reshape(24, 128).T)
    put("fcw", np.asarray(inp["ffn_conv_w"][l]).reshape(3, 44, 128).transpose(2, 1, 0))
    put("fcb", np.asarray(inp["ffn_conv_b"][l]).reshape(44, 128).T)
    put("dtb", np.broadcast_to(np.asarray(inp["ssd_dt_bias"][l])[None, :], (128, 32)))
    put("alog", np.broadcast_to(np.asarray(inp["ssd_a_log"][l])[None, :], (128, 32)))
    put("dsk", np.broadcast_to(np.asarray(inp["ssd_d"][l])[None, :], (128, 32)))
    return c


def build(S, layers=(0, 1), final=True, dbg=(), stop=99):
    nc = bass.Bass("TRN2", target_bir_lowering=False)
    NCH = S // 128
    NST = S // T
    TOPK = min(256, S // 4)
    L = 2
    dt_in = lambda name, shape: nc.dram_tensor(name, list(shape), F32, kind="ExternalInput").ap()
    x_d = dt_in("x", (S, D))
    w_in_d = dt_in("w_in", (L, D, IN_COLS))
    w_pa_d = dt_in("w_proj_attn", (L, D, D))
    w_pb_d = dt_in("w_proj_ssd", (L, SSD_INNER, D))
    w_out_d = dt_in("w_out", (L, D, D))
    w_up_d = dt_in("ffn_w_up", (L, D, 2 * FFN))
    w_dn_d = dt_in("ffn_w_down", (L, FFN, D))
    clay, ncst = _cst_layout()
    llay, nlc = _lc_layout()
    cst_d = dt_in("cst", (128, ncst))
    tab_d = dt_in("tab", (128, NCH * 48))
    lc_d = dt_in("lc", (L, 128, nlc))
    nfin_d = dt_in("nfin", (128, D))
    out_d = nc.dram_tensor("out", [S, D], F32, kind="ExternalOutput").ap()
    xmid_d = nc.dram_tensor("xmid", [S, D], F32, kind="Internal").ap()
    xl1_d = nc.dram_tensor("xl1", [S, D], F32, kind="Internal").ap()
    dbg_d = {}

    P = Prog(nc)
    st = contextlib.ExitStack()
    sb = lambda name, shape, dt: st.enter_context(nc.sbuf_tensor("sb_" + name, list(shape), dt))

    kT = sb("kT", (128, NKV, S), BF16)
    Vc = sb("Vc", (128, NCH, NKV, 129), BF16)
    kiT2 = sb("kiT2", (128, S), BF16)
    hst = sb("hst", (128, SSD_INNER), F32)
    wst = [sb("wst%d" % i, (128, 4, 512), F32) for i in range(2)]
    wbf = [sb("wbf%d" % i, (128, 8, 512), BF16) for i in range(2)]
    hT = sb("hT", (128, 8, T), BF16)
    aT = sb("aT", (128, 8, T), BF16)
    bT = sb("bT", (128, 16, T), BF16)
    cst = sb("cst", (128, ncst), F32)
    lcs = sb("lcs", (128, nlc), F32)
    tabs = sb("tabs", (128, 4, 48), F32)
    identb = sb("identb", (128, 128), BF16)
    trib = sb("trib", (128, 128), F32)
    xtail = sb("xtail", (128, 24, 3), F32)
    ftail = sb("ftail", (128, 44, 2), F32)
    negA = sb("negA", (128, 32), F32)
    epsT = sb("epsT", (128, 1), F32)
    negb = sb("negb", (128, 1), F32)
    AR_BYTES = 69 * 1024
    arena = sb("arena", (128, AR_BYTES // 4), F32)
    psum = st.enter_context(nc.psum_tensor("psum", [128, 8, 512], F32))

    def cv(name, j0=0, j1=None):
        o, n = clay[name]
        j1 = n if j1 is None else j1
        return cst[:, o + j0:o + j1]

    def lv(name, j0=0, j1=None):
        o, n = llay[name]
        j1 = n if j1 is None else j1
        return lcs[:, o + j0:o + j1]

    class View:
        pass

    def aview(off, shape, dt):
        n = int(np.prod(shape[1:]))
        esz = 4 if dt == F32 else 2
        assert off % 4 == 0 and off + n * esz <= AR_BYTES, (off, shape)
        nf = (n * esz + 3) // 4
        ap = arena[:, off // 4: off // 4 + nf]
        if dt != F32:
            ap = ap.bitcast(dt)
        if len(shape) == 3:
            ap = ap.rearrange("p (a b) -> p a b", a=shape[1])
        elif len(shape) == 4:
            ap = ap.rearrange("p (a b c) -> p a b c", a=shape[1], b=shape[2])
        return ap

    def psb(b):
        return psum[:, b, :].bitcast(BF16)

    def bc(ap, shape):
        return ap.to_broadcast(list(shape))

    wcnt = [0]
    hcnt = [0]

    def load_slab(wd, r0, nk, c0, ncols):
        slot = wcnt[0] % 2
        wcnt[0] += 1
        for h0 in range(0, nk, 4):
            hn_ = min(4, nk - h0)
            hs = hcnt[0] % 2
            hcnt[0] += 1
            src = wd[r0 + h0 * 128: r0 + (h0 + hn_) * 128, c0:c0 + ncols].rearrange("(k p) n -> p k n", p=128)
            P.dma(wst[hs][:, 0:hn_, 0:ncols], src, w=["wst%d" % hs])
            if hcnt[0] % 3 == 0:
                P.op("pool", lambda e, hs=hs, h0=h0, hn_=hn_, slot=slot: e.tensor_copy(
                    out=wbf[slot][:, h0:h0 + hn_, 0:ncols], in_=wst[hs][:, 0:hn_, 0:ncols]),
                    r=["wst%d" % hs], w=["wbf%d" % slot])
            else:
                P.op("act", lambda e, hs=hs, h0=h0, hn_=hn_, slot=slot: e.activation(
                    out=wbf[slot][:, h0:h0 + hn_, 0:ncols], in_=wst[hs][:, 0:hn_, 0:ncols], func=AF.Copy),
                    r=["wst%d" % hs], w=["wbf%d" % slot])
        return slot

    pending = []

    def sched(loads, fn, extra=0):
        pending.append((loads, fn, extra))

    def flush():
        tasks = pending[:]
        del pending[:]
        loaded = {}

        def do_load(i):
            if i not in loaded:
                loaded[i] = [load_slab(*a) for a in tasks[i][0]]

        for i, (loads, fn, extra) in enumerate(tasks):
            do_load(i)
            if i + 1 < len(tasks) and len(loads) + extra + len(tasks[i + 1][0]) <= 2:
                do_load(i + 1)
            fn(loaded[i])

    bankctr = [0]

    def mm(out, lhsT, rhs, start, stop, r, w):
        P.op("pe", lambda e: e.matmul(out, lhsT, rhs, start=start, stop=stop), r=r, w=w)

    def transpose(out, in_, r, w):
        P.op("pe", lambda e: e.transpose(out, in_, identb[:, :]), r=list(r) + ["identb"], w=w)

    def dbg_dump(name, ap, shape, keys):
        if name not in dbg:
            return
        if name not in dbg_d:
            dbg_d[name] = nc.dram_tensor("dbg_" + name, list(shape), ap.dtype, kind="ExternalOutput").ap()
        return dbg_d[name]

    P.dma(cst[:, :], cst_d[:, :], w=["cst"])
    P.op("dve", lambda e: e.tensor_copy(out=identb[:, :], in_=cv("ident")), r=["cst"], w=["identb"])
    P.op("dve", lambda e: e.memset(epsT[:, :], EPS), w=["epsT"])
    P.op("dve", lambda e: e.memset(negb[:, :], -30000.0), w=["negb"])
    P.op("pool", lambda e: e.memset(Vc[:, :, :, 128:129], 1.0), w=["Vc"])

    mix_scale = float(HD) ** -0.5

    def rmsnorm_to_hT(src_d, st_i, nw_name, xin_views, hn_views, small, key_prefix):
        for c in range(4):
            gc = st_i * 4 + c
            xin = xin_views[c % 2]
            xk = "xin%d" % (c % 2)
            hn_view = hn_views[c % 2]
            hk_ = "hn%d" % (c % 2)
            P.dma(xin, src_d[gc * 128:(gc + 1) * 128, :], r=[(key_prefix, gc)], w=[xk])
            ss = small[:, c:c + 1]
            rt = small[:, 4 + c:5 + c]
            rstd = small[:, 8 + c:9 + c]
            P.op("act", lambda e, xin=xin, ss=ss, hn_view=hn_view: e.activation(out=hn_view, in_=xin, func=AF.Square, accum_out=ss),
                 r=[xk], w=[hk_, "nsm%d" % c])
            P.op("act", lambda e, ss=ss, rt=rt: e.activation(out=rt, in_=ss, func=AF.Ln, bias=epsT[:, :], scale=1.0 / D),
                 r=["nsm%d" % c, "epsT"], w=["nsm%d" % c])
            P.op("act", lambda e, rt=rt, rstd=rstd: e.activation(out=rstd, in_=rt, func=AF.Exp, scale=-0.5),
                 r=["nsm%d" % c], w=["nsm%d" % c])
            P.op("dve", lambda e, xin=xin, rstd=rstd, hn_view=hn_view: e.tensor_scalar(out=hn_view, in0=xin, scalar1=rstd, scalar2=None,
                                                                                        op0=OP.mult), r=[xk, "nsm%d" % c], w=[hk_])
            b = 6 + (c % 2)
            for k in range(8):
                transpose(psb(b)[:, k * 128:(k + 1) * 128], hn_view[:, k * 128:(k + 1) * 128], r=[hk_], w=["ps%d" % b])
            P.op("dve", lambda e, b=b, c=c: e.tensor_tensor(
                out=hT[:, :, c * 128:(c + 1) * 128],
                in0=psb(b).rearrange("p (k t) -> p k t", k=8),
                in1=bc(lv(nw_name).unsqueeze(2), (128, 8, 128)), op=OP.mult),
                r=["ps%d" % b, "lcs"], w=["hT"])

    def proj_tok(lhsT_fn, nk_list, slot_list, ncols, evac, banks, lkeys):
        prev_post = None
        for c in range(4):
            b = banks[c % len(banks)]
            kk = 0
            tot = sum(nk_list)
            for si, slot in enumerate(slot_list):
                for k in range(nk_list[si]):
                    mm(psum[:, b, 0:ncols], lhsT_fn(kk, c), wbf[slot][:, k, 0:ncols], start=(kk == 0), stop=(kk == tot - 1),
                       r=lkeys + ["wbf%d" % slot], w=["ps%d" % b])
                    kk += 1
            post = evac(c, b)
            if prev_post is not None:
                prev_post()
            prev_post = post if callable(post) else None
        if prev_post is not None:
            prev_post()

    for l in layers:
        src_d = x_d if l == layers[0] else xl1_d
        last = (l == layers[-1])
        P.barrier()
        P.dma(lcs[:, :], lc_d[l], w=["lcs"])
        P.op("act", lambda e: e.activation(out=negA[:, :], in_=lv("alog"), func=AF.Exp), r=["lcs"], w=["negA"])
        P.op("dve", lambda e: e.tensor_scalar(out=negA[:, :], in0=negA[:, :], scalar1=-1.0, scalar2=None, op0=OP.mult),
             r=["negA"], w=["negA"])
        P.op("pool", lambda e: e.memset(hst[:, :], 0.0), w=["hst"])
        P.op("pool", lambda e: e.memset(xtail[:, :, :], 0.0), w=["xtail"])
        P.op("pool", lambda e: e.memset(ftail[:, :, :], 0.0), w=["ftail"])

        for st_i in range(NST):
            P.barrier()
            scores = aview(0, (128, S), F32)
            qT = aview(16384, (128, 8, T), BF16)
            qiT = aview(24576, (128, 4, T), BF16)
            maskT = aview(28672, (128, NCH if NCH <= 32 else 32, 128), BF16)
            junkb = aview(28672, (128, 4096), BF16)
            xin_v = [aview(36864, (128, D), F32), aview(40960, (128, D), F32)]
            hn_v = [aview(45056, (128, D), BF16), aview(68096, (128, D), BF16)]
            qtok = [aview(47104, (128, 4, 128), BF16), aview(48128, (128, 4, 128), BF16)]
            qitf = aview(49152, (128, 8, 64), F32)
            rbuf = [aview(53248 + 1024 * i, (128, 512), BF16) for i in range(4)]
            maskb = [aview(57344 + 1024 * i, (128, 512), BF16) for i in range(2)]
            Eb = [aview(59392 + 1024 * i, (128, 4, 128), BF16) for i in range(2)]
            dsg = aview(61440, (128, 8, 128), BF16)
            hb_ctr = [0]
            a_tok = aview(63488, (128, 8, 128), BF16)
            small = aview(65536, (128, 256), F32)
            ropet = [aview(66560 + 256 * i, (128, 64), F32) for i in range(2)] + [aview(70144 + 256 * i, (128, 64), F32) for i in range(2)]
            qitok = aview(67072, (128, 512), BF16)
            wabs = small[:, 16:48].rearrange("p (c h) -> p c h", c=4)
            wsgn = small[:, 48:80].rearrange("p (c h) -> p c h", c=4)
            kitok = small[:, 192:256].bitcast(BF16)
            ktok = qtok[1]

            P.dma(tabs[:, :, :], tab_d[:, st_i * 192:(st_i + 1) * 192].rearrange("p (c n) -> p c n", c=4), w=["tabs"])
            rmsnorm_to_hT(src_d, st_i, "nmix", xin_v, hn_v, small, "xsrc%d" % l)

            lhs_h = lambda k, c: hT[:, k, c * 128:(c + 1) * 128]

            def rope(c, src3, dst3, half, cos_o, sin_o, nh, rk, wk):
                cosv = bc(tabs[:, c, cos_o:cos_o + half].unsqueeze(1), (128, nh, half))
                sinv = bc(tabs[:, c, sin_o:sin_o + half].unsqueeze(1), (128, nh, half))
                x1 = src3[:, :, 0:half]
                x2 = src3[:, :, half:2 * half]
                t1 = ropet[0][:, 0:nh * half].rearrange("p (a b) -> p a b", a=nh)
                t2 = ropet[1][:, 0:nh * half].rearrange("p (a b) -> p a b", a=nh)
                rr = list(rk) + ["tabs"]
                P.op("dve", lambda e: e.tensor_tensor(out=t1, in0=x1, in1=cosv, op=OP.mult), r=rr, w=["rt1"])
                P.op("dve", lambda e: e.tensor_tensor(out=t2, in0=x2, in1=sinv, op=OP.mult), r=rr, w=["rt2"])
                P.op("dve", lambda e: e.tensor_tensor(out=dst3[:, :, 0:half], in0=t1, in1=t2, op=OP.subtract),
                     r=["rt1", "rt2"], w=wk)
                t3 = ropet[2][:, 0:nh * half].rearrange("p (a b) -> p a b", a=nh)
                t4 = ropet[3][:, 0:nh * half].rearrange("p (a b) -> p a b", a=nh)
                P.op("dve", lambda e: e.tensor_tensor(out=t3, in0=x2, in1=cosv, op=OP.mult), r=rr, w=["rt3"])
                P.op("dve", lambda e: e.tensor_tensor(out=t4, in0=x1, in1=sinv, op=OP.mult), r=rr, w=["rt4"])
                P.op("dve", lambda e: e.tensor_tensor(out=dst3[:, :, half:2 * half], in0=t3, in1=t4, op=OP.add),
                     r=["rt3", "rt4"], w=wk)

            for qs in range(2 if stop >= 2 else 0):

                def evac_q(c, b, qs=qs):
                    import os
                    SUB = int(os.environ.get("SUB", "9"))
                    qt = qtok[c % 2]
                    qk_ = "qtok%d" % (c % 2)
                    pv = psum[:, b, :].rearrange("p (h d) -> p h d", h=4)
                    if SUB <= 2:
                        return
                    P.op("act", lambda e: e.activation(out=qt[:, :, 32:128], in_=pv[:, :, 32:128], func=AF.Copy),
                         r=["ps%d" % b], w=[qk_])
                    if SUB <= 3:
                        return
                    rope(c, pv, qt, 16, 0, 16, 4, ["ps%d" % b], [qk_])
                    if SUB <= 4:
                        return
                    def post():
                        tb = 4 + (c % 2)
                        for h in range(4):
                            transpose(psb(tb)[:, h * 128:(h + 1) * 128], qt[:, h, :], r=[qk_], w=["ps%d" % tb])
                        P.op("act", lambda e: e.activation(out=qT[:, qs * 4:(qs + 1) * 4, c * 128:(c + 1) * 128],
                                                           in_=psb(tb)[:, 0:512].rearrange("p (h t) -> p h t", h=4), func=AF.Copy),
                             r=["ps%d" % tb], w=["qT"])
                    return post

                sched([(w_in_d[l], 0, 8, O_Q + qs * 512, 512)],
                      lambda s, evac_q=evac_q: proj_tok(lhs_h, [8], s, 512, evac_q, [0, 1, 2, 3], ["hT"]))


            def evac_kv(c, b):
                gc = st_i * 4 + c
                pv = psum[:, b, 0:256].rearrange("p (h d) -> p h d", h=2)
                kt = qtok[c % 2][:, 0:2, :]
                qk_ = "qtok%d" % (c % 2)
                P.op("act", lambda e: e.activation(out=kt[:, :, 32:128], in_=pv[:, :, 32:128], func=AF.Copy),
                     r=["ps%d" % b], w=[qk_])
                rope(c, pv, kt, 16, 0, 16, 2, ["ps%d" % b], [qk_])
                P.op("act", lambda e: e.activation(out=Vc[:, gc, :, 0:128],
                                                   in_=psum[:, b, 256:512].rearrange("p (h d) -> p h d", h=2), func=AF.Copy),
                     r=["ps%d" % b], w=[("Vc", gc)])
                def post():
                    tb = 4 + (c % 2)
                    for h in range(2):
                        transpose(psb(tb)[:, h * 128:(h + 1) * 128], kt[:, h, :], r=[qk_], w=["ps%d" % tb])
                    P.op("act", lambda e: e.activation(out=kT[:, :, gc * 128:(gc + 1) * 128],
                                                       in_=psb(tb)[:, 0:256].rearrange("p (h t) -> p h t", h=2), func=AF.Copy),
                         r=["ps%d" % tb], w=[("kT", gc)])
                return post

            if stop >= 3:
                sched([(w_in_d[l], 0, 8, O_K, 512)], lambda s: proj_tok(lhs_h, [8], s, 512, evac_kv, [0, 1, 2, 3], ["hT"]))


            def evac_ki(c, b):
                gc = st_i * 4 + c
                pv = psum[:, b, 0:64].rearrange("p (h d) -> p h d", h=1)
                k3 = kitok[:, 0:64].rearrange("p (h d) -> p h d", h=1)
                P.op("act", lambda e: e.activation(out=k3[:, :, 16:64], in_=pv[:, :, 16:64], func=AF.Copy),
                     r=["ps%d" % b], w=["kitok"])
                rope(c, pv, k3, 8, 32, 40, 1, ["ps%d" % b], ["kitok"])
                P.op("dve", lambda e: e.tensor_copy(out=kitok[:, 64:128], in_=kitok[:, 0:64]), r=["kitok"], w=["kitok"])
                P.op("act", lambda e: e.activation(out=wabs[:, c, :], in_=psum[:, b, 64:72], func=AF.Abs),
                     r=["ps%d" % b], w=["wabs%d" % c])
                P.op("act", lambda e: e.activation(out=wsgn[:, c, :], in_=psum[:, b, 64:72], func=AF.Sign),
                     r=["ps%d" % b], w=["wsgn%d" % c])
                tb = 4 + (c % 2)
                transpose(psb(tb)[:, 0:128], kitok[:, :], r=["kitok"], w=["ps%d" % tb])
                P.op("act", lambda e: e.activation(out=kiT2[:, gc * 128:(gc + 1) * 128], in_=psb(tb)[:, 0:128], func=AF.Copy),
                     r=["ps%d" % tb], w=[("kiT", gc)])

            if stop >= 4:
                sched([(w_in_d[l], 0, 8, O_KI, 72)], lambda s: proj_tok(lhs_h, [8], s, 72, evac_ki, [0, 1, 2, 3], ["hT"]))


            def evac_qi(c, b):
                pv = psum[:, b, :].rearrange("p (h d) -> p h d", h=8)
                P.op("act", lambda e: e.activation(out=qitf[:, :, 16:64], in_=pv[:, :, 16:64], func=AF.Copy),
                     r=["ps%d" % b], w=["qitf"])
                rope(c, pv, qitf, 8, 32, 40, 8, ["ps%d" % b], ["qitf"])
                P.op("dve", lambda e: e.tensor_tensor(out=qitok.rearrange("p (h d) -> p h d", h=8), in0=qitf,
                                                      in1=bc(wabs[:, c, :].unsqueeze(2), (128, 8, 64)), op=OP.mult),
                     r=["qitf", "wabs%d" % c], w=["qitok"])
                tb = 4 + (c % 2)
                for j in range(4):
                    transpose(psb(tb)[:, j * 128:(j + 1) * 128], qitok[:, j * 128:(j + 1) * 128], r=["qitok"], w=["ps%d" % tb])
                P.op("act", lambda e: e.activation(out=qiT[:, :, c * 128:(c + 1) * 128],
                                                   in_=psb(tb)[:, 0:512].rearrange("p (h t) -> p h t", h=4), func=AF.Copy),
                     r=["ps%d" % tb], w=["qiT"])

            if stop >= 5:
                sched([(w_in_d[l], 0, 8, O_QI, 512)], lambda s: proj_tok(lhs_h, [8], s, 512, evac_qi, [0, 1, 2, 3], ["hT"]))
            flush()

            P.barrier()
            scores_v = [aview(0, (128, S), F32), aview(36864, (128, S), F32)]
            maskT_v = [aview(28672, (128, 32, 128), BF16),
                       wst[0][:, :, :].rearrange("p a b -> p (a b)").bitcast(BF16).rearrange("p (j t) -> p j t", j=32)]
            mkey = ["maskT0", "wst0"]

            def geom(c):
                gc = st_i * 4 + c
                nk = gc + 1
                n = nk * 128
                return gc, nk, n, slice(c * 128, (c + 1) * 128), (n + 511) // 512

            def st1(c):
                gc, nk, n, tq, ngrp = geom(c)
                scores = scores_v[c % 2]
                kkeys = [("kiT", j) for j in range(nk)]
                for h in range(8):
                    P.op("pool", lambda e, h=h: e.tensor_scalar(out=dsg[:, h, :], in0=identb[:, :], scalar1=wsgn[:, c, h:h + 1],
                                                                scalar2=None, op0=OP.mult),
                         r=["identb", "wsgn%d" % c], w=["dsg"])
                for kg in range(ngrp):
                    w_ = min(512, n - kg * 512)
                    sb_ = 2
                    pend = None
                    for h in range(8):
                        b = hb_ctr[0] % 2
                        hb_ctr[0] += 1
                        pr = (h % 2) * 64
                        mm(psum[:, b, 0:w_], qiT[pr:pr + 64, h // 2, tq], kiT2[pr:pr + 64, kg * 512:kg * 512 + w_], True, True,
                           r=["qiT"] + kkeys, w=["ps%d" % b])
                        rb = rbuf[h % 4]
                        P.op("act", lambda e, b=b, rb=rb, w_=w_: e.activation(out=rb[:, 0:w_], in_=psum[:, b, 0:w_], func=AF.Relu),
                             r=["ps%d" % b], w=["rbuf%d" % (h % 4)])
                        if pend is not None:
                            pend()
                        pend = (lambda h=h, rb=rb: mm(psum[:, sb_, 0:w_], dsg[:, h, :], rb[:, 0:w_], (h == 0), (h == 7),
                                                      r=["dsg", "rbuf%d" % (h % 4)], w=["ps%d" % sb_]))
                    pend()
                    sc = scores[:, kg * 512:kg * 512 + w_]
                    P.op("act", lambda e, sb_=sb_, sc=sc, w_=w_: e.activation(out=sc, in_=psum[:, sb_, 0:w_], func=AF.Copy),
                         r=["ps%d" % sb_], w=[("sc", c % 2, kg)])

            def st2(c):
                gc, nk, n, tq, ngrp = geom(c)
                scores = scores_v[c % 2]
                maskT = maskT_v[c % 2]
                junkb = maskT.rearrange("p j t -> p (j t)")
                mk = mkey[c % 2]
                sckeys = [("sc", c % 2, kg) for kg in range(ngrp)]
                lastk = ("sc", c % 2, ngrp - 1)
                thr = small[:, 100 + (c % 2):101 + (c % 2)]
                tk = "thr%d" % (c % 2)
                if (gc * 128 + 128) <= TOPK:
                    P.op("dve", lambda e: e.tensor_tensor(out=scores[:, n - 128:n], in0=scores[:, n - 128:n], in1=cv("caus"),
                                                          op=OP.add), r=[lastk, "cst"], w=[lastk])
                    P.op("dve", lambda e: e.memset(thr, -1e29), w=[tk])
                else:
                    mx = small[:, 102:103]
                    mn = small[:, 103:104]
                    w0 = small[:, 104:105]
                    mid = small[:, 105:106]
                    cnt = small[:, 106:107]
                    dd = small[:, 107:108]
                    Wk = small[:, 110:110 + NIT + 2]
                    P.op("dve", lambda e: e.tensor_reduce(out=mx, in_=scores[:, 0:n], axis=AX.X, op=OP.max), r=sckeys, w=["bs_mx"])
                    P.op("dve", lambda e: e.tensor_reduce(out=mn, in_=scores[:, 0:n], axis=AX.X, op=OP.min), r=sckeys, w=["bs_mn"])
                    P.op("dve", lambda e: e.tensor_tensor(out=scores[:, n - 128:n], in0=scores[:, n - 128:n], in1=cv("caus"),
                                                          op=OP.add), r=[lastk, "cst"], w=[lastk])
                    P.op("dve", lambda e: e.tensor_tensor(out=w0, in0=mx, in1=mn, op=OP.subtract), r=["bs_mx", "bs_mn"], w=["bs_w0"])
                    P.op("dve", lambda e: e.tensor_scalar(out=Wk, in0=cv("pow2"), scalar1=w0, scalar2=None, op0=OP.mult),
                         r=["bs_w0", "cst"], w=["bs_wk"])
                    P.op("dve", lambda e: e.tensor_tensor(out=mid, in0=mn, in1=Wk[:, 0:1], op=OP.add), r=["bs_mn", "bs_wk"], w=["bs_mid"])
                    for it in range(NIT):
                        P.op("dve", lambda e: e.tensor_scalar(out=junkb[:, 0:n], in0=scores[:, 0:n], scalar1=mid, scalar2=None,
                                                              op0=OP.is_ge, op1=OP.add, accum_out=cnt),
                             r=sckeys + ["bs_mid"], w=[mk, "bs_cnt"])
                        lastit = (it == NIT - 1)
                        P.op("dve", lambda e, lastit=lastit: e.tensor_scalar(out=dd, in0=cnt, scalar1=float(TOPK),
                                                                              scalar2=(1.0 if lastit else 0.5),
                                                                              op0=OP.is_ge, op1=OP.subtract),
                             r=["bs_cnt"], w=["bs_dd"])
                        dst = thr if lastit else mid
                        P.op("dve", lambda e, it=it, dst=dst: e.scalar_tensor_tensor(
                            out=dst, in0=dd, scalar=Wk[:, it:it + 1], in1=mid, op0=OP.mult, op1=OP.add),
                            r=["bs_dd", "bs_wk", "bs_mid"], w=[tk if lastit else "bs_mid"])
                for kg in range(ngrp):
                    w_ = min(512, n - kg * 512)
                    mb = maskb[kg % 2]
                    P.op("dve", lambda e, mb=mb, kg=kg, w_=w_: e.tensor_scalar(
                        out=mb[:, 0:w_], in0=scores[:, kg * 512:kg * 512 + w_], scalar1=thr, scalar2=None, op0=OP.is_ge),
                        r=[("sc", c % 2, kg), tk], w=["maskb%d" % (kg % 2)])
                    tb = 3
                    nj = w_ // 128
                    for jj in range(nj):
                        transpose(psb(tb)[:, jj * 128:(jj + 1) * 128], mb[:, jj * 128:(jj + 1) * 128],
                                  r=["maskb%d" % (kg % 2)], w=["ps%d" % tb])
                    P.op("act", lambda e, tb=tb, kg=kg, nj=nj: e.activation(
                        out=maskT[:, kg * 4:kg * 4 + nj, :], in_=psb(tb)[:, 0:nj * 128].rearrange("p (j t) -> p j t", j=nj),
                        func=AF.Identity, scale=30000.0, bias=negb[:, :]), r=["ps%d" % tb, "negb"], w=[mk])
                if "sc" in dbg and gc == NCH - 1:
                    dd_ = dbg_dump("sc", scores, (128, S), None)
                    P.dma(dd_, scores, r=sckeys)
                    dd_ = dbg_dump("thr", thr, (128, 1), None)
                    P.dma(dd_, thr, r=[tk])

            def st3a(c):
                gc, nk, n, tq, ngrp = geom(c)
                maskT = maskT_v[c % 2]
                mk = mkey[c % 2]
                pend_pv = [None]
                for kvh in range(2):
                    if kvh == 1:
                        st3b_half(c, 0)
                    for j in range(nk):
                        lb = 3 if j % 2 == 0 else 2
                        mm(psum[:, lb, :], kT[:, kvh, j * 128:(j + 1) * 128], qT[:, kvh * 4:(kvh + 1) * 4, tq], True, False,
                           r=[("kT", j), "qT"], w=["ps%d" % lb])
                        mm(psum[:, lb, :], identb[:, :], bc(maskT[:, j, :].unsqueeze(1), (128, 4, 128)), False, True,
                           r=["identb", mk], w=["ps%d" % lb])
                        Ev = Eb[j % 2]
                        P.op("act", lambda e, lb=lb, Ev=Ev: e.activation(
                            out=Ev, in_=psum[:, lb, :].rearrange("p (g t) -> p g t", g=4), func=AF.Exp, scale=mix_scale),
                            r=["ps%d" % lb], w=["E%d" % (j % 2)])
                        if pend_pv[0] is not None:
                            pend_pv[0]()

                        def pv(j=j, Ev=Ev, kvh=kvh):
                            for g in range(4):
                                mm(psum[:, 4 + g, 0:129], Ev[:, g, :], Vc[:, j, kvh, :], (j == 0), (j == nk - 1),
                                   r=["E%d" % (j % 2), ("Vc", j), "Vc"], w=["ps%d" % (4 + g)])
                        pend_pv[0] = pv
                    pend_pv[0]()
                    pend_pv[0] = None

            def st3b_half(c, kvh):
                for g in range(4):
                    h = kvh * 4 + g
                    rden = small[:, 140 + h:141 + h]
                    P.op("dve", lambda e, g=g, rden=rden: e.reciprocal(out=rden, in_=psum[:, 4 + g, 128:129]),
                         r=["ps%d" % (4 + g)], w=["rden%d" % h])
                    P.op("act", lambda e, g=g, h=h, rden=rden: e.activation(
                        out=a_tok[:, h, :], in_=psum[:, 4 + g, 0:128], func=AF.Copy, scale=rden),
                        r=["ps%d" % (4 + g), "rden%d" % h], w=["a_tok"])

            def st3b(c):
                st3b_half(c, 1)
                tb = 3
                for h in range(8):
                    transpose(psb(tb)[:, h * 128:(h + 1) * 128], a_tok[:, h, :], r=["a_tok"], w=["ps%d" % tb])
                P.op("act", lambda e, tb=tb, c=c: e.activation(
                    out=aT[:, :, c * 128:(c + 1) * 128], in_=psb(tb).rearrange("p (h t) -> p h t", h=8), func=AF.Copy),
                    r=["ps%d" % tb], w=["aT"])

            if stop >= 6:
                st1(0)
                st1(1)
                st2(0)
                st3a(0)
                st1(2)
                st2(1)
                st3b(0)
                st3a(1)
                st1(3)
                st2(2)
                st3b(1)
                st3a(2)
                st2(3)
                st3b(2)
                st3a(3)
                st3b(3)

            if "aT" in dbg:
                dd_ = dbg_dump("aT", aT[:, :, :], (NST, 128, 8, T), None)
                P.dma(dd_[st_i], aT[:, :, :], r=["aT"])
            if "qT" in dbg:
                dd_ = dbg_dump("qT", qT, (NST, 128, 8, T), None)
                P.dma(dd_[st_i], qT, r=["qT"])
            if "hT" in dbg:
                dd_ = dbg_dump("hT", hT[:, :, :], (NST, 128, 8, T), None)
                P.dma(dd_[st_i], hT[:, :, :], r=["hT"])

            if stop >= 7:
                P.barrier()
                BT = aview(0, (128, 4, T), BF16)
                CT = aview(4096, (128, 4, T), BF16)
                Btok = aview(8192, (128, 4, 512), BF16)
                xacc = [aview(12288 + 2048 * i, (128, 512), F32) for i in range(2)]
                xact = [aview(16384 + 1024 * i, (128, 512), BF16) for i in range(2)]
                dtt = aview(18432, (128, 4, 32), F32)
                adt = aview(18944, (128, 4, 32), F32)

                def conv_fm(cc, b, wname, bname, ntap, tail, dst_silu, tailkey):
                    u = psum[:, b, :]
                    nx_ = len(xacc)
                    acc = xacc[cc % nx_]
                    ak = "xacc%d" % (cc % nx_)
                    wv = lambda j: lv(wname, cc * ntap + j, cc * ntap + j + 1)
                    P.op("act", lambda e: e.activation(out=acc, in_=u, func=AF.Identity, bias=lv(bname, cc, cc + 1),
                                                       scale=wv(ntap - 1)), r=["ps%d" % b, "lcs"], w=[ak])
                    for j in range(ntap - 1):
                        sh = ntap - 1 - j
                        P.op("dve", lambda e, j=j, sh=sh: e.scalar_tensor_tensor(
                            out=acc[:, sh:512], in0=u[:, 0:512 - sh], scalar=wv(j), in1=acc[:, sh:512], op0=OP.mult, op1=OP.add),
                            r=["ps%d" % b, "lcs", ak], w=[ak])
                        P.op("dve", lambda e, j=j, sh=sh: e.scalar_tensor_tensor(
                            out=acc[:, 0:sh], in0=tail[:, cc, ntap - 1 - sh:ntap - 1], scalar=wv(j), in1=acc[:, 0:sh],
                            op0=OP.mult, op1=OP.add), r=[tailkey, "lcs", ak], w=[ak])
                    P.op("dve", lambda e: e.tensor_copy(out=tail[:, cc, :], in_=u[:, 512 - (ntap - 1):512]),
                         r=["ps%d" % b], w=[tailkey])
                    return acc, ak

                def proj_fm(slot, ncc, cc0, handler):
                    prev_post = None
                    for j in range(ncc):
                        b = j % 4
                        for k in range(8):
                            mm(psum[:, b, :], wbf[slot][:, k, j * 128:(j + 1) * 128], hT[:, k, :], (k == 0), (k == 7),
                               r=["hT", "wbf%d" % slot], w=["ps%d" % b])
                        post = handler(cc0 + j, b)
                        if prev_post is not None:
                            prev_post()
                        prev_post = post if callable(post) else None
                    if prev_post is not None:
                        prev_post()

                def h_bc(cc, b):
                    acc, ak = conv_fm(cc, b, "xcw", "xcb", 4, xtail, None, "xtail")
                    g = (cc - 16) % 4
                    if cc < 20:
                        P.op("act", lambda e: e.activation(out=BT[:, g, :], in_=acc, func=AF.Silu), r=[ak], w=["BT"])
                        tb = 4 + (cc % 2)
                        for tc_ in range(4):
                            transpose(psb(tb)[:, tc_ * 128:(tc_ + 1) * 128], BT[:, g, tc_ * 128:(tc_ + 1) * 128], r=["BT"], w=["ps%d" % tb])
                        P.op("act", lambda e: e.activation(out=Btok[:, :, g * 128:(g + 1) * 128],
                                                           in_=psb(tb)[:, 0:512].rearrange("p (c n) -> p c n", c=4), func=AF.Copy),
                             r=["ps%d" % tb], w=["Btok"])
                    else:
                        P.op("act", lambda e: e.activation(out=CT[:, g, :], in_=acc, func=AF.Silu), r=[ak], w=["CT"])

                for half in range(2):
                    sched([(w_in_d[l], 0, 8, O_XBC + 2048 + half * 512, 512)],
                          lambda s, half=half: proj_fm(s[0], 4, 16 + half * 4, h_bc))

                def evac_dt(c, b):
                    P.op("dve", lambda e: e.tensor_tensor(out=dtt[:, c, :], in0=psum[:, b, 0:32], in1=lv("dtb"), op=OP.add),
                         r=["ps%d" % b, "lcs"], w=["dtt"])
                    P.op("act", lambda e: e.activation(out=dtt[:, c, :], in_=dtt[:, c, :], func=AF.Exp), r=["dtt"], w=["dtt"])
                    P.op("act", lambda e: e.activation(out=dtt[:, c, :], in_=dtt[:, c, :], func=AF.Ln, bias=1.0), r=["dtt"], w=["dtt"])
                    P.op("dve", lambda e: e.tensor_tensor(out=adt[:, c, :], in0=dtt[:, c, :], in1=negA[:, :], op=OP.mult),
                         r=["dtt", "negA"], w=["adt"])

                sched([(w_in_d[l], 0, 8, O_DT, 32)], lambda s: proj_tok(lhs_h, [8], s, 32, evac_dt, [0, 1, 2, 3], ["hT"]))

                _o = [19 * 1024]

                def _al(nbytes):
                    o = _o[0]
                    _o[0] += nbytes
                    return o

                TP = []
                for p_ in range(2):
                    d_ = dict(
                        xtok=aview(_al(4096), (128, 4, 512), BF16), zs=aview(_al(4096), (128, 4, 512), BF16),
                        Wp=aview(_al(4096), (128, 8, 128), F32), Lx=aview(_al(2048), (128, 8, 128), BF16),
                        MT=aview(_al(2048), (128, 8, 128), BF16), CBm=aview(_al(256), (128, 128), BF16),
                        Xdt=aview(_al(1024), (128, 8, 64), BF16), Xd2=aview(_al(1024), (128, 8, 64), BF16),
                        yo=aview(_al(2048), (128, 8, 64), F32), yy=aview(_al(2048), (128, 8, 64), F32),
                        hb=aview(_al(1024), (128, 512), BF16), sm2=aview(_al(256), (128, 64), F32),
                        btok_b=aview(_al(1024), (128, 512), BF16))
                    TP.append(d_)

                def make_hx(g):
                    p_ = g % 2
                    xtok = TP[p_]["xtok"]

                    def h_x(cc, b):
                        acc, ak = conv_fm(cc, b, "xcw", "xcb", 4, xtail, None, "xtail")
                        xa = xact[cc % 2]
                        xk = "xact%d" % (cc % 2)
                        P.op("act", lambda e: e.activation(out=xa, in_=acc, func=AF.Silu), r=[ak], w=[xk])
                        def post():
                            tb = 4 + (cc % 2)
                            for tc_ in range(4):
                                transpose(psb(tb)[:, tc_ * 128:(tc_ + 1) * 128], xa[:, tc_ * 128:(tc_ + 1) * 128], r=[xk], w=["ps%d" % tb])
                            j = cc % 4
                            P.op("act", lambda e: e.activation(out=xtok[:, :, j * 128:(j + 1) * 128],
                                                               in_=psb(tb)[:, 0:512].rearrange("p (c n) -> p c n", c=4), func=AF.Copy),
                                 r=["ps%d" % tb], w=["xtok%d" % p_])
                        return post
                    return h_x

                def make_evz(g):
                    p_ = g % 2
                    zs = TP[p_]["zs"]

                    def evac_z(c, b):
                        P.op("act", lambda e: e.activation(out=zs[:, c, :], in_=psum[:, b, :], func=AF.Silu), r=["ps%d" % b], w=["zs%d" % p_])
                    return evac_z

                def ssd_chain(g):
                    p_ = g % 2
                    t_ = TP[p_]
                    xtok, zs, Wp, Lx, MT, CBm, Xdt, Xd2, yo, yy, hb, sm2, btok_b = (
                        t_["xtok"], t_["zs"], t_["Wp"], t_["Lx"], t_["MT"], t_["CBm"], t_["Xdt"], t_["Xd2"], t_["yo"], t_["yy"],
                        t_["hb"], t_["sm2"], t_["btok_b"])
                    K_ = lambda n_: "%s%d" % (n_, p_)
                    b0, b1, b2, b3 = 4 * p_, 4 * p_ + 1, 4 * p_ + 2, 4 * p_ + 3
                    pk = lambda b: "ps%d" % b
                    hk = "hst%d" % g
                    hs3 = hst[:, g * 512:(g + 1) * 512].rearrange("p (e d) -> p e d", e=8)
                    for c in range(4):
                        ts_ = slice(c * 128, (c + 1) * 128)
                        xt3 = xtok[:, c, :].rearrange("p (e d) -> p e d", e=8)
                        dtg = dtt[:, c, g * 8:(g + 1) * 8]
                        adg = adt[:, c, g * 8:(g + 1) * 8]
                        P.op("dve", lambda e, xt3=xt3, dtg=dtg: e.tensor_tensor(out=Xdt, in0=xt3, in1=bc(dtg.unsqueeze(2), (128, 8, 64)), op=OP.mult),
                             r=[K_("xtok"), "dtt"], w=[K_("Xdt")])
                        P.op("dve", lambda e, adg=adg: e.tensor_tensor(out=Wp, in0=bc(cv("tri_le").unsqueeze(1), (128, 8, 128)),
                                                                       in1=bc(adg.unsqueeze(2), (128, 8, 128)), op=OP.mult),
                             r=["adt", "cst"], w=[K_("Wp")])
                        yield
                        for hh, bb in ((0, b0), (1, b1)):
                            mm(psum[:, bb, :], cv("sgt"), Wp[:, hh * 4:(hh + 1) * 4, :], True, True, r=["cst", K_("Wp")], w=[pk(bb)])
                            P.op("act", lambda e, hh=hh, bb=bb: e.activation(out=Lx[:, hh * 4:(hh + 1) * 4, :],
                                                                             in_=psum[:, bb, :].rearrange("p (e l) -> p e l", e=4), func=AF.Exp),
                                 r=[pk(bb)], w=[K_("Lx")])
                            P.op("act", lambda e, hh=hh, bb=bb: e.activation(out=sm2[:, hh * 4:(hh + 1) * 4],
                                                                             in_=psum[:, bb, :].rearrange("p (e l) -> p e l", e=4)[:, :, 127],
                                                                             func=AF.Exp), r=[pk(bb)], w=[K_("decay")])
                        yield
                        mm(psum[:, b2, 0:128], BT[:, g, ts_], CT[:, g, ts_], True, True, r=["BT", "CT"], w=[pk(b2)])
                        mm(psum[:, b2, 128:136], cv("tri_le"), adg, True, True, r=["cst", "adt"], w=[pk(b2)])
                        mm(psum[:, b2, 136:144], cv("ones"), adg, True, True, r=["cst", "adt"], w=[pk(b2)])
                        P.op("dve", lambda e: e.tensor_tensor(out=CBm, in0=psum[:, b2, 0:128], in1=cv("tri_le"), op=OP.mult),
                             r=[pk(b2), "cst"], w=[K_("CBm")])
                        P.op("act", lambda e: e.activation(out=sm2[:, 8:24], in_=psum[:, b2, 128:144], func=AF.Exp), r=[pk(b2)], w=[K_("eacs")])
                        yield
                        P.op("dve", lambda e: e.tensor_tensor(out=MT, in0=Lx, in1=bc(CBm.unsqueeze(1), (128, 8, 128)), op=OP.mult),
                             r=[K_("Lx"), K_("CBm")], w=[K_("MT")])
                        for e_ in range(8):
                            mm(psum[:, b3, e_ * 64:(e_ + 1) * 64], MT[:, e_, :], Xdt[:, e_, :], True, True, r=[K_("MT"), K_("Xdt")], w=[pk(b3)])
                        yield
                        P.op("act", lambda e: e.activation(out=hb, in_=hst[:, g * 512:(g + 1) * 512], func=AF.Copy), r=[hk], w=[K_("hb")])
                        mm(psum[:, b0, :], CT[:, g, ts_], hb, True, True, r=["CT", K_("hb")], w=[pk(b0)])
                        P.op("dve", lambda e: e.tensor_tensor(out=yo, in0=psum[:, b0, :].rearrange("p (e d) -> p e d", e=8),
                                                              in1=bc(sm2[:, 8:16].unsqueeze(2), (128, 8, 64)), op=OP.mult),
                             r=[pk(b0), K_("eacs")], w=[K_("yo")])
                        yield
                        P.op("dve", lambda e: e.tensor_tensor(out=yy, in0=psum[:, b3, :].rearrange("p (e d) -> p e d", e=8), in1=yo, op=OP.add),
                             r=[pk(b3), K_("yo")], w=[K_("yy")])
                        P.op("dve", lambda e, xt3=xt3: e.tensor_tensor(out=yo, in0=xt3, in1=bc(lv("dsk", g * 8, g * 8 + 8).unsqueeze(2), (128, 8, 64)),
                                                                       op=OP.mult), r=[K_("xtok"), "lcs"], w=[K_("yo")])
                        P.op("dve", lambda e: e.tensor_tensor(out=yy, in0=yy, in1=yo, op=OP.add), r=[K_("yy"), K_("yo")], w=[K_("yy")])
                        yield
                        P.op("dve", lambda e: e.tensor_tensor(out=Xd2, in0=Xdt, in1=bc(sm2[:, 0:8].unsqueeze(2), (128, 8, 64)), op=OP.mult),
                             r=[K_("Xdt"), K_("decay")], w=[K_("Xd2")])
                        mm(psum[:, b1, :], Btok[:, c, g * 128:(g + 1) * 128], Xd2.rearrange("p e d -> p (e d)"), True, True,
                           r=["Btok", K_("Xd2")], w=[pk(b1)])
                        P.op("dve", lambda e: e.tensor_tensor(out=hs3, in0=hs3, in1=bc(sm2[:, 16:24].unsqueeze(2), (128, 8, 64)), op=OP.mult),
                             r=[hk, K_("eacs")], w=[hk])
                        P.op("dve", lambda e: e.tensor_tensor(out=hs3, in0=psum[:, b1, :].rearrange("p (e d) -> p e d", e=8), in1=hs3, op=OP.add),
                             r=[pk(b1), hk], w=[hk])
                        yield
                        yf = yy.rearrange("p e d -> p (e d)")
                        P.op("dve", lambda e, yf=yf, c=c: e.tensor_tensor(out=yf, in0=yf, in1=zs[:, c, :], op=OP.mult), r=[K_("yy"), K_("zs")], w=[K_("yy")])
                        P.op("act", lambda e, yf=yf: e.activation(out=btok_b, in_=yf, func=AF.Square, accum_out=sm2[:, 24:25]),
                             r=[K_("yy")], w=[K_("btok_b"), K_("gss")])
                        P.op("act", lambda e: e.activation(out=sm2[:, 25:26], in_=sm2[:, 24:25], func=AF.Ln, bias=epsT[:, :], scale=1.0 / 512),
                             r=[K_("gss"), "epsT"], w=[K_("gss")])
                        P.op("act", lambda e: e.activation(out=sm2[:, 26:27], in_=sm2[:, 25:26], func=AF.Exp, scale=-0.5),
                             r=[K_("gss")], w=[K_("gss")])
                        yield
                        P.op("dve", lambda e, yf=yf: e.tensor_scalar(out=btok_b, in0=yf, scalar1=sm2[:, 26:27], scalar2=None, op0=OP.mult),
                             r=[K_("yy"), K_("gss")], w=[K_("btok_b")])
                        for j in range(4):
                            transpose(psb(b2)[:, j * 128:(j + 1) * 128], btok_b[:, j * 128:(j + 1) * 128], r=[K_("btok_b")], w=[pk(b2)])
                        P.op("dve", lambda e, ts_=ts_: e.tensor_tensor(out=bT[:, g * 4:(g + 1) * 4, ts_],
                                                                       in0=psb(b2)[:, 0:512].rearrange("p (j t) -> p j t", j=4),
                                                                       in1=bc(lv("snw", g * 4, g * 4 + 4).unsqueeze(2), (128, 4, 128)), op=OP.mult),
                             r=[pk(b2), "lcs"], w=["bT"])
                        yield

                def run_pair(g0):
                    gens = [ssd_chain(g0), ssd_chain(g0 + 1)]
                    while gens:
                        for gen in list(gens):
                            try:
                                next(gen)
                            except StopIteration:
                                gens.remove(gen)

                for g0 in (0, 2):
                    for g in (g0, g0 + 1):
                        sched([(w_in_d[l], 0, 8, O_XBC + g * 512, 512)], lambda s, g=g: proj_fm(s[0], 4, g * 4, make_hx(g)))
                        sched([(w_in_d[l], 0, 8, O_Z + g * 512, 512)],
                              lambda s, g=g: proj_tok(lhs_h, [8], s, 512, make_evz(g), [0, 1, 2, 3], ["hT"]))
                    sched([], lambda s, g0=g0: run_pair(g0))
                flush()
                if "bT" in dbg:
                    dd_ = dbg_dump("bT", bT[:, :, :], (NST, 128, 16, T), None)
                    P.dma(dd_[st_i], bT[:, :, :], r=["bT"])

            if stop >= 8:
                P.barrier()
                ga = aview(0, (128, 4, 512), BF16)
                gb = aview(4096, (128, 4, 512), BF16)
                mf = aview(8192, (128, 4, 512), F32)
                tmpf2 = [aview(16384, (128, 512), F32), aview(35840, (128, 512), F32)]
                mtok2 = [aview(18432, (128, 512), BF16), aview(37888, (128, 512), BF16)]
                mT = aview(19456, (128, 8, T), BF16)
                xh = aview(27648, (128, 4, 512), F32)
                for cs in range(2):
                    sched([(w_in_d[l], 0, 8, O_GA + cs * 512, 512)], lambda s: proj_tok(
                        lhs_h, [8], s, 512,
                        lambda c, b: P.op("act", lambda e: e.activation(out=ga[:, c, :], in_=psum[:, b, :], func=AF.Sigmoid),
                                          r=["ps%d" % b], w=["ga"]), [0, 1, 2, 3], ["hT"]))
                    sched([(w_in_d[l], 0, 8, O_GB + cs * 512, 512)], lambda s: proj_tok(
                        lhs_h, [8], s, 512,
                        lambda c, b: P.op("act", lambda e: e.activation(out=gb[:, c, :], in_=psum[:, b, :], func=AF.Sigmoid),
                                          r=["ps%d" % b], w=["gb"]), [0, 1, 2, 3], ["hT"]))
                    sched([(w_pa_d[l], 0, 8, cs * 512, 512)], lambda s: proj_tok(
                        lambda k, c: aT[:, k, c * 128:(c + 1) * 128], [8], s, 512,
                        lambda c, b: P.op("dve", lambda e: e.tensor_tensor(out=mf[:, c, :], in0=psum[:, b, :], in1=ga[:, c, :], op=OP.mult),
                                          r=["ps%d" % b, "ga"], w=["mf"]), [0, 1, 2, 3], ["aT"]))

                    def evac_pb(c, b, cs=cs):
                        tmpf = tmpf2[c % 2]
                        mtok = mtok2[c % 2]
                        tk_ = "tmpf%d" % (c % 2)
                        mk_ = "mtok%d" % (c % 2)
                        P.op("dve", lambda e: e.tensor_tensor(out=tmpf, in0=psum[:, b, :], in1=gb[:, c, :], op=OP.mult),
                             r=["ps%d" % b, "gb"], w=[tk_])
                        P.op("dve", lambda e: e.tensor_tensor(out=mtok, in0=tmpf, in1=mf[:, c, :], op=OP.add), r=[tk_, "mf"], w=[mk_])

                        def post():
                            tb = 6 + (c % 2)
                            for j in range(4):
                                transpose(psb(tb)[:, j * 128:(j + 1) * 128], mtok[:, j * 128:(j + 1) * 128], r=[mk_], w=["ps%d" % tb])
                            P.op("act", lambda e: e.activation(out=mT[:, cs * 4:(cs + 1) * 4, c * 128:(c + 1) * 128],
                                                               in_=psb(tb)[:, 0:512].rearrange("p (j t) -> p j t", j=4), func=AF.Copy),
                                 r=["ps%d" % tb], w=["mT"])
                        return post

                    sched([(w_pb_d[l], 0, 8, cs * 512, 512), (w_pb_d[l], 1024, 8, cs * 512, 512)],
                          lambda s, evac_pb=evac_pb: proj_tok(lambda k, c: bT[:, k, c * 128:(c + 1) * 128], [8, 8], s, 512, evac_pb,
                                                              [0, 1, 2, 3], ["bT"]))
                flush()
                for cs in range(2):
                    def out_task(s, cs=cs):
                        P.dma(xh, src_d[st_i * T:(st_i + 1) * T, cs * 512:(cs + 1) * 512].rearrange("(c p) n -> p c n", p=128),
                              r=[("xsrc%d" % l, st_i * 4 + c) for c in range(4)], w=["xh"])
                        proj_tok(lambda k, c: mT[:, k, c * 128:(c + 1) * 128], [8], s, 512,
                                 lambda c, b: P.op("dve", lambda e: e.tensor_tensor(out=xh[:, c, :], in0=psum[:, b, :], in1=xh[:, c, :], op=OP.add),
                                                   r=["ps%d" % b, "xh"], w=["xh"]), [0, 1, 2, 3], ["mT"])
                        P.dma(xmid_d[st_i * T:(st_i + 1) * T, cs * 512:(cs + 1) * 512].rearrange("(c p) n -> p c n", p=128), xh,
                              r=["xh"], w=[("xmid", st_i * 4 + c) for c in range(4)])
                    sched([(w_out_d[l], 0, 8, cs * 512, 512)], out_task)
                flush()

            if stop >= 9:
                P.barrier()
                xin_f = [aview(0, (128, D), F32), aview(4096, (128, D), F32)]
                hn_f = [aview(8192, (128, D), BF16), aview(63744, (128, D), BF16)]
                small_f = aview(10240, (128, 64), F32)
                xacc = [aview(10496 + 2048 * i, (128, 512), F32) for i in range(4)]
                sg = aview(18688, (128, 22, 512), BF16)
                xo = aview(41216, (128, 4, D), F32)
                nfin = aview(57600, (128, D), F32)
                osq = aview(61696, (128, D), BF16)
                rmsnorm_to_hT(xmid_d, st_i, "nffn", xin_f, hn_f, small_f, "xmid")

                def h_up(cc, b):
                    acc, ak = conv_fm(cc, b, "fcw", "fcb", 3, ftail, None, "ftail")
                    if cc < 22:
                        P.op("act", lambda e: e.activation(out=sg[:, cc, :], in_=acc, func=AF.Silu), r=[ak], w=[("sg", cc)])
                    else:
                        P.op("dve", lambda e: e.tensor_tensor(out=sg[:, cc - 22, :], in0=sg[:, cc - 22, :], in1=acc, op=OP.mult),
                             r=[ak, ("sg", cc - 22)], w=[("sg", cc - 22)])

                for sl in range(11):
                    sched([(w_up_d[l], 0, 8, sl * 512, 512)], lambda s, sl=sl: proj_fm(s[0], 4, sl * 4, h_up))
                sgk = [("sg", i) for i in range(22)]

                def down_task(slots, cs):
                    P.dma(xo[:, :, cs * 512:(cs + 1) * 512],
                          xmid_d[st_i * T:(st_i + 1) * T, cs * 512:(cs + 1) * 512].rearrange("(c p) n -> p c n", p=128),
                          r=[("xmid", st_i * 4 + c) for c in range(4)], w=["xo"])
                    for c in range(4):
                        kk = 0
                        for si in range(2):
                            for k in range(8):
                                mm(psum[:, c, :], sg[:, kk, c * 128:(c + 1) * 128], wbf[slots[si]][:, k, :], (kk == 0), False,
                                   r=sgk + ["wbf%d" % slots[si]], w=["ps%d" % c])
                                kk += 1
                    s2 = load_slab(w_dn_d[l], 2048, 6, cs * 512, 512)
                    for c in range(4):
                        for k in range(6):
                            mm(psum[:, c, :], sg[:, 16 + k, c * 128:(c + 1) * 128], wbf[s2][:, k, :], False, (k == 5),
                               r=sgk + ["wbf%d" % s2], w=["ps%d" % c])
                        P.op("dve", lambda e, c=c, cs=cs: e.tensor_tensor(out=xo[:, c, cs * 512:(cs + 1) * 512], in0=psum[:, c, :],
                                                                          in1=xo[:, c, cs * 512:(cs + 1) * 512], op=OP.add),
                             r=["ps%d" % c, "xo"], w=["xo"])

                for cs in range(2):
                    sched([(w_dn_d[l], 0, 8, cs * 512, 512), (w_dn_d[l], 1024, 8, cs * 512, 512)],
                          lambda s, cs=cs: down_task(s, cs), extra=1)
                flush()
                if not last:
                    P.dma(xl1_d[st_i * T:(st_i + 1) * T, :].rearrange("(c p) n -> p c n", p=128), xo, r=["xo"],
                          w=[("xsrc1", st_i * 4 + c) for c in range(4)])
                else:
                    P.dma(nfin, nfin_d[:, :], w=["nfin"])
                    for c in range(4):
                        ss = small_f[:, 32 + c:33 + c]
                        rt = small_f[:, 36 + c:37 + c]
                        rs = small_f[:, 40 + c:41 + c]
                        P.op("act", lambda e, c=c, ss=ss: e.activation(out=osq, in_=xo[:, c, :], func=AF.Square, accum_out=ss),
                             r=["xo"], w=["osq", "fsm%d" % c])
                        P.op("act", lambda e, ss=ss, rt=rt: e.activation(out=rt, in_=ss, func=AF.Ln, bias=epsT[:, :], scale=1.0 / D),
                             r=["fsm%d" % c, "epsT"], w=["fsm%d" % c])
                        P.op("act", lambda e, rt=rt, rs=rs: e.activation(out=rs, in_=rt, func=AF.Exp, scale=-0.5),
                             r=["fsm%d" % c], w=["fsm%d" % c])
                        P.op("dve", lambda e, c=c, rs=rs: e.scalar_tensor_tensor(out=xo[:, c, :], in0=xo[:, c, :], scalar=rs, in1=nfin,
                                                                                  op0=OP.mult, op1=OP.mult),
                             r=["xo", "fsm%d" % c, "nfin"], w=["xo"])
                    P.dma(out_d[st_i * T:(st_i + 1) * T, :].rearrange("(c p) n -> p c n", p=128), xo, r=["xo"])

    if "kT" in dbg:
        dd_ = dbg_dump("kT", kT[:, :, :], (128, NKV, S), None)
        P.dma(dd_, kT[:, :, :], r=[("kT", j) for j in range(NCH)])
    if "kiT" in dbg:
        dd_ = dbg_dump("kiT", kiT2[:, :], (128, S), None)
        P.dma(dd_, kiT2[:, :], r=[("kiT", j) for j in range(NCH)])
    P.emit()
    st.close()
    return nc


_NC_CACHE = {}


def kernel(**inputs):
    inp = {k: np.asarray(v) for k, v in inputs.items()}
    x = inp["x"].astype(np.float32, copy=False)
    B, S, _ = x.shape
    if S not in _NC_CACHE:
        _NC_CACHE[S] = build(S, layers=(0, 1))
    nc = _NC_CACHE[S]
    cst, tab = host_consts(S)
    lc = np.stack([host_layer_consts(inp, l) for l in range(2)])
    nfin = np.ascontiguousarray(np.broadcast_to(inp["norm_final_w"].astype(np.float32)[None, :], (128, D)))
    shared = {"w_in": np.ascontiguousarray(inp["w_in"], dtype=np.float32),
              "w_proj_attn": np.ascontiguousarray(inp["w_proj_attn"], dtype=np.float32),
              "w_proj_ssd": np.ascontiguousarray(inp["w_proj_ssd"], dtype=np.float32),
              "w_out": np.ascontiguousarray(inp["w_out"], dtype=np.float32),
              "ffn_w_up": np.ascontiguousarray(inp["ffn_w_up"], dtype=np.float32),
              "ffn_w_down": np.ascontiguousarray(inp["ffn_w_down"], dtype=np.float32),
              "cst": cst, "tab": tab, "lc": lc, "nfin": nfin}
    in_maps = [dict(shared, x=np.ascontiguousarray(x[b])) for b in range(B)]
    res = run_bass_kernel_spmd(nc, in_maps, core_ids=list(range(B)))
    return np.stack([np.asarray(r["out"], dtype=np.float32) for r in res.results], axis=0)
```
